# Optimizing a Trainium2 kernel written in Bass

```python
import math
import jax
import jax.numpy as jnp
from jax import lax
import numpy as np

D_MODEL = 1024
BATCH = 8
SEQ = 4096
DEPTH = 4

GRID_W = 64
CTX_LEN = 256
F32 = jnp.float32
EPS = 1e-6

BRANCH_W = 512
N_BRANCH = 3

S5_WIDTH = BRANCH_W
S5_GROUP = 16
S5_GROUPS = S5_WIDTH // S5_GROUP
S5_STATE = 64
S5_DT_MIN = 0.001
S5_DT_MAX = 0.1
S5_C_STD = 0.5

HEAD_DIM = 64
ATT_HEADS = BRANCH_W // HEAD_DIM
ATT_KV_HEADS = 2
ATT_WINDOW = 128
ATT_BLOCK = 128
ROPE_BASE = 10000.0

HG_HEADS = 4
HG_DK = 128
HG_DV = BRANCH_W // HG_HEADS
HG_K = HG_HEADS * HG_DK
HG_V = HG_HEADS * HG_DV
HG_CHUNK = 64

FFN_DIM = 2816
CONV_W = 3

Q_W = ATT_HEADS * HEAD_DIM
KV_W = ATT_KV_HEADS * HEAD_DIM
IN_SPLITS = (S5_WIDTH, Q_W, KV_W, KV_W, HG_K, HG_K, HG_K, HG_V, HG_V, N_BRANCH * D_MODEL)
IN_COLS = sum(IN_SPLITS)

kernel_name = 'hybrid_s5_swa_hgrn2_dit_trunk'


def rmsnorm(x, g):
    xf = x.astype(F32)
    y = xf * lax.rsqrt(jnp.mean(xf * xf, axis=-1, keepdims=True) + EPS)
    return (y * g.astype(F32)).astype(x.dtype)


def modulate(h, shift, scale):
    return h * (1.0 + scale) + shift


def split_columns(z):
    idx = []
    acc = 0
    for w in IN_SPLITS[:-1]:
        acc += w
        idx.append(acc)
    return jnp.split(z, idx, axis=-1)


def _flip(t, rev):
    return jnp.flip(t, axis=1) if rev else t


def axial_rope(L):
    rows = L // GRID_W
    row = jnp.repeat(jnp.arange(rows, dtype=F32), GRID_W)
    col = jnp.tile(jnp.arange(GRID_W, dtype=F32), rows)
    nf = HEAD_DIM // 4
    inv = ROPE_BASE ** (-jnp.arange(nf, dtype=F32) / nf)
    ang = jnp.concatenate([row[:, None] * inv, col[:, None] * inv], axis=-1)
    return jnp.cos(ang), jnp.sin(ang)


def apply_rope(x, cos, sin):
    xf = x.astype(F32)
    x1, x2 = jnp.split(xf, 2, axis=-1)
    c = cos[None, :, None, :]
    s = sin[None, :, None, :]
    return jnp.concatenate([x1 * c - x2 * s, x2 * c + x1 * s], axis=-1).astype(x.dtype)


def s5_discretise(lam_re, lam_im, log_dt, b_re, b_im):
    lr = lam_re.astype(F32)
    li = lam_im.astype(F32)
    dt = jnp.exp(log_dt.astype(F32))[:, None]
    mag = jnp.exp(dt * lr)
    ar = mag * jnp.cos(dt * li)
    ai = mag * jnp.sin(dt * li)
    den = lr * lr + li * li
    fr = ((ar - 1.0) * lr + ai * li) / den
    fi = (ai * lr - (ar - 1.0) * li) / den
    br = b_re.astype(F32)
    bi = b_im.astype(F32)
    bbr = fr[..., None] * br - fi[..., None] * bi
    bbi = fr[..., None] * bi + fi[..., None] * br
    return ar, ai, bbr, bbi


def _complex_affine_combine(e1, e2):
    a1r, a1i, b1r, b1i = e1
    a2r, a2i, b2r, b2i = e2
    ar = a2r * a1r - a2i * a1i
    ai = a2r * a1i + a2i * a1r
    br = a2r * b1r - a2i * b1i + b2r
    bi = a2r * b1i + a2i * b1r + b2i
    return ar, ai, br, bi


def s5_scan(u, ar, ai, bbr, bbi, h0r, h0i):
    bur = jnp.einsum('blgc,gnc->blgn', u, bbr)
    bui = jnp.einsum('blgc,gnc->blgn', u, bbi)
    if h0r is not None:
        bur = bur.at[:, 0].add(ar * h0r - ai * h0i)
        bui = bui.at[:, 0].add(ar * h0i + ai * h0r)
    L = u.shape[1]
    a_r = jnp.broadcast_to(ar[None, None], (1, L) + ar.shape)
    a_i = jnp.broadcast_to(ai[None, None], (1, L) + ai.shape)
    _, _, hr, hi = lax.associative_scan(_complex_affine_combine, (a_r, a_i, bur, bui), axis=1)
    return hr, hi


def s5_readout(hr, hi, c_re, c_im):
    return (jnp.einsum('blgn,gcn->blgc', hr, c_re.astype(F32))
            - jnp.einsum('blgn,gcn->blgc', hi, c_im.astype(F32)))


def s5_glu(y, w_glu, b_glu):
    B, L = y.shape[0], y.shape[1]
    y = jax.nn.gelu(y.reshape(B, L, S5_WIDTH))
    return y * jax.nn.sigmoid(y @ w_glu.astype(F32) + b_glu.astype(F32))


def s5_branch(u_c, u_l, p, with_ctx_out):
    B, Lc, L = u_c.shape[0], u_c.shape[1], u_l.shape[1]
    uc = u_c.astype(F32).reshape(B, Lc, S5_GROUPS, S5_GROUP)
    ul = u_l.astype(F32).reshape(B, L, S5_GROUPS, S5_GROUP)
    dsk = p['s5_d'].astype(F32).reshape(S5_GROUPS, S5_GROUP)
    y_l = ul * dsk
    y_c = uc * dsk
    for d in range(2):
        rev = d == 1
        ar, ai, bbr, bbi = s5_discretise(p['s5_lam_re'][d], p['s5_lam_im'][d], p['s5_log_dt'][d],
                                         p['s5_b_re'], p['s5_b_im'])
        hcr, hci = s5_scan(_flip(uc, rev), ar, ai, bbr, bbi, None, None)
        hlr, hli = s5_scan(_flip(ul, rev), ar, ai, bbr, bbi, hcr[:, -1], hci[:, -1])
        y_l = y_l + _flip(s5_readout(hlr, hli, p['s5_c_re'], p['s5_c_im']), rev)
        if with_ctx_out:
            y_c = y_c + _flip(s5_readout(hcr, hci, p['s5_c_re'], p['s5_c_im']), rev)
    out_l = s5_glu(y_l, p['s5_w_glu'], p['s5_b_glu'])
    out_c = s5_glu(y_c, p['s5_w_glu'], p['s5_b_glu']) if with_ctx_out else None
    return out_l, out_c


def latent_window_attention(q, k, v, k_ctx, v_ctx, sink):
    B, L = q.shape[0], q.shape[1]
    Lc = k_ctx.shape[1]
    nb = L // ATT_BLOCK
    grp = ATT_HEADS // ATT_KV_HEADS
    scale = HEAD_DIM ** -0.5
    qb = q.reshape(B, nb, ATT_BLOCK, ATT_KV_HEADS, grp, HEAD_DIM)
    pad = ((0, 0), (ATT_BLOCK, ATT_BLOCK), (0, 0), (0, 0))
    kp = jnp.pad(k, pad).reshape(B, nb + 2, ATT_BLOCK, ATT_KV_HEADS, HEAD_DIM)
    vp = jnp.pad(v, pad).reshape(B, nb + 2, ATT_BLOCK, ATT_KV_HEADS, HEAD_DIM)
    kb = jnp.concatenate([kp[:, :-2], kp[:, 1:-1], kp[:, 2:]], axis=2)
    vb = jnp.concatenate([vp[:, :-2], vp[:, 1:-1], vp[:, 2:]], axis=2)
    s_loc = jnp.einsum('bnqhgd,bnkhd->bnhgqk', qb, kb).astype(F32) * scale
    qi = jnp.arange(ATT_BLOCK)[:, None]
    kj = jnp.arange(3 * ATT_BLOCK)[None, :] - ATT_BLOCK
    kabs = jnp.arange(nb)[:, None, None] * ATT_BLOCK + kj[None]
    valid = (jnp.abs(kj - qi) <= ATT_WINDOW)[None] & (kabs >= 0) & (kabs < L)
    s_loc = jnp.where(valid[None, :, None, None], s_loc, -jnp.inf)
    s_ctx = jnp.einsum('bnqhgd,bchd->bnhgqc', qb, k_ctx).astype(F32) * scale
    s_snk = jnp.broadcast_to(sink.astype(F32).reshape(ATT_KV_HEADS, grp, 1, 1), s_loc.shape[:-1] + (1,))
    p = jax.nn.softmax(jnp.concatenate([s_snk, s_ctx, s_loc], axis=-1), axis=-1).astype(v.dtype)
    o = (jnp.einsum('bnhgqc,bchd->bnqhgd', p[..., 1:1 + Lc], v_ctx)
         + jnp.einsum('bnhgqk,bnkhd->bnqhgd', p[..., 1 + Lc:], vb))
    return o.reshape(B, L, ATT_HEADS * HEAD_DIM)


def context_attention(q, k, v, sink):
    B, Lc = q.shape[0], q.shape[1]
    grp = ATT_HEADS // ATT_KV_HEADS
    qg = q.reshape(B, Lc, ATT_KV_HEADS, grp, HEAD_DIM)
    s = jnp.einsum('bqhgd,bkhd->bhgqk', qg, k).astype(F32) * (HEAD_DIM ** -0.5)
    s_snk = jnp.broadcast_to(sink.astype(F32).reshape(ATT_KV_HEADS, grp, 1, 1), s.shape[:-1] + (1,))
    p = jax.nn.softmax(jnp.concatenate([s_snk, s], axis=-1), axis=-1).astype(v.dtype)
    o = jnp.einsum('bhgqk,bkhd->bqhgd', p[..., 1:], v)
    return o.reshape(B, Lc, ATT_HEADS * HEAD_DIM)


def hgrn2_lower_bounds(logits):
    pr = jax.nn.softmax(logits.astype(F32), axis=0)
    cs = jnp.cumsum(pr, axis=0)
    return cs - cs[:1]


def hgrn2_forget(z, lb):
    zf = z.astype(F32)
    lbf = lb.astype(F32)
    logf = jnp.logaddexp(jnp.log(lbf), jnp.log1p(-lbf) + jax.nn.log_sigmoid(zf))
    k = (1.0 - lbf) * jax.nn.sigmoid(-zf)
    return logf, k


def _heads(t, d):
    return t.reshape(t.shape[0], t.shape[1], HG_HEADS, d)


def hgrn2_chunk_scan(q, logf, k, v, s0, readout):
    B, L, H, dk = q.shape
    dv = v.shape[-1]
    nc = L // HG_CHUNK

    def chunks(t):
        return jnp.moveaxis(t.reshape(B, nc, HG_CHUNK, H, t.shape[-1]), 1, 0)

    tri = jnp.tril(jnp.ones((HG_CHUNK, HG_CHUNK), dtype=bool))[None, :, :, None, None]

    def step(S, xs):
        qc, gc, kc, vc = xs
        b = jnp.cumsum(gc, axis=1)
        b_end = b[:, -1]
        S_next = (S * jnp.exp(b_end)[..., None]
                  + jnp.einsum('bchk,bchv->bhkv', kc * jnp.exp(b_end[:, None] - b), vc))
        if not readout:
            return S_next, None
        o_inter = jnp.einsum('bchk,bhkv->bchv', qc * jnp.exp(b), S)
        decay = jnp.exp(jnp.where(tri, b[:, :, None] - b[:, None, :], -jnp.inf))
        att = jnp.einsum('bthk,bshk,btshk->bhts', qc, kc, decay)
        return S_next, o_inter + jnp.einsum('bhts,bshv->bthv', att, vc)

    if s0 is None:
        s0 = jnp.zeros((B, H, dk, dv), F32)
    s_fin, o = lax.scan(step, s0, (chunks(q), chunks(logf), chunks(k), chunks(v)))
    if readout:
        o = jnp.moveaxis(o, 0, 1).reshape(B, L, H, dv)
    return s_fin, o


def hgrn2_output(o, g, norm_g):
    B, L = o.shape[0], o.shape[1]
    o = o * lax.rsqrt(jnp.mean(o * o, axis=-1, keepdims=True) + EPS)
    o = o.reshape(B, L, HG_V) * norm_g.astype(F32)
    return o * jax.nn.sigmoid(g.astype(F32))


def hgrn2_branch(zc, zl, lb_fwd, lb_bwd, norm_g, with_ctx_out):
    q_c, ff_c, fb_c, i_c, g_c = zc
    q_l, ff_l, fb_l, i_l, g_l = zl
    qc = _heads(jax.nn.silu(q_c.astype(F32)), HG_DK)
    ql = _heads(jax.nn.silu(q_l.astype(F32)), HG_DK)
    vc = _heads(i_c.astype(F32), HG_DV)
    vl = _heads(i_l.astype(F32), HG_DV)
    o_l = []
    o_c = []
    for f_c, f_l, lb, rev in ((ff_c, ff_l, lb_fwd, False), (fb_c, fb_l, lb_bwd, True)):
        lfc, kc = hgrn2_forget(f_c, lb)
        lfl, kl = hgrn2_forget(f_l, lb)
        s_c, oc = hgrn2_chunk_scan(_flip(qc, rev), _flip(_heads(lfc, HG_DK), rev),
                                   _flip(_heads(kc, HG_DK), rev), _flip(vc, rev), None, with_ctx_out)
        _, ol = hgrn2_chunk_scan(_flip(ql, rev), _flip(_heads(lfl, HG_DK), rev),
                                 _flip(_heads(kl, HG_DK), rev), _flip(vl, rev), s_c, True)
        o_l.append(_flip(ol, rev))
        if with_ctx_out:
            o_c.append(_flip(oc, rev))
    y_l = hgrn2_output(o_l[0] + o_l[1], g_l, norm_g)
    y_c = hgrn2_output(o_c[0] + o_c[1], g_c, norm_g) if with_ctx_out else None
    return y_l, y_c


def merge_branches(ya, yb, yc, gate_logits, w_branch, w_out, dtype):
    y = jnp.stack([ya.astype(dtype), yb.astype(dtype), yc.astype(dtype)], axis=2)
    z = jnp.einsum('blnw,nwd->blnd', y, w_branch)
    g = jax.nn.sigmoid(gate_logits.reshape(z.shape).astype(F32))
    m = jnp.sum(g * z.astype(F32), axis=2).astype(dtype)
    return m @ w_out


def mixing_sublayer(hl, hc, p, lb_fwd, lb_bwd, cos, sin, with_ctx_out):
    B, L = hl.shape[0], hl.shape[1]
    Lc = hc.shape[1]
    s5_l, q_l, k_l, v_l, hq_l, hff_l, hfb_l, hi_l, hg_l, gate_l = split_columns(hl @ p['w_in'])
    s5_c, q_c, k_c, v_c, hq_c, hff_c, hfb_c, hi_c, hg_c, gate_c = split_columns(hc @ p['w_in'])
    ya_l, ya_c = s5_branch(s5_c, s5_l, p, with_ctx_out)
    kh_c = k_c.reshape(B, Lc, ATT_KV_HEADS, HEAD_DIM)
    vh_c = v_c.reshape(B, Lc, ATT_KV_HEADS, HEAD_DIM)
    qh_l = apply_rope(q_l.reshape(B, L, ATT_HEADS, HEAD_DIM), cos, sin)
    kh_l = apply_rope(k_l.reshape(B, L, ATT_KV_HEADS, HEAD_DIM), cos, sin)
    vh_l = v_l.reshape(B, L, ATT_KV_HEADS, HEAD_DIM)
    yb_l = latent_window_attention(qh_l, kh_l, vh_l, kh_c, vh_c, p['att_sink'])
    yc_l, yc_c = hgrn2_branch((hq_c, hff_c, hfb_c, hi_c, hg_c), (hq_l, hff_l, hfb_l, hi_l, hg_l),
                              lb_fwd, lb_bwd, p['hg_norm_g'], with_ctx_out)
    out_l = merge_branches(ya_l, yb_l, yc_l, gate_l, p['w_branch'], p['w_out'], hl.dtype)
    out_c = None
    if with_ctx_out:
        yb_c = context_attention(q_c.reshape(B, Lc, ATT_HEADS, HEAD_DIM), kh_c, vh_c, p['att_sink'])
        out_c = merge_branches(ya_c, yb_c, yc_c, gate_c, p['w_branch'], p['w_out'], hc.dtype)
    return out_l, out_c


def depthwise_conv(u, w, b):
    y = lax.conv_general_dilated(u, w[:, None, :].astype(u.dtype), window_strides=(1,), padding='SAME',
                                 dimension_numbers=('NWC', 'WIO', 'NWC'), feature_group_count=u.shape[-1])
    return y + b


def conv_ffn(h, w_up, conv_w, conv_b, w_down):
    u = depthwise_conv(h @ w_up, conv_w, conv_b)
    a, g = jnp.split(u, 2, axis=-1)
    return (jax.nn.silu(a) * g) @ w_down


def setup_inputs(seed: int = 0) -> dict:
    key = jax.random.key(seed)
    ks = jax.random.split(key, 28)
    D = D_MODEL

    def nrm(k, shape, scale):
        return jax.random.normal(k, shape, F32) * scale

    s5_shape = (DEPTH, 2, S5_GROUPS, S5_STATE)
    return {
        'x': nrm(ks[0], (BATCH, SEQ, D), 1.0),
        'c': nrm(ks[1], (BATCH, D), 1.0),
        'ctx': nrm(ks[2], (BATCH, CTX_LEN, D), 1.0),
        'c_ctx': nrm(ks[3], (D,), 1.0),
        'w_mod': nrm(ks[4], (DEPTH, D, 6 * D), 0.5 * D ** -0.5),
        'b_mod': nrm(ks[5], (DEPTH, 6 * D), 0.02),
        'norm_g': 1.0 + nrm(ks[6], (DEPTH, 4, D), 0.02),
        'w_in': nrm(ks[7], (DEPTH, D, IN_COLS), D ** -0.5),
        's5_lam_re': -0.5 + nrm(ks[8], s5_shape, 0.01),
        's5_lam_im': jnp.pi * jnp.arange(S5_STATE, dtype=F32) + nrm(ks[9], s5_shape, 0.01),
        's5_log_dt': jax.random.uniform(ks[10], (DEPTH, 2, S5_GROUPS), F32,
                                        math.log(S5_DT_MIN), math.log(S5_DT_MAX)),
        's5_b_re': nrm(ks[11], (DEPTH, S5_GROUPS, S5_STATE, S5_GROUP), (2 * S5_GROUP) ** -0.5),
        's5_b_im': nrm(ks[12], (DEPTH, S5_GROUPS, S5_STATE, S5_GROUP), (2 * S5_GROUP) ** -0.5),
        's5_c_re': nrm(ks[13], (DEPTH, S5_GROUPS, S5_GROUP, S5_STATE), S5_C_STD),
        's5_c_im': nrm(ks[14], (DEPTH, S5_GROUPS, S5_GROUP, S5_STATE), S5_C_STD),
        's5_d': nrm(ks[15], (DEPTH, S5_WIDTH), 1.0),
        's5_w_glu': nrm(ks[16], (DEPTH, S5_WIDTH, S5_WIDTH), S5_WIDTH ** -0.5),
        's5_b_glu': nrm(ks[17], (DEPTH, S5_WIDTH), 0.02),
        'att_sink': nrm(ks[18], (DEPTH, ATT_HEADS), 0.5),
        'hg_lb_logits': nrm(ks[19], (DEPTH, 2, HG_K), 0.5),
        'hg_norm_g': 1.0 + nrm(ks[20], (DEPTH, HG_V), 0.02),
        'w_branch': nrm(ks[21], (DEPTH, N_BRANCH, BRANCH_W, D), BRANCH_W ** -0.5),
        'w_out': nrm(ks[22], (DEPTH, D, D), D ** -0.5),
        'ffn_w_up': nrm(ks[23], (DEPTH, D, 2 * FFN_DIM), D ** -0.5),
        'ffn_conv_w': nrm(ks[24], (DEPTH, CONV_W, 2 * FFN_DIM), CONV_W ** -0.5),
        'ffn_conv_b': nrm(ks[25], (DEPTH, 2 * FFN_DIM), 0.02),
        'ffn_w_down': nrm(ks[26], (DEPTH, FFN_DIM, D), FFN_DIM ** -0.5),
    }


def reference(x, c, ctx, c_ctx, w_mod, b_mod, norm_g, w_in, s5_lam_re, s5_lam_im, s5_log_dt,
              s5_b_re, s5_b_im, s5_c_re, s5_c_im, s5_d, s5_w_glu, s5_b_glu, att_sink,
              hg_lb_logits, hg_norm_g, w_branch, w_out, ffn_w_up, ffn_conv_w, ffn_conv_b, ffn_w_down):
    L = x.shape[1]
    cos, sin = axial_rope(L)
    lb_all = hgrn2_lower_bounds(hg_lb_logits)
    xl = x
    xc = ctx
    for l in range(DEPTH):
        with_ctx_out = l < DEPTH - 1
        p = {
            'w_in': w_in[l], 's5_lam_re': s5_lam_re[l], 's5_lam_im': s5_lam_im[l],
            's5_log_dt': s5_log_dt[l], 's5_b_re': s5_b_re[l], 's5_b_im': s5_b_im[l],
            's5_c_re': s5_c_re[l], 's5_c_im': s5_c_im[l], 's5_d': s5_d[l],
            's5_w_glu': s5_w_glu[l], 's5_b_glu': s5_b_glu[l], 'att_sink': att_sink[l],
            'hg_norm_g': hg_norm_g[l], 'w_branch': w_branch[l], 'w_out': w_out[l],
        }
        mod_l = (jax.nn.silu(c) @ w_mod[l] + b_mod[l])[:, None, :]
        mod_c = jax.nn.silu(c_ctx) @ w_mod[l] + b_mod[l]
        sh1_l, sc1_l, g1_l, sh2_l, sc2_l, g2_l = jnp.split(mod_l, 6, axis=-1)
        sh1_c, sc1_c, g1_c, sh2_c, sc2_c, g2_c = jnp.split(mod_c, 6, axis=-1)
        hl = modulate(rmsnorm(xl, norm_g[l, 0]), sh1_l, sc1_l)
        hc = modulate(rmsnorm(xc, norm_g[l, 0]), sh1_c, sc1_c)
        ol, oc = mixing_sublayer(hl, hc, p, lb_all[l, 0], lb_all[l, 1], cos, sin, with_ctx_out)
        xl = xl + g1_l * rmsnorm(ol, norm_g[l, 1])
        hl = modulate(rmsnorm(xl, norm_g[l, 2]), sh2_l, sc2_l)
        xl = xl + g2_l * rmsnorm(conv_ffn(hl, ffn_w_up[l], ffn_conv_w[l], ffn_conv_b[l], ffn_w_down[l]),
                                 norm_g[l, 3])
        if with_ctx_out:
            xc = xc + g1_c * rmsnorm(oc, norm_g[l, 1])
            hc = modulate(rmsnorm(xc, norm_g[l, 2]), sh2_c, sc2_c)
            xc = xc + g2_c * rmsnorm(conv_ffn(hc, ffn_w_up[l], ffn_conv_w[l], ffn_conv_b[l], ffn_w_down[l]),
                                     norm_g[l, 3])
    return xl
```

```python
import numpy as np
import ml_dtypes
from contextlib import ExitStack
import concourse.bass as bass
import concourse.mybir as mybir
from concourse.bass_utils import run_bass_kernel_spmd

F32 = mybir.dt.float32
BF16 = mybir.dt.bfloat16
I32 = mybir.dt.int32
AF = mybir.ActivationFunctionType
ALU = mybir.AluOpType
PI = float(np.pi)


class Buf:
    __slots__ = ("w", "r")

    def __init__(self):
        self.w = None
        self.r = {}


class Prog:
    def __init__(self, nc, es, ndma=48):
        self.nc = nc
        self.eng = {"pe": nc.tensor, "act": nc.scalar, "dve": nc.vector, "pool": nc.gpsimd, "sp": nc.sync}
        self.sem = {}
        self.cnt = {}
        for k in self.eng:
            self.sem[k] = es.enter_context(nc.semaphore("s_" + k))
            self.cnt[k] = 0
        self.ndma = ndma
        for i in range(ndma):
            self.sem[("d", i)] = es.enter_context(nc.semaphore("s_d%d" % i))
            self.cnt[("d", i)] = 0
        self.known = {e: {} for e in self.eng}
        self.rr = 0
        self.nins = 0

    def _waits(self, e, r, w, extra=()):
        need = {}
        kn = self.known[e]

        def add(tok, same_ok):
            if tok is None:
                return
            k, v = tok
            if k == e and (e == "pe" or not same_ok):
                return
            if kn.get(k, 0) >= v:
                return
            if need.get(k, 0) < v:
                need[k] = v

        for b in r:
            add(b.w, True)
        for b in w:
            add(b.w, True)
            for t in b.r.values():
                add(t, False)
        for t in extra:
            add(t, True)
        E = self.eng[e]
        for k, v in need.items():
            E.wait_ge(self.sem[k], v)
            kn[k] = v
            self.nins += 1

    def op(self, e, fn, r=(), w=()):
        self._waits(e, r, w)
        ins = fn(self.eng[e])
        self.cnt[e] += 1
        ins.then_inc(self.sem[e], 1)
        self.nins += 1
        tok = (e, self.cnt[e])
        for b in r:
            b.r[e] = tok
        for b in w:
            b.w = tok
            b.r = {}
        return tok

    def dma(self, q, out, in_, r=(), w=()):
        i = self.rr
        self.rr = (i + 1) % self.ndma
        key = ("d", i)
        self._waits(q, r, w, extra=[(key, self.cnt[key])])
        ins = self.eng[q].dma_start(out=out, in_=in_)
        self.cnt[key] += 16
        ins.then_inc(self.sem[key], 16)
        self.nins += 1
        tok = (key, self.cnt[key])
        for b in r:
            b.r[key] = tok
        for b in w:
            b.w = tok
            b.r = {}
        return tok

    def barrier(self):
        toks = [(k, v) for k, v in self.cnt.items() if v > 0]
        for e, E in self.eng.items():
            kn = self.known[e]
            for k, v in toks:
                if kn.get(k, 0) < v:
                    E.wait_ge(self.sem[k], v)
                    kn[k] = v
                    self.nins += 1


class Tl:
    def __init__(self, t, nb=1):
        self.t = t
        self.bs = [Buf() for _ in range(nb)]

    @property
    def b(self):
        return self.bs[0]

    def __getitem__(self, k):
        return self.t[k]


DM = 1024
KC = 8
EPS = 1e-6
IN_COLS = 6912
C_S5, C_Q, C_K, C_V, C_HQ, C_FF, C_FB, C_HI, C_HG, C_GATE = 0, 512, 1024, 1152, 1280, 1792, 2304, 2816, 3328, 3840
FFN = 2816
NFC = 22
FULL = dict(L=4096, LC=256, DEPTH=4, taps=())


def build(cfg):
    L, LC, DEPTH = cfg["L"], cfg["LC"], cfg["DEPTH"]
    taps = set(cfg.get("taps", ()))
    stop_after = cfg.get("stop_after", None)
    NT = L + LC
    NB = NT // 128
    TILES = [(0, LC)] + [(LC + 512 * i, 512) for i in range(L // 512)]
    NTI = len(TILES)

    def tile_of(tok):
        for i, (t0, sz) in enumerate(TILES):
            if t0 <= tok < t0 + sz:
                return i
        raise ValueError

    nc = bass.Bass("TRN2", target_bir_lowering=False)

    def din(name, shape, dt=F32):
        return nc.dram_tensor(name, list(shape), dt, kind="ExternalInput").ap()

    def dscr(name, shape, dt):
        kind = "ExternalOutput" if name in taps else "Internal"
        return nc.dram_tensor(name, list(shape), dt, kind=kind).ap()

    x_in = din("x", [L, DM])
    ctx_in = din("ctx", [LC, DM])
    cT_in = din("cT", [128, KC, 2])
    w_mod = din("w_mod", [DEPTH, DM, 6 * DM])
    b_modT = din("b_modT", [128, DEPTH, 48])
    norm_gT = din("norm_gT", [128, DEPTH, 4, KC])
    w_in = din("w_in", [DEPTH, DM, IN_COLS])
    w_rot = din("w_rot", [DEPTH, DM, 640])
    ropeC_in = din("ropeC", [128, NT])
    ropeS_in = din("ropeS", [128, NT])
    s5_lr = din("s5_lr", [128, DEPTH, 2, 16])
    s5_li = din("s5_li", [128, DEPTH, 2, 16])
    s5_ldt = din("s5_ldt", [128, DEPTH, 2, 16])
    s5_Bre = din("s5_Bre", [DEPTH, 16, 128, 128])
    s5_Bim = din("s5_Bim", [DEPTH, 16, 128, 128])
    s5_Cre = din("s5_Cre", [DEPTH, 16, 128, 128])
    s5_Cim = din("s5_Cim", [DEPTH, 16, 128, 128])
    s5_dT = din("s5_dT", [128, DEPTH, 4])
    s5_bgT = din("s5_bgT", [128, DEPTH, 4])
    s5_wglu = din("s5_wglu", [DEPTH, 512, 512])
    sinkrow = din("sinkrow", [DEPTH, 1, 1024])
    hg_lbT = din("hg_lbT", [128, DEPTH, 8])
    hg_ngT = din("hg_ngT", [128, DEPTH, 4])
    w_branch = din("w_branch", [DEPTH, 3, 512, DM])
    w_out = din("w_out", [DEPTH, DM, DM])
    w_up = din("w_up", [DEPTH, DM, 2 * FFN])
    convT = din("convT", [128, DEPTH, 44, 3])
    convbT = din("convbT", [128, DEPTH, 44])
    w_down = din("w_down", [DEPTH, FFN, DM])
    c_ident = din("c_ident", [128, 128])
    c_maskP = din("c_maskP", [128, 512])
    c_maskN = din("c_maskN", [128, 512])
    c_iota = din("c_iota", [128, 2, 512])
    c_hgmask = din("c_hgmask", [32, 2, 128])
    c_rmask = din("c_rmask", [128, 512])
    out = nc.dram_tensor("out", [L, DM], F32, kind="ExternalOutput").ap()

    xT = dscr("xT", [KC, 128, NT], F32)
    zs5 = dscr("zs5", [4, 128, NT], BF16)
    qT = dscr("qT", [8, 64, NT], BF16)
    kT = dscr("kT", [2, 64, NT], BF16)
    vtm = dscr("vtm", [NT, 640], BF16)
    hgq = dscr("hgq", [4, 128, NT], BF16)
    lf = dscr("lf", [2, 4, 128, NT], F32)
    kk = dscr("kk", [2, 4, 128, NT], BF16)
    hgg = dscr("hgg", [4, 128, NT], BF16)
    gat = dscr("gat", [24, 128, NT], BF16)
    yT = dscr("yT", [12, 128, NT], BF16)
    ofw = dscr("ofw", [4, 128, NT], F32)
    ofw2 = dscr("ofw2", [4, 128, NT], F32)
    actT = dscr("actT", [NFC, 128, NT], BF16)
    dbg = dscr("dbg", [128, 4096], F32)
    d_xT, d_z, d_y, d_of, d_act = Buf(), Buf(), Buf(), Buf(), Buf()

    with ExitStack() as es:
        P = Prog(nc, es)
        op, dma = P.op, P.dma

        uid = [0]

        def SB(stack, name, shape, dt, nb=1):
            uid[0] += 1
            return Tl(stack.enter_context(nc.sbuf_tensor("%s_u%d" % (name, uid[0]), list(shape), dt)), nb)

        psum = [Tl(es.enter_context(nc.psum_tensor("ps%d" % i, [128, 512], F32))) for i in range(8)]
        psrr = [0, 0]
        nlong = [2]

        def nextps(long=False):
            if long:
                p = psum[psrr[1] % nlong[0]]
                psrr[1] += 1
            else:
                p = psum[nlong[0] + psrr[0] % (8 - nlong[0])]
                psrr[0] += 1
            return p

        def MM(o, l, r_, st, sp, rb, wb):
            op("pe", lambda E: E.matmul(o, l, r_, start=st, stop=sp), r=rb, w=wb)

        def TR(o, i, idn, rb, wb):
            op("pe", lambda E: E.transpose(o, i, idn), r=rb, w=wb)

        def ACT(o, i, f, rb, wb, bias=None, scale=None):
            kw = {}
            if bias is not None:
                kw["bias"] = bias
            if scale is not None:
                kw["scale"] = scale
            op("act", lambda E: E.activation(out=o, in_=i, func=f, **kw), r=rb, w=wb)

        def CP(e, o, i, rb, wb):
            if e == "act":
                op("act", lambda E: E.copy(out=o, in_=i), r=rb, w=wb)
            else:
                op(e, lambda E: E.tensor_copy(out=o, in_=i), r=rb, w=wb)

        def TT(e, o, a, b_, alu, rb, wb):
            op(e, lambda E: E.tensor_tensor(out=o, in0=a, in1=b_, op=alu), r=rb, w=wb)

        def TS(e, o, a, s1, s2, o0, o1, rb, wb):
            if s2 is None:
                op(e, lambda E: E.tensor_scalar(out=o, in0=a, scalar1=s1, scalar2=None, op0=o0), r=rb, w=wb)
            else:
                op(e, lambda E: E.tensor_scalar(out=o, in0=a, scalar1=s1, scalar2=s2, op0=o0, op1=o1), r=rb, w=wb)

        def STT(o, a, s, b_, o0, o1, rb, wb):
            op("dve", lambda E: E.scalar_tensor_tensor(out=o, in0=a, scalar=s, in1=b_, op0=o0, op1=o1), r=rb, w=wb)

        def SCAN(o, d0, d1, init, rb, wb):
            op("dve", lambda E: E.tensor_tensor_scan(out=o, data0=d0, data1=d1, initial=init, op0=ALU.mult, op1=ALU.add), r=rb, w=wb)

        def MSET(e, o, v, wb):
            op(e, lambda E: E.memset(o, v), w=wb)

        def DBG(c0, ap, n, tl):
            if "dbg" in taps:
                dma("pool", dbg[0:ap.shape[0], c0:c0 + n], ap, r=[tl.b])

        ident_f = SB(es, "ident_f", [128, 128], F32)
        ident_b = SB(es, "ident_b", [128, 128], BF16)
        ones_b = SB(es, "ones_b", [128, 128], BF16)
        neghalf = SB(es, "neghalf", [128, 512], F32)
        modv = SB(es, "modv", [128, DEPTH, 48, 2], F32)
        ngt = SB(es, "ngt", [128, DEPTH, 4, KC], F32)
        lbv = SB(es, "lbv", [128, DEPTH, 8], F32)
        omlb = SB(es, "omlb", [128, DEPTH, 8], F32)
        A1 = SB(es, "A1", [128, KC, 2], F32)
        G1 = SB(es, "G1", [128, KC, 2], F32)
        A2 = SB(es, "A2", [128, KC, 2], F32)
        G2 = SB(es, "G2", [128, KC, 2], F32)
        STAT = [ident_f.b, ident_b.b, ones_b.b, neghalf.b]

        dma("sp", ident_f[:], c_ident[:, :], w=[ident_f.b])
        CP("dve", ident_b[:], ident_f[:], [ident_f.b], [ident_b.b])
        MSET("pool", ones_b[:], 1.0, [ones_b.b])
        MSET("pool", neghalf[:], -0.5, [neghalf.b])
        epsb = SB(es, "epsb", [128, 1], F32)
        MSET("pool", epsb[:], EPS, [epsb.b])
        dma("sp", ngt[:], norm_gT[:, :, :, :], w=[ngt.b])

        with ExitStack() as ph:
            xr = [SB(ph, "xr%d" % i, [128, DM], F32) for i in range(2)]
            xtt = [SB(ph, "xtt%d" % i, [128, KC, 128], F32) for i in range(2)]
            for tb in range(NB):
                src = ctx_in[tb * 128:(tb + 1) * 128, :] if tb < LC // 128 else x_in[tb * 128 - LC:(tb + 1) * 128 - LC, :]
                a, o_ = xr[tb % 2], xtt[tb % 2]
                dma("sp", a[:], src, w=[a.b])
                pa, pb = nextps(), nextps()
                for kc in range(KC):
                    pp_ = pa if kc < 4 else pb
                    TR(pp_[:, (kc % 4) * 128:(kc % 4 + 1) * 128], a[:, kc * 128:(kc + 1) * 128], ident_f[:], [a.b, ident_f.b], [pp_.b])
                CP("act", o_[:, 0:4, :], pa[:].rearrange("p (k t) -> p k t", k=4), [pa.b], [o_.b])
                CP("dve", o_[:, 4:8, :], pb[:].rearrange("p (k t) -> p k t", k=4), [pb.b], [o_.b])
                dma("sp", xT[:, :, tb * 128:(tb + 1) * 128].rearrange("k p t -> p k t"), o_[:], r=[o_.b], w=[d_xT])
            cTt = SB(ph, "cTt", [128, KC, 2], F32)
            scb = SB(ph, "scb", [128, KC, 2], BF16)
            bmt = SB(ph, "bmt", [128, DEPTH, 48], F32)
            wm = [SB(ph, "wm%d" % i, [128, KC, 1024], BF16) for i in range(2)]
            dma("sp", cTt[:], cT_in[:, :, :], w=[cTt.b])
            dma("sp", bmt[:], b_modT[:, :, :], w=[bmt.b])
            ACT(scb[:], cTt[:], AF.Silu, [cTt.b], [scb.b])
            for l in range(DEPTH):
                pm = nextps()
                for grp in range(6):
                    wt = wm[(l * 6 + grp) % 2]
                    dma("pool", wt[:], w_mod[l, :, grp * 1024:(grp + 1) * 1024].rearrange("(k p) n -> p k n", p=128), w=[wt.b])
                    for j in range(8):
                        oc = grp * 8 + j
                        for kc in range(KC):
                            MM(pm[:, oc * 2:oc * 2 + 2], wt[:, kc, j * 128:(j + 1) * 128], scb[:, kc, :], kc == 0, kc == KC - 1, [wt.b, scb.b], [pm.b])
                TT("dve", modv[:, l, :, :], pm[:, 0:96].rearrange("p (c w) -> p c w", w=2),
                   bmt[:, l, :].unsqueeze(2).to_broadcast([128, 48, 2]), ALU.add, [pm.b, bmt.b], [modv.b])
            lg = SB(ph, "lg", [128, DEPTH, 8], F32)
            sm = SB(ph, "sm", [128, 8], F32)
            dma("sp", lg[:], hg_lbT[:, :, :], w=[lg.b])
            ACT(lg[:], lg[:], AF.Exp, [lg.b], [lg.b])
            CP("dve", sm[:], lg[:, 0, :], [lg.b], [sm.b])
            for l in range(1, DEPTH):
                TT("dve", sm[:], sm[:], lg[:, l, :], ALU.add, [sm.b, lg.b], [sm.b])
            op("dve", lambda E: E.reciprocal(out=sm[:], in_=sm[:]), r=[sm.b], w=[sm.b])
            MSET("dve", lbv[:, 0, :], 0.0, [lbv.b])
            for l in range(1, DEPTH):
                TT("dve", lg[:, l, :], lg[:, l, :], sm[:], ALU.mult, [lg.b, sm.b], [lg.b])
                TT("dve", lbv[:, l, :], lbv[:, l - 1, :], lg[:, l, :], ALU.add, [lbv.b, lg.b], [lbv.b])
            TS("dve", omlb[:], lbv[:], -1.0, 1.0, ALU.mult, ALU.add, [lbv.b], [omlb.b])
            P.barrier()

        def mod_scalars(l):
            for (Aq, sc0, gi) in ((A1, 8, 0), (A2, 32, 2)):
                TS("dve", Aq[:], modv[:, l, sc0:sc0 + 8, :], 1.0, None, ALU.add, None, [modv.b], [Aq.b])
                TT("dve", Aq[:], Aq[:], ngt[:, l, gi, :].unsqueeze(2).to_broadcast([128, KC, 2]), ALU.mult, [Aq.b, ngt.b], [Aq.b])
            for (Gq, g0, gi) in ((G1, 16, 1), (G2, 40, 3)):
                TT("dve", Gq[:], modv[:, l, g0:g0 + 8, :], ngt[:, l, gi, :].unsqueeze(2).to_broadcast([128, KC, 2]), ALU.mult, [modv.b, ngt.b], [Gq.b])

        def norm_phase(ph, l, Aq, sh0, hT):
            xts = [SB(ph, "nxt%d" % i, [128, KC, 512], F32) for i in range(2)]
            sq = SB(ph, "nsq", [128, KC, 512], BF16)
            rs = SB(ph, "nrs", [128, 512], F32)
            tmp = SB(ph, "ntmp", [128, KC, 512], F32)
            for ti, (t0, sz) in enumerate(TILES):
                w_ = 1 if ti == 0 else 0
                xt = xts[ti % 2]
                dma("sp", xt[:, :, 0:sz], xT[:, :, t0:t0 + sz].rearrange("k p t -> p k t"), r=[d_xT], w=[xt.b])
                ACT(sq[:, :, 0:sz], xt[:, :, 0:sz], AF.Square, [xt.b], [sq.b])
                ps = nextps()
                for kc in range(KC):
                    MM(ps[:, 0:sz], ones_b[:], sq[:, kc, 0:sz], kc == 0, kc == KC - 1, [ones_b.b, sq.b], [ps.b])
                ACT(rs[:, 0:sz], ps[:, 0:sz], AF.Ln, [ps.b, epsb.b], [rs.b], bias=epsb[:, 0:1], scale=1.0 / DM)
                ACT(rs[:, 0:sz], rs[:, 0:sz], AF.Exp, [rs.b], [rs.b], scale=-0.5)
                for kc in range(KC):
                    STT(tmp[:, kc, 0:sz], xt[:, kc, 0:sz], Aq[:, kc, w_:w_ + 1], rs[:, 0:sz], ALU.mult, ALU.mult, [xt.b, Aq.b, rs.b], [tmp.b])
                    ACT(hT[:, kc, t0:t0 + sz], tmp[:, kc, 0:sz], AF.Identity, [tmp.b, modv.b], [hT.bs[ti]],
                        bias=modv[:, l, sh0 + kc, w_:w_ + 1])

        def epilogue(ot, xt, Gq, ti, t0, sz, sq, rs, tmp):
            w_ = 1 if ti == 0 else 0
            ACT(sq[:, :, 0:sz], ot[:, :, 0:sz], AF.Square, [ot.b], [sq.b])
            ps = nextps()
            for kc in range(KC):
                MM(ps[:, 0:sz], ones_b[:], sq[:, kc, 0:sz], kc == 0, kc == KC - 1, [ones_b.b, sq.b], [ps.b])
            ACT(rs[:, 0:sz], ps[:, 0:sz], AF.Ln, [ps.b, epsb.b], [rs.b], bias=epsb[:, 0:1], scale=1.0 / DM)
            ACT(rs[:, 0:sz], rs[:, 0:sz], AF.Exp, [rs.b], [rs.b], scale=-0.5)
            for kc in range(KC):
                STT(tmp[:, kc, 0:sz], ot[:, kc, 0:sz], Gq[:, kc, w_:w_ + 1], rs[:, 0:sz], ALU.mult, ALU.mult, [ot.b, Gq.b, rs.b], [tmp.b])
            TT("pool", xt[:, :, 0:sz], xt[:, :, 0:sz], tmp[:, :, 0:sz], ALU.add, [xt.b, tmp.b], [xt.b])
            dma("sp", xT[:, :, t0:t0 + sz].rearrange("k p t -> p k t"), xt[:, :, 0:sz], r=[xt.b], w=[d_xT])

        for l in range(DEPTH):
            mod_scalars(l)
            with ExitStack() as ph:
                hT = SB(ph, "hT", [128, KC, NT], BF16, nb=NTI)
                with ExitStack() as ph1:
                    norm_phase(ph1, l, A1, 0, hT)
                    P.barrier()
                wts = [SB(ph, "wt%d" % i, [128, KC, 512], BF16) for i in range(3)]
                wrr = [0]
                stg = [SB(ph, "stg%d" % i, [128, NT], BF16) for i in range(3)]
                srr = [0]
                stgf = [SB(ph, "stgf%d" % i, [128, NT], F32) for i in range(2)]
                tmpa = [SB(ph, "tmpa%d" % i, [128, 512], F32) for i in range(2)]
                tmpb = [SB(ph, "tmpb%d" % i, [128, 512], F32) for i in range(2)]

                def load_w(src, ncols):
                    wt = wts[wrr[0] % 3]
                    wrr[0] += 1
                    dma("pool", wt[:, :, 0:ncols], src.rearrange("(k p) n -> p k n", p=128), w=[wt.b])
                    return wt

                def next_stg():
                    s = stg[srr[0] % 3]
                    srr[0] += 1
                    return s

                def proj(wt, off, M, cons):
                    for ti, (t0, sz) in enumerate(TILES):
                        ps = nextps()
                        for kc in range(KC):
                            MM(ps[0:M, 0:sz], wt[:, kc, off:off + M], hT[:, kc, t0:t0 + sz], kc == 0, kc == KC - 1, [wt.b, hT.bs[ti]], [ps.b])
                        cons(ti, t0, sz, ps)

                def simple_group(col0, nchunks, func, dst):
                    for g0 in range(0, nchunks, 4):
                        n = min(4, nchunks - g0)
                        wt = load_w(w_in[l, :, col0 + g0 * 128:col0 + (g0 + n) * 128], n * 128)
                        for c in range(n):
                            s = next_stg()

                            def cons(ti, t0, sz, ps, s=s):
                                if func is None:
                                    CP("act", s[:, t0:t0 + sz], ps[:, 0:sz], [ps.b], [s.b])
                                else:
                                    ACT(s[:, t0:t0 + sz], ps[:, 0:sz], func, [ps.b], [s.b])
                            proj(wt, c * 128, 128, cons)
                            dma("sp", dst[g0 + c], s[:], r=[s.b], w=[d_z])

                simple_group(C_S5, 4, None, zs5)
                with ExitStack() as phq:
                    ropeC = SB(phq, "ropeC", [128, NT], F32)
                    ropeS = SB(phq, "ropeS", [128, NT], F32)
                    dma("sp", ropeC[:], ropeC_in[:, :], w=[ropeC.b])
                    dma("sp", ropeS[:], ropeS_in[:, :], w=[ropeS.b])
                    for (cbase, rbase, nh_, dst) in ((C_Q, 0, 8, qT), (C_K, 512, 2, kT)):
                        for g0 in range(0, nh_, 4):
                            n = min(4, nh_ - g0)
                            wa = load_w(w_in[l, :, cbase + g0 * 64:cbase + (g0 + n) * 64], n * 64)
                            wb = load_w(w_rot[l, :, rbase + g0 * 64:rbase + (g0 + n) * 64], n * 64)
                            for c in range(n // 2):
                                s = next_stg()
                                for ti, (t0, sz) in enumerate(TILES):
                                    p1, p2 = nextps(), nextps()
                                    for kc in range(KC):
                                        MM(p1[:, 0:sz], wa[:, kc, c * 128:(c + 1) * 128], hT[:, kc, t0:t0 + sz], kc == 0, kc == KC - 1, [wa.b, hT.bs[ti]], [p1.b])
                                    for kc in range(KC):
                                        MM(p2[:, 0:sz], wb[:, kc, c * 128:(c + 1) * 128], hT[:, kc, t0:t0 + sz], kc == 0, kc == KC - 1, [wb.b, hT.bs[ti]], [p2.b])
                                    ta, tb_ = tmpa[ti % 2], tmpb[ti % 2]
                                    TT("dve", ta[:, 0:sz], p1[:, 0:sz], ropeC[:, t0:t0 + sz], ALU.mult, [p1.b, ropeC.b], [ta.b])
                                    TT("dve", tb_[:, 0:sz], p2[:, 0:sz], ropeS[:, t0:t0 + sz], ALU.mult, [p2.b, ropeS.b], [tb_.b])
                                    TT("pool", s[:, t0:t0 + sz], ta[:, 0:sz], tb_[:, 0:sz], ALU.add, [ta.b, tb_.b], [s.b])
                                h0 = g0 + 2 * c
                                dma("sp", dst[h0:h0 + 2].rearrange("h d t -> (h d) t"), s[:], r=[s.b], w=[d_z])
                    P.barrier()
                with ExitStack() as ph3:
                    wv = SB(ph3, "wv", [128, KC, 640], BF16)
                    vst = [SB(ph3, "vst%d" % i, [128, 640], BF16) for i in range(2)]
                    dma("pool", wv[:, :, 0:128], w_in[l, :, C_V:C_V + 128].rearrange("(k p) n -> p k n", p=128), w=[wv.b])
                    dma("pool", wv[:, :, 128:640], w_in[l, :, C_HI:C_HI + 512].rearrange("(k p) n -> p k n", p=128), w=[wv.b])
                    for tb in range(NB):
                        ti = tile_of(tb * 128)
                        pa, pb = nextps(), nextps()
                        for kc in range(KC):
                            MM(pa[:, 0:512], hT[:, kc, tb * 128:(tb + 1) * 128], wv[:, kc, 128:640], kc == 0, kc == KC - 1, [wv.b, hT.bs[ti]], [pa.b])
                        for kc in range(KC):
                            MM(pb[:, 0:128], hT[:, kc, tb * 128:(tb + 1) * 128], wv[:, kc, 0:128], kc == 0, kc == KC - 1, [wv.b, hT.bs[ti]], [pb.b])
                        v = vst[tb % 2]
                        CP("act", v[:, 0:128], pb[:, 0:128], [pb.b], [v.b])
                        CP("dve", v[:, 128:640], pa[:, 0:512], [pa.b], [v.b])
                        dma("sp", vtm[tb * 128:(tb + 1) * 128, :], v[:], r=[v.b], w=[d_z])
                simple_group(C_HQ, 4, AF.Silu, hgq)
                for d in range(2):
                    wt = load_w(w_in[l, :, C_FF + d * 512:C_FF + (d + 1) * 512], 512)
                    for c in range(4):
                        s = next_stg()
                        sf = stgf[c % 2]
                        li_ = d * 4 + c

                        def cons(ti, t0, sz, ps, s=s, sf=sf, li_=li_):
                            ta = tmpa[ti % 2]
                            ACT(ta[:, 0:sz], ps[:, 0:sz], AF.Exp, [ps.b], [ta.b], scale=-1.0)
                            TS("dve", ta[:, 0:sz], ta[:, 0:sz], 1.0, None, ALU.add, None, [ta.b], [ta.b])
                            op("dve", lambda E: E.reciprocal(out=ta[:, 0:sz], in_=ta[:, 0:sz]), r=[ta.b], w=[ta.b])
                            TS("dve", ta[:, 0:sz], ta[:, 0:sz], omlb[:, l, li_:li_ + 1], lbv[:, l, li_:li_ + 1], ALU.mult, ALU.add, [ta.b, omlb.b, lbv.b], [ta.b])
                            ACT(sf[:, t0:t0 + sz], ta[:, 0:sz], AF.Ln, [ta.b], [sf.b])
                            TS("dve", s[:, t0:t0 + sz], ta[:, 0:sz], -1.0, 1.0, ALU.mult, ALU.add, [ta.b], [s.b])
                        proj(wt, c * 128, 128, cons)
                        dma("sp", lf[d, c], sf[:], r=[sf.b], w=[d_z])
                        dma("sp", kk[d, c], s[:], r=[s.b], w=[d_z])
                simple_group(C_HG, 4, AF.Sigmoid, hgg)
                simple_group(C_GATE, 24, AF.Sigmoid, gat)
                P.barrier()
            if stop_after == "P2":
                break
            with ExitStack() as ph:
                uT = SB(ph, "uT", [128, 4, NT], BF16)
                yacc = SB(ph, "yacc", [128, NT], F32)
                for c in range(4):
                    dma("sp", uT[:, c, :], zs5[c], r=[d_z], w=[uT.b])
                y2T = uT
                sm_ = {n: SB(ph, "s5" + n, [128, 2, 16], F32) for n in
                       ("lr", "li", "dt", "th", "rho", "sn", "cs", "thr", "ar", "ai", "den", "fr", "fi", "t1", "t2", "tf", "dl", "rho8", "th8")}
                smi = SB(ph, "s5i", [128, 2, 16], I32)
                dma("sp", sm_["lr"][:], s5_lr[:, l, :, :], w=[sm_["lr"].b])
                dma("sp", sm_["li"][:], s5_li[:, l, :, :], w=[sm_["li"].b])
                dma("sp", sm_["dt"][:], s5_ldt[:, l, :, :], w=[sm_["dt"].b])

                def reduce_angle(src, dst, tf, ti_):
                    TS("dve", tf[:], src[:], 1.0 / (2 * PI), None, ALU.mult, None, [src.b], [tf.b])
                    CP("dve", ti_[:], tf[:], [tf.b], [ti_.b])
                    CP("dve", tf[:], ti_[:], [ti_.b], [tf.b])
                    STT(dst[:], tf[:], -2 * PI, src[:], ALU.mult, ALU.add, [tf.b, src.b], [dst.b])
                    TS("dve", dst[:], dst[:], -PI, PI, ALU.max, ALU.min, [dst.b], [dst.b])

                S = sm_
                ACT(S["dt"][:], S["dt"][:], AF.Exp, [S["dt"].b], [S["dt"].b])
                TT("dve", S["th"][:], S["dt"][:], S["li"][:], ALU.mult, [S["dt"].b, S["li"].b], [S["th"].b])
                TT("dve", S["dl"][:], S["dt"][:], S["lr"][:], ALU.mult, [S["dt"].b, S["lr"].b], [S["dl"].b])
                ACT(S["rho"][:], S["dl"][:], AF.Exp, [S["dl"].b], [S["rho"].b])
                reduce_angle(S["th"], S["thr"], S["t1"], smi)
                ACT(S["sn"][:], S["thr"][:], AF.Sin, [S["thr"].b], [S["sn"].b])
                TS("dve", S["t2"][:], S["thr"][:], PI / 2, None, ALU.add, None, [S["thr"].b], [S["t2"].b])
                reduce_angle(S["t2"], S["cs"], S["t1"], smi)
                ACT(S["cs"][:], S["cs"][:], AF.Sin, [S["cs"].b], [S["cs"].b])
                TT("dve", S["ar"][:], S["rho"][:], S["cs"][:], ALU.mult, [S["rho"].b, S["cs"].b], [S["ar"].b])
                TT("dve", S["ai"][:], S["rho"][:], S["sn"][:], ALU.mult, [S["rho"].b, S["sn"].b], [S["ai"].b])
                TT("dve", S["den"][:], S["lr"][:], S["lr"][:], ALU.mult, [S["lr"].b], [S["den"].b])
                TT("dve", S["t1"][:], S["li"][:], S["li"][:], ALU.mult, [S["li"].b], [S["t1"].b])
                TT("dve", S["den"][:], S["den"][:], S["t1"][:], ALU.add, [S["den"].b, S["t1"].b], [S["den"].b])
                op("dve", lambda E: E.reciprocal(out=S["den"][:], in_=S["den"][:]), r=[S["den"].b], w=[S["den"].b])
                TS("dve", S["ar"][:], S["ar"][:], -1.0, None, ALU.add, None, [S["ar"].b], [S["ar"].b])
                TT("dve", S["fr"][:], S["ar"][:], S["lr"][:], ALU.mult, [S["ar"].b, S["lr"].b], [S["fr"].b])
                TT("dve", S["t1"][:], S["ai"][:], S["li"][:], ALU.mult, [S["ai"].b, S["li"].b], [S["t1"].b])
                TT("dve", S["fr"][:], S["fr"][:], S["t1"][:], ALU.add, [S["fr"].b, S["t1"].b], [S["fr"].b])
                TT("dve", S["fr"][:], S["fr"][:], S["den"][:], ALU.mult, [S["fr"].b, S["den"].b], [S["fr"].b])
                TT("dve", S["fi"][:], S["ai"][:], S["lr"][:], ALU.mult, [S["ai"].b, S["lr"].b], [S["fi"].b])
                TT("dve", S["t1"][:], S["ar"][:], S["li"][:], ALU.mult, [S["ar"].b, S["li"].b], [S["t1"].b])
                TT("dve", S["fi"][:], S["fi"][:], S["t1"][:], ALU.subtract, [S["fi"].b, S["t1"].b], [S["fi"].b])
                TT("dve", S["fi"][:], S["fi"][:], S["den"][:], ALU.mult, [S["fi"].b, S["den"].b], [S["fi"].b])

                pwr = SB(ph, "pwr", [128, 9, 2, 16], F32)
                pwi = SB(ph, "pwi", [128, 9, 2, 16], F32)
                npwr = SB(ph, "npwr", [128, 9, 2, 16], F32)
                for tau in range(9):
                    TS("dve", S["t1"][:], S["thr"][:], float(tau), None, ALU.mult, None, [S["thr"].b], [S["t1"].b])
                    reduce_angle(S["t1"], S["t2"], S["tf"], smi)
                    ACT(S["sn"][:], S["t2"][:], AF.Sin, [S["t2"].b], [S["sn"].b])
                    TS("dve", S["t1"][:], S["t2"][:], PI / 2, None, ALU.add, None, [S["t2"].b], [S["t1"].b])
                    reduce_angle(S["t1"], S["cs"], S["tf"], smi)
                    ACT(S["cs"][:], S["cs"][:], AF.Sin, [S["cs"].b], [S["cs"].b])
                    TS("dve", S["t1"][:], S["dl"][:], float(tau), None, ALU.mult, None, [S["dl"].b], [S["t1"].b])
                    ACT(S["t1"][:], S["t1"][:], AF.Exp, [S["t1"].b], [S["t1"].b])
                    TT("dve", pwr[:, tau], S["t1"][:], S["cs"][:], ALU.mult, [S["t1"].b, S["cs"].b], [pwr.b])
                    TT("dve", pwi[:, tau], S["t1"][:], S["sn"][:], ALU.mult, [S["t1"].b, S["sn"].b], [pwi.b])
                TS("dve", npwr[:], pwr[:], -1.0, None, ALU.mult, None, [pwr.b], [npwr.b])
                TS("dve", S["t1"][:], S["dl"][:], 8.0, None, ALU.mult, None, [S["dl"].b], [S["t1"].b])
                ACT(S["rho8"][:], S["t1"][:], AF.Exp, [S["t1"].b], [S["rho8"].b])
                TS("dve", S["t1"][:], S["thr"][:], 8.0, None, ALU.mult, None, [S["thr"].b], [S["t1"].b])
                reduce_angle(S["t1"], S["th8"], S["tf"], smi)

                bp = [SB(ph, "bp%d" % i, [128, 2, 128], F32) for i in range(2)]
                bb = [[SB(ph, "bb%d_%d" % (d, pp), [128, 2, 128], F32) for pp in range(4)] for d in range(2)]
                cf = [SB(ph, "cf%d" % pp, [128, 2, 128], F32) for pp in range(4)]
                lhsC = [SB(ph, "lhsC%d" % pp, [128, 2, 128], BF16) for pp in range(4)]
                xs = [SB(ph, "xs%d" % i, [128, 3, 128], F32) for i in range(2)]
                Xb = [[SB(ph, "Xb%d_%d" % (i, pp), [128, 2, 128], BF16) for pp in range(4)] for i in range(2)]
                lhsP = SB(ph, "lhsP", [128, 8, 4, 2, 128], BF16)
                BD = SB(ph, "BD", [128, 8, 128], BF16)
                lhsQ = SB(ph, "lhsQ", [128, 8, 4, 2, 128], BF16)
                qs = [SB(ph, "qs%d" % i, [128, 2, 128], F32) for i in range(2)]
                diagD = SB(ph, "diagD", [128, 128], F32)
                sdT = SB(ph, "sdT", [128, 4], F32)
                dma("sp", sdT[:], s5_dT[:, l, :], w=[sdT.b])
                iot = SB(ph, "iot64", [128, 64], F32)
                dma("sp", iot[:], c_iota[:, 0, 0:64], w=[iot.b])
                a64 = [SB(ph, "a64_%d" % i, [128, 64], F32) for i in range(3)]
                a64i = SB(ph, "a64i", [128, 64], I32)
                tabC = SB(ph, "tabC", [128, 4, 64], F32)
                tabS = SB(ph, "tabS", [128, 4, 64], F32)
                tabN = SB(ph, "tabN", [128, 4, 64], F32)
                Vt = [SB(ph, "Vt%d" % i, [128, 4, 2, 64], F32) for i in range(2)]
                Wk = [{n: SB(ph, "wk%s%d" % (n, i), [128, 4, 64], F32) for n in ("m1", "m2", "m3", "m4", "gr", "gi")} for i in range(2)]
                Hre = SB(ph, "Hre", [128, 4, 65], F32)
                Him = SB(ph, "Him", [128, 4, 65], F32)
                Hb = [SB(ph, "Hb%d" % i, [128, 2, 4, 64], BF16) for i in range(2)]
                nlong[0] = 4
                vt = SB(ph, "vt", [128, NB, 128], BF16)
                dma("sp", vt[:], vtm[:, 0:128].rearrange("(b p) c -> p b c", p=128), r=[d_z], w=[vt.b])
                mP = SB(ph, "mP", [128, 512], BF16)
                mN = SB(ph, "mN", [128, 512], BF16)
                dma("pool", mP[:], c_maskP[:, :], w=[mP.b])
                dma("pool", mN[:], c_maskN[:, :], w=[mN.b])
                kT2 = SB(ph, "kT2", [64, 2, NT], BF16)
                dma("sp", kT2[:], kT[:, :, :].rearrange("h d t -> d h t"), r=[d_z], w=[kT2.b])
                srow = SB(ph, "srow", [1, 1024], F32)
                dma("sp", srow[:], sinkrow[l, :, :], w=[srow.b])
                sinkts = [SB(ph, "sinkt%d" % i, [128, 512], BF16) for i in range(2)]
                for kvh in range(2):
                    MSET("pool", sinkts[kvh][:], 0.0, [sinkts[kvh].b])
                    ACT(sinkts[kvh][0:1, :], srow[:, kvh * 512:(kvh + 1) * 512], AF.Exp, [srow.b], [sinkts[kvh].b])
                qblk = [SB(ph, "qblk%d" % i, [64, 4, 128], BF16) for i in range(3)]
                oblk = [SB(ph, "oblk%d" % i, [64, 4, 128], BF16) for i in range(3)]
                pts = [SB(ph, "pt%d" % i, [128, 512], BF16) for i in range(3)]
                rds = [SB(ph, "rd%d" % i, [64, 512], F32) for i in range(3)]
                att_steps = [(kvh, qb) for kvh in range(2) for qb in range(NB)]
                att_st = {"i": 0, "ipt": 0, "pend": []}

                def att_issue():
                    idx = att_st["i"]
                    if idx >= len(att_steps):
                        return
                    att_st["i"] += 1
                    kvh, qb = att_steps[idx]
                    qv = qblk[idx % 3]
                    dma("sp", qv[:], qT[4 * kvh:4 * kvh + 4, :, qb * 128:(qb + 1) * 128].rearrange("h d t -> d h t"), r=[d_z], w=[qv.b])
                    if qb < LC // 128:
                        keys = [(kt_, None) for kt_ in range(LC // 128)]
                    else:
                        n = qb - LC // 128
                        keys = [(kt_, None) for kt_ in range(LC // 128)]
                        if n - 1 >= 0:
                            keys.append((qb - 1, mP))
                        keys.append((qb, None))
                        if n + 1 < L // 128:
                            keys.append((qb + 1, mN))
                    pso, psd = nextps(long=True), nextps(long=True)
                    for i, (kt_, msk) in enumerate(keys):
                        pss = nextps()
                        MM(pss[:, :], kT2[:, kvh, kt_ * 128:(kt_ + 1) * 128], qv[:], True, msk is None, [kT2.b, qv.b], [pss.b])
                        if msk is not None:
                            MM(pss[:, :], ident_b[:], msk[:], False, True, [ident_b.b, msk.b], [pss.b])
                        pt = pts[att_st["ipt"] % 3]
                        att_st["ipt"] += 1
                        ACT(pt[:], pss[:, :], AF.Exp, [pss.b], [pt.b], scale=0.125)
                        MM(pso[0:64, :], vt[:, kt_, kvh * 64:(kvh + 1) * 64], pt[:], i == 0, i == len(keys) - 1, [vt.b, pt.b], [pso.b])
                        MM(psd[0:64, :], ones_b[:, 0:64], pt[:], i == 0, False, [ones_b.b, pt.b], [psd.b])
                    MM(psd[0:64, :], ones_b[:, 0:64], sinkts[kvh][:], False, True, [ones_b.b, sinkts[kvh].b], [psd.b])
                    rd = rds[idx % 3]
                    ACT(rd[:], psd[0:64, :], AF.Ln, [psd.b], [rd.b])
                    ACT(rd[:], rd[:], AF.Exp, [rd.b], [rd.b], scale=-1.0)
                    att_st["pend"].append((idx, kvh, qb, pso, rd))

                def att_finalize():
                    if not att_st["pend"]:
                        return
                    idx, kvh, qb, pso, rd = att_st["pend"].pop(0)
                    o_ = oblk[idx % 3]
                    TT("dve", o_[:], pso[0:64, :].rearrange("p (h q) -> p h q", h=4), rd[:].rearrange("p (h q) -> p h q", h=4),
                       ALU.mult, [pso.b, rd.b], [o_.b])
                    dma("sp", yT[4 + 2 * kvh:6 + 2 * kvh, :, qb * 128:(qb + 1) * 128].rearrange("c (two d) t -> d (c two) t", two=2), o_[:],
                        r=[o_.b], w=[d_y])

                it = 0
                for fc in range(4):
                    TS("dve", diagD[:], ident_f[:], sdT[:, fc:fc + 1], None, ALU.mult, None, [ident_f.b, sdT.b], [diagD.b])
                    for pp in range(4):
                        pr = fc * 4 + pp
                        b_ = bp[pp % 2]
                        dma("sp", b_[:, 0, :], s5_Bre[l, pr], w=[b_.b])
                        dma("sp", b_[:, 1, :], s5_Bim[l, pr], w=[b_.b])
                        dma("sp", cf[pp][:, 0, :], s5_Cre[l, pr], w=[cf[pp].b])
                        dma("sp", cf[pp][:, 1, :], s5_Cim[l, pr], w=[cf[pp].b])
                        CP("act", lhsC[pp][:, 0, :], cf[pp][:, 0, :], [cf[pp].b], [lhsC[pp].b])
                        TS("dve", lhsC[pp][:, 1, :], cf[pp][:, 1, :], -1.0, None, ALU.mult, None, [cf[pp].b], [lhsC[pp].b])
                        for d in range(2):
                            fr_ = S["fr"][:, d, pr:pr + 1]
                            fi_ = S["fi"][:, d, pr:pr + 1]
                            x_ = xs[d]
                            o_ = bb[d][pp]
                            TS("dve", x_[:, 0, :], b_[:, 1, :], fi_, None, ALU.mult, None, [b_.b, S["fi"].b], [x_.b])
                            STT(o_[:, 0, :], b_[:, 0, :], fr_, x_[:, 0, :], ALU.mult, ALU.subtract, [b_.b, S["fr"].b, x_.b], [o_.b])
                            TS("dve", x_[:, 1, :], b_[:, 0, :], fi_, None, ALU.mult, None, [b_.b, S["fi"].b], [x_.b])
                            STT(o_[:, 1, :], b_[:, 1, :], fr_, x_[:, 1, :], ALU.mult, ALU.add, [b_.b, S["fr"].b, x_.b], [o_.b])
                    for d in range(2):
                        for tau in range(8):
                            psb_t = nextps()
                            psb = psb_t[:].bitcast(BF16)
                            for pp in range(4):
                                pr = fc * 4 + pp
                                ar = pwr[:, tau, d, pr:pr + 1]
                                ai = pwi[:, tau, d, pr:pr + 1]
                                x_ = xs[(tau * 4 + pp) % 2]
                                o_ = bb[d][pp]
                                xb = Xb[tau % 2][pp]
                                TS("dve", x_[:, 1, :], o_[:, 1, :], ai, None, ALU.mult, None, [o_.b, pwi.b], [x_.b])
                                STT(x_[:, 0, :], o_[:, 0, :], ar, x_[:, 1, :], ALU.mult, ALU.subtract, [o_.b, pwr.b, x_.b], [x_.b])
                                TS("dve", x_[:, 2, :], o_[:, 0, :], ai, None, ALU.mult, None, [o_.b, pwi.b], [x_.b])
                                CP("act", xb[:, 0, :], x_[:, 0, :], [x_.b], [xb.b])
                                STT(x_[:, 1, :], o_[:, 1, :], ar, x_[:, 2, :], ALU.mult, ALU.add, [o_.b, pwr.b, x_.b], [x_.b])
                                CP("act", xb[:, 1, :], x_[:, 1, :], [x_.b], [xb.b])
                                for ri in range(2):
                                    TR(psb[:, (pp * 2 + ri) * 128:(pp * 2 + ri + 1) * 128], xb[:, ri, :], ident_b[:], [xb.b, ident_b.b], [psb_t.b])
                            CP("act", lhsP[:, tau, :, :, :], psb[:, 0:1024].rearrange("p (a b c) -> p a b c", a=4, b=2), [psb_t.b], [lhsP.b])
                            psd_ = nextps()
                            for pp in range(4):
                                for ri in range(2):
                                    MM(psd_[:, 0:128], Xb[tau % 2][pp][:, ri, :], lhsC[pp][:, ri, :], pp == 0 and ri == 0, pp == 3 and ri == 1,
                                       [Xb[tau % 2][pp].b, lhsC[pp].b], [psd_.b])
                            if tau == 0 and d == 0:
                                TT("dve", BD[:, tau, :], psd_[:, 0:128], diagD[:], ALU.add, [psd_.b, diagD.b], [BD.b])
                            else:
                                CP("act", BD[:, tau, :], psd_[:, 0:128], [psd_.b], [BD.b])
                        for t in range(8):
                            for pp in range(4):
                                pr = fc * 4 + pp
                                ar = pwr[:, t + 1, d, pr:pr + 1]
                                nar = npwr[:, t + 1, d, pr:pr + 1]
                                ai = pwi[:, t + 1, d, pr:pr + 1]
                                q_ = qs[(t * 4 + pp) % 2]
                                c_ = cf[pp]
                                TS("dve", q_[:, 0, :], c_[:, 1, :], ai, None, ALU.mult, None, [c_.b, pwi.b], [q_.b])
                                STT(lhsQ[:, t, pp, 0, :], c_[:, 0, :], ar, q_[:, 0, :], ALU.mult, ALU.subtract, [c_.b, pwr.b, q_.b], [lhsQ.b])
                                TS("dve", q_[:, 1, :], c_[:, 0, :], ai, None, ALU.mult, None, [c_.b, pwi.b], [q_.b])
                                STT(lhsQ[:, t, pp, 1, :], c_[:, 1, :], nar, q_[:, 1, :], ALU.mult, ALU.subtract, [c_.b, npwr.b, q_.b], [lhsQ.b])
                        for pp in range(4):
                            pr = fc * 4 + pp
                            a0, a1, a2 = a64
                            TS("dve", a0[:], iot[:], S["th8"][:, d, pr:pr + 1], None, ALU.mult, None, [iot.b, S["th8"].b], [a0.b])
                            reduce_angle(a0, a1, a2, a64i)
                            ACT(tabS[:, pp, :], a1[:], AF.Sin, [a1.b], [tabS.b])
                            TS("dve", a0[:], a1[:], PI / 2, None, ALU.add, None, [a1.b], [a0.b])
                            reduce_angle(a0, a1, a2, a64i)
                            ACT(tabC[:, pp, :], a1[:], AF.Sin, [a1.b], [tabC.b])
                        TS("dve", tabN[:], tabS[:], -1.0, None, ALU.mult, None, [tabS.b], [tabN.b])
                        tbs = [tabC.b, tabS.b, tabN.b]
                        order = list(range(NTI)) if d == 0 else [0] + list(range(NTI - 1, 0, -1))
                        MSET("dve", Hre[:, :, 0:1], 0.0, [Hre.b])
                        MSET("dve", Him[:, :, 0:1], 0.0, [Him.b])
                        for oi, ti in enumerate(order):
                            t0, sz = TILES[ti]
                            NJ = sz // 8
                            useq = uT[:, fc, t0:t0 + sz] if d == 0 else uT[:, fc, t0:t0 + sz][:, ::-1]
                            us = [useq[:, s_::8] for s_ in range(8)]
                            V = Vt[it % 2]
                            W = Wk[it % 2]
                            hb = Hb[it % 2]
                            it += 1
                            att_finalize()
                            pv = nextps()
                            pvv = pv[:].rearrange("p (a b c) -> p a b c", a=4, b=2)
                            for pp in range(4):
                                for ri in range(2):
                                    for s_ in range(8):
                                        MM(pvv[:, pp, ri, 0:NJ], lhsP[:, 7 - s_, pp, ri, :], us[s_], s_ == 0, s_ == 7, [lhsP.b, uT.b], [pv.b])
                            CP("act", V[:, :, :, 0:NJ], pvv[:, :, :, 0:NJ], [pv.b], [V.b])
                            tC, tS, tN = tabC[:, :, 0:NJ], tabS[:, :, 0:NJ], tabN[:, :, 0:NJ]
                            vre, vim = V[:, :, 0, 0:NJ], V[:, :, 1, 0:NJ]
                            TT("dve", W["m1"][:, :, 0:NJ], vre, tC, ALU.mult, [V.b] + tbs, [W["m1"].b])
                            TT("dve", W["m2"][:, :, 0:NJ], vim, tS, ALU.mult, [V.b] + tbs, [W["m2"].b])
                            TT("dve", W["m1"][:, :, 0:NJ], W["m1"][:, :, 0:NJ], W["m2"][:, :, 0:NJ], ALU.add, [W["m1"].b, W["m2"].b], [W["m1"].b])
                            TT("pool", W["m3"][:, :, 0:NJ], vim, tC, ALU.mult, [V.b] + tbs, [W["m3"].b])
                            TT("pool", W["m4"][:, :, 0:NJ], vre, tN, ALU.mult, [V.b] + tbs, [W["m4"].b])
                            TT("pool", W["m3"][:, :, 0:NJ], W["m3"][:, :, 0:NJ], W["m4"][:, :, 0:NJ], ALU.add, [W["m3"].b, W["m4"].b], [W["m3"].b])
                            for pp in range(4):
                                pr = fc * 4 + pp
                                rho_b = S["rho8"][:, d, pr:pr + 1].to_broadcast([128, NJ])
                                SCAN(W["gr"][:, pp, 0:NJ], rho_b, W["m1"][:, pp, 0:NJ], Hre[:, pp, 0:1], [W["m1"].b, S["rho8"].b, Hre.b], [W["gr"].b])
                                SCAN(W["gi"][:, pp, 0:NJ], rho_b, W["m3"][:, pp, 0:NJ], Him[:, pp, 0:1], [W["m3"].b, S["rho8"].b, Him.b], [W["gi"].b])
                            TT("dve", W["m2"][:, :, 0:NJ], W["gr"][:, :, 0:NJ], tC, ALU.mult, [W["gr"].b] + tbs, [W["m2"].b])
                            TT("dve", W["m4"][:, :, 0:NJ], W["gi"][:, :, 0:NJ], tN, ALU.mult, [W["gi"].b] + tbs, [W["m4"].b])
                            TT("dve", Hre[:, :, 1:NJ + 1], W["m2"][:, :, 0:NJ], W["m4"][:, :, 0:NJ], ALU.add, [W["m2"].b, W["m4"].b], [Hre.b])
                            TT("pool", W["m1"][:, :, 0:NJ], W["gi"][:, :, 0:NJ], tC, ALU.mult, [W["gi"].b] + tbs, [W["m1"].b])
                            TT("pool", W["m3"][:, :, 0:NJ], W["gr"][:, :, 0:NJ], tS, ALU.mult, [W["gr"].b] + tbs, [W["m3"].b])
                            TT("pool", Him[:, :, 1:NJ + 1], W["m1"][:, :, 0:NJ], W["m3"][:, :, 0:NJ], ALU.add, [W["m1"].b, W["m3"].b], [Him.b])
                            CP("pool", hb[:, 0, :, 0:NJ], Hre[:, :, 0:NJ], [Hre.b], [hb.b])
                            CP("pool", hb[:, 1, :, 0:NJ], Him[:, :, 0:NJ], [Him.b], [hb.b])
                            att_issue()
                            py = nextps(long=True)
                            pyv = py[:].rearrange("p (t j) -> p t j", t=8)
                            for t in range(8):
                                nmm = (t + 1) + 8
                                imm = 0
                                for s_ in range(t + 1):
                                    imm += 1
                                    MM(pyv[:, t, 0:NJ], BD[:, t - s_, :], us[s_], imm == 1, imm == nmm, [BD.b, uT.b], [py.b])
                                for pp in range(4):
                                    for ri in range(2):
                                        imm += 1
                                        MM(pyv[:, t, 0:NJ], lhsQ[:, t, pp, ri, :], hb[:, ri, pp, 0:NJ], imm == 1, imm == nmm, [lhsQ.b, hb.b], [py.b])
                            yv = yacc[:, t0:t0 + sz] if d == 0 else yacc[:, t0:t0 + sz][:, ::-1]
                            yv = yv.rearrange("p (j t) -> p t j", t=8)
                            if d == 0:
                                CP("act", yv, pyv[:, :, 0:NJ], [py.b], [yacc.b])
                            else:
                                TT("dve", yv, pyv[:, :, 0:NJ], yv, ALU.add, [py.b, yacc.b], [yacc.b])
                            CP("dve", Hre[:, :, 0:1], Hre[:, :, NJ:NJ + 1], [Hre.b], [Hre.b])
                            CP("dve", Him[:, :, 0:1], Him[:, :, NJ:NJ + 1], [Him.b], [Him.b])
                    ACT(y2T[:, fc, :], yacc[:], AF.Gelu_apprx_tanh, [yacc.b], [y2T.b])
                while att_st["i"] < len(att_steps) or att_st["pend"]:
                    att_finalize()
                    att_issue()
                wg = SB(ph, "wg", [128, 4, 512], BF16)
                bg = SB(ph, "bg", [128, 4], F32)
                gs = [SB(ph, "gs%d" % i, [128, 512], F32) for i in range(2)]
                yst = [SB(ph, "yst%d" % i, [128, 512], BF16) for i in range(2)]
                dma("pool", wg[:], s5_wglu[l].rearrange("(k p) n -> p k n", p=128), w=[wg.b])
                dma("sp", bg[:], s5_bgT[:, l, :], w=[bg.b])
                for co in range(4):
                    for ti, (t0, sz) in enumerate(TILES):
                        ys = yst[ti % 2]
                        ps = nextps()
                        for k in range(4):
                            MM(ps[:, 0:sz], wg[:, k, co * 128:(co + 1) * 128], y2T[:, k, t0:t0 + sz], k == 0, k == 3, [wg.b, y2T.b], [ps.b])
                        g_ = gs[ti % 2]
                        ACT(g_[:, 0:sz], ps[:, 0:sz], AF.Sigmoid, [ps.b, bg.b], [g_.b], bias=bg[:, co:co + 1])
                        TT("dve", ys[:, 0:sz], y2T[:, co, t0:t0 + sz], g_[:, 0:sz], ALU.mult, [y2T.b, g_.b], [ys.b])
                        dma("sp", yT[co, :, t0:t0 + sz], ys[:, 0:sz], r=[ys.b], w=[d_y])
                P.barrier()
                nlong[0] = 2
            if stop_after == "S5":
                break
            if stop_after == "ATT":
                break
            with ExitStack() as ph:
                hgm = SB(ph, "hgm", [32, 2, 128], F32)
                rmask = SB(ph, "rmask", [128, 512], F32)
                hng = SB(ph, "hng", [128, 4], F32)
                dma("sp", hgm[:], c_hgmask[:, :, :], w=[hgm.b])
                dma("sp", rmask[:], c_rmask[:, :], w=[rmask.b])
                dma("sp", hng[:], hg_ngT[:, l, :], w=[hng.b])
                D2 = range(2)
                Sf = [SB(ph, "Sf%d" % d, [128, 4, 128], F32) for d in D2]
                Sb = [SB(ph, "Sb%d" % d, [128, 4, 128], BF16) for d in D2]
                lfts = [SB(ph, "lft%d" % d, [128, 4, 512], F32) for d in D2]
                kkts = [SB(ph, "kkt%d" % d, [128, 4, 512], BF16) for d in D2]
                hqts = [SB(ph, "hqt%d" % d, [128, 4, 512], BF16) for d in D2]
                vchs = [SB(ph, "vch%d" % d, [32, 16, 512], BF16) for d in D2]
                bts = [SB(ph, "hbt%d" % d, [128, 4, 512], F32) for d in D2]
                e1s = [SB(ph, "he1%d" % d, [128, 4, 512], F32) for d in D2]
                e2s = [SB(ph, "he2%d" % d, [128, 4, 512], F32) for d in D2]
                qts = [SB(ph, "hqt_%d" % d, [128, 4, 512], BF16) for d in D2]
                kts = [SB(ph, "hkt_%d" % d, [128, 4, 512], BF16) for d in D2]
                khs = [SB(ph, "hkh_%d" % d, [128, 4, 512], BF16) for d in D2]
                ots = [SB(ph, "hot%d" % d, [128, 4, 512], F32) for d in D2]
                attm = [[SB(ph, "attm%d_%d" % (d, i), [32, 128], BF16) for i in range(2)] for d in D2]
                ktm = [[SB(ph, "ktm%d_%d" % (d, i), [32, 512], BF16) for i in range(2)] for d in D2]
                orders = [list(range(NTI)), [0] + list(range(NTI - 1, 0, -1))]
                for d in D2:
                    MSET("pool", Sf[d][:], 0.0, [Sf[d].b])
                    MSET("pool", Sb[d][:], 0.0, [Sb[d].b])
                ich = [0, 0]

                def hg_setup(d, ti):
                    t0, sz = TILES[ti]
                    nch = sz // 32
                    lft, kkt, hqt, vch, bt, e1, e2, qt, kt, kh = lfts[d], kkts[d], hqts[d], vchs[d], bts[d], e1s[d], e2s[d], qts[d], kts[d], khs[d]
                    dma("sp", lft[:, :, 0:sz], lf[d, :, :, t0:t0 + sz].rearrange("h p t -> p h t"), r=[d_z], w=[lft.b])
                    dma("sp", kkt[:, :, 0:sz], kk[d, :, :, t0:t0 + sz].rearrange("h p t -> p h t"), r=[d_z], w=[kkt.b])
                    dma("sp", hqt[:, :, 0:sz], hgq[:, :, t0:t0 + sz].rearrange("h p t -> p h t"), r=[d_z], w=[hqt.b])
                    dma("sp", vch[:, 0:nch, :], vtm[t0:t0 + sz, 128:640].rearrange("(c p) f -> p c f", p=32), r=[d_z], w=[vch.b])
                    for h in range(4):
                        if d == 0:
                            SCAN(bt[:, h, 0:sz], rmask[:, 0:sz], lft[:, h, 0:sz], 0.0, [rmask.b, lft.b], [bt.b])
                        else:
                            SCAN(bt[:, h, 0:sz][:, ::-1], rmask[:, 0:sz], lft[:, h, 0:sz][:, ::-1], 0.0, [rmask.b, lft.b], [bt.b])
                    jl0 = 31 if d == 0 else 0
                    b4 = bt[:, :, 0:sz].rearrange("p h (c t) -> p h c t", t=32)
                    TT("dve", e2[:, :, 0:sz].rearrange("p h (c t) -> p h c t", t=32), b4[:, :, :, jl0:jl0 + 1].to_broadcast([128, 4, nch, 32]), b4,
                       ALU.subtract, [bt.b], [e2.b])
                    ACT(e1[:, :, 0:sz], bt[:, :, 0:sz], AF.Exp, [bt.b], [e1.b], scale=-1.0)
                    ACT(e2[:, :, 0:sz], e2[:, :, 0:sz], AF.Exp, [e2.b], [e2.b])
                    ACT(bt[:, :, 0:sz], bt[:, :, 0:sz], AF.Exp, [bt.b], [bt.b])
                    TT("dve", qt[:, :, 0:sz], hqt[:, :, 0:sz], bt[:, :, 0:sz], ALU.mult, [hqt.b, bt.b], [qt.b])
                    TT("pool", kt[:, :, 0:sz], kkt[:, :, 0:sz], e1[:, :, 0:sz], ALU.mult, [kkt.b, e1.b], [kt.b])
                    TT("dve", kh[:, :, 0:sz], kkt[:, :, 0:sz], e2[:, :, 0:sz], ALU.mult, [kkt.b, e2.b], [kh.b])

                def hg_chunk(d, ci):
                    vch, bt, qt, kt, kh, ot = vchs[d], bts[d], qts[d], kts[d], khs[d], ots[d]
                    c0 = ci * 32
                    am, km = attm[d][ich[d] % 2], ktm[d][ich[d] % 2]
                    ich[d] += 1
                    psA = nextps()
                    for h in range(4):
                        MM(psA[0:32, h * 32:(h + 1) * 32], kt[:, h, c0:c0 + 32], qt[:, h, c0:c0 + 32], True, True, [kt.b, qt.b], [psA.b])
                    TT("dve", am[:], psA[0:32, 0:128], hgm[:, d, :], ALU.mult, [psA.b, hgm.b], [am.b])
                    psT = nextps()
                    psTb = psT[:].bitcast(BF16)
                    for h in range(4):
                        TR(psTb[0:32, h * 128:(h + 1) * 128], kh[:, h, c0:c0 + 32], ident_b[:], [kh.b, ident_b.b], [psT.b])
                    CP("act", km[:], psTb[0:32, 0:512], [psT.b], [km.b])
                    psO = nextps()
                    for h in range(4):
                        MM(psO[:, h * 32:(h + 1) * 32], vch[:, ci, h * 128:(h + 1) * 128], am[:, h * 32:(h + 1) * 32], True, False, [vch.b, am.b], [psO.b])
                        MM(psO[:, h * 32:(h + 1) * 32], Sb[d][:, h, :], qt[:, h, c0:c0 + 32], False, True, [Sb[d].b, qt.b], [psO.b])
                    CP("act", ot[:, :, c0:c0 + 32], psO[:, 0:128].rearrange("p (h t) -> p h t", h=4), [psO.b], [ot.b])
                    psS = nextps()
                    for h in range(4):
                        MM(psS[:, h * 128:(h + 1) * 128], km[:, h * 128:(h + 1) * 128], vch[:, ci, h * 128:(h + 1) * 128], True, True, [km.b, vch.b], [psS.b])
                    jl = c0 + 31 if d == 0 else c0
                    for h in range(4):
                        STT(Sf[d][:, h, :], Sf[d][:, h, :], bt[:, h, jl:jl + 1], psS[:, h * 128:(h + 1) * 128], ALU.mult, ALU.add, [Sf[d].b, bt.b, psS.b], [Sf[d].b])
                    CP("act", Sb[d][:], Sf[d][:], [Sf[d].b], [Sb[d].b])

                obw = ofw2
                for oi in range(NTI):
                    for d in D2:
                        hg_setup(d, orders[d][oi])
                    nch = TILES[orders[0][oi]][1] // 32
                    for k_ in range(nch):
                        for d in D2:
                            hg_chunk(d, k_ if d == 0 else nch - 1 - k_)
                    for d in D2:
                        t0, sz = TILES[orders[d][oi]]
                        dma("sp", (ofw if d == 0 else obw)[:, :, t0:t0 + sz].rearrange("h p t -> p h t"), ots[d][:, :, 0:sz], r=[ots[d].b], w=[d_of])
                sq = SB(ph, "hsq", [128, 4, 512], BF16)
                rs = SB(ph, "hrs", [128, 4, 512], F32)
                hggts = [SB(ph, "hggt%d" % i, [128, 4, 512], BF16) for i in range(2)]
                ysts = [SB(ph, "hyst%d" % i, [128, 4, 512], BF16) for i in range(2)]
                for ti, (t0, sz) in enumerate(TILES):
                    oa, obt, yh = ots[ti % 2], e1s[ti % 2], e2s[ti % 2]
                    hggt, yst = hggts[ti % 2], ysts[ti % 2]
                    dma("sp", oa[:, :, 0:sz], ofw[:, :, t0:t0 + sz].rearrange("h p t -> p h t"), r=[d_of], w=[oa.b])
                    dma("sp", obt[:, :, 0:sz], obw[:, :, t0:t0 + sz].rearrange("h p t -> p h t"), r=[d_of], w=[obt.b])
                    dma("sp", hggt[:, :, 0:sz], hgg[:, :, t0:t0 + sz].rearrange("h p t -> p h t"), r=[d_z], w=[hggt.b])
                    TT("pool", oa[:, :, 0:sz], oa[:, :, 0:sz], obt[:, :, 0:sz], ALU.add, [oa.b, obt.b], [oa.b])
                    ACT(sq[:, :, 0:sz], oa[:, :, 0:sz], AF.Square, [oa.b], [sq.b])
                    for h in range(4):
                        ps = nextps()
                        MM(ps[:, 0:sz], ones_b[:], sq[:, h, 0:sz], True, True, [ones_b.b, sq.b], [ps.b])
                        ACT(rs[:, h, 0:sz], ps[:, 0:sz], AF.Ln, [ps.b, epsb.b], [rs.b], bias=epsb[:, 0:1], scale=1.0 / 128)
                    ACT(rs[:, :, 0:sz], rs[:, :, 0:sz], AF.Exp, [rs.b], [rs.b], scale=-0.5)
                    for h in range(4):
                        STT(yh[:, h, 0:sz], oa[:, h, 0:sz], hng[:, h:h + 1], rs[:, h, 0:sz], ALU.mult, ALU.mult, [oa.b, hng.b, rs.b], [yh.b])
                    TT("pool", yst[:, :, 0:sz], yh[:, :, 0:sz], hggt[:, :, 0:sz], ALU.mult, [yh.b, hggt.b], [yst.b])
                    dma("sp", yT[8:12, :, t0:t0 + sz].rearrange("h p t -> p h t"), yst[:, :, 0:sz], r=[yst.b], w=[d_y])
                P.barrier()
            if stop_after == "HG":
                break
            with ExitStack() as ph:
                wbr = SB(ph, "wbr", [128, 12, DM], BF16)
                wou = SB(ph, "wou", [128, KC, DM], BF16)
                for n in range(3):
                    dma("pool", wbr[:, n * 4:(n + 1) * 4, :], w_branch[l, n].rearrange("(k p) d -> p k d", p=128), w=[wbr.b])
                dma("pool", wou[:], w_out[l].rearrange("(k p) d -> p k d", p=128), w=[wou.b])
                yts = [SB(ph, "yt%d" % i, [128, 12, 512], BF16) for i in range(2)]
                gts = [SB(ph, "gt%d" % i, [128, 24, 512], BF16) for i in range(2)]
                xt = SB(ph, "mxt", [128, KC, 512], F32)
                ot = SB(ph, "mot", [128, KC, 512], F32)
                mt = SB(ph, "mmt", [128, KC, 512], BF16)
                macc = [SB(ph, "macc%d" % i, [128, 512], F32) for i in range(2)]
                mtmp = [SB(ph, "mtmp%d" % i, [128, 512], F32) for i in range(2)]
                sq = SB(ph, "msq", [128, KC, 512], BF16)
                rs = SB(ph, "mrs", [128, 512], F32)
                tmp = SB(ph, "mtp", [128, KC, 512], F32)
                for ti, (t0, sz) in enumerate(TILES):
                    yt, gt = yts[ti % 2], gts[ti % 2]
                    dma("sp", yt[:, :, 0:sz], yT[:, :, t0:t0 + sz].rearrange("c p t -> p c t"), r=[d_y], w=[yt.b])
                    dma("sp", gt[:, :, 0:sz], gat[:, :, t0:t0 + sz].rearrange("c p t -> p c t"), r=[d_z], w=[gt.b])
                    dma("sp", xt[:, :, 0:sz], xT[:, :, t0:t0 + sz].rearrange("k p t -> p k t"), r=[d_xT], w=[xt.b])
                    for dc in range(KC):
                        ma, mp_ = macc[dc % 2], mtmp[dc % 2]
                        for n in range(3):
                            ps = nextps()
                            for k in range(4):
                                MM(ps[:, 0:sz], wbr[:, n * 4 + k, dc * 128:(dc + 1) * 128], yt[:, n * 4 + k, 0:sz], k == 0, k == 3, [wbr.b, yt.b], [ps.b])
                            g_ = gt[:, n * 8 + dc, 0:sz]
                            if n == 0:
                                TT("dve", ma[:, 0:sz], ps[:, 0:sz], g_, ALU.mult, [ps.b, gt.b], [ma.b])
                            elif n == 1:
                                TT("dve", mp_[:, 0:sz], ps[:, 0:sz], g_, ALU.mult, [ps.b, gt.b], [mp_.b])
                                TT("pool", ma[:, 0:sz], ma[:, 0:sz], mp_[:, 0:sz], ALU.add, [ma.b, mp_.b], [ma.b])
                            else:
                                TT("dve", mp_[:, 0:sz], ps[:, 0:sz], g_, ALU.mult, [ps.b, gt.b], [mp_.b])
                                TT("pool", mt[:, dc, 0:sz], ma[:, 0:sz], mp_[:, 0:sz], ALU.add, [ma.b, mp_.b], [mt.b])
                    for dc in range(KC):
                        ps = nextps()
                        for kc in range(KC):
                            MM(ps[:, 0:sz], wou[:, kc, dc * 128:(dc + 1) * 128], mt[:, kc, 0:sz], kc == 0, kc == KC - 1, [wou.b, mt.b], [ps.b])
                        CP("act", ot[:, dc, 0:sz], ps[:, 0:sz], [ps.b], [ot.b])
                    epilogue(ot, xt, G1, ti, t0, sz, sq, rs, tmp)
                P.barrier()
            if stop_after == "MIX":
                break
            with ExitStack() as ph:
                hT = SB(ph, "hT2", [128, KC, NT], BF16, nb=NTI)
                with ExitStack() as ph1:
                    norm_phase(ph1, l, A2, 24, hT)
                    P.barrier()
                NU = NT + 3
                Uas = [SB(ph, "Ua%d" % i, [128, NU], F32) for i in range(2)]
                Ugs = [SB(ph, "Ug%d" % i, [128, NU], F32) for i in range(2)]
                Ya = SB(ph, "Ya", [128, NU], F32)
                Yg = SB(ph, "Yg", [128, NU], F32)
                ast = [SB(ph, "ast%d" % i, [128, NU], BF16) for i in range(2)]
                cw = SB(ph, "cw", [128, 44, 3], F32)
                cb = SB(ph, "cb", [128, 44], F32)
                wua = [SB(ph, "wua%d" % i, [128, KC, 128], BF16) for i in range(2)]
                wug = [SB(ph, "wug%d" % i, [128, KC, 128], BF16) for i in range(2)]
                dma("sp", cw[:], convT[:, l, :, :], w=[cw.b])
                dma("sp", cb[:], convbT[:, l, :], w=[cb.b])
                for i_ in range(2):
                    MSET("pool", Uas[i_][:], 0.0, [Uas[i_].b])
                    MSET("pool", Ugs[i_][:], 0.0, [Ugs[i_].b])

                def ucol(t):
                    return t + 1 if t < LC else t + 2
                NY = NT + 1
                for j in range(NFC):
                    wa, wg_ = wua[j % 2], wug[j % 2]
                    Ua, Ug = Uas[j % 2], Ugs[j % 2]
                    dma("pool", wa[:], w_up[l, :, j * 128:(j + 1) * 128].rearrange("(k p) n -> p k n", p=128), w=[wa.b])
                    dma("pool", wg_[:], w_up[l, :, FFN + j * 128:FFN + (j + 1) * 128].rearrange("(k p) n -> p k n", p=128), w=[wg_.b])
                    for (wt, U) in ((wa, Ua), (wg_, Ug)):
                        for ti, (t0, sz) in enumerate(TILES):
                            ps = nextps()
                            for kc in range(KC):
                                MM(ps[:, 0:sz], wt[:, kc, :], hT[:, kc, t0:t0 + sz], kc == 0, kc == KC - 1, [wt.b, hT.bs[ti]], [ps.b])
                            CP("act", U[:, ucol(t0):ucol(t0) + sz], ps[:, 0:sz], [ps.b], [U.b])
                    for (U, Y, cj) in ((Ua, Ya, j), (Ug, Yg, NFC + j)):
                        ACT(Y[:, 0:NY], U[:, 0:NY], AF.Identity, [U.b, cw.b, cb.b], [Y.b], bias=cb[:, cj:cj + 1], scale=cw[:, cj, 0:1])
                        STT(Y[:, 0:NY], U[:, 1:NY + 1], cw[:, cj, 1:2], Y[:, 0:NY], ALU.mult, ALU.add, [U.b, cw.b, Y.b], [Y.b])
                        STT(Y[:, 0:NY], U[:, 2:NY + 2], cw[:, cj, 2:3], Y[:, 0:NY], ALU.mult, ALU.add, [U.b, cw.b, Y.b], [Y.b])
                    a_ = ast[j % 2]
                    ACT(Ya[:, 0:NY], Ya[:, 0:NY], AF.Silu, [Ya.b], [Ya.b])
                    TT("dve", a_[:, 0:NY], Ya[:, 0:NY], Yg[:, 0:NY], ALU.mult, [Ya.b, Yg.b], [a_.b])
                    dma("sp", actT[j, :, 0:LC], a_[:, 0:LC], r=[a_.b], w=[d_act])
                    dma("sp", actT[j, :, LC:NT], a_[:, LC + 1:NT + 1], r=[a_.b], w=[d_act])
                P.barrier()
            with ExitStack() as ph:
                wdn = SB(ph, "wdn", [128, NFC, DM], BF16)
                dma("pool", wdn[:, 0:11, :], w_down[l, 0:11 * 128, :].rearrange("(k p) d -> p k d", p=128), w=[wdn.b])
                dma("pool", wdn[:, 11:22, :], w_down[l, 11 * 128:22 * 128, :].rearrange("(k p) d -> p k d", p=128), w=[wdn.b])
                ats = [SB(ph, "at%d" % i, [128, NFC, 512], BF16) for i in range(2)]
                xt = SB(ph, "fxt", [128, KC, 512], F32)
                ot = SB(ph, "fot", [128, KC, 512], F32)
                sq = SB(ph, "fsq", [128, KC, 512], BF16)
                rs = SB(ph, "frs", [128, 512], F32)
                tmp = SB(ph, "ftp", [128, KC, 512], F32)
                for ti, (t0, sz) in enumerate(TILES):
                    at = ats[ti % 2]
                    dma("sp", at[:, :, 0:sz], actT[:, :, t0:t0 + sz].rearrange("c p t -> p c t"), r=[d_act], w=[at.b])
                    dma("sp", xt[:, :, 0:sz], xT[:, :, t0:t0 + sz].rearrange("k p t -> p k t"), r=[d_xT], w=[xt.b])
                    for dc in range(KC):
                        ps = nextps()
                        for k in range(NFC):
                            MM(ps[:, 0:sz], wdn[:, k, dc * 128:(dc + 1) * 128], at[:, k, 0:sz], k == 0, k == NFC - 1, [wdn.b, at.b], [ps.b])
                        CP("act", ot[:, dc, 0:sz], ps[:, 0:sz], [ps.b], [ot.b])
                    epilogue(ot, xt, G2, ti, t0, sz, sq, rs, tmp)
                P.barrier()
        if stop_after is None:
            with ExitStack() as ph:
                xtt = [SB(ph, "fxtt%d" % i, [128, KC, 128], F32) for i in range(2)]
                orow = [SB(ph, "orow%d" % i, [128, DM], F32) for i in range(2)]
                for tb in range(LC // 128, NB):
                    a, o_ = xtt[tb % 2], orow[tb % 2]
                    dma("sp", a[:], xT[:, :, tb * 128:(tb + 1) * 128].rearrange("k p t -> p k t"), r=[d_xT], w=[a.b])
                    pa, pb = nextps(), nextps()
                    for kc in range(KC):
                        pp_ = pa if kc < 4 else pb
                        TR(pp_[:, (kc % 4) * 128:(kc % 4 + 1) * 128], a[:, kc, :], ident_f[:], [a.b, ident_f.b], [pp_.b])
                    CP("act", o_[:, 0:512], pa[:, :], [pa.b], [o_.b])
                    CP("dve", o_[:, 512:1024], pb[:, :], [pb.b], [o_.b])
                    dma("sp", out[tb * 128 - LC:(tb + 1) * 128 - LC, :], o_[:], r=[o_.b])
        P.barrier()
    return nc


def prep_shared(inp, cfg):
    L, LC, DEPTH = cfg["L"], cfg["LC"], cfg["DEPTH"]
    NT = L + LC
    f = lambda a: np.ascontiguousarray(np.asarray(a, dtype=np.float32))
    sh = {}
    sh["w_mod"] = f(inp["w_mod"][:DEPTH])
    sh["b_modT"] = f(np.asarray(inp["b_mod"])[:DEPTH].reshape(DEPTH, 48, 128).transpose(2, 0, 1))
    sh["norm_gT"] = f(np.asarray(inp["norm_g"])[:DEPTH].reshape(DEPTH, 4, KC, 128).transpose(3, 0, 1, 2))
    w_in = np.asarray(inp["w_in"])[:DEPTH]
    sh["w_in"] = f(w_in)
    idx = []
    for h in range(8):
        idx += [C_Q + h * 64 + (d + 32) % 64 for d in range(64)]
    for h in range(2):
        idx += [C_K + h * 64 + (d + 32) % 64 for d in range(64)]
    sh["w_rot"] = f(w_in[:, :, np.array(idx)])
    rows = L // 64
    row = np.repeat(np.arange(rows, dtype=np.float32), 64)
    col = np.tile(np.arange(64, dtype=np.float32), rows)
    inv = (10000.0 ** (-np.arange(16, dtype=np.float32) / 16)).astype(np.float32)
    ang = np.concatenate([row[:, None] * inv, col[:, None] * inv], axis=-1)
    cos, sin = np.cos(ang).T, np.sin(ang).T
    C = np.ones((64, NT), np.float32)
    S = np.zeros((64, NT), np.float32)
    C[0:32, LC:] = cos
    C[32:64, LC:] = cos
    S[0:32, LC:] = -sin
    S[32:64, LC:] = sin
    sh["ropeC"], sh["ropeS"] = np.concatenate([C, C], 0), np.concatenate([S, S], 0)
    def st(a):
        a = np.asarray(a)[:DEPTH].reshape(DEPTH, 2, 16, 2, 64)
        return f(a.transpose(3, 4, 0, 1, 2).reshape(128, DEPTH, 2, 16))
    sh["s5_lr"] = st(inp["s5_lam_re"])
    sh["s5_li"] = st(inp["s5_lam_im"])
    ldt = np.asarray(inp["s5_log_dt"])[:DEPTH]
    sh["s5_ldt"] = st(np.repeat(ldt[..., None], 64, axis=-1))
    def padB(b):
        b = np.asarray(b)[:DEPTH]
        o = np.zeros((DEPTH, 16, 128, 128), np.float32)
        for pr in range(16):
            for g2 in range(2):
                s0 = (pr % 4) * 32 + g2 * 16
                o[:, pr, g2 * 64:(g2 + 1) * 64, s0:s0 + 16] = b[:, 2 * pr + g2]
        return o
    def padC(c):
        c = np.asarray(c)[:DEPTH]
        o = np.zeros((DEPTH, 16, 128, 128), np.float32)
        for pr in range(16):
            for g2 in range(2):
                s0 = (pr % 4) * 32 + g2 * 16
                o[:, pr, g2 * 64:(g2 + 1) * 64, s0:s0 + 16] = c[:, 2 * pr + g2].transpose(0, 2, 1)
        return o
    sh["s5_Bre"], sh["s5_Bim"] = padB(inp["s5_b_re"]), padB(inp["s5_b_im"])
    sh["s5_Cre"], sh["s5_Cim"] = padC(inp["s5_c_re"]), padC(inp["s5_c_im"])
    sh["s5_dT"] = f(np.asarray(inp["s5_d"])[:DEPTH].reshape(DEPTH, 4, 128).transpose(2, 0, 1))
    sh["s5_bgT"] = f(np.asarray(inp["s5_b_glu"])[:DEPTH].reshape(DEPTH, 4, 128).transpose(2, 0, 1))
    sh["s5_wglu"] = f(inp["s5_w_glu"][:DEPTH])
    sh["sinkrow"] = f(np.repeat(np.asarray(inp["att_sink"])[:DEPTH], 128, axis=-1).reshape(DEPTH, 1, 1024))
    sh["hg_lbT"] = f(np.asarray(inp["hg_lb_logits"])[:DEPTH].reshape(DEPTH, 2, 4, 128).transpose(3, 0, 1, 2).reshape(128, DEPTH, 8))
    sh["hg_ngT"] = f(np.asarray(inp["hg_norm_g"])[:DEPTH].reshape(DEPTH, 4, 128).transpose(2, 0, 1))
    sh["w_branch"] = f(inp["w_branch"][:DEPTH])
    sh["w_out"] = f(inp["w_out"][:DEPTH])
    sh["w_up"] = f(inp["ffn_w_up"][:DEPTH])
    sh["convT"] = f(np.asarray(inp["ffn_conv_w"])[:DEPTH].reshape(DEPTH, 3, 44, 128).transpose(3, 0, 2, 1))
    sh["convbT"] = f(np.asarray(inp["ffn_conv_b"])[:DEPTH].reshape(DEPTH, 44, 128).transpose(2, 0, 1))
    sh["w_down"] = f(inp["ffn_w_down"][:DEPTH])
    sh["c_ident"] = np.eye(128, dtype=np.float32)
    k = np.arange(128)[:, None]
    q = np.tile(np.arange(128), 4)[None, :]
    sh["c_maskP"] = np.where(k >= q, 0.0, -30000.0).astype(np.float32)
    sh["c_maskN"] = np.where(k <= q, 0.0, -30000.0).astype(np.float32)
    io = np.zeros((128, 2, 512), np.float32)
    io[:, 0, :] = np.arange(1, 513, dtype=np.float32)[None]
    io[:, 1, :] = (512 - np.arange(512, dtype=np.float32))[None]
    sh["c_iota"] = io
    s_ = np.arange(32)[:, None]
    t_ = np.tile(np.arange(32), 4)[None, :]
    hm = np.zeros((32, 2, 128), np.float32)
    hm[:, 0, :] = (s_ <= t_)
    hm[:, 1, :] = (s_ >= t_)
    sh["c_hgmask"] = hm
    rm = np.ones((128, 512), np.float32)
    rm[:, ::32] = 0.0
    sh["c_rmask"] = rm
    return sh


def prep_core(inp, b, cfg, sh):
    L, LC = cfg["L"], cfg["LC"]
    m = dict(sh)
    m["x"] = np.ascontiguousarray(np.asarray(inp["x"], dtype=np.float32)[b, :L])
    m["ctx"] = np.ascontiguousarray(np.asarray(inp["ctx"], dtype=np.float32)[b, :LC])
    cT = np.stack([np.asarray(inp["c"], dtype=np.float32)[b], np.asarray(inp["c_ctx"], dtype=np.float32)], axis=-1)
    m["cT"] = np.ascontiguousarray(cT.reshape(KC, 128, 2).transpose(1, 0, 2))
    return m


_NC_CACHE = {}


def kernel(**inputs):
    cfg = FULL
    key = "full"
    if key not in _NC_CACHE:
        _NC_CACHE[key] = build(cfg)
    nc = _NC_CACHE[key]
    sh = prep_shared(inputs, cfg)
    in_maps = [prep_core(inputs, b, cfg, sh) for b in range(8)]
    res = run_bass_kernel_spmd(nc, in_maps, core_ids=list(range(8)))
    return np.stack([np.asarray(r["out"], dtype=np.float32) for r in res.results], axis=0)
```

```python
import numpy as np
import ml_dtypes
from contextlib import ExitStack
import concourse.bass as bass
import concourse.mybir as mybir
from concourse.bass_utils import run_bass_kernel_spmd

F32 = mybir.dt.float32
BF16 = mybir.dt.bfloat16
I32 = mybir.dt.int32
AF = mybir.ActivationFunctionType
ALU = mybir.AluOpType
PI = float(np.pi)


class Buf:
    __slots__ = ("w", "r")

    def __init__(self):
        self.w = None
        self.r = {}


class Prog:
    def __init__(self, nc, es, ndma=48):
        self.nc = nc
        self.eng = {"pe": nc.tensor, "act": nc.scalar, "dve": nc.vector, "pool": nc.gpsimd, "sp": nc.sync}
        self.sem = {}
        self.cnt = {}
        for k in self.eng:
            self.sem[k] = es.enter_context(nc.semaphore("s_" + k))
            self.cnt[k] = 0
        self.ndma = ndma
        for i in range(ndma):
            self.sem[("d", i)] = es.enter_context(nc.semaphore("s_d%d" % i))
            self.cnt[("d", i)] = 0
        self.known = {e: {} for e in self.eng}
        self.rr = 0
        self.nins = 0

    def _waits(self, e, r, w, extra=()):
        need = {}
        kn = self.known[e]

        def add(tok, same_ok):
            if tok is None:
                return
            k, v = tok
            if k == e and (e == "pe" or not same_ok):
                return
            if kn.get(k, 0) >= v:
                return
            if need.get(k, 0) < v:
                need[k] = v

        for b in r:
            add(b.w, True)
        for b in w:
            add(b.w, True)
            for t in b.r.values():
                add(t, False)
        for t in extra:
            add(t, True)
        E = self.eng[e]
        for k, v in need.items():
            E.wait_ge(self.sem[k], v)
            kn[k] = v
            self.nins += 1

    def op(self, e, fn, r=(), w=()):
        self._waits(e, r, w)
        ins = fn(self.eng[e])
        self.cnt[e] += 1
        ins.then_inc(self.sem[e], 1)
        self.nins += 1
        tok = (e, self.cnt[e])
        for b in r:
            b.r[e] = tok
        for b in w:
            b.w = tok
            b.r = {}
        return tok

    def dma(self, q, out, in_, r=(), w=()):
        i = self.rr
        self.rr = (i + 1) % self.ndma
        key = ("d", i)
        self._waits(q, r, w, extra=[(key, self.cnt[key])])
        ins = self.eng[q].dma_start(out=out, in_=in_)
        self.cnt[key] += 16
        ins.then_inc(self.sem[key], 16)
        self.nins += 1
        tok = (key, self.cnt[key])
        for b in r:
            b.r[key] = tok
        for b in w:
            b.w = tok
            b.r = {}
        return tok

    def barrier(self):
        toks = [(k, v) for k, v in self.cnt.items() if v > 0]
        for e, E in self.eng.items():
            kn = self.known[e]
            for k, v in toks:
                if kn.get(k, 0) < v:
                    E.wait_ge(self.sem[k], v)
                    kn[k] = v
                    self.nins += 1


class Tl:
    def __init__(self, t, nb=1):
        self.t = t
        self.bs = [Buf() for _ in range(nb)]

    @property
    def b(self):
        return self.bs[0]

    def __getitem__(self, k):
        return self.t[k]


DM = 1024
KC = 8
EPS = 1e-6
IN_COLS = 6912
C_S5, C_Q, C_K, C_V, C_HQ, C_FF, C_FB, C_HI, C_HG, C_GATE = 0, 512, 1024, 1152, 1280, 1792, 2304, 2816, 3328, 3840
FFN = 2816
NFC = 22
FULL = dict(L=4096, LC=256, DEPTH=4, taps=())


def build(cfg):
    L, LC, DEPTH = cfg["L"], cfg["LC"], cfg["DEPTH"]
    taps = set(cfg.get("taps", ()))
    stop_after = cfg.get("stop_after", None)
    NT = L + LC
    NB = NT // 128
    TILES = [(0, LC)] + [(LC + 512 * i, 512) for i in range(L // 512)]
    NTI = len(TILES)

    def tile_of(tok):
        for i, (t0, sz) in enumerate(TILES):
            if t0 <= tok < t0 + sz:
                return i
        raise ValueError

    nc = bass.Bass("TRN2", target_bir_lowering=False)

    def din(name, shape, dt=F32):
        return nc.dram_tensor(name, list(shape), dt, kind="ExternalInput").ap()

    def dscr(name, shape, dt):
        kind = "ExternalOutput" if name in taps else "Internal"
        return nc.dram_tensor(name, list(shape), dt, kind=kind).ap()

    x_in = din("x", [L, DM])
    ctx_in = din("ctx", [LC, DM])
    cT_in = din("cT", [128, KC, 2])
    w_mod = din("w_mod", [DEPTH, DM, 6 * DM])
    b_modT = din("b_modT", [128, DEPTH, 48])
    norm_gT = din("norm_gT", [128, DEPTH, 4, KC])
    w_in = din("w_in", [DEPTH, DM, IN_COLS])
    w_rot = din("w_rot", [DEPTH, DM, 640])
    ropeC_in = din("ropeC", [128, NT])
    ropeS_in = din("ropeS", [128, NT])
    s5_lr = din("s5_lr", [128, DEPTH, 2, 16])
    s5_li = din("s5_li", [128, DEPTH, 2, 16])
    s5_ldt = din("s5_ldt", [128, DEPTH, 2, 16])
    s5_Bre = din("s5_Bre", [DEPTH, 16, 128, 128])
    s5_Bim = din("s5_Bim", [DEPTH, 16, 128, 128])
    s5_Cre = din("s5_Cre", [DEPTH, 16, 128, 128])
    s5_Cim = din("s5_Cim", [DEPTH, 16, 128, 128])
    s5_dT = din("s5_dT", [128, DEPTH, 4])
    s5_bgT = din("s5_bgT", [128, DEPTH, 4])
    s5_wglu = din("s5_wglu", [DEPTH, 512, 512])
    sinkrow = din("sinkrow", [DEPTH, 1, 1024])
    hg_lbT = din("hg_lbT", [128, DEPTH, 8])
    hg_ngT = din("hg_ngT", [128, DEPTH, 4])
    w_branch = din("w_branch", [DEPTH, 3, 512, DM])
    w_out = din("w_out", [DEPTH, DM, DM])
    w_up = din("w_up", [DEPTH, DM, 2 * FFN])
    convT = din("convT", [128, DEPTH, 44, 3])
    convbT = din("convbT", [128, DEPTH, 44])
    w_down = din("w_down", [DEPTH, FFN, DM])
    c_ident = din("c_ident", [128, 128])
    c_maskP = din("c_maskP", [128, 512])
    c_maskN = din("c_maskN", [128, 512])
    c_iota = din("c_iota", [128, 2, 512])
    c_hgmask = din("c_hgmask", [32, 2, 128])
    c_rmask = din("c_rmask", [128, 512])
    out = nc.dram_tensor("out", [L, DM], F32, kind="ExternalOutput").ap()

    xT = dscr("xT", [KC, 128, NT], F32)
    zs5 = dscr("zs5", [4, 128, NT], BF16)
    qT = dscr("qT", [8, 64, NT], BF16)
    kT = dscr("kT", [2, 64, NT], BF16)
    vtm = dscr("vtm", [NT, 640], BF16)
    hgq = dscr("hgq", [4, 128, NT], BF16)
    lf = dscr("lf", [2, 4, 128, NT], F32)
    kk = dscr("kk", [2, 4, 128, NT], BF16)
    hgg = dscr("hgg", [4, 128, NT], BF16)
    gat = dscr("gat", [24, 128, NT], BF16)
    yT = dscr("yT", [12, 128, NT], BF16)
    ofw = dscr("ofw", [4, 128, NT], F32)
    ofw2 = dscr("ofw2", [4, 128, NT], F32)
    actT = dscr("actT", [NFC, 128, NT], BF16)
    dbg = dscr("dbg", [128, 4096], F32)
    d_xT, d_z, d_y, d_of, d_act = Buf(), Buf(), Buf(), Buf(), Buf()

    with ExitStack() as es:
        P = Prog(nc, es)
        op, dma = P.op, P.dma

        uid = [0]

        def SB(stack, name, shape, dt, nb=1):
            uid[0] += 1
            return Tl(stack.enter_context(nc.sbuf_tensor("%s_u%d" % (name, uid[0]), list(shape), dt)), nb)

        psum = [Tl(es.enter_context(nc.psum_tensor("ps%d" % i, [128, 512], F32))) for i in range(8)]
        psrr = [0, 0]
        nlong = [2]

        def nextps(long=False):
            if long:
                p = psum[psrr[1] % nlong[0]]
                psrr[1] += 1
            else:
                p = psum[nlong[0] + psrr[0] % (8 - nlong[0])]
                psrr[0] += 1
            return p

        def MM(o, l, r_, st, sp, rb, wb):
            op("pe", lambda E: E.matmul(o, l, r_, start=st, stop=sp), r=rb, w=wb)

        def TR(o, i, idn, rb, wb):
            op("pe", lambda E: E.transpose(o, i, idn), r=rb, w=wb)

        def ACT(o, i, f, rb, wb, bias=None, scale=None):
            kw = {}
            if bias is not None:
                kw["bias"] = bias
            if scale is not None:
                kw["scale"] = scale
            op("act", lambda E: E.activation(out=o, in_=i, func=f, **kw), r=rb, w=wb)

        def CP(e, o, i, rb, wb):
            if e == "act":
                op("act", lambda E: E.copy(out=o, in_=i), r=rb, w=wb)
            else:
                op(e, lambda E: E.tensor_copy(out=o, in_=i), r=rb, w=wb)

        def TT(e, o, a, b_, alu, rb, wb):
            op(e, lambda E: E.tensor_tensor(out=o, in0=a, in1=b_, op=alu), r=rb, w=wb)

        def TS(e, o, a, s1, s2, o0, o1, rb, wb):
            if s2 is None:
                op(e, lambda E: E.tensor_scalar(out=o, in0=a, scalar1=s1, scalar2=None, op0=o0), r=rb, w=wb)
            else:
                op(e, lambda E: E.tensor_scalar(out=o, in0=a, scalar1=s1, scalar2=s2, op0=o0, op1=o1), r=rb, w=wb)

        def STT(o, a, s, b_, o0, o1, rb, wb):
            op("dve", lambda E: E.scalar_tensor_tensor(out=o, in0=a, scalar=s, in1=b_, op0=o0, op1=o1), r=rb, w=wb)

        def SCAN(o, d0, d1, init, rb, wb):
            op("dve", lambda E: E.tensor_tensor_scan(out=o, data0=d0, data1=d1, initial=init, op0=ALU.mult, op1=ALU.add), r=rb, w=wb)

        def MSET(e, o, v, wb):
            op(e, lambda E: E.memset(o, v), w=wb)

        def DBG(c0, ap, n, tl):
            if "dbg" in taps:
                dma("pool", dbg[0:ap.shape[0], c0:c0 + n], ap, r=[tl.b])

        ident_f = SB(es, "ident_f", [128, 128], F32)
        ident_b = SB(es, "ident_b", [128, 128], BF16)
        ones_b = SB(es, "ones_b", [128, 128], BF16)
        neghalf = SB(es, "neghalf", [128, 512], F32)
        modv = SB(es, "modv", [128, DEPTH, 48, 2], F32)
        ngt = SB(es, "ngt", [128, DEPTH, 4, KC], F32)
        lbv = SB(es, "lbv", [128, DEPTH, 8], F32)
        omlb = SB(es, "omlb", [128, DEPTH, 8], F32)
        A1 = SB(es, "A1", [128, KC, 2], F32)
        G1 = SB(es, "G1", [128, KC, 2], F32)
        A2 = SB(es, "A2", [128, KC, 2], F32)
        G2 = SB(es, "G2", [128, KC, 2], F32)
        STAT = [ident_f.b, ident_b.b, ones_b.b, neghalf.b]

        dma("sp", ident_f[:], c_ident[:, :], w=[ident_f.b])
        CP("dve", ident_b[:], ident_f[:], [ident_f.b], [ident_b.b])
        MSET("pool", ones_b[:], 1.0, [ones_b.b])
        MSET("pool", neghalf[:], -0.5, [neghalf.b])
        epsb = SB(es, "epsb", [128, 1], F32)
        MSET("pool", epsb[:], EPS, [epsb.b])
        dma("sp", ngt[:], norm_gT[:, :, :, :], w=[ngt.b])

        with ExitStack() as ph:
            xr = [SB(ph, "xr%d" % i, [128, DM], F32) for i in range(2)]
            xtt = [SB(ph, "xtt%d" % i, [128, KC, 128], F32) for i in range(2)]
            for tb in range(NB):
                src = ctx_in[tb * 128:(tb + 1) * 128, :] if tb < LC // 128 else x_in[tb * 128 - LC:(tb + 1) * 128 - LC, :]
                a, o_ = xr[tb % 2], xtt[tb % 2]
                dma("sp", a[:], src, w=[a.b])
                pa, pb = nextps(), nextps()
                for kc in range(KC):
                    pp_ = pa if kc < 4 else pb
                    TR(pp_[:, (kc % 4) * 128:(kc % 4 + 1) * 128], a[:, kc * 128:(kc + 1) * 128], ident_f[:], [a.b, ident_f.b], [pp_.b])
                CP("act", o_[:, 0:4, :], pa[:].rearrange("p (k t) -> p k t", k=4), [pa.b], [o_.b])
                CP("dve", o_[:, 4:8, :], pb[:].rearrange("p (k t) -> p k t", k=4), [pb.b], [o_.b])
                dma("sp", xT[:, :, tb * 128:(tb + 1) * 128].rearrange("k p t -> p k t"), o_[:], r=[o_.b], w=[d_xT])
            cTt = SB(ph, "cTt", [128, KC, 2], F32)
            scb = SB(ph, "scb", [128, KC, 2], BF16)
            bmt = SB(ph, "bmt", [128, DEPTH, 48], F32)
            wm = [SB(ph, "wm%d" % i, [128, KC, 1024], BF16) for i in range(2)]
            dma("sp", cTt[:], cT_in[:, :, :], w=[cTt.b])
            dma("sp", bmt[:], b_modT[:, :, :], w=[bmt.b])
            ACT(scb[:], cTt[:], AF.Silu, [cTt.b], [scb.b])
            for l in range(DEPTH):
                pm = nextps()
                for grp in range(6):
                    wt = wm[(l * 6 + grp) % 2]
                    dma("pool", wt[:], w_mod[l, :, grp * 1024:(grp + 1) * 1024].rearrange("(k p) n -> p k n", p=128), w=[wt.b])
                    for j in range(8):
                        oc = grp * 8 + j
                        for kc in range(KC):
                            MM(pm[:, oc * 2:oc * 2 + 2], wt[:, kc, j * 128:(j + 1) * 128], scb[:, kc, :], kc == 0, kc == KC - 1, [wt.b, scb.b], [pm.b])
                TT("dve", modv[:, l, :, :], pm[:, 0:96].rearrange("p (c w) -> p c w", w=2),
                   bmt[:, l, :].unsqueeze(2).to_broadcast([128, 48, 2]), ALU.add, [pm.b, bmt.b], [modv.b])
            lg = SB(ph, "lg", [128, DEPTH, 8], F32)
            sm = SB(ph, "sm", [128, 8], F32)
            dma("sp", lg[:], hg_lbT[:, :, :], w=[lg.b])
            ACT(lg[:], lg[:], AF.Exp, [lg.b], [lg.b])
            CP("dve", sm[:], lg[:, 0, :], [lg.b], [sm.b])
            for l in range(1, DEPTH):
                TT("dve", sm[:], sm[:], lg[:, l, :], ALU.add, [sm.b, lg.b], [sm.b])
            op("dve", lambda E: E.reciprocal(out=sm[:], in_=sm[:]), r=[sm.b], w=[sm.b])
            MSET("dve", lbv[:, 0, :], 0.0, [lbv.b])
            for l in range(1, DEPTH):
                TT("dve", lg[:, l, :], lg[:, l, :], sm[:], ALU.mult, [lg.b, sm.b], [lg.b])
                TT("dve", lbv[:, l, :], lbv[:, l - 1, :], lg[:, l, :], ALU.add, [lbv.b, lg.b], [lbv.b])
            TS("dve", omlb[:], lbv[:], -1.0, 1.0, ALU.mult, ALU.add, [lbv.b], [omlb.b])
            P.barrier()

        def mod_scalars(l):
            for (Aq, sc0, gi) in ((A1, 8, 0), (A2, 32, 2)):
                TS("dve", Aq[:], modv[:, l, sc0:sc0 + 8, :], 1.0, None, ALU.add, None, [modv.b], [Aq.b])
                TT("dve", Aq[:], Aq[:], ngt[:, l, gi, :].unsqueeze(2).to_broadcast([128, KC, 2]), ALU.mult, [Aq.b, ngt.b], [Aq.b])
            for (Gq, g0, gi) in ((G1, 16, 1), (G2, 40, 3)):
                TT("dve", Gq[:], modv[:, l, g0:g0 + 8, :], ngt[:, l, gi, :].unsqueeze(2).to_broadcast([128, KC, 2]), ALU.mult, [modv.b, ngt.b], [Gq.b])

        def norm_phase(ph, l, Aq, sh0, hT):
            xts = [SB(ph, "nxt%d" % i, [128, KC, 512], F32) for i in range(2)]
            sq = SB(ph, "nsq", [128, KC, 512], BF16)
            rs = SB(ph, "nrs", [128, 512], F32)
            tmp = SB(ph, "ntmp", [128, KC, 512], F32)
            for ti, (t0, sz) in enumerate(TILES):
                w_ = 1 if ti == 0 else 0
                xt = xts[ti % 2]
                dma("sp", xt[:, :, 0:sz], xT[:, :, t0:t0 + sz].rearrange("k p t -> p k t"), r=[d_xT], w=[xt.b])
                ACT(sq[:, :, 0:sz], xt[:, :, 0:sz], AF.Square, [xt.b], [sq.b])
                ps = nextps()
                for kc in range(KC):
                    MM(ps[:, 0:sz], ones_b[:], sq[:, kc, 0:sz], kc == 0, kc == KC - 1, [ones_b.b, sq.b], [ps.b])
                ACT(rs[:, 0:sz], ps[:, 0:sz], AF.Ln, [ps.b, epsb.b], [rs.b], bias=epsb[:, 0:1], scale=1.0 / DM)
                ACT(rs[:, 0:sz], rs[:, 0:sz], AF.Exp, [rs.b], [rs.b], scale=-0.5)
                for kc in range(KC):
                    STT(tmp[:, kc, 0:sz], xt[:, kc, 0:sz], Aq[:, kc, w_:w_ + 1], rs[:, 0:sz], ALU.mult, ALU.mult, [xt.b, Aq.b, rs.b], [tmp.b])
                    ACT(hT[:, kc, t0:t0 + sz], tmp[:, kc, 0:sz], AF.Identity, [tmp.b, modv.b], [hT.bs[ti]],
                        bias=modv[:, l, sh0 + kc, w_:w_ + 1])

        def epilogue(ot, xt, Gq, ti, t0, sz, sq, rs, tmp):
            w_ = 1 if ti == 0 else 0
            ACT(sq[:, :, 0:sz], ot[:, :, 0:sz], AF.Square, [ot.b], [sq.b])
            ps = nextps()
            for kc in range(KC):
                MM(ps[:, 0:sz], ones_b[:], sq[:, kc, 0:sz], kc == 0, kc == KC - 1, [ones_b.b, sq.b], [ps.b])
            ACT(rs[:, 0:sz], ps[:, 0:sz], AF.Ln, [ps.b, epsb.b], [rs.b], bias=epsb[:, 0:1], scale=1.0 / DM)
            ACT(rs[:, 0:sz], rs[:, 0:sz], AF.Exp, [rs.b], [rs.b], scale=-0.5)
            for kc in range(KC):
                STT(tmp[:, kc, 0:sz], ot[:, kc, 0:sz], Gq[:, kc, w_:w_ + 1], rs[:, 0:sz], ALU.mult, ALU.mult, [ot.b, Gq.b, rs.b], [tmp.b])
            TT("pool", xt[:, :, 0:sz], xt[:, :, 0:sz], tmp[:, :, 0:sz], ALU.add, [xt.b, tmp.b], [xt.b])
            dma("sp", xT[:, :, t0:t0 + sz].rearrange("k p t -> p k t"), xt[:, :, 0:sz], r=[xt.b], w=[d_xT])

        for l in range(DEPTH):
            mod_scalars(l)
            with ExitStack() as ph:
                hT = SB(ph, "hT", [128, KC, NT], BF16, nb=NTI)
                with ExitStack() as ph1:
                    norm_phase(ph1, l, A1, 0, hT)
                    P.barrier()
                wts = [SB(ph, "wt%d" % i, [128, KC, 512], BF16) for i in range(3)]
                wrr = [0]
                stg = [SB(ph, "stg%d" % i, [128, NT], BF16) for i in range(3)]
                srr = [0]
                stgf = [SB(ph, "stgf%d" % i, [128, NT], F32) for i in range(2)]
                tmpa = [SB(ph, "tmpa%d" % i, [128, 512], F32) for i in range(2)]
                tmpb = [SB(ph, "tmpb%d" % i, [128, 512], F32) for i in range(2)]

                def load_w(src, ncols):
                    wt = wts[wrr[0] % 3]
                    wrr[0] += 1
                    dma("pool", wt[:, :, 0:ncols], src.rearrange("(k p) n -> p k n", p=128), w=[wt.b])
                    return wt

                def next_stg():
                    s = stg[srr[0] % 3]
                    srr[0] += 1
                    return s

                def proj(wt, off, M, cons):
                    for ti, (t0, sz) in enumerate(TILES):
                        ps = nextps()
                        for kc in range(KC):
                            MM(ps[0:M, 0:sz], wt[:, kc, off:off + M], hT[:, kc, t0:t0 + sz], kc == 0, kc == KC - 1, [wt.b, hT.bs[ti]], [ps.b])
                        cons(ti, t0, sz, ps)

                def simple_group(col0, nchunks, func, dst):
                    for g0 in range(0, nchunks, 4):
                        n = min(4, nchunks - g0)
                        wt = load_w(w_in[l, :, col0 + g0 * 128:col0 + (g0 + n) * 128], n * 128)
                        for c in range(n):
                            s = next_stg()

                            def cons(ti, t0, sz, ps, s=s):
                                if func is None:
                                    CP("act", s[:, t0:t0 + sz], ps[:, 0:sz], [ps.b], [s.b])
                                else:
                                    ACT(s[:, t0:t0 + sz], ps[:, 0:sz], func, [ps.b], [s.b])
                            proj(wt, c * 128, 128, cons)
                            dma("sp", dst[g0 + c], s[:], r=[s.b], w=[d_z])

                simple_group(C_S5, 4, None, zs5)
                with ExitStack() as phq:
                    ropeC = SB(phq, "ropeC", [128, NT], F32)
                    ropeS = SB(phq, "ropeS", [128, NT], F32)
                    dma("sp", ropeC[:], ropeC_in[:, :], w=[ropeC.b])
                    dma("sp", ropeS[:], ropeS_in[:, :], w=[ropeS.b])
                    for (cbase, rbase, nh_, dst) in ((C_Q, 0, 8, qT), (C_K, 512, 2, kT)):
                        for g0 in range(0, nh_, 4):
                            n = min(4, nh_ - g0)
                            wa = load_w(w_in[l, :, cbase + g0 * 64:cbase + (g0 + n) * 64], n * 64)
                            wb = load_w(w_rot[l, :, rbase + g0 * 64:rbase + (g0 + n) * 64], n * 64)
                            for c in range(n // 2):
                                s = next_stg()
                                for ti, (t0, sz) in enumerate(TILES):
                                    p1, p2 = nextps(), nextps()
                                    for kc in range(KC):
                                        MM(p1[:, 0:sz], wa[:, kc, c * 128:(c + 1) * 128], hT[:, kc, t0:t0 + sz], kc == 0, kc == KC - 1, [wa.b, hT.bs[ti]], [p1.b])
                                    for kc in range(KC):
                                        MM(p2[:, 0:sz], wb[:, kc, c * 128:(c + 1) * 128], hT[:, kc, t0:t0 + sz], kc == 0, kc == KC - 1, [wb.b, hT.bs[ti]], [p2.b])
                                    ta, tb_ = tmpa[ti % 2], tmpb[ti % 2]
                                    TT("dve", ta[:, 0:sz], p1[:, 0:sz], ropeC[:, t0:t0 + sz], ALU.mult, [p1.b, ropeC.b], [ta.b])
                                    TT("dve", tb_[:, 0:sz], p2[:, 0:sz], ropeS[:, t0:t0 + sz], ALU.mult, [p2.b, ropeS.b], [tb_.b])
                                    TT("pool", s[:, t0:t0 + sz], ta[:, 0:sz], tb_[:, 0:sz], ALU.add, [ta.b, tb_.b], [s.b])
                                h0 = g0 + 2 * c
                                dma("sp", dst[h0:h0 + 2].rearrange("h d t -> (h d) t"), s[:], r=[s.b], w=[d_z])
                    P.barrier()
                with ExitStack() as ph3:
                    wv = SB(ph3, "wv", [128, KC, 640], BF16)
                    vst = [SB(ph3, "vst%d" % i, [128, 640], BF16) for i in range(2)]
                    dma("pool", wv[:, :, 0:128], w_in[l, :, C_V:C_V + 128].rearrange("(k p) n -> p k n", p=128), w=[wv.b])
                    dma("pool", wv[:, :, 128:640], w_in[l, :, C_HI:C_HI + 512].rearrange("(k p) n -> p k n", p=128), w=[wv.b])
                    for tb in range(NB):
                        ti = tile_of(tb * 128)
                        pa, pb = nextps(), nextps()
                        for kc in range(KC):
                            MM(pa[:, 0:512], hT[:, kc, tb * 128:(tb + 1) * 128], wv[:, kc, 128:640], kc == 0, kc == KC - 1, [wv.b, hT.bs[ti]], [pa.b])
                        for kc in range(KC):
                            MM(pb[:, 0:128], hT[:, kc, tb * 128:(tb + 1) * 128], wv[:, kc, 0:128], kc == 0, kc == KC - 1, [wv.b, hT.bs[ti]], [pb.b])
                        v = vst[tb % 2]
                        CP("act", v[:, 0:128], pb[:, 0:128], [pb.b], [v.b])
                        CP("dve", v[:, 128:640], pa[:, 0:512], [pa.b], [v.b])
                        dma("sp", vtm[tb * 128:(tb + 1) * 128, :], v[:], r=[v.b], w=[d_z])
                simple_group(C_HQ, 4, AF.Silu, hgq)
                for d in range(2):
                    wt = load_w(w_in[l, :, C_FF + d * 512:C_FF + (d + 1) * 512], 512)
                    for c in range(4):
                        s = next_stg()
                        sf = stgf[c % 2]
                        li_ = d * 4 + c

                        def cons(ti, t0, sz, ps, s=s, sf=sf, li_=li_):
                            ta = tmpa[ti % 2]
                            ACT(ta[:, 0:sz], ps[:, 0:sz], AF.Exp, [ps.b], [ta.b], scale=-1.0)
                            TS("dve", ta[:, 0:sz], ta[:, 0:sz], 1.0, None, ALU.add, None, [ta.b], [ta.b])
                            op("dve", lambda E: E.reciprocal(out=ta[:, 0:sz], in_=ta[:, 0:sz]), r=[ta.b], w=[ta.b])
                            TS("dve", ta[:, 0:sz], ta[:, 0:sz], omlb[:, l, li_:li_ + 1], lbv[:, l, li_:li_ + 1], ALU.mult, ALU.add, [ta.b, omlb.b, lbv.b], [ta.b])
                            ACT(sf[:, t0:t0 + sz], ta[:, 0:sz], AF.Ln, [ta.b], [sf.b])
                            TS("dve", s[:, t0:t0 + sz], ta[:, 0:sz], -1.0, 1.0, ALU.mult, ALU.add, [ta.b], [s.b])
                        proj(wt, c * 128, 128, cons)
                        dma("sp", lf[d, c], sf[:], r=[sf.b], w=[d_z])
                        dma("sp", kk[d, c], s[:], r=[s.b], w=[d_z])
                simple_group(C_HG, 4, AF.Sigmoid, hgg)
                simple_group(C_GATE, 24, AF.Sigmoid, gat)
                P.barrier()
            if stop_after == "P2":
                break
            with ExitStack() as ph:
                uT = SB(ph, "uT", [128, 4, NT], BF16)
                yacc = SB(ph, "yacc", [128, NT], F32)
                for c in range(4):
                    dma("sp", uT[:, c, :], zs5[c], r=[d_z], w=[uT.b])
                y2T = uT
                sm_ = {n: SB(ph, "s5" + n, [128, 2, 16], F32) for n in
                       ("lr", "li", "dt", "th", "rho", "sn", "cs", "thr", "ar", "ai", "den", "fr", "fi", "t1", "t2", "tf", "dl", "rho8", "th8")}
                smi = SB(ph, "s5i", [128, 2, 16], I32)
                dma("sp", sm_["lr"][:], s5_lr[:, l, :, :], w=[sm_["lr"].b])
                dma("sp", sm_["li"][:], s5_li[:, l, :, :], w=[sm_["li"].b])
                dma("sp", sm_["dt"][:], s5_ldt[:, l, :, :], w=[sm_["dt"].b])

                def reduce_angle(src, dst, tf, ti_):
                    TS("dve", tf[:], src[:], 1.0 / (2 * PI), None, ALU.mult, None, [src.b], [tf.b])
                    CP("dve", ti_[:], tf[:], [tf.b], [ti_.b])
                    CP("dve", tf[:], ti_[:], [ti_.b], [tf.b])
                    STT(dst[:], tf[:], -2 * PI, src[:], ALU.mult, ALU.add, [tf.b, src.b], [dst.b])
                    TS("dve", dst[:], dst[:], -PI, PI, ALU.max, ALU.min, [dst.b], [dst.b])

                S = sm_
                ACT(S["dt"][:], S["dt"][:], AF.Exp, [S["dt"].b], [S["dt"].b])
                TT("dve", S["th"][:], S["dt"][:], S["li"][:], ALU.mult, [S["dt"].b, S["li"].b], [S["th"].b])
                TT("dve", S["dl"][:], S["dt"][:], S["lr"][:], ALU.mult, [S["dt"].b, S["lr"].b], [S["dl"].b])
                ACT(S["rho"][:], S["dl"][:], AF.Exp, [S["dl"].b], [S["rho"].b])
                reduce_angle(S["th"], S["thr"], S["t1"], smi)
                ACT(S["sn"][:], S["thr"][:], AF.Sin, [S["thr"].b], [S["sn"].b])
                TS("dve", S["t2"][:], S["thr"][:], PI / 2, None, ALU.add, None, [S["thr"].b], [S["t2"].b])
                reduce_angle(S["t2"], S["cs"], S["t1"], smi)
                ACT(S["cs"][:], S["cs"][:], AF.Sin, [S["cs"].b], [S["cs"].b])
                TT("dve", S["ar"][:], S["rho"][:], S["cs"][:], ALU.mult, [S["rho"].b, S["cs"].b], [S["ar"].b])
                TT("dve", S["ai"][:], S["rho"][:], S["sn"][:], ALU.mult, [S["rho"].b, S["sn"].b], [S["ai"].b])
                TT("dve", S["den"][:], S["lr"][:], S["lr"][:], ALU.mult, [S["lr"].b], [S["den"].b])
                TT("dve", S["t1"][:], S["li"][:], S["li"][:], ALU.mult, [S["li"].b], [S["t1"].b])
                TT("dve", S["den"][:], S["den"][:], S["t1"][:], ALU.add, [S["den"].b, S["t1"].b], [S["den"].b])
                op("dve", lambda E: E.reciprocal(out=S["den"][:], in_=S["den"][:]), r=[S["den"].b], w=[S["den"].b])
                TS("dve", S["ar"][:], S["ar"][:], -1.0, None, ALU.add, None, [S["ar"].b], [S["ar"].b])
                TT("dve", S["fr"][:], S["ar"][:], S["lr"][:], ALU.mult, [S["ar"].b, S["lr"].b], [S["fr"].b])
                TT("dve", S["t1"][:], S["ai"][:], S["li"][:], ALU.mult, [S["ai"].b, S["li"].b], [S["t1"].b])
                TT("dve", S["fr"][:], S["fr"][:], S["t1"][:], ALU.add, [S["fr"].b, S["t1"].b], [S["fr"].b])
                TT("dve", S["fr"][:], S["fr"][:], S["den"][:], ALU.mult, [S["fr"].b, S["den"].b], [S["fr"].b])
                TT("dve", S["fi"][:], S["ai"][:], S["lr"][:], ALU.mult, [S["ai"].b, S["lr"].b], [S["fi"].b])
                TT("dve", S["t1"][:], S["ar"][:], S["li"][:], ALU.mult, [S["ar"].b, S["li"].b], [S["t1"].b])
                TT("dve", S["fi"][:], S["fi"][:], S["t1"][:], ALU.subtract, [S["fi"].b, S["t1"].b], [S["fi"].b])
                TT("dve", S["fi"][:], S["fi"][:], S["den"][:], ALU.mult, [S["fi"].b, S["den"].b], [S["fi"].b])

                pwr = SB(ph, "pwr", [128, 9, 2, 16], F32)
                pwi = SB(ph, "pwi", [128, 9, 2, 16], F32)
                npwr = SB(ph, "npwr", [128, 9, 2, 16], F32)
                for tau in range(9):
                    TS("dve", S["t1"][:], S["thr"][:], float(tau), None, ALU.mult, None, [S["thr"].b], [S["t1"].b])
                    reduce_angle(S["t1"], S["t2"], S["tf"], smi)
                    ACT(S["sn"][:], S["t2"][:], AF.Sin, [S["t2"].b], [S["sn"].b])
                    TS("dve", S["t1"][:], S["t2"][:], PI / 2, None, ALU.add, None, [S["t2"].b], [S["t1"].b])
                    reduce_angle(S["t1"], S["cs"], S["tf"], smi)
                    ACT(S["cs"][:], S["cs"][:], AF.Sin, [S["cs"].b], [S["cs"].b])
                    TS("dve", S["t1"][:], S["dl"][:], float(tau), None, ALU.mult, None, [S["dl"].b], [S["t1"].b])
                    ACT(S["t1"][:], S["t1"][:], AF.Exp, [S["t1"].b], [S["t1"].b])
                    TT("dve", pwr[:, tau], S["t1"][:], S["cs"][:], ALU.mult, [S["t1"].b, S["cs"].b], [pwr.b])
                    TT("dve", pwi[:, tau], S["t1"][:], S["sn"][:], ALU.mult, [S["t1"].b, S["sn"].b], [pwi.b])
                TS("dve", npwr[:], pwr[:], -1.0, None, ALU.mult, None, [pwr.b], [npwr.b])
                TS("dve", S["t1"][:], S["dl"][:], 8.0, None, ALU.mult, None, [S["dl"].b], [S["t1"].b])
                ACT(S["rho8"][:], S["t1"][:], AF.Exp, [S["t1"].b], [S["rho8"].b])
                TS("dve", S["t1"][:], S["thr"][:], 8.0, None, ALU.mult, None, [S["thr"].b], [S["t1"].b])
                reduce_angle(S["t1"], S["th8"], S["tf"], smi)

                bp = [SB(ph, "bp%d" % i, [128, 2, 128], F32) for i in range(2)]
                bb = [[SB(ph, "bb%d_%d" % (d, pp), [128, 2, 128], BF16) for pp in range(4)] for d in range(2)]
                dgs = [SB(ph, "dg%d" % i, [128, 3, 4, 128], BF16) for i in range(2)]
                Xb4 = [SB(ph, "Xb4_%d" % i, [128, 4, 2, 128], BF16) for i in range(2)]
                identb4 = SB(ph, "identb4", [128, 4, 128], BF16)
                for i_ in range(4):
                    CP("dve", identb4[:, i_, :], ident_b[:], [ident_b.b], [identb4.b])
                npwi = SB(ph, "npwi", [128, 9, 2, 16], F32)
                TS("dve", npwi[:], pwi[:], -1.0, None, ALU.mult, None, [pwi.b], [npwi.b])
                cf = [SB(ph, "cf%d" % pp, [128, 2, 128], F32) for pp in range(4)]
                lhsC = [SB(ph, "lhsC%d" % pp, [128, 2, 128], BF16) for pp in range(4)]
                xs = [SB(ph, "xs%d" % i, [128, 3, 128], F32) for i in range(2)]
                lhsP = SB(ph, "lhsP", [128, 8, 4, 2, 128], BF16)
                BD = SB(ph, "BD", [128, 8, 128], BF16)
                lhsQ = SB(ph, "lhsQ", [128, 8, 4, 2, 128], BF16)
                diagD = SB(ph, "diagD", [128, 128], F32)
                sdT = SB(ph, "sdT", [128, 4], F32)
                dma("sp", sdT[:], s5_dT[:, l, :], w=[sdT.b])
                iot = SB(ph, "iot64", [128, 64], F32)
                dma("sp", iot[:], c_iota[:, 0, 0:64], w=[iot.b])
                a64 = [SB(ph, "a64_%d" % i, [128, 64], F32) for i in range(3)]
                a64i = SB(ph, "a64i", [128, 64], I32)
                tabC = SB(ph, "tabC", [128, 4, 64], F32)
                tabS = SB(ph, "tabS", [128, 4, 64], F32)
                tabN = SB(ph, "tabN", [128, 4, 64], F32)
                Vt = [SB(ph, "Vt%d" % i, [128, 4, 2, 64], F32) for i in range(2)]
                Wk = [{n: SB(ph, "wk%s%d" % (n, i), [128, 4, 64], F32) for n in ("m1", "m2", "m3", "m4", "gr", "gi")} for i in range(2)]
                Hre = SB(ph, "Hre", [128, 4, 65], F32)
                Him = SB(ph, "Him", [128, 4, 65], F32)
                Hb = [SB(ph, "Hb%d" % i, [128, 2, 4, 64], BF16) for i in range(2)]
                nlong[0] = 4
                vt = SB(ph, "vt", [128, NB, 128], BF16)
                dma("sp", vt[:], vtm[:, 0:128].rearrange("(b p) c -> p b c", p=128), r=[d_z], w=[vt.b])
                mP = SB(ph, "mP", [128, 512], BF16)
                mN = SB(ph, "mN", [128, 512], BF16)
                dma("pool", mP[:], c_maskP[:, :], w=[mP.b])
                dma("pool", mN[:], c_maskN[:, :], w=[mN.b])
                kT2 = SB(ph, "kT2", [64, 2, NT], BF16)
                dma("sp", kT2[:], kT[:, :, :].rearrange("h d t -> d h t"), r=[d_z], w=[kT2.b])
                srow = SB(ph, "srow", [1, 1024], F32)
                dma("sp", srow[:], sinkrow[l, :, :], w=[srow.b])
                sinkts = [SB(ph, "sinkt%d" % i, [128, 512], BF16) for i in range(2)]
                for kvh in range(2):
                    MSET("pool", sinkts[kvh][:], 0.0, [sinkts[kvh].b])
                    ACT(sinkts[kvh][0:1, :], srow[:, kvh * 512:(kvh + 1) * 512], AF.Exp, [srow.b], [sinkts[kvh].b])
                qblk = [SB(ph, "qblk%d" % i, [64, 4, 128], BF16) for i in range(3)]
                oblk = [SB(ph, "oblk%d" % i, [64, 4, 128], BF16) for i in range(3)]
                pts = [SB(ph, "pt%d" % i, [128, 512], BF16) for i in range(3)]
                rds = [SB(ph, "rd%d" % i, [64, 512], F32) for i in range(3)]
                att_steps = [(kvh, qb) for kvh in range(2) for qb in range(NB)]
                att_st = {"i": 0, "ipt": 0, "pend": []}

                def att_issue():
                    idx = att_st["i"]
                    if idx >= len(att_steps):
                        return
                    att_st["i"] += 1
                    kvh, qb = att_steps[idx]
                    qv = qblk[idx % 3]
                    dma("sp", qv[:], qT[4 * kvh:4 * kvh + 4, :, qb * 128:(qb + 1) * 128].rearrange("h d t -> d h t"), r=[d_z], w=[qv.b])
                    if qb < LC // 128:
                        keys = [(kt_, None) for kt_ in range(LC // 128)]
                    else:
                        n = qb - LC // 128
                        keys = [(kt_, None) for kt_ in range(LC // 128)]
                        if n - 1 >= 0:
                            keys.append((qb - 1, mP))
                        keys.append((qb, None))
                        if n + 1 < L // 128:
                            keys.append((qb + 1, mN))
                    pso, psd = nextps(long=True), nextps(long=True)
                    for i, (kt_, msk) in enumerate(keys):
                        pss = nextps()
                        MM(pss[:, :], kT2[:, kvh, kt_ * 128:(kt_ + 1) * 128], qv[:], True, msk is None, [kT2.b, qv.b], [pss.b])
                        if msk is not None:
                            MM(pss[:, :], ident_b[:], msk[:], False, True, [ident_b.b, msk.b], [pss.b])
                        pt = pts[att_st["ipt"] % 3]
                        att_st["ipt"] += 1
                        ACT(pt[:], pss[:, :], AF.Exp, [pss.b], [pt.b], scale=0.125)
                        MM(pso[0:64, :], vt[:, kt_, kvh * 64:(kvh + 1) * 64], pt[:], i == 0, i == len(keys) - 1, [vt.b, pt.b], [pso.b])
                        MM(psd[0:64, :], ones_b[:, 0:64], pt[:], i == 0, False, [ones_b.b, pt.b], [psd.b])
                    MM(psd[0:64, :], ones_b[:, 0:64], sinkts[kvh][:], False, True, [ones_b.b, sinkts[kvh].b], [psd.b])
                    rd = rds[idx % 3]
                    ACT(rd[:], psd[0:64, :], AF.Ln, [psd.b], [rd.b])
                    ACT(rd[:], rd[:], AF.Exp, [rd.b], [rd.b], scale=-1.0)
                    att_st["pend"].append((idx, kvh, qb, pso, rd))

                def att_finalize():
                    if not att_st["pend"]:
                        return
                    idx, kvh, qb, pso, rd = att_st["pend"].pop(0)
                    o_ = oblk[idx % 3]
                    TT("dve", o_[:], pso[0:64, :].rearrange("p (h q) -> p h q", h=4), rd[:].rearrange("p (h q) -> p h q", h=4),
                       ALU.mult, [pso.b, rd.b], [o_.b])
                    dma("sp", yT[4 + 2 * kvh:6 + 2 * kvh, :, qb * 128:(qb + 1) * 128].rearrange("c (two d) t -> d (c two) t", two=2), o_[:],
                        r=[o_.b], w=[d_y])

                it = 0
                for fc in range(4):
                    TS("dve", diagD[:], ident_f[:], sdT[:, fc:fc + 1], None, ALU.mult, None, [ident_f.b, sdT.b], [diagD.b])
                    for pp in range(4):
                        pr = fc * 4 + pp
                        b_ = bp[pp % 2]
                        dma("sp", b_[:, 0, :], s5_Bre[l, pr], w=[b_.b])
                        dma("sp", b_[:, 1, :], s5_Bim[l, pr], w=[b_.b])
                        dma("sp", cf[pp][:, 0, :], s5_Cre[l, pr], w=[cf[pp].b])
                        dma("sp", cf[pp][:, 1, :], s5_Cim[l, pr], w=[cf[pp].b])
                        CP("act", lhsC[pp][:, 0, :], cf[pp][:, 0, :], [cf[pp].b], [lhsC[pp].b])
                        TS("dve", lhsC[pp][:, 1, :], cf[pp][:, 1, :], -1.0, None, ALU.mult, None, [cf[pp].b], [lhsC[pp].b])
                        for d in range(2):
                            fr_ = S["fr"][:, d, pr:pr + 1]
                            fi_ = S["fi"][:, d, pr:pr + 1]
                            x_ = xs[d]
                            o_ = bb[d][pp]
                            TS("dve", x_[:, 0, :], b_[:, 1, :], fi_, None, ALU.mult, None, [b_.b, S["fi"].b], [x_.b])
                            STT(o_[:, 0, :], b_[:, 0, :], fr_, x_[:, 0, :], ALU.mult, ALU.subtract, [b_.b, S["fr"].b, x_.b], [o_.b])
                            TS("dve", x_[:, 1, :], b_[:, 0, :], fi_, None, ALU.mult, None, [b_.b, S["fi"].b], [x_.b])
                            STT(o_[:, 1, :], b_[:, 1, :], fr_, x_[:, 1, :], ALU.mult, ALU.add, [b_.b, S["fr"].b, x_.b], [o_.b])
                    for d in range(2):
                        sl4 = slice(fc * 4, fc * 4 + 4)
                        for tau in range(9):
                            dg = dgs[tau % 2]
                            for vi, pw_ in enumerate((pwr, pwi, npwi)):
                                TT("dve", dg[:, vi], identb4[:], pw_[:, tau, d, sl4].unsqueeze(2).to_broadcast([128, 4, 128]), ALU.mult,
                                   [identb4.b, pw_.b], [dg.b])

                            def four(dst_banks, combos):
                                for pp in range(4):
                                    pq = dst_banks[pp // 2]
                                    base = (pp % 2) * 256
                                    for ri, (l0, r0, l1, r1, bufs) in enumerate(combos(pp)):
                                        o_ap = pq[:, base + ri * 128:base + (ri + 1) * 128]
                                        MM(o_ap, l0, r0, True, False, bufs, [pq.b])
                                        MM(o_ap, l1, r1, False, True, bufs, [pq.b])

                            def vw(pq):
                                return pq[:].rearrange("p (a b c) -> p a b c", a=2, b=2)
                            if tau < 8:
                                pa, pb = nextps(), nextps()
                                four((pa, pb), lambda pp: (
                                    (bb[d][pp][:, 0, :], dg[:, 0, pp, :], bb[d][pp][:, 1, :], dg[:, 2, pp, :], [bb[d][pp].b, dg.b]),
                                    (bb[d][pp][:, 1, :], dg[:, 0, pp, :], bb[d][pp][:, 0, :], dg[:, 1, pp, :], [bb[d][pp].b, dg.b])))
                                CP("act", lhsP[:, tau, 0:2, :, :], vw(pa), [pa.b], [lhsP.b])
                                CP("act", lhsP[:, tau, 2:4, :, :], vw(pb), [pb.b], [lhsP.b])
                                pc, pd = nextps(), nextps()
                                four((pc, pd), lambda pp: (
                                    (dg[:, 0, pp, :], bb[d][pp][:, 0, :], dg[:, 2, pp, :], bb[d][pp][:, 1, :], [bb[d][pp].b, dg.b]),
                                    (dg[:, 0, pp, :], bb[d][pp][:, 1, :], dg[:, 1, pp, :], bb[d][pp][:, 0, :], [bb[d][pp].b, dg.b])))
                                xb = Xb4[tau % 2]
                                CP("act", xb[:, 0:2, :, :], vw(pc), [pc.b], [xb.b])
                                CP("act", xb[:, 2:4, :, :], vw(pd), [pd.b], [xb.b])
                                psd_ = nextps()
                                for pp in range(4):
                                    for ri in range(2):
                                        MM(psd_[:, 0:128], xb[:, pp, ri, :], lhsC[pp][:, ri, :], pp == 0 and ri == 0, pp == 3 and ri == 1,
                                           [xb.b, lhsC[pp].b], [psd_.b])
                                if tau == 0 and d == 0:
                                    TT("dve", BD[:, tau, :], psd_[:, 0:128], diagD[:], ALU.add, [psd_.b, diagD.b], [BD.b])
                                else:
                                    CP("act", BD[:, tau, :], psd_[:, 0:128], [psd_.b], [BD.b])
                            if tau >= 1:
                                t = tau - 1
                                pe_, pf = nextps(), nextps()
                                four((pe_, pf), lambda pp: (
                                    (dg[:, 0, pp, :], lhsC[pp][:, 0, :], dg[:, 1, pp, :], lhsC[pp][:, 1, :], [lhsC[pp].b, dg.b]),
                                    (dg[:, 2, pp, :], lhsC[pp][:, 0, :], dg[:, 0, pp, :], lhsC[pp][:, 1, :], [lhsC[pp].b, dg.b])))
                                CP("act", lhsQ[:, t, 0:2, :, :], vw(pe_), [pe_.b], [lhsQ.b])
                                CP("act", lhsQ[:, t, 2:4, :, :], vw(pf), [pf.b], [lhsQ.b])
                        for pp in range(4):
                            pr = fc * 4 + pp
                            a0, a1, a2 = a64
                            TS("dve", a0[:], iot[:], S["th8"][:, d, pr:pr + 1], None, ALU.mult, None, [iot.b, S["th8"].b], [a0.b])
                            reduce_angle(a0, a1, a2, a64i)
                            ACT(tabS[:, pp, :], a1[:], AF.Sin, [a1.b], [tabS.b])
                            TS("dve", a0[:], a1[:], PI / 2, None, ALU.add, None, [a1.b], [a0.b])
                            reduce_angle(a0, a1, a2, a64i)
                            ACT(tabC[:, pp, :], a1[:], AF.Sin, [a1.b], [tabC.b])
                        TS("dve", tabN[:], tabS[:], -1.0, None, ALU.mult, None, [tabS.b], [tabN.b])
                        tbs = [tabC.b, tabS.b, tabN.b]
                        order = list(range(NTI)) if d == 0 else [0] + list(range(NTI - 1, 0, -1))
                        MSET("dve", Hre[:, :, 0:1], 0.0, [Hre.b])
                        MSET("dve", Him[:, :, 0:1], 0.0, [Him.b])
                        for oi, ti in enumerate(order):
                            t0, sz = TILES[ti]
                            NJ = sz // 8
                            useq = uT[:, fc, t0:t0 + sz] if d == 0 else uT[:, fc, t0:t0 + sz][:, ::-1]
                            us = [useq[:, s_::8] for s_ in range(8)]
                            V = Vt[it % 2]
                            W = Wk[it % 2]
                            hb = Hb[it % 2]
                            it += 1
                            att_finalize()
                            pv = nextps()
                            pvv = pv[:].rearrange("p (a b c) -> p a b c", a=4, b=2)
                            for pp in range(4):
                                for ri in range(2):
                                    for s_ in range(8):
                                        MM(pvv[:, pp, ri, 0:NJ], lhsP[:, 7 - s_, pp, ri, :], us[s_], s_ == 0, s_ == 7, [lhsP.b, uT.b], [pv.b])
                            CP("act", V[:, :, :, 0:NJ], pvv[:, :, :, 0:NJ], [pv.b], [V.b])
                            tC, tS, tN = tabC[:, :, 0:NJ], tabS[:, :, 0:NJ], tabN[:, :, 0:NJ]
                            vre, vim = V[:, :, 0, 0:NJ], V[:, :, 1, 0:NJ]
                            TT("dve", W["m1"][:, :, 0:NJ], vre, tC, ALU.mult, [V.b] + tbs, [W["m1"].b])
                            TT("dve", W["m2"][:, :, 0:NJ], vim, tS, ALU.mult, [V.b] + tbs, [W["m2"].b])
                            TT("dve", W["m1"][:, :, 0:NJ], W["m1"][:, :, 0:NJ], W["m2"][:, :, 0:NJ], ALU.add, [W["m1"].b, W["m2"].b], [W["m1"].b])
                            TT("pool", W["m3"][:, :, 0:NJ], vim, tC, ALU.mult, [V.b] + tbs, [W["m3"].b])
                            TT("pool", W["m4"][:, :, 0:NJ], vre, tN, ALU.mult, [V.b] + tbs, [W["m4"].b])
                            TT("pool", W["m3"][:, :, 0:NJ], W["m3"][:, :, 0:NJ], W["m4"][:, :, 0:NJ], ALU.add, [W["m3"].b, W["m4"].b], [W["m3"].b])
                            for pp in range(4):
                                pr = fc * 4 + pp
                                rho_b = S["rho8"][:, d, pr:pr + 1].to_broadcast([128, NJ])
                                SCAN(W["gr"][:, pp, 0:NJ], rho_b, W["m1"][:, pp, 0:NJ], Hre[:, pp, 0:1], [W["m1"].b, S["rho8"].b, Hre.b], [W["gr"].b])
                                SCAN(W["gi"][:, pp, 0:NJ], rho_b, W["m3"][:, pp, 0:NJ], Him[:, pp, 0:1], [W["m3"].b, S["rho8"].b, Him.b], [W["gi"].b])
                            TT("dve", W["m2"][:, :, 0:NJ], W["gr"][:, :, 0:NJ], tC, ALU.mult, [W["gr"].b] + tbs, [W["m2"].b])
                            TT("dve", W["m4"][:, :, 0:NJ], W["gi"][:, :, 0:NJ], tN, ALU.mult, [W["gi"].b] + tbs, [W["m4"].b])
                            TT("dve", Hre[:, :, 1:NJ + 1], W["m2"][:, :, 0:NJ], W["m4"][:, :, 0:NJ], ALU.add, [W["m2"].b, W["m4"].b], [Hre.b])
                            TT("pool", W["m1"][:, :, 0:NJ], W["gi"][:, :, 0:NJ], tC, ALU.mult, [W["gi"].b] + tbs, [W["m1"].b])
                            TT("pool", W["m3"][:, :, 0:NJ], W["gr"][:, :, 0:NJ], tS, ALU.mult, [W["gr"].b] + tbs, [W["m3"].b])
                            TT("pool", Him[:, :, 1:NJ + 1], W["m1"][:, :, 0:NJ], W["m3"][:, :, 0:NJ], ALU.add, [W["m1"].b, W["m3"].b], [Him.b])
                            CP("pool", hb[:, 0, :, 0:NJ], Hre[:, :, 0:NJ], [Hre.b], [hb.b])
                            CP("pool", hb[:, 1, :, 0:NJ], Him[:, :, 0:NJ], [Him.b], [hb.b])
                            att_issue()
                            py = nextps(long=True)
                            pyv = py[:].rearrange("p (t j) -> p t j", t=8)
                            for t in range(8):
                                nmm = (t + 1) + 8
                                imm = 0
                                for s_ in range(t + 1):
                                    imm += 1
                                    MM(pyv[:, t, 0:NJ], BD[:, t - s_, :], us[s_], imm == 1, imm == nmm, [BD.b, uT.b], [py.b])
                                for pp in range(4):
                                    for ri in range(2):
                                        imm += 1
                                        MM(pyv[:, t, 0:NJ], lhsQ[:, t, pp, ri, :], hb[:, ri, pp, 0:NJ], imm == 1, imm == nmm, [lhsQ.b, hb.b], [py.b])
                            yv = yacc[:, t0:t0 + sz] if d == 0 else yacc[:, t0:t0 + sz][:, ::-1]
                            yv = yv.rearrange("p (j t) -> p t j", t=8)
                            if d == 0:
                                CP("act", yv, pyv[:, :, 0:NJ], [py.b], [yacc.b])
                            else:
                                TT("dve", yv, pyv[:, :, 0:NJ], yv, ALU.add, [py.b, yacc.b], [yacc.b])
                            CP("dve", Hre[:, :, 0:1], Hre[:, :, NJ:NJ + 1], [Hre.b], [Hre.b])
                            CP("dve", Him[:, :, 0:1], Him[:, :, NJ:NJ + 1], [Him.b], [Him.b])
                    ACT(y2T[:, fc, :], yacc[:], AF.Gelu_apprx_tanh, [yacc.b], [y2T.b])
                while att_st["i"] < len(att_steps) or att_st["pend"]:
                    att_finalize()
                    att_issue()
                wg = SB(ph, "wg", [128, 4, 512], BF16)
                bg = SB(ph, "bg", [128, 4], F32)
                gs = [SB(ph, "gs%d" % i, [128, 512], F32) for i in range(2)]
                yst = [SB(ph, "yst%d" % i, [128, 512], BF16) for i in range(2)]
                dma("pool", wg[:], s5_wglu[l].rearrange("(k p) n -> p k n", p=128), w=[wg.b])
                dma("sp", bg[:], s5_bgT[:, l, :], w=[bg.b])
                for co in range(4):
                    for ti, (t0, sz) in enumerate(TILES):
                        ys = yst[ti % 2]
                        ps = nextps()
                        for k in range(4):
                            MM(ps[:, 0:sz], wg[:, k, co * 128:(co + 1) * 128], y2T[:, k, t0:t0 + sz], k == 0, k == 3, [wg.b, y2T.b], [ps.b])
                        g_ = gs[ti % 2]
                        ACT(g_[:, 0:sz], ps[:, 0:sz], AF.Sigmoid, [ps.b, bg.b], [g_.b], bias=bg[:, co:co + 1])
                        TT("dve", ys[:, 0:sz], y2T[:, co, t0:t0 + sz], g_[:, 0:sz], ALU.mult, [y2T.b, g_.b], [ys.b])
                        dma("sp", yT[co, :, t0:t0 + sz], ys[:, 0:sz], r=[ys.b], w=[d_y])
                P.barrier()
                nlong[0] = 2
            if stop_after == "S5":
                break
            if stop_after == "ATT":
                break
            with ExitStack() as ph:
                hgm = SB(ph, "hgm", [32, 2, 128], F32)
                rmask = SB(ph, "rmask", [128, 512], F32)
                hng = SB(ph, "hng", [128, 4], F32)
                dma("sp", hgm[:], c_hgmask[:, :, :], w=[hgm.b])
                dma("sp", rmask[:], c_rmask[:, :], w=[rmask.b])
                dma("sp", hng[:], hg_ngT[:, l, :], w=[hng.b])
                D2 = range(2)
                Sf = [SB(ph, "Sf%d" % d, [128, 4, 128], F32) for d in D2]
                Sb = [SB(ph, "Sb%d" % d, [128, 4, 128], BF16) for d in D2]
                lfts = [SB(ph, "lft%d" % d, [128, 4, 512], F32) for d in D2]
                kkts = [SB(ph, "kkt%d" % d, [128, 4, 512], BF16) for d in D2]
                hqts = [SB(ph, "hqt%d" % d, [128, 4, 512], BF16) for d in D2]
                vchs = [SB(ph, "vch%d" % d, [32, 16, 512], BF16) for d in D2]
                bts = [SB(ph, "hbt%d" % d, [128, 4, 512], F32) for d in D2]
                e1s = [SB(ph, "he1%d" % d, [128, 4, 512], F32) for d in D2]
                e2s = [SB(ph, "he2%d" % d, [128, 4, 512], F32) for d in D2]
                qts = [SB(ph, "hqt_%d" % d, [128, 4, 512], BF16) for d in D2]
                kts = [SB(ph, "hkt_%d" % d, [128, 4, 512], BF16) for d in D2]
                khs = [SB(ph, "hkh_%d" % d, [128, 4, 512], BF16) for d in D2]
                ots = [SB(ph, "hot%d" % d, [128, 4, 512], F32) for d in D2]
                attm = [[SB(ph, "attm%d_%d" % (d, i), [32, 128], BF16) for i in range(2)] for d in D2]
                ktm = [[SB(ph, "ktm%d_%d" % (d, i), [32, 512], BF16) for i in range(2)] for d in D2]
                orders = [list(range(NTI)), [0] + list(range(NTI - 1, 0, -1))]
                for d in D2:
                    MSET("pool", Sf[d][:], 0.0, [Sf[d].b])
                    MSET("pool", Sb[d][:], 0.0, [Sb[d].b])
                ich = [0, 0]

                def hg_setup(d, ti):
                    t0, sz = TILES[ti]
                    nch = sz // 32
                    lft, kkt, hqt, vch, bt, e1, e2, qt, kt, kh = lfts[d], kkts[d], hqts[d], vchs[d], bts[d], e1s[d], e2s[d], qts[d], kts[d], khs[d]
                    dma("sp", lft[:, :, 0:sz], lf[d, :, :, t0:t0 + sz].rearrange("h p t -> p h t"), r=[d_z], w=[lft.b])
                    dma("sp", kkt[:, :, 0:sz], kk[d, :, :, t0:t0 + sz].rearrange("h p t -> p h t"), r=[d_z], w=[kkt.b])
                    dma("sp", hqt[:, :, 0:sz], hgq[:, :, t0:t0 + sz].rearrange("h p t -> p h t"), r=[d_z], w=[hqt.b])
                    dma("sp", vch[:, 0:nch, :], vtm[t0:t0 + sz, 128:640].rearrange("(c p) f -> p c f", p=32), r=[d_z], w=[vch.b])
                    for h in range(4):
                        if d == 0:
                            SCAN(bt[:, h, 0:sz], rmask[:, 0:sz], lft[:, h, 0:sz], 0.0, [rmask.b, lft.b], [bt.b])
                        else:
                            SCAN(bt[:, h, 0:sz][:, ::-1], rmask[:, 0:sz], lft[:, h, 0:sz][:, ::-1], 0.0, [rmask.b, lft.b], [bt.b])
                    jl0 = 31 if d == 0 else 0
                    b4 = bt[:, :, 0:sz].rearrange("p h (c t) -> p h c t", t=32)
                    TT("dve", e2[:, :, 0:sz].rearrange("p h (c t) -> p h c t", t=32), b4[:, :, :, jl0:jl0 + 1].to_broadcast([128, 4, nch, 32]), b4,
                       ALU.subtract, [bt.b], [e2.b])
                    ACT(e1[:, :, 0:sz], bt[:, :, 0:sz], AF.Exp, [bt.b], [e1.b], scale=-1.0)
                    ACT(e2[:, :, 0:sz], e2[:, :, 0:sz], AF.Exp, [e2.b], [e2.b])
                    ACT(bt[:, :, 0:sz], bt[:, :, 0:sz], AF.Exp, [bt.b], [bt.b])
                    TT("dve", qt[:, :, 0:sz], hqt[:, :, 0:sz], bt[:, :, 0:sz], ALU.mult, [hqt.b, bt.b], [qt.b])
                    TT("pool", kt[:, :, 0:sz], kkt[:, :, 0:sz], e1[:, :, 0:sz], ALU.mult, [kkt.b, e1.b], [kt.b])
                    TT("dve", kh[:, :, 0:sz], kkt[:, :, 0:sz], e2[:, :, 0:sz], ALU.mult, [kkt.b, e2.b], [kh.b])

                def hg_chunk(d, ci):
                    vch, bt, qt, kt, kh, ot = vchs[d], bts[d], qts[d], kts[d], khs[d], ots[d]
                    c0 = ci * 32
                    am, km = attm[d][ich[d] % 2], ktm[d][ich[d] % 2]
                    ich[d] += 1
                    psA = nextps()
                    for h in range(4):
                        MM(psA[0:32, h * 32:(h + 1) * 32], kt[:, h, c0:c0 + 32], qt[:, h, c0:c0 + 32], True, True, [kt.b, qt.b], [psA.b])
                    TT("dve", am[:], psA[0:32, 0:128], hgm[:, d, :], ALU.mult, [psA.b, hgm.b], [am.b])
                    psT = nextps()
                    psTb = psT[:].bitcast(BF16)
                    for h in range(4):
                        TR(psTb[0:32, h * 128:(h + 1) * 128], kh[:, h, c0:c0 + 32], ident_b[:], [kh.b, ident_b.b], [psT.b])
                    CP("act", km[:], psTb[0:32, 0:512], [psT.b], [km.b])
                    psO = nextps()
                    for h in range(4):
                        MM(psO[:, h * 32:(h + 1) * 32], vch[:, ci, h * 128:(h + 1) * 128], am[:, h * 32:(h + 1) * 32], True, False, [vch.b, am.b], [psO.b])
                        MM(psO[:, h * 32:(h + 1) * 32], Sb[d][:, h, :], qt[:, h, c0:c0 + 32], False, True, [Sb[d].b, qt.b], [psO.b])
                    CP("act", ot[:, :, c0:c0 + 32], psO[:, 0:128].rearrange("p (h t) -> p h t", h=4), [psO.b], [ot.b])
                    psS = nextps()
                    for h in range(4):
                        MM(psS[:, h * 128:(h + 1) * 128], km[:, h * 128:(h + 1) * 128], vch[:, ci, h * 128:(h + 1) * 128], True, True, [km.b, vch.b], [psS.b])
                    jl = c0 + 31 if d == 0 else c0
                    for h in range(4):
                        STT(Sf[d][:, h, :], Sf[d][:, h, :], bt[:, h, jl:jl + 1], psS[:, h * 128:(h + 1) * 128], ALU.mult, ALU.add, [Sf[d].b, bt.b, psS.b], [Sf[d].b])
                    CP("act", Sb[d][:], Sf[d][:], [Sf[d].b], [Sb[d].b])

                obw = ofw2
                for oi in range(NTI):
                    for d in D2:
                        hg_setup(d, orders[d][oi])
                    nch = TILES[orders[0][oi]][1] // 32
                    for k_ in range(nch):
                        for d in D2:
                            hg_chunk(d, k_ if d == 0 else nch - 1 - k_)
                    for d in D2:
                        t0, sz = TILES[orders[d][oi]]
                        dma("sp", (ofw if d == 0 else obw)[:, :, t0:t0 + sz].rearrange("h p t -> p h t"), ots[d][:, :, 0:sz], r=[ots[d].b], w=[d_of])
                sq = SB(ph, "hsq", [128, 4, 512], BF16)
                rs = SB(ph, "hrs", [128, 4, 512], F32)
                hggts = [SB(ph, "hggt%d" % i, [128, 4, 512], BF16) for i in range(2)]
                ysts = [SB(ph, "hyst%d" % i, [128, 4, 512], BF16) for i in range(2)]
                for ti, (t0, sz) in enumerate(TILES):
                    oa, obt, yh = ots[ti % 2], e1s[ti % 2], e2s[ti % 2]
                    hggt, yst = hggts[ti % 2], ysts[ti % 2]
                    dma("sp", oa[:, :, 0:sz], ofw[:, :, t0:t0 + sz].rearrange("h p t -> p h t"), r=[d_of], w=[oa.b])
                    dma("sp", obt[:, :, 0:sz], obw[:, :, t0:t0 + sz].rearrange("h p t -> p h t"), r=[d_of], w=[obt.b])
                    dma("sp", hggt[:, :, 0:sz], hgg[:, :, t0:t0 + sz].rearrange("h p t -> p h t"), r=[d_z], w=[hggt.b])
                    TT("pool", oa[:, :, 0:sz], oa[:, :, 0:sz], obt[:, :, 0:sz], ALU.add, [oa.b, obt.b], [oa.b])
                    ACT(sq[:, :, 0:sz], oa[:, :, 0:sz], AF.Square, [oa.b], [sq.b])
                    for h in range(4):
                        ps = nextps()
                        MM(ps[:, 0:sz], ones_b[:], sq[:, h, 0:sz], True, True, [ones_b.b, sq.b], [ps.b])
                        ACT(rs[:, h, 0:sz], ps[:, 0:sz], AF.Ln, [ps.b, epsb.b], [rs.b], bias=epsb[:, 0:1], scale=1.0 / 128)
                    ACT(rs[:, :, 0:sz], rs[:, :, 0:sz], AF.Exp, [rs.b], [rs.b], scale=-0.5)
                    for h in range(4):
                        STT(yh[:, h, 0:sz], oa[:, h, 0:sz], hng[:, h:h + 1], rs[:, h, 0:sz], ALU.mult, ALU.mult, [oa.b, hng.b, rs.b], [yh.b])
                    TT("pool", yst[:, :, 0:sz], yh[:, :, 0:sz], hggt[:, :, 0:sz], ALU.mult, [yh.b, hggt.b], [yst.b])
                    dma("sp", yT[8:12, :, t0:t0 + sz].rearrange("h p t -> p h t"), yst[:, :, 0:sz], r=[yst.b], w=[d_y])
                P.barrier()
            if stop_after == "HG":
                break
            with ExitStack() as ph:
                wbr = SB(ph, "wbr", [128, 12, DM], BF16)
                wou = SB(ph, "wou", [128, KC, DM], BF16)
                for n in range(3):
                    dma("pool", wbr[:, n * 4:(n + 1) * 4, :], w_branch[l, n].rearrange("(k p) d -> p k d", p=128), w=[wbr.b])
                dma("pool", wou[:], w_out[l].rearrange("(k p) d -> p k d", p=128), w=[wou.b])
                yts = [SB(ph, "yt%d" % i, [128, 12, 512], BF16) for i in range(2)]
                gts = [SB(ph, "gt%d" % i, [128, 24, 512], BF16) for i in range(2)]
                xt = SB(ph, "mxt", [128, KC, 512], F32)
                ot = SB(ph, "mot", [128, KC, 512], F32)
                mt = SB(ph, "mmt", [128, KC, 512], BF16)
                macc = [SB(ph, "macc%d" % i, [128, 512], F32) for i in range(2)]
                mtmp = [SB(ph, "mtmp%d" % i, [128, 512], F32) for i in range(2)]
                sq = SB(ph, "msq", [128, KC, 512], BF16)
                rs = SB(ph, "mrs", [128, 512], F32)
                tmp = SB(ph, "mtp", [128, KC, 512], F32)
                for ti, (t0, sz) in enumerate(TILES):
                    yt, gt = yts[ti % 2], gts[ti % 2]
                    dma("sp", yt[:, :, 0:sz], yT[:, :, t0:t0 + sz].rearrange("c p t -> p c t"), r=[d_y], w=[yt.b])
                    dma("sp", gt[:, :, 0:sz], gat[:, :, t0:t0 + sz].rearrange("c p t -> p c t"), r=[d_z], w=[gt.b])
                    dma("sp", xt[:, :, 0:sz], xT[:, :, t0:t0 + sz].rearrange("k p t -> p k t"), r=[d_xT], w=[xt.b])
                    for dc in range(KC):
                        ma, mp_ = macc[dc % 2], mtmp[dc % 2]
                        for n in range(3):
                            ps = nextps()
                            for k in range(4):
                                MM(ps[:, 0:sz], wbr[:, n * 4 + k, dc * 128:(dc + 1) * 128], yt[:, n * 4 + k, 0:sz], k == 0, k == 3, [wbr.b, yt.b], [ps.b])
                            g_ = gt[:, n * 8 + dc, 0:sz]
                            if n == 0:
                                TT("dve", ma[:, 0:sz], ps[:, 0:sz], g_, ALU.mult, [ps.b, gt.b], [ma.b])
                            elif n == 1:
                                TT("dve", mp_[:, 0:sz], ps[:, 0:sz], g_, ALU.mult, [ps.b, gt.b], [mp_.b])
                                TT("pool", ma[:, 0:sz], ma[:, 0:sz], mp_[:, 0:sz], ALU.add, [ma.b, mp_.b], [ma.b])
                            else:
                                TT("dve", mp_[:, 0:sz], ps[:, 0:sz], g_, ALU.mult, [ps.b, gt.b], [mp_.b])
                                TT("pool", mt[:, dc, 0:sz], ma[:, 0:sz], mp_[:, 0:sz], ALU.add, [ma.b, mp_.b], [mt.b])
                    for dc in range(KC):
                        ps = nextps()
                        for kc in range(KC):
                            MM(ps[:, 0:sz], wou[:, kc, dc * 128:(dc + 1) * 128], mt[:, kc, 0:sz], kc == 0, kc == KC - 1, [wou.b, mt.b], [ps.b])
                        CP("act", ot[:, dc, 0:sz], ps[:, 0:sz], [ps.b], [ot.b])
                    epilogue(ot, xt, G1, ti, t0, sz, sq, rs, tmp)
                P.barrier()
            if stop_after == "MIX":
                break
            with ExitStack() as ph:
                hT = SB(ph, "hT2", [128, KC, NT], BF16, nb=NTI)
                with ExitStack() as ph1:
                    norm_phase(ph1, l, A2, 24, hT)
                    P.barrier()
                NU = NT + 3
                Uas = [SB(ph, "Ua%d" % i, [128, NU], F32) for i in range(2)]
                Ugs = [SB(ph, "Ug%d" % i, [128, NU], F32) for i in range(2)]
                Ya = SB(ph, "Ya", [128, NU], F32)
                Yg = SB(ph, "Yg", [128, NU], F32)
                ast = [SB(ph, "ast%d" % i, [128, NU], BF16) for i in range(2)]
                cw = SB(ph, "cw", [128, 44, 3], F32)
                cb = SB(ph, "cb", [128, 44], F32)
                wua = [SB(ph, "wua%d" % i, [128, KC, 128], BF16) for i in range(2)]
                wug = [SB(ph, "wug%d" % i, [128, KC, 128], BF16) for i in range(2)]
                dma("sp", cw[:], convT[:, l, :, :], w=[cw.b])
                dma("sp", cb[:], convbT[:, l, :], w=[cb.b])
                for i_ in range(2):
                    MSET("pool", Uas[i_][:], 0.0, [Uas[i_].b])
                    MSET("pool", Ugs[i_][:], 0.0, [Ugs[i_].b])

                def ucol(t):
                    return t + 1 if t < LC else t + 2
                NY = NT + 1
                for j in range(NFC):
                    wa, wg_ = wua[j % 2], wug[j % 2]
                    Ua, Ug = Uas[j % 2], Ugs[j % 2]
                    dma("pool", wa[:], w_up[l, :, j * 128:(j + 1) * 128].rearrange("(k p) n -> p k n", p=128), w=[wa.b])
                    dma("pool", wg_[:], w_up[l, :, FFN + j * 128:FFN + (j + 1) * 128].rearrange("(k p) n -> p k n", p=128), w=[wg_.b])
                    for (wt, U) in ((wa, Ua), (wg_, Ug)):
                        for ti, (t0, sz) in enumerate(TILES):
                            ps = nextps()
                            for kc in range(KC):
                                MM(ps[:, 0:sz], wt[:, kc, :], hT[:, kc, t0:t0 + sz], kc == 0, kc == KC - 1, [wt.b, hT.bs[ti]], [ps.b])
                            CP("act", U[:, ucol(t0):ucol(t0) + sz], ps[:, 0:sz], [ps.b], [U.b])
                    for (U, Y, cj) in ((Ua, Ya, j), (Ug, Yg, NFC + j)):
                        ACT(Y[:, 0:NY], U[:, 0:NY], AF.Identity, [U.b, cw.b, cb.b], [Y.b], bias=cb[:, cj:cj + 1], scale=cw[:, cj, 0:1])
                        STT(Y[:, 0:NY], U[:, 1:NY + 1], cw[:, cj, 1:2], Y[:, 0:NY], ALU.mult, ALU.add, [U.b, cw.b, Y.b], [Y.b])
                        STT(Y[:, 0:NY], U[:, 2:NY + 2], cw[:, cj, 2:3], Y[:, 0:NY], ALU.mult, ALU.add, [U.b, cw.b, Y.b], [Y.b])
                    a_ = ast[j % 2]
                    ACT(Ya[:, 0:NY], Ya[:, 0:NY], AF.Silu, [Ya.b], [Ya.b])
                    TT("dve", a_[:, 0:NY], Ya[:, 0:NY], Yg[:, 0:NY], ALU.mult, [Ya.b, Yg.b], [a_.b])
                    dma("sp", actT[j, :, 0:LC], a_[:, 0:LC], r=[a_.b], w=[d_act])
                    dma("sp", actT[j, :, LC:NT], a_[:, LC + 1:NT + 1], r=[a_.b], w=[d_act])
                P.barrier()
            with ExitStack() as ph:
                wdn = SB(ph, "wdn", [128, NFC, DM], BF16)
                dma("pool", wdn[:, 0:11, :], w_down[l, 0:11 * 128, :].rearrange("(k p) d -> p k d", p=128), w=[wdn.b])
                dma("pool", wdn[:, 11:22, :], w_down[l, 11 * 128:22 * 128, :].rearrange("(k p) d -> p k d", p=128), w=[wdn.b])
                ats = [SB(ph, "at%d" % i, [128, NFC, 512], BF16) for i in range(2)]
                xt = SB(ph, "fxt", [128, KC, 512], F32)
                ot = SB(ph, "fot", [128, KC, 512], F32)
                sq = SB(ph, "fsq", [128, KC, 512], BF16)
                rs = SB(ph, "frs", [128, 512], F32)
                tmp = SB(ph, "ftp", [128, KC, 512], F32)
                for ti, (t0, sz) in enumerate(TILES):
                    at = ats[ti % 2]
                    dma("sp", at[:, :, 0:sz], actT[:, :, t0:t0 + sz].rearrange("c p t -> p c t"), r=[d_act], w=[at.b])
                    dma("sp", xt[:, :, 0:sz], xT[:, :, t0:t0 + sz].rearrange("k p t -> p k t"), r=[d_xT], w=[xt.b])
                    for dc in range(KC):
                        ps = nextps()
                        for k in range(NFC):
                            MM(ps[:, 0:sz], wdn[:, k, dc * 128:(dc + 1) * 128], at[:, k, 0:sz], k == 0, k == NFC - 1, [wdn.b, at.b], [ps.b])
                        CP("act", ot[:, dc, 0:sz], ps[:, 0:sz], [ps.b], [ot.b])
                    epilogue(ot, xt, G2, ti, t0, sz, sq, rs, tmp)
                P.barrier()
        if stop_after is None:
            with ExitStack() as ph:
                xtt = [SB(ph, "fxtt%d" % i, [128, KC, 128], F32) for i in range(2)]
                orow = [SB(ph, "orow%d" % i, [128, DM], F32) for i in range(2)]
                for tb in range(LC // 128, NB):
                    a, o_ = xtt[tb % 2], orow[tb % 2]
                    dma("sp", a[:], xT[:, :, tb * 128:(tb + 1) * 128].rearrange("k p t -> p k t"), r=[d_xT], w=[a.b])
                    pa, pb = nextps(), nextps()
                    for kc in range(KC):
                        pp_ = pa if kc < 4 else pb
                        TR(pp_[:, (kc % 4) * 128:(kc % 4 + 1) * 128], a[:, kc, :], ident_f[:], [a.b, ident_f.b], [pp_.b])
                    CP("act", o_[:, 0:512], pa[:, :], [pa.b], [o_.b])
                    CP("dve", o_[:, 512:1024], pb[:, :], [pb.b], [o_.b])
                    dma("sp", out[tb * 128 - LC:(tb + 1) * 128 - LC, :], o_[:], r=[o_.b])
        P.barrier()
    return nc


def prep_shared(inp, cfg):
    L, LC, DEPTH = cfg["L"], cfg["LC"], cfg["DEPTH"]
    NT = L + LC
    f = lambda a: np.ascontiguousarray(np.asarray(a, dtype=np.float32))
    sh = {}
    sh["w_mod"] = f(inp["w_mod"][:DEPTH])
    sh["b_modT"] = f(np.asarray(inp["b_mod"])[:DEPTH].reshape(DEPTH, 48, 128).transpose(2, 0, 1))
    sh["norm_gT"] = f(np.asarray(inp["norm_g"])[:DEPTH].reshape(DEPTH, 4, KC, 128).transpose(3, 0, 1, 2))
    w_in = np.asarray(inp["w_in"])[:DEPTH]
    sh["w_in"] = f(w_in)
    idx = []
    for h in range(8):
        idx += [C_Q + h * 64 + (d + 32) % 64 for d in range(64)]
    for h in range(2):
        idx += [C_K + h * 64 + (d + 32) % 64 for d in range(64)]
    sh["w_rot"] = f(w_in[:, :, np.array(idx)])
    rows = L // 64
    row = np.repeat(np.arange(rows, dtype=np.float32), 64)
    col = np.tile(np.arange(64, dtype=np.float32), rows)
    inv = (10000.0 ** (-np.arange(16, dtype=np.float32) / 16)).astype(np.float32)
    ang = np.concatenate([row[:, None] * inv, col[:, None] * inv], axis=-1)
    cos, sin = np.cos(ang).T, np.sin(ang).T
    C = np.ones((64, NT), np.float32)
    S = np.zeros((64, NT), np.float32)
    C[0:32, LC:] = cos
    C[32:64, LC:] = cos
    S[0:32, LC:] = -sin
    S[32:64, LC:] = sin
    sh["ropeC"], sh["ropeS"] = np.concatenate([C, C], 0), np.concatenate([S, S], 0)
    def st(a):
        a = np.asarray(a)[:DEPTH].reshape(DEPTH, 2, 16, 2, 64)
        return f(a.transpose(3, 4, 0, 1, 2).reshape(128, DEPTH, 2, 16))
    sh["s5_lr"] = st(inp["s5_lam_re"])
    sh["s5_li"] = st(inp["s5_lam_im"])
    ldt = np.asarray(inp["s5_log_dt"])[:DEPTH]
    sh["s5_ldt"] = st(np.repeat(ldt[..., None], 64, axis=-1))
    def padB(b):
        b = np.asarray(b)[:DEPTH]
        o = np.zeros((DEPTH, 16, 128, 128), np.float32)
        for pr in range(16):
            for g2 in range(2):
                s0 = (pr % 4) * 32 + g2 * 16
                o[:, pr, g2 * 64:(g2 + 1) * 64, s0:s0 + 16] = b[:, 2 * pr + g2]
        return o
    def padC(c):
        c = np.asarray(c)[:DEPTH]
        o = np.zeros((DEPTH, 16, 128, 128), np.float32)
        for pr in range(16):
            for g2 in range(2):
                s0 = (pr % 4) * 32 + g2 * 16
                o[:, pr, g2 * 64:(g2 + 1) * 64, s0:s0 + 16] = c[:, 2 * pr + g2].transpose(0, 2, 1)
        return o
    sh["s5_Bre"], sh["s5_Bim"] = padB(inp["s5_b_re"]), padB(inp["s5_b_im"])
    sh["s5_Cre"], sh["s5_Cim"] = padC(inp["s5_c_re"]), padC(inp["s5_c_im"])
    sh["s5_dT"] = f(np.asarray(inp["s5_d"])[:DEPTH].reshape(DEPTH, 4, 128).transpose(2, 0, 1))
    sh["s5_bgT"] = f(np.asarray(inp["s5_b_glu"])[:DEPTH].reshape(DEPTH, 4, 128).transpose(2, 0, 1))
    sh["s5_wglu"] = f(inp["s5_w_glu"][:DEPTH])
    sh["sinkrow"] = f(np.repeat(np.asarray(inp["att_sink"])[:DEPTH], 128, axis=-1).reshape(DEPTH, 1, 1024))
    sh["hg_lbT"] = f(np.asarray(inp["hg_lb_logits"])[:DEPTH].reshape(DEPTH, 2, 4, 128).transpose(3, 0, 1, 2).reshape(128, DEPTH, 8))
    sh["hg_ngT"] = f(np.asarray(inp["hg_norm_g"])[:DEPTH].reshape(DEPTH, 4, 128).transpose(2, 0, 1))
    sh["w_branch"] = f(inp["w_branch"][:DEPTH])
    sh["w_out"] = f(inp["w_out"][:DEPTH])
    sh["w_up"] = f(inp["ffn_w_up"][:DEPTH])
    sh["convT"] = f(np.asarray(inp["ffn_conv_w"])[:DEPTH].reshape(DEPTH, 3, 44, 128).transpose(3, 0, 2, 1))
    sh["convbT"] = f(np.asarray(inp["ffn_conv_b"])[:DEPTH].reshape(DEPTH, 44, 128).transpose(2, 0, 1))
    sh["w_down"] = f(inp["ffn_w_down"][:DEPTH])
    sh["c_ident"] = np.eye(128, dtype=np.float32)
    k = np.arange(128)[:, None]
    q = np.tile(np.arange(128), 4)[None, :]
    sh["c_maskP"] = np.where(k >= q, 0.0, -30000.0).astype(np.float32)
    sh["c_maskN"] = np.where(k <= q, 0.0, -30000.0).astype(np.float32)
    io = np.zeros((128, 2, 512), np.float32)
    io[:, 0, :] = np.arange(1, 513, dtype=np.float32)[None]
    io[:, 1, :] = (512 - np.arange(512, dtype=np.float32))[None]
    sh["c_iota"] = io
    s_ = np.arange(32)[:, None]
    t_ = np.tile(np.arange(32), 4)[None, :]
    hm = np.zeros((32, 2, 128), np.float32)
    hm[:, 0, :] = (s_ <= t_)
    hm[:, 1, :] = (s_ >= t_)
    sh["c_hgmask"] = hm
    rm = np.ones((128, 512), np.float32)
    rm[:, ::32] = 0.0
    sh["c_rmask"] = rm
    return sh


def prep_core(inp, b, cfg, sh):
    L, LC = cfg["L"], cfg["LC"]
    m = dict(sh)
    m["x"] = np.ascontiguousarray(np.asarray(inp["x"], dtype=np.float32)[b, :L])
    m["ctx"] = np.ascontiguousarray(np.asarray(inp["ctx"], dtype=np.float32)[b, :LC])
    cT = np.stack([np.asarray(inp["c"], dtype=np.float32)[b], np.asarray(inp["c_ctx"], dtype=np.float32)], axis=-1)
    m["cT"] = np.ascontiguousarray(cT.reshape(KC, 128, 2).transpose(1, 0, 2))
    return m


_NC_CACHE = {}


def kernel(**inputs):
    cfg = FULL
    key = "full"
    if key not in _NC_CACHE:
        _NC_CACHE[key] = build(cfg)
    nc = _NC_CACHE[key]
    sh = prep_shared(inputs, cfg)
    in_maps = [prep_core(inputs, b, cfg, sh) for b in range(8)]
    res = run_bass_kernel_spmd(nc, in_maps, core_ids=list(range(8)))
    return np.stack([np.asarray(r["out"], dtype=np.float32) for r in res.results], axis=0)
```

```python
import numpy as np
import ml_dtypes
from contextlib import ExitStack
import concourse.bass as bass
import concourse.mybir as mybir
from concourse.bass_utils import run_bass_kernel_spmd

F32 = mybir.dt.float32
BF16 = mybir.dt.bfloat16
I32 = mybir.dt.int32
AF = mybir.ActivationFunctionType
ALU = mybir.AluOpType
PI = float(np.pi)


class Buf:
    __slots__ = ("w", "r")

    def __init__(self):
        self.w = None
        self.r = {}


class Prog:
    def __init__(self, nc, es, ndma=48):
        self.nc = nc
        self.eng = {"pe": nc.tensor, "act": nc.scalar, "dve": nc.vector, "pool": nc.gpsimd, "sp": nc.sync}
        self.sem = {}
        self.cnt = {}
        for k in self.eng:
            self.sem[k] = es.enter_context(nc.semaphore("s_" + k))
            self.cnt[k] = 0
        self.ndma = ndma
        for i in range(ndma):
            self.sem[("d", i)] = es.enter_context(nc.semaphore("s_d%d" % i))
            self.cnt[("d", i)] = 0
        self.known = {e: {} for e in self.eng}
        self.rr = 0
        self.nins = 0

    def _waits(self, e, r, w, extra=()):
        need = {}
        kn = self.known[e]

        def add(tok, same_ok):
            if tok is None:
                return
            k, v = tok
            if k == e and (e == "pe" or not same_ok):
                return
            if kn.get(k, 0) >= v:
                return
            if need.get(k, 0) < v:
                need[k] = v

        for b in r:
            add(b.w, True)
        for b in w:
            add(b.w, True)
            for t in b.r.values():
                add(t, False)
        for t in extra:
            add(t, True)
        E = self.eng[e]
        for k, v in need.items():
            E.wait_ge(self.sem[k], v)
            kn[k] = v
            self.nins += 1

    def op(self, e, fn, r=(), w=()):
        self._waits(e, r, w)
        ins = fn(self.eng[e])
        self.cnt[e] += 1
        ins.then_inc(self.sem[e], 1)
        self.nins += 1
        tok = (e, self.cnt[e])
        for b in r:
            b.r[e] = tok
        for b in w:
            b.w = tok
            b.r = {}
        return tok

    def dma(self, q, out, in_, r=(), w=()):
        i = self.rr
        self.rr = (i + 1) % self.ndma
        key = ("d", i)
        self._waits(q, r, w, extra=[(key, self.cnt[key])])
        ins = self.eng[q].dma_start(out=out, in_=in_)
        self.cnt[key] += 16
        ins.then_inc(self.sem[key], 16)
        self.nins += 1
        tok = (key, self.cnt[key])
        for b in r:
            b.r[key] = tok
        for b in w:
            b.w = tok
            b.r = {}
        return tok

    def barrier(self):
        toks = [(k, v) for k, v in self.cnt.items() if v > 0]
        for e, E in self.eng.items():
            kn = self.known[e]
            for k, v in toks:
                if kn.get(k, 0) < v:
                    E.wait_ge(self.sem[k], v)
                    kn[k] = v
                    self.nins += 1


class Tl:
    def __init__(self, t, nb=1):
        self.t = t
        self.bs = [Buf() for _ in range(nb)]

    @property
    def b(self):
        return self.bs[0]

    def __getitem__(self, k):
        return self.t[k]


DM = 1024
KC = 8
EPS = 1e-6
IN_COLS = 6912
C_S5, C_Q, C_K, C_V, C_HQ, C_FF, C_FB, C_HI, C_HG, C_GATE = 0, 512, 1024, 1152, 1280, 1792, 2304, 2816, 3328, 3840
FFN = 2816
NFC = 22
FULL = dict(L=4096, LC=256, DEPTH=4, taps=())


def build(cfg):
    L, LC, DEPTH = cfg["L"], cfg["LC"], cfg["DEPTH"]
    taps = set(cfg.get("taps", ()))
    stop_after = cfg.get("stop_after", None)
    NT = L + LC
    NB = NT // 128
    TILES = [(0, LC)] + [(LC + 512 * i, 512) for i in range(L // 512)]
    NTI = len(TILES)

    def tile_of(tok):
        for i, (t0, sz) in enumerate(TILES):
            if t0 <= tok < t0 + sz:
                return i
        raise ValueError

    nc = bass.Bass("TRN2", target_bir_lowering=False)

    def din(name, shape, dt=F32):
        return nc.dram_tensor(name, list(shape), dt, kind="ExternalInput").ap()

    def dscr(name, shape, dt):
        kind = "ExternalOutput" if name in taps else "Internal"
        return nc.dram_tensor(name, list(shape), dt, kind=kind).ap()

    x_in = din("x", [L, DM])
    ctx_in = din("ctx", [LC, DM])
    cT_in = din("cT", [128, KC, 2])
    w_mod = din("w_mod", [DEPTH, DM, 6 * DM])
    b_modT = din("b_modT", [128, DEPTH, 48])
    norm_gT = din("norm_gT", [128, DEPTH, 4, KC])
    w_in = din("w_in", [DEPTH, DM, IN_COLS])
    w_rot = din("w_rot", [DEPTH, DM, 640])
    ropeC_in = din("ropeC", [128, NT])
    ropeS_in = din("ropeS", [128, NT])
    s5_lr = din("s5_lr", [128, DEPTH, 2, 16])
    s5_li = din("s5_li", [128, DEPTH, 2, 16])
    s5_ldt = din("s5_ldt", [128, DEPTH, 2, 16])
    s5_Bre = din("s5_Bre", [DEPTH, 16, 128, 128])
    s5_Bim = din("s5_Bim", [DEPTH, 16, 128, 128])
    s5_Cre = din("s5_Cre", [DEPTH, 16, 128, 128])
    s5_Cim = din("s5_Cim", [DEPTH, 16, 128, 128])
    s5_dT = din("s5_dT", [128, DEPTH, 4])
    s5_bgT = din("s5_bgT", [128, DEPTH, 4])
    s5_wglu = din("s5_wglu", [DEPTH, 512, 512])
    sinkrow = din("sinkrow", [DEPTH, 1, 1024])
    hg_lbT = din("hg_lbT", [128, DEPTH, 8])
    hg_ngT = din("hg_ngT", [128, DEPTH, 4])
    w_branch = din("w_branch", [DEPTH, 3, 512, DM])
    w_out = din("w_out", [DEPTH, DM, DM])
    w_up = din("w_up", [DEPTH, DM, 2 * FFN])
    convT = din("convT", [128, DEPTH, 44, 3])
    convbT = din("convbT", [128, DEPTH, 44])
    w_down = din("w_down", [DEPTH, FFN, DM])
    c_ident = din("c_ident", [128, 128])
    c_maskP = din("c_maskP", [128, 512])
    c_maskN = din("c_maskN", [128, 512])
    c_iota = din("c_iota", [128, 2, 512])
    c_hgmask = din("c_hgmask", [32, 2, 128])
    c_rmask = din("c_rmask", [128, 512])
    out = nc.dram_tensor("out", [L, DM], F32, kind="ExternalOutput").ap()

    xT = dscr("xT", [KC, 128, NT], F32)
    zs5 = dscr("zs5", [4, 128, NT], BF16)
    qT = dscr("qT", [8, 64, NT], BF16)
    kT = dscr("kT", [2, 64, NT], BF16)
    vtm = dscr("vtm", [NT, 640], BF16)
    hgq = dscr("hgq", [4, 128, NT], BF16)
    lf = dscr("lf", [2, 4, 128, NT], F32)
    kk = dscr("kk", [2, 4, 128, NT], BF16)
    hgg = dscr("hgg", [4, 128, NT], BF16)
    gat = dscr("gat", [24, 128, NT], BF16)
    yT = dscr("yT", [12, 128, NT], BF16)
    ofw = dscr("ofw", [4, 128, NT], F32)
    ofw2 = dscr("ofw2", [4, 128, NT], F32)
    actT = dscr("actT", [NFC, 128, NT], BF16)
    dbg = dscr("dbg", [128, 4096], F32)
    d_xT, d_z, d_y, d_of, d_act = Buf(), Buf(), Buf(), Buf(), Buf()

    with ExitStack() as es:
        P = Prog(nc, es)
        op, dma = P.op, P.dma

        uid = [0]

        def SB(stack, name, shape, dt, nb=1):
            uid[0] += 1
            return Tl(stack.enter_context(nc.sbuf_tensor("%s_u%d" % (name, uid[0]), list(shape), dt)), nb)

        psum = [Tl(es.enter_context(nc.psum_tensor("ps%d" % i, [128, 512], F32))) for i in range(8)]
        psrr = [0, 0]
        nlong = [2]

        def nextps(long=False):
            if long:
                p = psum[psrr[1] % nlong[0]]
                psrr[1] += 1
            else:
                p = psum[nlong[0] + psrr[0] % (8 - nlong[0])]
                psrr[0] += 1
            return p

        def MM(o, l, r_, st, sp, rb, wb):
            op("pe", lambda E: E.matmul(o, l, r_, start=st, stop=sp), r=rb, w=wb)

        def TR(o, i, idn, rb, wb):
            op("pe", lambda E: E.transpose(o, i, idn), r=rb, w=wb)

        def ACT(o, i, f, rb, wb, bias=None, scale=None):
            kw = {}
            if bias is not None:
                kw["bias"] = bias
            if scale is not None:
                kw["scale"] = scale
            op("act", lambda E: E.activation(out=o, in_=i, func=f, **kw), r=rb, w=wb)

        def CP(e, o, i, rb, wb):
            if e == "act":
                op("act", lambda E: E.copy(out=o, in_=i), r=rb, w=wb)
            else:
                op(e, lambda E: E.tensor_copy(out=o, in_=i), r=rb, w=wb)

        def TT(e, o, a, b_, alu, rb, wb):
            op(e, lambda E: E.tensor_tensor(out=o, in0=a, in1=b_, op=alu), r=rb, w=wb)

        def TS(e, o, a, s1, s2, o0, o1, rb, wb):
            if s2 is None:
                op(e, lambda E: E.tensor_scalar(out=o, in0=a, scalar1=s1, scalar2=None, op0=o0), r=rb, w=wb)
            else:
                op(e, lambda E: E.tensor_scalar(out=o, in0=a, scalar1=s1, scalar2=s2, op0=o0, op1=o1), r=rb, w=wb)

        def STT(o, a, s, b_, o0, o1, rb, wb):
            op("dve", lambda E: E.scalar_tensor_tensor(out=o, in0=a, scalar=s, in1=b_, op0=o0, op1=o1), r=rb, w=wb)

        def SCAN(o, d0, d1, init, rb, wb):
            op("dve", lambda E: E.tensor_tensor_scan(out=o, data0=d0, data1=d1, initial=init, op0=ALU.mult, op1=ALU.add), r=rb, w=wb)

        def MSET(e, o, v, wb):
            op(e, lambda E: E.memset(o, v), w=wb)

        def DBG(c0, ap, n, tl):
            if "dbg" in taps:
                dma("pool", dbg[0:ap.shape[0], c0:c0 + n], ap, r=[tl.b])

        ident_f = SB(es, "ident_f", [128, 128], F32)
        ident_b = SB(es, "ident_b", [128, 128], BF16)
        ones_b = SB(es, "ones_b", [128, 128], BF16)
        neghalf = SB(es, "neghalf", [128, 512], F32)
        modv = SB(es, "modv", [128, DEPTH, 48, 2], F32)
        ngt = SB(es, "ngt", [128, DEPTH, 4, KC], F32)
        lbv = SB(es, "lbv", [128, DEPTH, 8], F32)
        omlb = SB(es, "omlb", [128, DEPTH, 8], F32)
        A1 = SB(es, "A1", [128, KC, 2], F32)
        G1 = SB(es, "G1", [128, KC, 2], F32)
        A2 = SB(es, "A2", [128, KC, 2], F32)
        G2 = SB(es, "G2", [128, KC, 2], F32)
        STAT = [ident_f.b, ident_b.b, ones_b.b, neghalf.b]

        dma("sp", ident_f[:], c_ident[:, :], w=[ident_f.b])
        CP("dve", ident_b[:], ident_f[:], [ident_f.b], [ident_b.b])
        MSET("pool", ones_b[:], 1.0, [ones_b.b])
        MSET("pool", neghalf[:], -0.5, [neghalf.b])
        epsb = SB(es, "epsb", [128, 1], F32)
        MSET("pool", epsb[:], EPS, [epsb.b])
        dma("sp", ngt[:], norm_gT[:, :, :, :], w=[ngt.b])

        with ExitStack() as ph:
            xr = [SB(ph, "xr%d" % i, [128, DM], F32) for i in range(2)]
            xtt = [SB(ph, "xtt%d" % i, [128, KC, 128], F32) for i in range(2)]
            for tb in range(NB):
                src = ctx_in[tb * 128:(tb + 1) * 128, :] if tb < LC // 128 else x_in[tb * 128 - LC:(tb + 1) * 128 - LC, :]
                a, o_ = xr[tb % 2], xtt[tb % 2]
                dma("sp", a[:], src, w=[a.b])
                pa, pb = nextps(), nextps()
                for kc in range(KC):
                    pp_ = pa if kc < 4 else pb
                    TR(pp_[:, (kc % 4) * 128:(kc % 4 + 1) * 128], a[:, kc * 128:(kc + 1) * 128], ident_f[:], [a.b, ident_f.b], [pp_.b])
                CP("act", o_[:, 0:4, :], pa[:].rearrange("p (k t) -> p k t", k=4), [pa.b], [o_.b])
                CP("dve", o_[:, 4:8, :], pb[:].rearrange("p (k t) -> p k t", k=4), [pb.b], [o_.b])
                dma("sp", xT[:, :, tb * 128:(tb + 1) * 128].rearrange("k p t -> p k t"), o_[:], r=[o_.b], w=[d_xT])
            cTt = SB(ph, "cTt", [128, KC, 2], F32)
            scb = SB(ph, "scb", [128, KC, 2], BF16)
            bmt = SB(ph, "bmt", [128, DEPTH, 48], F32)
            wm = [SB(ph, "wm%d" % i, [128, KC, 1024], BF16) for i in range(2)]
            dma("sp", cTt[:], cT_in[:, :, :], w=[cTt.b])
            dma("sp", bmt[:], b_modT[:, :, :], w=[bmt.b])
            ACT(scb[:], cTt[:], AF.Silu, [cTt.b], [scb.b])
            for l in range(DEPTH):
                pm = nextps()
                for grp in range(6):
                    wt = wm[(l * 6 + grp) % 2]
                    dma("pool", wt[:], w_mod[l, :, grp * 1024:(grp + 1) * 1024].rearrange("(k p) n -> p k n", p=128), w=[wt.b])
                    for j in range(8):
                        oc = grp * 8 + j
                        for kc in range(KC):
                            MM(pm[:, oc * 2:oc * 2 + 2], wt[:, kc, j * 128:(j + 1) * 128], scb[:, kc, :], kc == 0, kc == KC - 1, [wt.b, scb.b], [pm.b])
                TT("dve", modv[:, l, :, :], pm[:, 0:96].rearrange("p (c w) -> p c w", w=2),
                   bmt[:, l, :].unsqueeze(2).to_broadcast([128, 48, 2]), ALU.add, [pm.b, bmt.b], [modv.b])
            lg = SB(ph, "lg", [128, DEPTH, 8], F32)
            sm = SB(ph, "sm", [128, 8], F32)
            dma("sp", lg[:], hg_lbT[:, :, :], w=[lg.b])
            ACT(lg[:], lg[:], AF.Exp, [lg.b], [lg.b])
            CP("dve", sm[:], lg[:, 0, :], [lg.b], [sm.b])
            for l in range(1, DEPTH):
                TT("dve", sm[:], sm[:], lg[:, l, :], ALU.add, [sm.b, lg.b], [sm.b])
            op("dve", lambda E: E.reciprocal(out=sm[:], in_=sm[:]), r=[sm.b], w=[sm.b])
            MSET("dve", lbv[:, 0, :], 0.0, [lbv.b])
            for l in range(1, DEPTH):
                TT("dve", lg[:, l, :], lg[:, l, :], sm[:], ALU.mult, [lg.b, sm.b], [lg.b])
                TT("dve", lbv[:, l, :], lbv[:, l - 1, :], lg[:, l, :], ALU.add, [lbv.b, lg.b], [lbv.b])
            TS("dve", omlb[:], lbv[:], -1.0, 1.0, ALU.mult, ALU.add, [lbv.b], [omlb.b])
            P.barrier()

        def mod_scalars(l):
            for (Aq, sc0, gi) in ((A1, 8, 0), (A2, 32, 2)):
                TS("dve", Aq[:], modv[:, l, sc0:sc0 + 8, :], 1.0, None, ALU.add, None, [modv.b], [Aq.b])
                TT("dve", Aq[:], Aq[:], ngt[:, l, gi, :].unsqueeze(2).to_broadcast([128, KC, 2]), ALU.mult, [Aq.b, ngt.b], [Aq.b])
            for (Gq, g0, gi) in ((G1, 16, 1), (G2, 40, 3)):
                TT("dve", Gq[:], modv[:, l, g0:g0 + 8, :], ngt[:, l, gi, :].unsqueeze(2).to_broadcast([128, KC, 2]), ALU.mult, [modv.b, ngt.b], [Gq.b])

        def norm_phase(ph, l, Aq, sh0, hT):
            xts = [SB(ph, "nxt%d" % i, [128, KC, 512], F32) for i in range(2)]
            sqs = [SB(ph, "nsq%d" % i, [128, KC, 512], BF16) for i in range(2)]
            rss = [SB(ph, "nrs%d" % i, [128, 512], F32) for i in range(2)]
            tmp = SB(ph, "ntmp", [128, KC, 512], F32, nb=KC)

            def stage_a(ti):
                t0, sz = TILES[ti]
                xt, sq, rs = xts[ti % 2], sqs[ti % 2], rss[ti % 2]
                dma("sp", xt[:, :, 0:sz], xT[:, :, t0:t0 + sz].rearrange("k p t -> p k t"), r=[d_xT], w=[xt.b])
                ACT(sq[:, :, 0:sz], xt[:, :, 0:sz], AF.Square, [xt.b], [sq.b])
                ps = nextps()
                for kc in range(KC):
                    MM(ps[:, 0:sz], ones_b[:], sq[:, kc, 0:sz], kc == 0, kc == KC - 1, [ones_b.b, sq.b], [ps.b])
                ACT(rs[:, 0:sz], ps[:, 0:sz], AF.Ln, [ps.b, epsb.b], [rs.b], bias=epsb[:, 0:1], scale=1.0 / DM)
                ACT(rs[:, 0:sz], rs[:, 0:sz], AF.Exp, [rs.b], [rs.b], scale=-0.5)

            def stage_b(ti):
                t0, sz = TILES[ti]
                w_ = 1 if ti == 0 else 0
                xt, rs = xts[ti % 2], rss[ti % 2]
                for kc in range(KC):
                    STT(tmp[:, kc, 0:sz], xt[:, kc, 0:sz], Aq[:, kc, w_:w_ + 1], rs[:, 0:sz], ALU.mult, ALU.mult, [xt.b, Aq.b, rs.b], [tmp.bs[kc]])
                    ACT(hT[:, kc, t0:t0 + sz], tmp[:, kc, 0:sz], AF.Identity, [tmp.bs[kc], modv.b], [hT.bs[ti]],
                        bias=modv[:, l, sh0 + kc, w_:w_ + 1])
            stage_a(0)
            for ti in range(NTI):
                stage_b(ti)
                if ti + 1 < NTI:
                    stage_a(ti + 1)

        def epilogue(ot, xt, Gq, ti, t0, sz, sq, rs, tmp):
            w_ = 1 if ti == 0 else 0
            ACT(sq[:, :, 0:sz], ot[:, :, 0:sz], AF.Square, [ot.b], [sq.b])
            ps = nextps()
            for kc in range(KC):
                MM(ps[:, 0:sz], ones_b[:], sq[:, kc, 0:sz], kc == 0, kc == KC - 1, [ones_b.b, sq.b], [ps.b])
            ACT(rs[:, 0:sz], ps[:, 0:sz], AF.Ln, [ps.b, epsb.b], [rs.b], bias=epsb[:, 0:1], scale=1.0 / DM)
            ACT(rs[:, 0:sz], rs[:, 0:sz], AF.Exp, [rs.b], [rs.b], scale=-0.5)
            for kc in range(KC):
                STT(ot[:, kc, 0:sz], ot[:, kc, 0:sz], Gq[:, kc, w_:w_ + 1], rs[:, 0:sz], ALU.mult, ALU.mult, [ot.b, Gq.b, rs.b], [ot.b])
                TT("pool", xt[:, kc, 0:sz], xt[:, kc, 0:sz], ot[:, kc, 0:sz], ALU.add, [xt.b, ot.b], [xt.b])
            dma("sp", xT[:, :, t0:t0 + sz].rearrange("k p t -> p k t"), xt[:, :, 0:sz], r=[xt.b], w=[d_xT])

        for l in range(DEPTH):
            mod_scalars(l)
            with ExitStack() as ph:
                hT = SB(ph, "hT", [128, KC, NT], BF16, nb=NTI)
                with ExitStack() as ph1:
                    norm_phase(ph1, l, A1, 0, hT)
                    P.barrier()
                wts = [SB(ph, "wt%d" % i, [128, KC, 512], BF16) for i in range(3)]
                wrr = [0]
                stg = [SB(ph, "stg%d" % i, [128, NT], BF16) for i in range(3)]
                srr = [0]
                stgf = [SB(ph, "stgf%d" % i, [128, NT], F32) for i in range(2)]
                tmpa = [SB(ph, "tmpa%d" % i, [128, 512], F32) for i in range(2)]
                tmpb = [SB(ph, "tmpb%d" % i, [128, 512], F32) for i in range(2)]

                def load_w(src, ncols):
                    wt = wts[wrr[0] % 3]
                    wrr[0] += 1
                    dma("pool", wt[:, :, 0:ncols], src.rearrange("(k p) n -> p k n", p=128), w=[wt.b])
                    return wt

                def next_stg():
                    s = stg[srr[0] % 3]
                    srr[0] += 1
                    return s

                def proj(wt, off, M, cons):
                    for ti, (t0, sz) in enumerate(TILES):
                        ps = nextps()
                        for kc in range(KC):
                            MM(ps[0:M, 0:sz], wt[:, kc, off:off + M], hT[:, kc, t0:t0 + sz], kc == 0, kc == KC - 1, [wt.b, hT.bs[ti]], [ps.b])
                        cons(ti, t0, sz, ps)

                def simple_group(col0, nchunks, func, dst):
                    for g0 in range(0, nchunks, 4):
                        n = min(4, nchunks - g0)
                        wt = load_w(w_in[l, :, col0 + g0 * 128:col0 + (g0 + n) * 128], n * 128)
                        for c in range(n):
                            s = next_stg()

                            def cons(ti, t0, sz, ps, s=s):
                                if func is None:
                                    CP("act", s[:, t0:t0 + sz], ps[:, 0:sz], [ps.b], [s.b])
                                else:
                                    ACT(s[:, t0:t0 + sz], ps[:, 0:sz], func, [ps.b], [s.b])
                            proj(wt, c * 128, 128, cons)
                            dma("sp", dst[g0 + c], s[:], r=[s.b], w=[d_z])

                simple_group(C_S5, 4, None, zs5)
                with ExitStack() as phq:
                    ropeC = SB(phq, "ropeC", [128, NT], F32)
                    ropeS = SB(phq, "ropeS", [128, NT], F32)
                    dma("sp", ropeC[:], ropeC_in[:, :], w=[ropeC.b])
                    dma("sp", ropeS[:], ropeS_in[:, :], w=[ropeS.b])
                    for (cbase, rbase, nh_, dst) in ((C_Q, 0, 8, qT), (C_K, 512, 2, kT)):
                        for g0 in range(0, nh_, 4):
                            n = min(4, nh_ - g0)
                            wa = load_w(w_in[l, :, cbase + g0 * 64:cbase + (g0 + n) * 64], n * 64)
                            wb = load_w(w_rot[l, :, rbase + g0 * 64:rbase + (g0 + n) * 64], n * 64)
                            for c in range(n // 2):
                                s = next_stg()
                                for ti, (t0, sz) in enumerate(TILES):
                                    p1, p2 = nextps(), nextps()
                                    for kc in range(KC):
                                        MM(p1[:, 0:sz], wa[:, kc, c * 128:(c + 1) * 128], hT[:, kc, t0:t0 + sz], kc == 0, kc == KC - 1, [wa.b, hT.bs[ti]], [p1.b])
                                    for kc in range(KC):
                                        MM(p2[:, 0:sz], wb[:, kc, c * 128:(c + 1) * 128], hT[:, kc, t0:t0 + sz], kc == 0, kc == KC - 1, [wb.b, hT.bs[ti]], [p2.b])
                                    ta, tb_ = tmpa[ti % 2], tmpb[ti % 2]
                                    TT("dve", ta[:, 0:sz], p1[:, 0:sz], ropeC[:, t0:t0 + sz], ALU.mult, [p1.b, ropeC.b], [ta.b])
                                    TT("dve", tb_[:, 0:sz], p2[:, 0:sz], ropeS[:, t0:t0 + sz], ALU.mult, [p2.b, ropeS.b], [tb_.b])
                                    TT("pool", s[:, t0:t0 + sz], ta[:, 0:sz], tb_[:, 0:sz], ALU.add, [ta.b, tb_.b], [s.b])
                                h0 = g0 + 2 * c
                                dma("sp", dst[h0:h0 + 2].rearrange("h d t -> (h d) t"), s[:], r=[s.b], w=[d_z])
                    P.barrier()
                with ExitStack() as ph3:
                    wv = SB(ph3, "wv", [128, KC, 640], BF16)
                    vst = [SB(ph3, "vst%d" % i, [128, 640], BF16) for i in range(2)]
                    dma("pool", wv[:, :, 0:128], w_in[l, :, C_V:C_V + 128].rearrange("(k p) n -> p k n", p=128), w=[wv.b])
                    dma("pool", wv[:, :, 128:640], w_in[l, :, C_HI:C_HI + 512].rearrange("(k p) n -> p k n", p=128), w=[wv.b])
                    for tb in range(NB):
                        ti = tile_of(tb * 128)
                        pa, pb = nextps(), nextps()
                        for kc in range(KC):
                            MM(pa[:, 0:512], hT[:, kc, tb * 128:(tb + 1) * 128], wv[:, kc, 128:640], kc == 0, kc == KC - 1, [wv.b, hT.bs[ti]], [pa.b])
                        for kc in range(KC):
                            MM(pb[:, 0:128], hT[:, kc, tb * 128:(tb + 1) * 128], wv[:, kc, 0:128], kc == 0, kc == KC - 1, [wv.b, hT.bs[ti]], [pb.b])
                        v = vst[tb % 2]
                        CP("act", v[:, 0:128], pb[:, 0:128], [pb.b], [v.b])
                        CP("dve", v[:, 128:640], pa[:, 0:512], [pa.b], [v.b])
                        dma("sp", vtm[tb * 128:(tb + 1) * 128, :], v[:], r=[v.b], w=[d_z])
                simple_group(C_HQ, 4, AF.Silu, hgq)
                for d in range(2):
                    wt = load_w(w_in[l, :, C_FF + d * 512:C_FF + (d + 1) * 512], 512)
                    for c in range(4):
                        s = next_stg()
                        sf = stgf[c % 2]
                        li_ = d * 4 + c

                        def cons(ti, t0, sz, ps, s=s, sf=sf, li_=li_):
                            ta = tmpa[ti % 2]
                            ACT(ta[:, 0:sz], ps[:, 0:sz], AF.Exp, [ps.b], [ta.b], scale=-1.0)
                            TS("dve", ta[:, 0:sz], ta[:, 0:sz], 1.0, None, ALU.add, None, [ta.b], [ta.b])
                            op("dve", lambda E: E.reciprocal(out=ta[:, 0:sz], in_=ta[:, 0:sz]), r=[ta.b], w=[ta.b])
                            TS("dve", ta[:, 0:sz], ta[:, 0:sz], omlb[:, l, li_:li_ + 1], lbv[:, l, li_:li_ + 1], ALU.mult, ALU.add, [ta.b, omlb.b, lbv.b], [ta.b])
                            ACT(sf[:, t0:t0 + sz], ta[:, 0:sz], AF.Ln, [ta.b], [sf.b])
                            TS("dve", s[:, t0:t0 + sz], ta[:, 0:sz], -1.0, 1.0, ALU.mult, ALU.add, [ta.b], [s.b])
                        proj(wt, c * 128, 128, cons)
                        dma("sp", lf[d, c], sf[:], r=[sf.b], w=[d_z])
                        dma("sp", kk[d, c], s[:], r=[s.b], w=[d_z])
                simple_group(C_HG, 4, AF.Sigmoid, hgg)
                simple_group(C_GATE, 24, AF.Sigmoid, gat)
                P.barrier()
            if stop_after == "P2":
                break
            with ExitStack() as ph:
                uT = SB(ph, "uT", [128, 4, NT], BF16)
                yacc = SB(ph, "yacc", [128, NT], F32)
                for c in range(4):
                    dma("sp", uT[:, c, :], zs5[c], r=[d_z], w=[uT.b])
                y2T = uT
                sm_ = {n: SB(ph, "s5" + n, [128, 2, 16], F32) for n in
                       ("lr", "li", "dt", "th", "rho", "sn", "cs", "thr", "ar", "ai", "den", "fr", "fi", "t1", "t2", "tf", "dl", "rho8", "th8")}
                smi = SB(ph, "s5i", [128, 2, 16], I32)
                dma("sp", sm_["lr"][:], s5_lr[:, l, :, :], w=[sm_["lr"].b])
                dma("sp", sm_["li"][:], s5_li[:, l, :, :], w=[sm_["li"].b])
                dma("sp", sm_["dt"][:], s5_ldt[:, l, :, :], w=[sm_["dt"].b])

                def reduce_angle(src, dst, tf, ti_):
                    TS("dve", tf[:], src[:], 1.0 / (2 * PI), None, ALU.mult, None, [src.b], [tf.b])
                    CP("dve", ti_[:], tf[:], [tf.b], [ti_.b])
                    CP("dve", tf[:], ti_[:], [ti_.b], [tf.b])
                    STT(dst[:], tf[:], -2 * PI, src[:], ALU.mult, ALU.add, [tf.b, src.b], [dst.b])
                    TS("dve", dst[:], dst[:], -PI, PI, ALU.max, ALU.min, [dst.b], [dst.b])

                S = sm_
                ACT(S["dt"][:], S["dt"][:], AF.Exp, [S["dt"].b], [S["dt"].b])
                TT("dve", S["th"][:], S["dt"][:], S["li"][:], ALU.mult, [S["dt"].b, S["li"].b], [S["th"].b])
                TT("dve", S["dl"][:], S["dt"][:], S["lr"][:], ALU.mult, [S["dt"].b, S["lr"].b], [S["dl"].b])
                ACT(S["rho"][:], S["dl"][:], AF.Exp, [S["dl"].b], [S["rho"].b])
                reduce_angle(S["th"], S["thr"], S["t1"], smi)
                ACT(S["sn"][:], S["thr"][:], AF.Sin, [S["thr"].b], [S["sn"].b])
                TS("dve", S["t2"][:], S["thr"][:], PI / 2, None, ALU.add, None, [S["thr"].b], [S["t2"].b])
                reduce_angle(S["t2"], S["cs"], S["t1"], smi)
                ACT(S["cs"][:], S["cs"][:], AF.Sin, [S["cs"].b], [S["cs"].b])
                TT("dve", S["ar"][:], S["rho"][:], S["cs"][:], ALU.mult, [S["rho"].b, S["cs"].b], [S["ar"].b])
                TT("dve", S["ai"][:], S["rho"][:], S["sn"][:], ALU.mult, [S["rho"].b, S["sn"].b], [S["ai"].b])
                TT("dve", S["den"][:], S["lr"][:], S["lr"][:], ALU.mult, [S["lr"].b], [S["den"].b])
                TT("dve", S["t1"][:], S["li"][:], S["li"][:], ALU.mult, [S["li"].b], [S["t1"].b])
                TT("dve", S["den"][:], S["den"][:], S["t1"][:], ALU.add, [S["den"].b, S["t1"].b], [S["den"].b])
                op("dve", lambda E: E.reciprocal(out=S["den"][:], in_=S["den"][:]), r=[S["den"].b], w=[S["den"].b])
                TS("dve", S["ar"][:], S["ar"][:], -1.0, None, ALU.add, None, [S["ar"].b], [S["ar"].b])
                TT("dve", S["fr"][:], S["ar"][:], S["lr"][:], ALU.mult, [S["ar"].b, S["lr"].b], [S["fr"].b])
                TT("dve", S["t1"][:], S["ai"][:], S["li"][:], ALU.mult, [S["ai"].b, S["li"].b], [S["t1"].b])
                TT("dve", S["fr"][:], S["fr"][:], S["t1"][:], ALU.add, [S["fr"].b, S["t1"].b], [S["fr"].b])
                TT("dve", S["fr"][:], S["fr"][:], S["den"][:], ALU.mult, [S["fr"].b, S["den"].b], [S["fr"].b])
                TT("dve", S["fi"][:], S["ai"][:], S["lr"][:], ALU.mult, [S["ai"].b, S["lr"].b], [S["fi"].b])
                TT("dve", S["t1"][:], S["ar"][:], S["li"][:], ALU.mult, [S["ar"].b, S["li"].b], [S["t1"].b])
                TT("dve", S["fi"][:], S["fi"][:], S["t1"][:], ALU.subtract, [S["fi"].b, S["t1"].b], [S["fi"].b])
                TT("dve", S["fi"][:], S["fi"][:], S["den"][:], ALU.mult, [S["fi"].b, S["den"].b], [S["fi"].b])

                pwr = SB(ph, "pwr", [128, 9, 2, 16], F32)
                pwi = SB(ph, "pwi", [128, 9, 2, 16], F32)
                npwr = SB(ph, "npwr", [128, 9, 2, 16], F32)
                for tau in range(9):
                    TS("dve", S["t1"][:], S["thr"][:], float(tau), None, ALU.mult, None, [S["thr"].b], [S["t1"].b])
                    reduce_angle(S["t1"], S["t2"], S["tf"], smi)
                    ACT(S["sn"][:], S["t2"][:], AF.Sin, [S["t2"].b], [S["sn"].b])
                    TS("dve", S["t1"][:], S["t2"][:], PI / 2, None, ALU.add, None, [S["t2"].b], [S["t1"].b])
                    reduce_angle(S["t1"], S["cs"], S["tf"], smi)
                    ACT(S["cs"][:], S["cs"][:], AF.Sin, [S["cs"].b], [S["cs"].b])
                    TS("dve", S["t1"][:], S["dl"][:], float(tau), None, ALU.mult, None, [S["dl"].b], [S["t1"].b])
                    ACT(S["t1"][:], S["t1"][:], AF.Exp, [S["t1"].b], [S["t1"].b])
                    TT("dve", pwr[:, tau], S["t1"][:], S["cs"][:], ALU.mult, [S["t1"].b, S["cs"].b], [pwr.b])
                    TT("dve", pwi[:, tau], S["t1"][:], S["sn"][:], ALU.mult, [S["t1"].b, S["sn"].b], [pwi.b])
                TS("dve", npwr[:], pwr[:], -1.0, None, ALU.mult, None, [pwr.b], [npwr.b])
                TS("dve", S["t1"][:], S["dl"][:], 8.0, None, ALU.mult, None, [S["dl"].b], [S["t1"].b])
                ACT(S["rho8"][:], S["t1"][:], AF.Exp, [S["t1"].b], [S["rho8"].b])
                TS("dve", S["t1"][:], S["thr"][:], 8.0, None, ALU.mult, None, [S["thr"].b], [S["t1"].b])
                reduce_angle(S["t1"], S["th8"], S["tf"], smi)

                bp = [SB(ph, "bp%d" % i, [128, 2, 128], F32) for i in range(2)]
                bb = [[SB(ph, "bb%d_%d" % (d, pp), [128, 2, 128], BF16) for pp in range(4)] for d in range(2)]
                dgs = [SB(ph, "dg%d" % i, [128, 3, 4, 128], BF16) for i in range(2)]
                Xb4 = [SB(ph, "Xb4_%d" % i, [128, 4, 2, 128], BF16) for i in range(2)]
                identb4 = SB(ph, "identb4", [128, 4, 128], BF16)
                for i_ in range(4):
                    CP("dve", identb4[:, i_, :], ident_b[:], [ident_b.b], [identb4.b])
                npwi = SB(ph, "npwi", [128, 9, 2, 16], F32)
                TS("dve", npwi[:], pwi[:], -1.0, None, ALU.mult, None, [pwi.b], [npwi.b])
                cf = [SB(ph, "cf%d" % pp, [128, 2, 128], F32) for pp in range(4)]
                lhsC = [SB(ph, "lhsC%d" % pp, [128, 2, 128], BF16) for pp in range(4)]
                xs = [SB(ph, "xs%d" % i, [128, 3, 128], F32) for i in range(2)]
                lhsP = SB(ph, "lhsP", [128, 8, 4, 2, 128], BF16)
                BD = SB(ph, "BD", [128, 8, 128], BF16)
                lhsQ = SB(ph, "lhsQ", [128, 8, 4, 2, 128], BF16)
                diagD = SB(ph, "diagD", [128, 128], F32)
                sdT = SB(ph, "sdT", [128, 4], F32)
                dma("sp", sdT[:], s5_dT[:, l, :], w=[sdT.b])
                iot = SB(ph, "iot64", [128, 64], F32)
                dma("sp", iot[:], c_iota[:, 0, 0:64], w=[iot.b])
                a64 = [SB(ph, "a64_%d" % i, [128, 64], F32) for i in range(3)]
                a64i = SB(ph, "a64i", [128, 64], I32)
                tabC = SB(ph, "tabC", [128, 4, 64], F32)
                tabS = SB(ph, "tabS", [128, 4, 64], F32)
                tabN = SB(ph, "tabN", [128, 4, 64], F32)
                Vt = [SB(ph, "Vt%d" % i, [128, 4, 2, 64], F32) for i in range(2)]
                Wk = [{n: SB(ph, "wk%s%d" % (n, i), [128, 4, 64], F32) for n in ("m1", "m2", "m3", "m4", "gr", "gi")} for i in range(2)]
                Hre = SB(ph, "Hre", [128, 4, 65], F32)
                Him = SB(ph, "Him", [128, 4, 65], F32)
                Hb = [SB(ph, "Hb%d" % i, [128, 2, 4, 64], BF16) for i in range(2)]
                nlong[0] = 4
                vt = SB(ph, "vt", [128, NB, 128], BF16)
                dma("sp", vt[:], vtm[:, 0:128].rearrange("(b p) c -> p b c", p=128), r=[d_z], w=[vt.b])
                mP = SB(ph, "mP", [128, 512], BF16)
                mN = SB(ph, "mN", [128, 512], BF16)
                dma("pool", mP[:], c_maskP[:, :], w=[mP.b])
                dma("pool", mN[:], c_maskN[:, :], w=[mN.b])
                kT2 = SB(ph, "kT2", [64, 2, NT], BF16)
                dma("sp", kT2[:], kT[:, :, :].rearrange("h d t -> d h t"), r=[d_z], w=[kT2.b])
                srow = SB(ph, "srow", [1, 1024], F32)
                dma("sp", srow[:], sinkrow[l, :, :], w=[srow.b])
                sinkts = [SB(ph, "sinkt%d" % i, [128, 512], BF16) for i in range(2)]
                for kvh in range(2):
                    MSET("pool", sinkts[kvh][:], 0.0, [sinkts[kvh].b])
                    ACT(sinkts[kvh][0:1, :], srow[:, kvh * 512:(kvh + 1) * 512], AF.Exp, [srow.b], [sinkts[kvh].b])
                qblk = [SB(ph, "qblk%d" % i, [64, 4, 128], BF16) for i in range(3)]
                oblk = [SB(ph, "oblk%d" % i, [64, 4, 128], BF16) for i in range(3)]
                pts = [SB(ph, "pt%d" % i, [128, 512], BF16) for i in range(3)]
                rds = [SB(ph, "rd%d" % i, [64, 512], F32) for i in range(3)]
                att_steps = [(kvh, qb) for kvh in range(2) for qb in range(NB)]
                att_st = {"i": 0, "ipt": 0, "pend": []}

                def att_issue():
                    idx = att_st["i"]
                    if idx >= len(att_steps):
                        return
                    att_st["i"] += 1
                    kvh, qb = att_steps[idx]
                    qv = qblk[idx % 3]
                    dma("sp", qv[:], qT[4 * kvh:4 * kvh + 4, :, qb * 128:(qb + 1) * 128].rearrange("h d t -> d h t"), r=[d_z], w=[qv.b])
                    if qb < LC // 128:
                        keys = [(kt_, None) for kt_ in range(LC // 128)]
                    else:
                        n = qb - LC // 128
                        keys = [(kt_, None) for kt_ in range(LC // 128)]
                        if n - 1 >= 0:
                            keys.append((qb - 1, mP))
                        keys.append((qb, None))
                        if n + 1 < L // 128:
                            keys.append((qb + 1, mN))
                    pso, psd = nextps(long=True), nextps(long=True)
                    for i, (kt_, msk) in enumerate(keys):
                        pss = nextps()
                        MM(pss[:, :], kT2[:, kvh, kt_ * 128:(kt_ + 1) * 128], qv[:], True, msk is None, [kT2.b, qv.b], [pss.b])
                        if msk is not None:
                            MM(pss[:, :], ident_b[:], msk[:], False, True, [ident_b.b, msk.b], [pss.b])
                        pt = pts[att_st["ipt"] % 3]
                        att_st["ipt"] += 1
                        ACT(pt[:], pss[:, :], AF.Exp, [pss.b], [pt.b], scale=0.125)
                        MM(pso[0:64, :], vt[:, kt_, kvh * 64:(kvh + 1) * 64], pt[:], i == 0, i == len(keys) - 1, [vt.b, pt.b], [pso.b])
                        MM(psd[0:64, :], ones_b[:, 0:64], pt[:], i == 0, False, [ones_b.b, pt.b], [psd.b])
                    MM(psd[0:64, :], ones_b[:, 0:64], sinkts[kvh][:], False, True, [ones_b.b, sinkts[kvh].b], [psd.b])
                    rd = rds[idx % 3]
                    ACT(rd[:], psd[0:64, :], AF.Ln, [psd.b], [rd.b])
                    ACT(rd[:], rd[:], AF.Exp, [rd.b], [rd.b], scale=-1.0)
                    att_st["pend"].append((idx, kvh, qb, pso, rd))

                def att_finalize():
                    if not att_st["pend"]:
                        return
                    idx, kvh, qb, pso, rd = att_st["pend"].pop(0)
                    o_ = oblk[idx % 3]
                    TT("dve", o_[:], pso[0:64, :].rearrange("p (h q) -> p h q", h=4), rd[:].rearrange("p (h q) -> p h q", h=4),
                       ALU.mult, [pso.b, rd.b], [o_.b])
                    dma("sp", yT[4 + 2 * kvh:6 + 2 * kvh, :, qb * 128:(qb + 1) * 128].rearrange("c (two d) t -> d (c two) t", two=2), o_[:],
                        r=[o_.b], w=[d_y])

                it = 0
                for fc in range(4):
                    TS("dve", diagD[:], ident_f[:], sdT[:, fc:fc + 1], None, ALU.mult, None, [ident_f.b, sdT.b], [diagD.b])
                    for pp in range(4):
                        pr = fc * 4 + pp
                        b_ = bp[pp % 2]
                        dma("sp", b_[:, 0, :], s5_Bre[l, pr], w=[b_.b])
                        dma("sp", b_[:, 1, :], s5_Bim[l, pr], w=[b_.b])
                        dma("sp", cf[pp][:, 0, :], s5_Cre[l, pr], w=[cf[pp].b])
                        dma("sp", cf[pp][:, 1, :], s5_Cim[l, pr], w=[cf[pp].b])
                        CP("act", lhsC[pp][:, 0, :], cf[pp][:, 0, :], [cf[pp].b], [lhsC[pp].b])
                        TS("dve", lhsC[pp][:, 1, :], cf[pp][:, 1, :], -1.0, None, ALU.mult, None, [cf[pp].b], [lhsC[pp].b])
                        for d in range(2):
                            fr_ = S["fr"][:, d, pr:pr + 1]
                            fi_ = S["fi"][:, d, pr:pr + 1]
                            x_ = xs[d]
                            o_ = bb[d][pp]
                            TS("dve", x_[:, 0, :], b_[:, 1, :], fi_, None, ALU.mult, None, [b_.b, S["fi"].b], [x_.b])
                            STT(o_[:, 0, :], b_[:, 0, :], fr_, x_[:, 0, :], ALU.mult, ALU.subtract, [b_.b, S["fr"].b, x_.b], [o_.b])
                            TS("dve", x_[:, 1, :], b_[:, 0, :], fi_, None, ALU.mult, None, [b_.b, S["fi"].b], [x_.b])
                            STT(o_[:, 1, :], b_[:, 1, :], fr_, x_[:, 1, :], ALU.mult, ALU.add, [b_.b, S["fr"].b, x_.b], [o_.b])
                    for d in range(2):
                        sl4 = slice(fc * 4, fc * 4 + 4)
                        for tau in range(9):
                            dg = dgs[tau % 2]
                            for vi, pw_ in enumerate((pwr, pwi, npwi)):
                                TT("dve", dg[:, vi], identb4[:], pw_[:, tau, d, sl4].unsqueeze(2).to_broadcast([128, 4, 128]), ALU.mult,
                                   [identb4.b, pw_.b], [dg.b])

                            def four(dst_banks, combos):
                                for pp in range(4):
                                    pq = dst_banks[pp // 2]
                                    base = (pp % 2) * 256
                                    for ri, (l0, r0, l1, r1, bufs) in enumerate(combos(pp)):
                                        o_ap = pq[:, base + ri * 128:base + (ri + 1) * 128]
                                        MM(o_ap, l0, r0, True, False, bufs, [pq.b])
                                        MM(o_ap, l1, r1, False, True, bufs, [pq.b])

                            def vw(pq):
                                return pq[:].rearrange("p (a b c) -> p a b c", a=2, b=2)
                            if tau < 8:
                                pa, pb = nextps(), nextps()
                                four((pa, pb), lambda pp: (
                                    (bb[d][pp][:, 0, :], dg[:, 0, pp, :], bb[d][pp][:, 1, :], dg[:, 2, pp, :], [bb[d][pp].b, dg.b]),
                                    (bb[d][pp][:, 1, :], dg[:, 0, pp, :], bb[d][pp][:, 0, :], dg[:, 1, pp, :], [bb[d][pp].b, dg.b])))
                                CP("act", lhsP[:, tau, 0:2, :, :], vw(pa), [pa.b], [lhsP.b])
                                CP("act", lhsP[:, tau, 2:4, :, :], vw(pb), [pb.b], [lhsP.b])
                                pc, pd = nextps(), nextps()
                                four((pc, pd), lambda pp: (
                                    (dg[:, 0, pp, :], bb[d][pp][:, 0, :], dg[:, 2, pp, :], bb[d][pp][:, 1, :], [bb[d][pp].b, dg.b]),
                                    (dg[:, 0, pp, :], bb[d][pp][:, 1, :], dg[:, 1, pp, :], bb[d][pp][:, 0, :], [bb[d][pp].b, dg.b])))
                                xb = Xb4[tau % 2]
                                CP("act", xb[:, 0:2, :, :], vw(pc), [pc.b], [xb.b])
                                CP("act", xb[:, 2:4, :, :], vw(pd), [pd.b], [xb.b])
                                psd_ = nextps()
                                for pp in range(4):
                                    for ri in range(2):
                                        MM(psd_[:, 0:128], xb[:, pp, ri, :], lhsC[pp][:, ri, :], pp == 0 and ri == 0, pp == 3 and ri == 1,
                                           [xb.b, lhsC[pp].b], [psd_.b])
                                if tau == 0 and d == 0:
                                    TT("dve", BD[:, tau, :], psd_[:, 0:128], diagD[:], ALU.add, [psd_.b, diagD.b], [BD.b])
                                else:
                                    CP("act", BD[:, tau, :], psd_[:, 0:128], [psd_.b], [BD.b])
                            if tau >= 1:
                                t = tau - 1
                                pe_, pf = nextps(), nextps()
                                four((pe_, pf), lambda pp: (
                                    (dg[:, 0, pp, :], lhsC[pp][:, 0, :], dg[:, 1, pp, :], lhsC[pp][:, 1, :], [lhsC[pp].b, dg.b]),
                                    (dg[:, 2, pp, :], lhsC[pp][:, 0, :], dg[:, 0, pp, :], lhsC[pp][:, 1, :], [lhsC[pp].b, dg.b])))
                                CP("act", lhsQ[:, t, 0:2, :, :], vw(pe_), [pe_.b], [lhsQ.b])
                                CP("act", lhsQ[:, t, 2:4, :, :], vw(pf), [pf.b], [lhsQ.b])
                        for pp in range(4):
                            pr = fc * 4 + pp
                            a0, a1, a2 = a64
                            TS("dve", a0[:], iot[:], S["th8"][:, d, pr:pr + 1], None, ALU.mult, None, [iot.b, S["th8"].b], [a0.b])
                            reduce_angle(a0, a1, a2, a64i)
                            ACT(tabS[:, pp, :], a1[:], AF.Sin, [a1.b], [tabS.b])
                            TS("dve", a0[:], a1[:], PI / 2, None, ALU.add, None, [a1.b], [a0.b])
                            reduce_angle(a0, a1, a2, a64i)
                            ACT(tabC[:, pp, :], a1[:], AF.Sin, [a1.b], [tabC.b])
                        TS("dve", tabN[:], tabS[:], -1.0, None, ALU.mult, None, [tabS.b], [tabN.b])
                        tbs = [tabC.b, tabS.b, tabN.b]
                        order = list(range(NTI)) if d == 0 else [0] + list(range(NTI - 1, 0, -1))
                        MSET("dve", Hre[:, :, 0:1], 0.0, [Hre.b])
                        MSET("dve", Him[:, :, 0:1], 0.0, [Him.b])
                        for oi, ti in enumerate(order):
                            t0, sz = TILES[ti]
                            NJ = sz // 8
                            useq = uT[:, fc, t0:t0 + sz] if d == 0 else uT[:, fc, t0:t0 + sz][:, ::-1]
                            us = [useq[:, s_::8] for s_ in range(8)]
                            V = Vt[it % 2]
                            W = Wk[it % 2]
                            hb = Hb[it % 2]
                            it += 1
                            att_finalize()
                            pv = nextps()
                            pvv = pv[:].rearrange("p (a b c) -> p a b c", a=4, b=2)
                            for pp in range(4):
                                for ri in range(2):
                                    for s_ in range(8):
                                        MM(pvv[:, pp, ri, 0:NJ], lhsP[:, 7 - s_, pp, ri, :], us[s_], s_ == 0, s_ == 7, [lhsP.b, uT.b], [pv.b])
                            CP("act", V[:, :, :, 0:NJ], pvv[:, :, :, 0:NJ], [pv.b], [V.b])
                            tC, tS, tN = tabC[:, :, 0:NJ], tabS[:, :, 0:NJ], tabN[:, :, 0:NJ]
                            vre, vim = V[:, :, 0, 0:NJ], V[:, :, 1, 0:NJ]
                            TT("dve", W["m1"][:, :, 0:NJ], vre, tC, ALU.mult, [V.b] + tbs, [W["m1"].b])
                            TT("dve", W["m2"][:, :, 0:NJ], vim, tS, ALU.mult, [V.b] + tbs, [W["m2"].b])
                            TT("dve", W["m1"][:, :, 0:NJ], W["m1"][:, :, 0:NJ], W["m2"][:, :, 0:NJ], ALU.add, [W["m1"].b, W["m2"].b], [W["m1"].b])
                            TT("pool", W["m3"][:, :, 0:NJ], vim, tC, ALU.mult, [V.b] + tbs, [W["m3"].b])
                            TT("pool", W["m4"][:, :, 0:NJ], vre, tN, ALU.mult, [V.b] + tbs, [W["m4"].b])
                            TT("pool", W["m3"][:, :, 0:NJ], W["m3"][:, :, 0:NJ], W["m4"][:, :, 0:NJ], ALU.add, [W["m3"].b, W["m4"].b], [W["m3"].b])
                            for pp in range(4):
                                pr = fc * 4 + pp
                                rho_b = S["rho8"][:, d, pr:pr + 1].to_broadcast([128, NJ])
                                SCAN(W["gr"][:, pp, 0:NJ], rho_b, W["m1"][:, pp, 0:NJ], Hre[:, pp, 0:1], [W["m1"].b, S["rho8"].b, Hre.b], [W["gr"].b])
                                SCAN(W["gi"][:, pp, 0:NJ], rho_b, W["m3"][:, pp, 0:NJ], Him[:, pp, 0:1], [W["m3"].b, S["rho8"].b, Him.b], [W["gi"].b])
                            TT("dve", W["m2"][:, :, 0:NJ], W["gr"][:, :, 0:NJ], tC, ALU.mult, [W["gr"].b] + tbs, [W["m2"].b])
                            TT("dve", W["m4"][:, :, 0:NJ], W["gi"][:, :, 0:NJ], tN, ALU.mult, [W["gi"].b] + tbs, [W["m4"].b])
                            TT("dve", Hre[:, :, 1:NJ + 1], W["m2"][:, :, 0:NJ], W["m4"][:, :, 0:NJ], ALU.add, [W["m2"].b, W["m4"].b], [Hre.b])
                            TT("pool", W["m1"][:, :, 0:NJ], W["gi"][:, :, 0:NJ], tC, ALU.mult, [W["gi"].b] + tbs, [W["m1"].b])
                            TT("pool", W["m3"][:, :, 0:NJ], W["gr"][:, :, 0:NJ], tS, ALU.mult, [W["gr"].b] + tbs, [W["m3"].b])
                            TT("pool", Him[:, :, 1:NJ + 1], W["m1"][:, :, 0:NJ], W["m3"][:, :, 0:NJ], ALU.add, [W["m1"].b, W["m3"].b], [Him.b])
                            CP("pool", hb[:, 0, :, 0:NJ], Hre[:, :, 0:NJ], [Hre.b], [hb.b])
                            CP("pool", hb[:, 1, :, 0:NJ], Him[:, :, 0:NJ], [Him.b], [hb.b])
                            att_issue()
                            py = nextps(long=True)
                            pyv = py[:].rearrange("p (t j) -> p t j", t=8)
                            for t in range(8):
                                nmm = (t + 1) + 8
                                imm = 0
                                for s_ in range(t + 1):
                                    imm += 1
                                    MM(pyv[:, t, 0:NJ], BD[:, t - s_, :], us[s_], imm == 1, imm == nmm, [BD.b, uT.b], [py.b])
                                for pp in range(4):
                                    for ri in range(2):
                                        imm += 1
                                        MM(pyv[:, t, 0:NJ], lhsQ[:, t, pp, ri, :], hb[:, ri, pp, 0:NJ], imm == 1, imm == nmm, [lhsQ.b, hb.b], [py.b])
                            yv = yacc[:, t0:t0 + sz] if d == 0 else yacc[:, t0:t0 + sz][:, ::-1]
                            yv = yv.rearrange("p (j t) -> p t j", t=8)
                            if d == 0:
                                CP("act", yv, pyv[:, :, 0:NJ], [py.b], [yacc.b])
                            else:
                                TT("dve", yv, pyv[:, :, 0:NJ], yv, ALU.add, [py.b, yacc.b], [yacc.b])
                            CP("dve", Hre[:, :, 0:1], Hre[:, :, NJ:NJ + 1], [Hre.b], [Hre.b])
                            CP("dve", Him[:, :, 0:1], Him[:, :, NJ:NJ + 1], [Him.b], [Him.b])
                    ACT(y2T[:, fc, :], yacc[:], AF.Gelu_apprx_tanh, [yacc.b], [y2T.b])
                while att_st["i"] < len(att_steps) or att_st["pend"]:
                    att_finalize()
                    att_issue()
                wg = SB(ph, "wg", [128, 4, 512], BF16)
                bg = SB(ph, "bg", [128, 4], F32)
                gs = [SB(ph, "gs%d" % i, [128, 512], F32) for i in range(2)]
                yst = [SB(ph, "yst%d" % i, [128, 512], BF16) for i in range(2)]
                dma("pool", wg[:], s5_wglu[l].rearrange("(k p) n -> p k n", p=128), w=[wg.b])
                dma("sp", bg[:], s5_bgT[:, l, :], w=[bg.b])
                for co in range(4):
                    for ti, (t0, sz) in enumerate(TILES):
                        ys = yst[ti % 2]
                        ps = nextps()
                        for k in range(4):
                            MM(ps[:, 0:sz], wg[:, k, co * 128:(co + 1) * 128], y2T[:, k, t0:t0 + sz], k == 0, k == 3, [wg.b, y2T.b], [ps.b])
                        g_ = gs[ti % 2]
                        ACT(g_[:, 0:sz], ps[:, 0:sz], AF.Sigmoid, [ps.b, bg.b], [g_.b], bias=bg[:, co:co + 1])
                        TT("dve", ys[:, 0:sz], y2T[:, co, t0:t0 + sz], g_[:, 0:sz], ALU.mult, [y2T.b, g_.b], [ys.b])
                        dma("sp", yT[co, :, t0:t0 + sz], ys[:, 0:sz], r=[ys.b], w=[d_y])
                P.barrier()
                nlong[0] = 2
            if stop_after == "S5":
                break
            if stop_after == "ATT":
                break
            with ExitStack() as ph:
                hgm = SB(ph, "hgm", [32, 2, 128], F32)
                rmask = SB(ph, "rmask", [128, 512], F32)
                hng = SB(ph, "hng", [128, 4], F32)
                dma("sp", hgm[:], c_hgmask[:, :, :], w=[hgm.b])
                dma("sp", rmask[:], c_rmask[:, :], w=[rmask.b])
                dma("sp", hng[:], hg_ngT[:, l, :], w=[hng.b])
                D2 = range(2)
                Sf = [SB(ph, "Sf%d" % d, [128, 4, 128], F32) for d in D2]
                Sb = [SB(ph, "Sb%d" % d, [128, 4, 128], BF16) for d in D2]
                lfts = [SB(ph, "lft%d" % d, [128, 4, 512], F32) for d in D2]
                kkts = [SB(ph, "kkt%d" % d, [128, 4, 512], BF16) for d in D2]
                hqts = [SB(ph, "hqt%d" % d, [128, 4, 512], BF16) for d in D2]
                vchs = [SB(ph, "vch%d" % d, [32, 16, 512], BF16) for d in D2]
                bts = [SB(ph, "hbt%d" % d, [128, 4, 512], F32) for d in D2]
                e1s = [SB(ph, "he1%d" % d, [128, 4, 512], F32) for d in D2]
                e2s = [SB(ph, "he2%d" % d, [128, 4, 512], F32) for d in D2]
                qts = [SB(ph, "hqt_%d" % d, [128, 4, 512], BF16) for d in D2]
                kts = [SB(ph, "hkt_%d" % d, [128, 4, 512], BF16) for d in D2]
                khs = [SB(ph, "hkh_%d" % d, [128, 4, 512], BF16) for d in D2]
                ots = [SB(ph, "hot%d" % d, [128, 4, 512], F32) for d in D2]
                attm = [[SB(ph, "attm%d_%d" % (d, i), [32, 128], BF16) for i in range(2)] for d in D2]
                ktm = [[SB(ph, "ktm%d_%d" % (d, i), [32, 512], BF16) for i in range(2)] for d in D2]
                orders = [list(range(NTI)), [0] + list(range(NTI - 1, 0, -1))]
                for d in D2:
                    MSET("pool", Sf[d][:], 0.0, [Sf[d].b])
                    MSET("pool", Sb[d][:], 0.0, [Sb[d].b])
                ich = [0, 0]

                def hg_setup(d, ti):
                    t0, sz = TILES[ti]
                    nch = sz // 32
                    lft, kkt, hqt, vch, bt, e1, e2, qt, kt, kh = lfts[d], kkts[d], hqts[d], vchs[d], bts[d], e1s[d], e2s[d], qts[d], kts[d], khs[d]
                    dma("sp", lft[:, :, 0:sz], lf[d, :, :, t0:t0 + sz].rearrange("h p t -> p h t"), r=[d_z], w=[lft.b])
                    dma("sp", kkt[:, :, 0:sz], kk[d, :, :, t0:t0 + sz].rearrange("h p t -> p h t"), r=[d_z], w=[kkt.b])
                    dma("sp", hqt[:, :, 0:sz], hgq[:, :, t0:t0 + sz].rearrange("h p t -> p h t"), r=[d_z], w=[hqt.b])
                    dma("sp", vch[:, 0:nch, :], vtm[t0:t0 + sz, 128:640].rearrange("(c p) f -> p c f", p=32), r=[d_z], w=[vch.b])
                    for h in range(4):
                        if d == 0:
                            SCAN(bt[:, h, 0:sz], rmask[:, 0:sz], lft[:, h, 0:sz], 0.0, [rmask.b, lft.b], [bt.b])
                        else:
                            SCAN(bt[:, h, 0:sz][:, ::-1], rmask[:, 0:sz], lft[:, h, 0:sz][:, ::-1], 0.0, [rmask.b, lft.b], [bt.b])
                    jl0 = 31 if d == 0 else 0
                    b4 = bt[:, :, 0:sz].rearrange("p h (c t) -> p h c t", t=32)
                    TT("dve", e2[:, :, 0:sz].rearrange("p h (c t) -> p h c t", t=32), b4[:, :, :, jl0:jl0 + 1].to_broadcast([128, 4, nch, 32]), b4,
                       ALU.subtract, [bt.b], [e2.b])
                    ACT(e1[:, :, 0:sz], bt[:, :, 0:sz], AF.Exp, [bt.b], [e1.b], scale=-1.0)
                    ACT(e2[:, :, 0:sz], e2[:, :, 0:sz], AF.Exp, [e2.b], [e2.b])
                    ACT(bt[:, :, 0:sz], bt[:, :, 0:sz], AF.Exp, [bt.b], [bt.b])
                    TT("dve", qt[:, :, 0:sz], hqt[:, :, 0:sz], bt[:, :, 0:sz], ALU.mult, [hqt.b, bt.b], [qt.b])
                    TT("pool", kt[:, :, 0:sz], kkt[:, :, 0:sz], e1[:, :, 0:sz], ALU.mult, [kkt.b, e1.b], [kt.b])
                    TT("dve", kh[:, :, 0:sz], kkt[:, :, 0:sz], e2[:, :, 0:sz], ALU.mult, [kkt.b, e2.b], [kh.b])

                def hg_chunk(d, ci):
                    vch, bt, qt, kt, kh, ot = vchs[d], bts[d], qts[d], kts[d], khs[d], ots[d]
                    c0 = ci * 32
                    am, km = attm[d][ich[d] % 2], ktm[d][ich[d] % 2]
                    ich[d] += 1
                    psA = nextps()
                    for h in range(4):
                        MM(psA[0:32, h * 32:(h + 1) * 32], kt[:, h, c0:c0 + 32], qt[:, h, c0:c0 + 32], True, True, [kt.b, qt.b], [psA.b])
                    TT("dve", am[:], psA[0:32, 0:128], hgm[:, d, :], ALU.mult, [psA.b, hgm.b], [am.b])
                    psT = nextps()
                    psTb = psT[:].bitcast(BF16)
                    for h in range(4):
                        TR(psTb[0:32, h * 128:(h + 1) * 128], kh[:, h, c0:c0 + 32], ident_b[:], [kh.b, ident_b.b], [psT.b])
                    CP("act", km[:], psTb[0:32, 0:512], [psT.b], [km.b])
                    psO = nextps()
                    for h in range(4):
                        MM(psO[:, h * 32:(h + 1) * 32], vch[:, ci, h * 128:(h + 1) * 128], am[:, h * 32:(h + 1) * 32], True, False, [vch.b, am.b], [psO.b])
                        MM(psO[:, h * 32:(h + 1) * 32], Sb[d][:, h, :], qt[:, h, c0:c0 + 32], False, True, [Sb[d].b, qt.b], [psO.b])
                    CP("act", ot[:, :, c0:c0 + 32], psO[:, 0:128].rearrange("p (h t) -> p h t", h=4), [psO.b], [ot.b])
                    psS = nextps()
                    for h in range(4):
                        MM(psS[:, h * 128:(h + 1) * 128], km[:, h * 128:(h + 1) * 128], vch[:, ci, h * 128:(h + 1) * 128], True, True, [km.b, vch.b], [psS.b])
                    jl = c0 + 31 if d == 0 else c0
                    for h in range(4):
                        STT(Sf[d][:, h, :], Sf[d][:, h, :], bt[:, h, jl:jl + 1], psS[:, h * 128:(h + 1) * 128], ALU.mult, ALU.add, [Sf[d].b, bt.b, psS.b], [Sf[d].b])
                    CP("act", Sb[d][:], Sf[d][:], [Sf[d].b], [Sb[d].b])

                obw = ofw2
                for oi in range(NTI):
                    for d in D2:
                        hg_setup(d, orders[d][oi])
                    nch = TILES[orders[0][oi]][1] // 32
                    for k_ in range(nch):
                        for d in D2:
                            hg_chunk(d, k_ if d == 0 else nch - 1 - k_)
                    for d in D2:
                        t0, sz = TILES[orders[d][oi]]
                        dma("sp", (ofw if d == 0 else obw)[:, :, t0:t0 + sz].rearrange("h p t -> p h t"), ots[d][:, :, 0:sz], r=[ots[d].b], w=[d_of])
                sq = SB(ph, "hsq", [128, 4, 512], BF16)
                rs = SB(ph, "hrs", [128, 4, 512], F32)
                hggts = [SB(ph, "hggt%d" % i, [128, 4, 512], BF16) for i in range(2)]
                ysts = [SB(ph, "hyst%d" % i, [128, 4, 512], BF16) for i in range(2)]
                for ti, (t0, sz) in enumerate(TILES):
                    oa, obt, yh = ots[ti % 2], e1s[ti % 2], e2s[ti % 2]
                    hggt, yst = hggts[ti % 2], ysts[ti % 2]
                    dma("sp", oa[:, :, 0:sz], ofw[:, :, t0:t0 + sz].rearrange("h p t -> p h t"), r=[d_of], w=[oa.b])
                    dma("sp", obt[:, :, 0:sz], obw[:, :, t0:t0 + sz].rearrange("h p t -> p h t"), r=[d_of], w=[obt.b])
                    dma("sp", hggt[:, :, 0:sz], hgg[:, :, t0:t0 + sz].rearrange("h p t -> p h t"), r=[d_z], w=[hggt.b])
                    TT("pool", oa[:, :, 0:sz], oa[:, :, 0:sz], obt[:, :, 0:sz], ALU.add, [oa.b, obt.b], [oa.b])
                    ACT(sq[:, :, 0:sz], oa[:, :, 0:sz], AF.Square, [oa.b], [sq.b])
                    for h in range(4):
                        ps = nextps()
                        MM(ps[:, 0:sz], ones_b[:], sq[:, h, 0:sz], True, True, [ones_b.b, sq.b], [ps.b])
                        ACT(rs[:, h, 0:sz], ps[:, 0:sz], AF.Ln, [ps.b, epsb.b], [rs.b], bias=epsb[:, 0:1], scale=1.0 / 128)
                    ACT(rs[:, :, 0:sz], rs[:, :, 0:sz], AF.Exp, [rs.b], [rs.b], scale=-0.5)
                    for h in range(4):
                        STT(yh[:, h, 0:sz], oa[:, h, 0:sz], hng[:, h:h + 1], rs[:, h, 0:sz], ALU.mult, ALU.mult, [oa.b, hng.b, rs.b], [yh.b])
                    TT("pool", yst[:, :, 0:sz], yh[:, :, 0:sz], hggt[:, :, 0:sz], ALU.mult, [yh.b, hggt.b], [yst.b])
                    dma("sp", yT[8:12, :, t0:t0 + sz].rearrange("h p t -> p h t"), yst[:, :, 0:sz], r=[yst.b], w=[d_y])
                P.barrier()
            if stop_after == "HG":
                break
            with ExitStack() as ph:
                wbr = SB(ph, "wbr", [128, 12, DM], BF16)
                wou = SB(ph, "wou", [128, KC, DM], BF16)
                for n in range(3):
                    dma("pool", wbr[:, n * 4:(n + 1) * 4, :], w_branch[l, n].rearrange("(k p) d -> p k d", p=128), w=[wbr.b])
                dma("pool", wou[:], w_out[l].rearrange("(k p) d -> p k d", p=128), w=[wou.b])
                yts = [SB(ph, "yt%d" % i, [128, 12, 512], BF16) for i in range(2)]
                gts = [SB(ph, "gt%d" % i, [128, 24, 512], BF16) for i in range(2)]
                xt = SB(ph, "mxt", [128, KC, 512], F32)
                mots = [SB(ph, "mot%d" % i, [128, KC, 512], F32) for i in range(2)]
                mt = SB(ph, "mmt", [128, KC, 512], BF16)
                macc = [SB(ph, "macc%d" % i, [128, 512], F32) for i in range(2)]
                mtmp = [SB(ph, "mtmp%d" % i, [128, 512], F32) for i in range(2)]
                sq = SB(ph, "msq", [128, KC, 512], BF16)
                rs = SB(ph, "mrs", [128, 512], F32)
                for ti, (t0, sz) in enumerate(TILES):
                    yt, gt = yts[ti % 2], gts[ti % 2]
                    ot = mots[ti % 2]
                    dma("sp", yt[:, :, 0:sz], yT[:, :, t0:t0 + sz].rearrange("c p t -> p c t"), r=[d_y], w=[yt.b])
                    dma("sp", gt[:, :, 0:sz], gat[:, :, t0:t0 + sz].rearrange("c p t -> p c t"), r=[d_z], w=[gt.b])
                    dma("sp", xt[:, :, 0:sz], xT[:, :, t0:t0 + sz].rearrange("k p t -> p k t"), r=[d_xT], w=[xt.b])
                    for dc in range(KC):
                        ma, mp_ = macc[dc % 2], mtmp[dc % 2]
                        for n in range(3):
                            ps = nextps()
                            for k in range(4):
                                MM(ps[:, 0:sz], wbr[:, n * 4 + k, dc * 128:(dc + 1) * 128], yt[:, n * 4 + k, 0:sz], k == 0, k == 3, [wbr.b, yt.b], [ps.b])
                            g_ = gt[:, n * 8 + dc, 0:sz]
                            if n == 0:
                                TT("dve", ma[:, 0:sz], ps[:, 0:sz], g_, ALU.mult, [ps.b, gt.b], [ma.b])
                            elif n == 1:
                                TT("dve", mp_[:, 0:sz], ps[:, 0:sz], g_, ALU.mult, [ps.b, gt.b], [mp_.b])
                                TT("pool", ma[:, 0:sz], ma[:, 0:sz], mp_[:, 0:sz], ALU.add, [ma.b, mp_.b], [ma.b])
                            else:
                                TT("dve", mp_[:, 0:sz], ps[:, 0:sz], g_, ALU.mult, [ps.b, gt.b], [mp_.b])
                                TT("pool", mt[:, dc, 0:sz], ma[:, 0:sz], mp_[:, 0:sz], ALU.add, [ma.b, mp_.b], [mt.b])
                    for dc in range(KC):
                        ps = nextps()
                        for kc in range(KC):
                            MM(ps[:, 0:sz], wou[:, kc, dc * 128:(dc + 1) * 128], mt[:, kc, 0:sz], kc == 0, kc == KC - 1, [wou.b, mt.b], [ps.b])
                        CP("act", ot[:, dc, 0:sz], ps[:, 0:sz], [ps.b], [ot.b])
                    epilogue(ot, xt, G1, ti, t0, sz, sq, rs, None)
                P.barrier()
            if stop_after == "MIX":
                break
            with ExitStack() as ph:
                hT = SB(ph, "hT2", [128, KC, NT], BF16, nb=NTI)
                with ExitStack() as ph1:
                    norm_phase(ph1, l, A2, 24, hT)
                    P.barrier()
                NU = NT + 3
                Uas = [SB(ph, "Ua%d" % i, [128, NU], F32) for i in range(2)]
                Ugs = [SB(ph, "Ug%d" % i, [128, NU], F32) for i in range(2)]
                Ya = SB(ph, "Ya", [128, NU], F32)
                Yg = SB(ph, "Yg", [128, NU], F32)
                ast = [SB(ph, "ast%d" % i, [128, NU], BF16) for i in range(2)]
                cw = SB(ph, "cw", [128, 44, 3], F32)
                cb = SB(ph, "cb", [128, 44], F32)
                wua = [SB(ph, "wua%d" % i, [128, KC, 128], BF16) for i in range(2)]
                wug = [SB(ph, "wug%d" % i, [128, KC, 128], BF16) for i in range(2)]
                dma("sp", cw[:], convT[:, l, :, :], w=[cw.b])
                dma("sp", cb[:], convbT[:, l, :], w=[cb.b])
                for i_ in range(2):
                    MSET("pool", Uas[i_][:], 0.0, [Uas[i_].b])
                    MSET("pool", Ugs[i_][:], 0.0, [Ugs[i_].b])

                def ucol(t):
                    return t + 1 if t < LC else t + 2
                NY = NT + 1
                for j in range(NFC):
                    wa, wg_ = wua[j % 2], wug[j % 2]
                    Ua, Ug = Uas[j % 2], Ugs[j % 2]
                    dma("pool", wa[:], w_up[l, :, j * 128:(j + 1) * 128].rearrange("(k p) n -> p k n", p=128), w=[wa.b])
                    dma("pool", wg_[:], w_up[l, :, FFN + j * 128:FFN + (j + 1) * 128].rearrange("(k p) n -> p k n", p=128), w=[wg_.b])
                    for (wt, U) in ((wa, Ua), (wg_, Ug)):
                        for ti, (t0, sz) in enumerate(TILES):
                            ps = nextps()
                            for kc in range(KC):
                                MM(ps[:, 0:sz], wt[:, kc, :], hT[:, kc, t0:t0 + sz], kc == 0, kc == KC - 1, [wt.b, hT.bs[ti]], [ps.b])
                            CP("act", U[:, ucol(t0):ucol(t0) + sz], ps[:, 0:sz], [ps.b], [U.b])
                    for (U, Y, cj) in ((Ua, Ya, j), (Ug, Yg, NFC + j)):
                        ACT(Y[:, 0:NY], U[:, 0:NY], AF.Identity, [U.b, cw.b, cb.b], [Y.b], bias=cb[:, cj:cj + 1], scale=cw[:, cj, 0:1])
                        STT(Y[:, 0:NY], U[:, 1:NY + 1], cw[:, cj, 1:2], Y[:, 0:NY], ALU.mult, ALU.add, [U.b, cw.b, Y.b], [Y.b])
                        STT(Y[:, 0:NY], U[:, 2:NY + 2], cw[:, cj, 2:3], Y[:, 0:NY], ALU.mult, ALU.add, [U.b, cw.b, Y.b], [Y.b])
                    a_ = ast[j % 2]
                    ACT(Ya[:, 0:NY], Ya[:, 0:NY], AF.Silu, [Ya.b], [Ya.b])
                    TT("dve", a_[:, 0:NY], Ya[:, 0:NY], Yg[:, 0:NY], ALU.mult, [Ya.b, Yg.b], [a_.b])
                    dma("sp", actT[j, :, 0:LC], a_[:, 0:LC], r=[a_.b], w=[d_act])
                    dma("sp", actT[j, :, LC:NT], a_[:, LC + 1:NT + 1], r=[a_.b], w=[d_act])
                P.barrier()
            with ExitStack() as ph:
                wdn = SB(ph, "wdn", [128, NFC, DM], BF16)
                dma("pool", wdn[:, 0:11, :], w_down[l, 0:11 * 128, :].rearrange("(k p) d -> p k d", p=128), w=[wdn.b])
                dma("pool", wdn[:, 11:22, :], w_down[l, 11 * 128:22 * 128, :].rearrange("(k p) d -> p k d", p=128), w=[wdn.b])
                ats = [SB(ph, "at%d" % i, [128, NFC, 512], BF16) for i in range(2)]
                xt = SB(ph, "fxt", [128, KC, 512], F32)
                fots = [SB(ph, "fot%d" % i, [128, KC, 512], F32) for i in range(2)]
                sq = SB(ph, "fsq", [128, KC, 512], BF16)
                rs = SB(ph, "frs", [128, 512], F32)
                for ti, (t0, sz) in enumerate(TILES):
                    at = ats[ti % 2]
                    ot = fots[ti % 2]
                    dma("sp", at[:, :, 0:sz], actT[:, :, t0:t0 + sz].rearrange("c p t -> p c t"), r=[d_act], w=[at.b])
                    dma("sp", xt[:, :, 0:sz], xT[:, :, t0:t0 + sz].rearrange("k p t -> p k t"), r=[d_xT], w=[xt.b])
                    for dc in range(KC):
                        ps = nextps()
                        for k in range(NFC):
                            MM(ps[:, 0:sz], wdn[:, k, dc * 128:(dc + 1) * 128], at[:, k, 0:sz], k == 0, k == NFC - 1, [wdn.b, at.b], [ps.b])
                        CP("act", ot[:, dc, 0:sz], ps[:, 0:sz], [ps.b], [ot.b])
                    epilogue(ot, xt, G2, ti, t0, sz, sq, rs, None)
                P.barrier()
        if stop_after is None:
            with ExitStack() as ph:
                xtt = [SB(ph, "fxtt%d" % i, [128, KC, 128], F32) for i in range(2)]
                orow = [SB(ph, "orow%d" % i, [128, DM], F32) for i in range(2)]
                for tb in range(LC // 128, NB):
                    a, o_ = xtt[tb % 2], orow[tb % 2]
                    dma("sp", a[:], xT[:, :, tb * 128:(tb + 1) * 128].rearrange("k p t -> p k t"), r=[d_xT], w=[a.b])
                    pa, pb = nextps(), nextps()
                    for kc in range(KC):
                        pp_ = pa if kc < 4 else pb
                        TR(pp_[:, (kc % 4) * 128:(kc % 4 + 1) * 128], a[:, kc, :], ident_f[:], [a.b, ident_f.b], [pp_.b])
                    CP("act", o_[:, 0:512], pa[:, :], [pa.b], [o_.b])
                    CP("dve", o_[:, 512:1024], pb[:, :], [pb.b], [o_.b])
                    dma("sp", out[tb * 128 - LC:(tb + 1) * 128 - LC, :], o_[:], r=[o_.b])
        P.barrier()
    return nc


def prep_shared(inp, cfg):
    L, LC, DEPTH = cfg["L"], cfg["LC"], cfg["DEPTH"]
    NT = L + LC
    f = lambda a: np.ascontiguousarray(np.asarray(a, dtype=np.float32))
    sh = {}
    sh["w_mod"] = f(inp["w_mod"][:DEPTH])
    sh["b_modT"] = f(np.asarray(inp["b_mod"])[:DEPTH].reshape(DEPTH, 48, 128).transpose(2, 0, 1))
    sh["norm_gT"] = f(np.asarray(inp["norm_g"])[:DEPTH].reshape(DEPTH, 4, KC, 128).transpose(3, 0, 1, 2))
    w_in = np.asarray(inp["w_in"])[:DEPTH]
    sh["w_in"] = f(w_in)
    idx = []
    for h in range(8):
        idx += [C_Q + h * 64 + (d + 32) % 64 for d in range(64)]
    for h in range(2):
        idx += [C_K + h * 64 + (d + 32) % 64 for d in range(64)]
    sh["w_rot"] = f(w_in[:, :, np.array(idx)])
    rows = L // 64
    row = np.repeat(np.arange(rows, dtype=np.float32), 64)
    col = np.tile(np.arange(64, dtype=np.float32), rows)
    inv = (10000.0 ** (-np.arange(16, dtype=np.float32) / 16)).astype(np.float32)
    ang = np.concatenate([row[:, None] * inv, col[:, None] * inv], axis=-1)
    cos, sin = np.cos(ang).T, np.sin(ang).T
    C = np.ones((64, NT), np.float32)
    S = np.zeros((64, NT), np.float32)
    C[0:32, LC:] = cos
    C[32:64, LC:] = cos
    S[0:32, LC:] = -sin
    S[32:64, LC:] = sin
    sh["ropeC"], sh["ropeS"] = np.concatenate([C, C], 0), np.concatenate([S, S], 0)
    def st(a):
        a = np.asarray(a)[:DEPTH].reshape(DEPTH, 2, 16, 2, 64)
        return f(a.transpose(3, 4, 0, 1, 2).reshape(128, DEPTH, 2, 16))
    sh["s5_lr"] = st(inp["s5_lam_re"])
    sh["s5_li"] = st(inp["s5_lam_im"])
    ldt = np.asarray(inp["s5_log_dt"])[:DEPTH]
    sh["s5_ldt"] = st(np.repeat(ldt[..., None], 64, axis=-1))
    def padB(b):
        b = np.asarray(b)[:DEPTH]
        o = np.zeros((DEPTH, 16, 128, 128), np.float32)
        for pr in range(16):
            for g2 in range(2):
                s0 = (pr % 4) * 32 + g2 * 16
                o[:, pr, g2 * 64:(g2 + 1) * 64, s0:s0 + 16] = b[:, 2 * pr + g2]
        return o
    def padC(c):
        c = np.asarray(c)[:DEPTH]
        o = np.zeros((DEPTH, 16, 128, 128), np.float32)
        for pr in range(16):
            for g2 in range(2):
                s0 = (pr % 4) * 32 + g2 * 16
                o[:, pr, g2 * 64:(g2 + 1) * 64, s0:s0 + 16] = c[:, 2 * pr + g2].transpose(0, 2, 1)
        return o
    sh["s5_Bre"], sh["s5_Bim"] = padB(inp["s5_b_re"]), padB(inp["s5_b_im"])
    sh["s5_Cre"], sh["s5_Cim"] = padC(inp["s5_c_re"]), padC(inp["s5_c_im"])
    sh["s5_dT"] = f(np.asarray(inp["s5_d"])[:DEPTH].reshape(DEPTH, 4, 128).transpose(2, 0, 1))
    sh["s5_bgT"] = f(np.asarray(inp["s5_b_glu"])[:DEPTH].reshape(DEPTH, 4, 128).transpose(2, 0, 1))
    sh["s5_wglu"] = f(inp["s5_w_glu"][:DEPTH])
    sh["sinkrow"] = f(np.repeat(np.asarray(inp["att_sink"])[:DEPTH], 128, axis=-1).reshape(DEPTH, 1, 1024))
    sh["hg_lbT"] = f(np.asarray(inp["hg_lb_logits"])[:DEPTH].reshape(DEPTH, 2, 4, 128).transpose(3, 0, 1, 2).reshape(128, DEPTH, 8))
    sh["hg_ngT"] = f(np.asarray(inp["hg_norm_g"])[:DEPTH].reshape(DEPTH, 4, 128).transpose(2, 0, 1))
    sh["w_branch"] = f(inp["w_branch"][:DEPTH])
    sh["w_out"] = f(inp["w_out"][:DEPTH])
    sh["w_up"] = f(inp["ffn_w_up"][:DEPTH])
    sh["convT"] = f(np.asarray(inp["ffn_conv_w"])[:DEPTH].reshape(DEPTH, 3, 44, 128).transpose(3, 0, 2, 1))
    sh["convbT"] = f(np.asarray(inp["ffn_conv_b"])[:DEPTH].reshape(DEPTH, 44, 128).transpose(2, 0, 1))
    sh["w_down"] = f(inp["ffn_w_down"][:DEPTH])
    sh["c_ident"] = np.eye(128, dtype=np.float32)
    k = np.arange(128)[:, None]
    q = np.tile(np.arange(128), 4)[None, :]
    sh["c_maskP"] = np.where(k >= q, 0.0, -30000.0).astype(np.float32)
    sh["c_maskN"] = np.where(k <= q, 0.0, -30000.0).astype(np.float32)
    io = np.zeros((128, 2, 512), np.float32)
    io[:, 0, :] = np.arange(1, 513, dtype=np.float32)[None]
    io[:, 1, :] = (512 - np.arange(512, dtype=np.float32))[None]
    sh["c_iota"] = io
    s_ = np.arange(32)[:, None]
    t_ = np.tile(np.arange(32), 4)[None, :]
    hm = np.zeros((32, 2, 128), np.float32)
    hm[:, 0, :] = (s_ <= t_)
    hm[:, 1, :] = (s_ >= t_)
    sh["c_hgmask"] = hm
    rm = np.ones((128, 512), np.float32)
    rm[:, ::32] = 0.0
    sh["c_rmask"] = rm
    return sh


def prep_core(inp, b, cfg, sh):
    L, LC = cfg["L"], cfg["LC"]
    m = dict(sh)
    m["x"] = np.ascontiguousarray(np.asarray(inp["x"], dtype=np.float32)[b, :L])
    m["ctx"] = np.ascontiguousarray(np.asarray(inp["ctx"], dtype=np.float32)[b, :LC])
    cT = np.stack([np.asarray(inp["c"], dtype=np.float32)[b], np.asarray(inp["c_ctx"], dtype=np.float32)], axis=-1)
    m["cT"] = np.ascontiguousarray(cT.reshape(KC, 128, 2).transpose(1, 0, 2))
    return m


_NC_CACHE = {}


def kernel(**inputs):
    cfg = FULL
    key = "full"
    if key not in _NC_CACHE:
        _NC_CACHE[key] = build(cfg)
    nc = _NC_CACHE[key]
    sh = prep_shared(inputs, cfg)
    in_maps = [prep_core(inputs, b, cfg, sh) for b in range(8)]
    res = run_bass_kernel_spmd(nc, in_maps, core_ids=list(range(8)))
    return np.stack([np.asarray(r["out"], dtype=np.float32) for r in res.results], axis=0)
```

```python
import numpy as np
import ml_dtypes
from contextlib import ExitStack
import concourse.bass as bass
import concourse.mybir as mybir
from concourse.bass_utils import run_bass_kernel_spmd

F32 = mybir.dt.float32
BF16 = mybir.dt.bfloat16
I32 = mybir.dt.int32
AF = mybir.ActivationFunctionType
ALU = mybir.AluOpType
PI = float(np.pi)


class Buf:
    __slots__ = ("w", "r")

    def __init__(self):
        self.w = None
        self.r = {}


class Prog:
    def __init__(self, nc, es, ndma=48):
        self.nc = nc
        self.eng = {"pe": nc.tensor, "act": nc.scalar, "dve": nc.vector, "pool": nc.gpsimd, "sp": nc.sync}
        self.sem = {}
        self.cnt = {}
        for k in self.eng:
            self.sem[k] = es.enter_context(nc.semaphore("s_" + k))
            self.cnt[k] = 0
        self.ndma = ndma
        for i in range(ndma):
            self.sem[("d", i)] = es.enter_context(nc.semaphore("s_d%d" % i))
            self.cnt[("d", i)] = 0
        self.known = {e: {} for e in self.eng}
        self.rr = 0
        self.nins = 0

    def _waits(self, e, r, w, extra=()):
        need = {}
        kn = self.known[e]

        def add(tok, same_ok):
            if tok is None:
                return
            k, v = tok
            if k == e and (e == "pe" or not same_ok):
                return
            if kn.get(k, 0) >= v:
                return
            if need.get(k, 0) < v:
                need[k] = v

        for b in r:
            add(b.w, True)
        for b in w:
            add(b.w, True)
            for t in b.r.values():
                add(t, False)
        for t in extra:
            add(t, True)
        E = self.eng[e]
        for k, v in need.items():
            E.wait_ge(self.sem[k], v)
            kn[k] = v
            self.nins += 1

    def op(self, e, fn, r=(), w=()):
        self._waits(e, r, w)
        ins = fn(self.eng[e])
        self.cnt[e] += 1
        ins.then_inc(self.sem[e], 1)
        self.nins += 1
        tok = (e, self.cnt[e])
        for b in r:
            b.r[e] = tok
        for b in w:
            b.w = tok
            b.r = {}
        return tok

    def dma(self, q, out, in_, r=(), w=()):
        i = self.rr
        self.rr = (i + 1) % self.ndma
        key = ("d", i)
        self._waits(q, r, w, extra=[(key, self.cnt[key])])
        ins = self.eng[q].dma_start(out=out, in_=in_)
        self.cnt[key] += 16
        ins.then_inc(self.sem[key], 16)
        self.nins += 1
        tok = (key, self.cnt[key])
        for b in r:
            b.r[key] = tok
        for b in w:
            b.w = tok
            b.r = {}
        return tok

    def barrier(self):
        toks = [(k, v) for k, v in self.cnt.items() if v > 0]
        for e, E in self.eng.items():
            kn = self.known[e]
            for k, v in toks:
                if kn.get(k, 0) < v:
                    E.wait_ge(self.sem[k], v)
                    kn[k] = v
                    self.nins += 1


class Tl:
    def __init__(self, t, nb=1):
        self.t = t
        self.bs = [Buf() for _ in range(nb)]

    @property
    def b(self):
        return self.bs[0]

    def __getitem__(self, k):
        return self.t[k]


DM = 1024
KC = 8
EPS = 1e-6
IN_COLS = 6912
C_S5, C_Q, C_K, C_V, C_HQ, C_FF, C_FB, C_HI, C_HG, C_GATE = 0, 512, 1024, 1152, 1280, 1792, 2304, 2816, 3328, 3840
FFN = 2816
NFC = 22
FULL = dict(L=4096, LC=256, DEPTH=4, taps=())


def build(cfg):
    L, LC, DEPTH = cfg["L"], cfg["LC"], cfg["DEPTH"]
    taps = set(cfg.get("taps", ()))
    stop_after = cfg.get("stop_after", None)
    NT = L + LC
    NB = NT // 128
    TILES = [(0, LC)] + [(LC + 512 * i, 512) for i in range(L // 512)]
    NTI = len(TILES)

    def tile_of(tok):
        for i, (t0, sz) in enumerate(TILES):
            if t0 <= tok < t0 + sz:
                return i
        raise ValueError

    nc = bass.Bass("TRN2", target_bir_lowering=False)

    def din(name, shape, dt=F32):
        return nc.dram_tensor(name, list(shape), dt, kind="ExternalInput").ap()

    def dscr(name, shape, dt):
        kind = "ExternalOutput" if name in taps else "Internal"
        return nc.dram_tensor(name, list(shape), dt, kind=kind).ap()

    x_in = din("x", [L, DM])
    ctx_in = din("ctx", [LC, DM])
    cT_in = din("cT", [128, KC, 2])
    w_mod = din("w_mod", [DEPTH, DM, 6 * DM])
    b_modT = din("b_modT", [128, DEPTH, 48])
    norm_gT = din("norm_gT", [128, DEPTH, 4, KC])
    w_in = din("w_in", [DEPTH, DM, IN_COLS])
    w_rot = din("w_rot", [DEPTH, DM, 640])
    ropeC_in = din("ropeC", [128, NT])
    ropeS_in = din("ropeS", [128, NT])
    s5_lr = din("s5_lr", [128, DEPTH, 2, 16])
    s5_li = din("s5_li", [128, DEPTH, 2, 16])
    s5_ldt = din("s5_ldt", [128, DEPTH, 2, 16])
    s5_Bre = din("s5_Bre", [DEPTH, 16, 128, 128])
    s5_Bim = din("s5_Bim", [DEPTH, 16, 128, 128])
    s5_Cre = din("s5_Cre", [DEPTH, 16, 128, 128])
    s5_Cim = din("s5_Cim", [DEPTH, 16, 128, 128])
    s5_dT = din("s5_dT", [128, DEPTH, 4])
    s5_bgT = din("s5_bgT", [128, DEPTH, 4])
    s5_wglu = din("s5_wglu", [DEPTH, 512, 512])
    sinkrow = din("sinkrow", [DEPTH, 1, 1024])
    hg_lbT = din("hg_lbT", [128, DEPTH, 8])
    hg_ngT = din("hg_ngT", [128, DEPTH, 4])
    w_branch = din("w_branch", [DEPTH, 3, 512, DM])
    w_out = din("w_out", [DEPTH, DM, DM])
    w_up = din("w_up", [DEPTH, DM, 2 * FFN])
    convT = din("convT", [128, DEPTH, 44, 3])
    convbT = din("convbT", [128, DEPTH, 44])
    w_down = din("w_down", [DEPTH, FFN, DM])
    c_ident = din("c_ident", [128, 128])
    c_maskP = din("c_maskP", [128, 512])
    c_maskN = din("c_maskN", [128, 512])
    c_iota = din("c_iota", [128, 2, 512])
    c_hgmask = din("c_hgmask", [32, 2, 128])
    c_rmask = din("c_rmask", [128, 512])
    out = nc.dram_tensor("out", [L, DM], F32, kind="ExternalOutput").ap()

    xT = dscr("xT", [KC, 128, NT], F32)
    zs5 = dscr("zs5", [4, 128, NT], BF16)
    qT = dscr("qT", [8, 64, NT], BF16)
    kT = dscr("kT", [2, 64, NT], BF16)
    vtm = dscr("vtm", [NT, 640], BF16)
    hgq = dscr("hgq", [4, 128, NT], BF16)
    lf = dscr("lf", [2, 4, 128, NT], F32)
    kk = dscr("kk", [2, 4, 128, NT], BF16)
    hgg = dscr("hgg", [4, 128, NT], BF16)
    gat = dscr("gat", [24, 128, NT], BF16)
    yT = dscr("yT", [12, 128, NT], BF16)
    ofw = dscr("ofw", [4, 128, NT], F32)
    ofw2 = dscr("ofw2", [4, 128, NT], F32)
    actT = dscr("actT", [NFC, 128, NT], BF16)
    dbg = dscr("dbg", [128, 4096], F32)
    d_xT, d_z, d_y, d_of, d_act = Buf(), Buf(), Buf(), Buf(), Buf()

    with ExitStack() as es:
        P = Prog(nc, es)
        op, dma = P.op, P.dma

        uid = [0]

        def SB(stack, name, shape, dt, nb=1):
            uid[0] += 1
            return Tl(stack.enter_context(nc.sbuf_tensor("%s_u%d" % (name, uid[0]), list(shape), dt)), nb)

        psum = [Tl(es.enter_context(nc.psum_tensor("ps%d" % i, [128, 512], F32))) for i in range(8)]
        psrr = [0, 0]
        nlong = [2]

        def nextps(long=False):
            if long:
                p = psum[psrr[1] % nlong[0]]
                psrr[1] += 1
            else:
                p = psum[nlong[0] + psrr[0] % (8 - nlong[0])]
                psrr[0] += 1
            return p

        def MM(o, l, r_, st, sp, rb, wb):
            op("pe", lambda E: E.matmul(o, l, r_, start=st, stop=sp), r=rb, w=wb)

        def TR(o, i, idn, rb, wb):
            op("pe", lambda E: E.transpose(o, i, idn), r=rb, w=wb)

        def ACT(o, i, f, rb, wb, bias=None, scale=None):
            kw = {}
            if bias is not None:
                kw["bias"] = bias
            if scale is not None:
                kw["scale"] = scale
            op("act", lambda E: E.activation(out=o, in_=i, func=f, **kw), r=rb, w=wb)

        def CP(e, o, i, rb, wb):
            if e == "act":
                op("act", lambda E: E.copy(out=o, in_=i), r=rb, w=wb)
            else:
                op(e, lambda E: E.tensor_copy(out=o, in_=i), r=rb, w=wb)

        def TT(e, o, a, b_, alu, rb, wb):
            op(e, lambda E: E.tensor_tensor(out=o, in0=a, in1=b_, op=alu), r=rb, w=wb)

        def TS(e, o, a, s1, s2, o0, o1, rb, wb):
            if s2 is None:
                op(e, lambda E: E.tensor_scalar(out=o, in0=a, scalar1=s1, scalar2=None, op0=o0), r=rb, w=wb)
            else:
                op(e, lambda E: E.tensor_scalar(out=o, in0=a, scalar1=s1, scalar2=s2, op0=o0, op1=o1), r=rb, w=wb)

        def STT(o, a, s, b_, o0, o1, rb, wb):
            op("dve", lambda E: E.scalar_tensor_tensor(out=o, in0=a, scalar=s, in1=b_, op0=o0, op1=o1), r=rb, w=wb)

        def SCAN(o, d0, d1, init, rb, wb):
            op("dve", lambda E: E.tensor_tensor_scan(out=o, data0=d0, data1=d1, initial=init, op0=ALU.mult, op1=ALU.add), r=rb, w=wb)

        def MSET(e, o, v, wb):
            op(e, lambda E: E.memset(o, v), w=wb)

        def DBG(c0, ap, n, tl):
            if "dbg" in taps:
                dma("pool", dbg[0:ap.shape[0], c0:c0 + n], ap, r=[tl.b])

        ident_f = SB(es, "ident_f", [128, 128], F32)
        ident_b = SB(es, "ident_b", [128, 128], BF16)
        ones_b = SB(es, "ones_b", [128, 128], BF16)
        neghalf = SB(es, "neghalf", [128, 512], F32)
        modv = SB(es, "modv", [128, DEPTH, 48, 2], F32)
        ngt = SB(es, "ngt", [128, DEPTH, 4, KC], F32)
        lbv = SB(es, "lbv", [128, DEPTH, 8], F32)
        omlb = SB(es, "omlb", [128, DEPTH, 8], F32)
        A1 = SB(es, "A1", [128, KC, 2], F32)
        G1 = SB(es, "G1", [128, KC, 2], F32)
        A2 = SB(es, "A2", [128, KC, 2], F32)
        G2 = SB(es, "G2", [128, KC, 2], F32)
        STAT = [ident_f.b, ident_b.b, ones_b.b, neghalf.b]

        dma("sp", ident_f[:], c_ident[:, :], w=[ident_f.b])
        CP("dve", ident_b[:], ident_f[:], [ident_f.b], [ident_b.b])
        MSET("pool", ones_b[:], 1.0, [ones_b.b])
        MSET("pool", neghalf[:], -0.5, [neghalf.b])
        epsb = SB(es, "epsb", [128, 1], F32)
        MSET("pool", epsb[:], EPS, [epsb.b])
        dma("sp", ngt[:], norm_gT[:, :, :, :], w=[ngt.b])

        with ExitStack() as ph:
            xr = [SB(ph, "xr%d" % i, [128, DM], F32) for i in range(2)]
            xtt = [SB(ph, "xtt%d" % i, [128, KC, 128], F32) for i in range(2)]
            for tb in range(NB):
                src = ctx_in[tb * 128:(tb + 1) * 128, :] if tb < LC // 128 else x_in[tb * 128 - LC:(tb + 1) * 128 - LC, :]
                a, o_ = xr[tb % 2], xtt[tb % 2]
                dma("sp", a[:], src, w=[a.b])
                pa, pb = nextps(), nextps()
                for kc in range(KC):
                    pp_ = pa if kc < 4 else pb
                    TR(pp_[:, (kc % 4) * 128:(kc % 4 + 1) * 128], a[:, kc * 128:(kc + 1) * 128], ident_f[:], [a.b, ident_f.b], [pp_.b])
                CP("act", o_[:, 0:4, :], pa[:].rearrange("p (k t) -> p k t", k=4), [pa.b], [o_.b])
                CP("dve", o_[:, 4:8, :], pb[:].rearrange("p (k t) -> p k t", k=4), [pb.b], [o_.b])
                dma("sp", xT[:, :, tb * 128:(tb + 1) * 128].rearrange("k p t -> p k t"), o_[:], r=[o_.b], w=[d_xT])
            cTt = SB(ph, "cTt", [128, KC, 2], F32)
            scb = SB(ph, "scb", [128, KC, 2], BF16)
            bmt = SB(ph, "bmt", [128, DEPTH, 48], F32)
            wm = [SB(ph, "wm%d" % i, [128, KC, 1024], BF16) for i in range(2)]
            dma("sp", cTt[:], cT_in[:, :, :], w=[cTt.b])
            dma("sp", bmt[:], b_modT[:, :, :], w=[bmt.b])
            ACT(scb[:], cTt[:], AF.Silu, [cTt.b], [scb.b])
            for l in range(DEPTH):
                pm = nextps()
                for grp in range(6):
                    wt = wm[(l * 6 + grp) % 2]
                    dma("pool", wt[:], w_mod[l, :, grp * 1024:(grp + 1) * 1024].rearrange("(k p) n -> p k n", p=128), w=[wt.b])
                    for j in range(8):
                        oc = grp * 8 + j
                        for kc in range(KC):
                            MM(pm[:, oc * 2:oc * 2 + 2], wt[:, kc, j * 128:(j + 1) * 128], scb[:, kc, :], kc == 0, kc == KC - 1, [wt.b, scb.b], [pm.b])
                TT("dve", modv[:, l, :, :], pm[:, 0:96].rearrange("p (c w) -> p c w", w=2),
                   bmt[:, l, :].unsqueeze(2).to_broadcast([128, 48, 2]), ALU.add, [pm.b, bmt.b], [modv.b])
            lg = SB(ph, "lg", [128, DEPTH, 8], F32)
            sm = SB(ph, "sm", [128, 8], F32)
            dma("sp", lg[:], hg_lbT[:, :, :], w=[lg.b])
            ACT(lg[:], lg[:], AF.Exp, [lg.b], [lg.b])
            CP("dve", sm[:], lg[:, 0, :], [lg.b], [sm.b])
            for l in range(1, DEPTH):
                TT("dve", sm[:], sm[:], lg[:, l, :], ALU.add, [sm.b, lg.b], [sm.b])
            op("dve", lambda E: E.reciprocal(out=sm[:], in_=sm[:]), r=[sm.b], w=[sm.b])
            MSET("dve", lbv[:, 0, :], 0.0, [lbv.b])
            for l in range(1, DEPTH):
                TT("dve", lg[:, l, :], lg[:, l, :], sm[:], ALU.mult, [lg.b, sm.b], [lg.b])
                TT("dve", lbv[:, l, :], lbv[:, l - 1, :], lg[:, l, :], ALU.add, [lbv.b, lg.b], [lbv.b])
            TS("dve", omlb[:], lbv[:], -1.0, 1.0, ALU.mult, ALU.add, [lbv.b], [omlb.b])
            P.barrier()

        def mod_scalars(l):
            for (Aq, sc0, gi) in ((A1, 8, 0), (A2, 32, 2)):
                TS("dve", Aq[:], modv[:, l, sc0:sc0 + 8, :], 1.0, None, ALU.add, None, [modv.b], [Aq.b])
                TT("dve", Aq[:], Aq[:], ngt[:, l, gi, :].unsqueeze(2).to_broadcast([128, KC, 2]), ALU.mult, [Aq.b, ngt.b], [Aq.b])
            for (Gq, g0, gi) in ((G1, 16, 1), (G2, 40, 3)):
                TT("dve", Gq[:], modv[:, l, g0:g0 + 8, :], ngt[:, l, gi, :].unsqueeze(2).to_broadcast([128, KC, 2]), ALU.mult, [modv.b, ngt.b], [Gq.b])

        def norm_phase(ph, l, Aq, sh0, hT):
            xts = [SB(ph, "nxt%d" % i, [128, KC, 512], F32) for i in range(2)]
            sqs = [SB(ph, "nsq%d" % i, [128, KC, 512], BF16) for i in range(2)]
            rss = [SB(ph, "nrs%d" % i, [128, 512], F32) for i in range(2)]
            tmp = SB(ph, "ntmp", [128, KC, 512], F32, nb=KC)

            def stage_a(ti):
                t0, sz = TILES[ti]
                xt, sq, rs = xts[ti % 2], sqs[ti % 2], rss[ti % 2]
                dma("sp", xt[:, :, 0:sz], xT[:, :, t0:t0 + sz].rearrange("k p t -> p k t"), r=[d_xT], w=[xt.b])
                ACT(sq[:, :, 0:sz], xt[:, :, 0:sz], AF.Square, [xt.b], [sq.b])
                ps = nextps()
                for kc in range(KC):
                    MM(ps[:, 0:sz], ones_b[:], sq[:, kc, 0:sz], kc == 0, kc == KC - 1, [ones_b.b, sq.b], [ps.b])
                ACT(rs[:, 0:sz], ps[:, 0:sz], AF.Ln, [ps.b, epsb.b], [rs.b], bias=epsb[:, 0:1], scale=1.0 / DM)
                ACT(rs[:, 0:sz], rs[:, 0:sz], AF.Exp, [rs.b], [rs.b], scale=-0.5)

            def stage_b(ti):
                t0, sz = TILES[ti]
                w_ = 1 if ti == 0 else 0
                xt, rs = xts[ti % 2], rss[ti % 2]
                for kc in range(KC):
                    STT(tmp[:, kc, 0:sz], xt[:, kc, 0:sz], Aq[:, kc, w_:w_ + 1], rs[:, 0:sz], ALU.mult, ALU.mult, [xt.b, Aq.b, rs.b], [tmp.bs[kc]])
                    ACT(hT[:, kc, t0:t0 + sz], tmp[:, kc, 0:sz], AF.Identity, [tmp.bs[kc], modv.b], [hT.bs[ti]],
                        bias=modv[:, l, sh0 + kc, w_:w_ + 1])
            stage_a(0)
            for ti in range(NTI):
                stage_b(ti)
                if ti + 1 < NTI:
                    stage_a(ti + 1)

        def epilogue(ot, xt, Gq, ti, t0, sz, sq, rs, tmp):
            w_ = 1 if ti == 0 else 0
            ACT(sq[:, :, 0:sz], ot[:, :, 0:sz], AF.Square, [ot.b], [sq.b])
            ps = nextps()
            for kc in range(KC):
                MM(ps[:, 0:sz], ones_b[:], sq[:, kc, 0:sz], kc == 0, kc == KC - 1, [ones_b.b, sq.b], [ps.b])
            ACT(rs[:, 0:sz], ps[:, 0:sz], AF.Ln, [ps.b, epsb.b], [rs.b], bias=epsb[:, 0:1], scale=1.0 / DM)
            ACT(rs[:, 0:sz], rs[:, 0:sz], AF.Exp, [rs.b], [rs.b], scale=-0.5)
            for kc in range(KC):
                STT(ot[:, kc, 0:sz], ot[:, kc, 0:sz], Gq[:, kc, w_:w_ + 1], rs[:, 0:sz], ALU.mult, ALU.mult, [ot.b, Gq.b, rs.b], [ot.b])
                TT("pool", xt[:, kc, 0:sz], xt[:, kc, 0:sz], ot[:, kc, 0:sz], ALU.add, [xt.b, ot.b], [xt.b])
            dma("pool", xT[:, :, t0:t0 + sz].rearrange("k p t -> p k t"), xt[:, :, 0:sz], r=[xt.b], w=[d_xT])

        for l in range(DEPTH):
            mod_scalars(l)
            with ExitStack() as ph:
                hT = SB(ph, "hT", [128, KC, NT], BF16, nb=NTI)
                with ExitStack() as ph1:
                    norm_phase(ph1, l, A1, 0, hT)
                    P.barrier()
                wts = [SB(ph, "wt%d" % i, [128, KC, 512], BF16) for i in range(3)]
                wrr = [0]
                stg = [SB(ph, "stg%d" % i, [128, NT], BF16) for i in range(3)]
                srr = [0]
                stgf = [SB(ph, "stgf%d" % i, [128, NT], F32) for i in range(2)]
                tmpa = [SB(ph, "tmpa%d" % i, [128, 512], F32) for i in range(2)]
                tmpb = [SB(ph, "tmpb%d" % i, [128, 512], F32) for i in range(2)]

                def load_w(src, ncols):
                    wt = wts[wrr[0] % 3]
                    wrr[0] += 1
                    dma("pool", wt[:, :, 0:ncols], src.rearrange("(k p) n -> p k n", p=128), w=[wt.b])
                    return wt

                def next_stg():
                    s = stg[srr[0] % 3]
                    srr[0] += 1
                    return s

                def proj(wt, off, M, cons):
                    for ti, (t0, sz) in enumerate(TILES):
                        ps = nextps()
                        for kc in range(KC):
                            MM(ps[0:M, 0:sz], wt[:, kc, off:off + M], hT[:, kc, t0:t0 + sz], kc == 0, kc == KC - 1, [wt.b, hT.bs[ti]], [ps.b])
                        cons(ti, t0, sz, ps)

                def simple_group(col0, nchunks, func, dst):
                    for g0 in range(0, nchunks, 4):
                        n = min(4, nchunks - g0)
                        wt = load_w(w_in[l, :, col0 + g0 * 128:col0 + (g0 + n) * 128], n * 128)
                        for c in range(n):
                            s = next_stg()

                            def cons(ti, t0, sz, ps, s=s):
                                if func is None:
                                    CP("act", s[:, t0:t0 + sz], ps[:, 0:sz], [ps.b], [s.b])
                                else:
                                    ACT(s[:, t0:t0 + sz], ps[:, 0:sz], func, [ps.b], [s.b])
                            proj(wt, c * 128, 128, cons)
                            dma("sp", dst[g0 + c], s[:], r=[s.b], w=[d_z])

                simple_group(C_S5, 4, None, zs5)
                with ExitStack() as phq:
                    ropeC = SB(phq, "ropeC", [128, NT], F32)
                    ropeS = SB(phq, "ropeS", [128, NT], F32)
                    dma("sp", ropeC[:], ropeC_in[:, :], w=[ropeC.b])
                    dma("sp", ropeS[:], ropeS_in[:, :], w=[ropeS.b])
                    for (cbase, rbase, nh_, dst) in ((C_Q, 0, 8, qT), (C_K, 512, 2, kT)):
                        for g0 in range(0, nh_, 4):
                            n = min(4, nh_ - g0)
                            wa = load_w(w_in[l, :, cbase + g0 * 64:cbase + (g0 + n) * 64], n * 64)
                            wb = load_w(w_rot[l, :, rbase + g0 * 64:rbase + (g0 + n) * 64], n * 64)
                            for c in range(n // 2):
                                s = next_stg()
                                for ti, (t0, sz) in enumerate(TILES):
                                    p1, p2 = nextps(), nextps()
                                    for kc in range(KC):
                                        MM(p1[:, 0:sz], wa[:, kc, c * 128:(c + 1) * 128], hT[:, kc, t0:t0 + sz], kc == 0, kc == KC - 1, [wa.b, hT.bs[ti]], [p1.b])
                                    for kc in range(KC):
                                        MM(p2[:, 0:sz], wb[:, kc, c * 128:(c + 1) * 128], hT[:, kc, t0:t0 + sz], kc == 0, kc == KC - 1, [wb.b, hT.bs[ti]], [p2.b])
                                    ta, tb_ = tmpa[ti % 2], tmpb[ti % 2]
                                    TT("dve", ta[:, 0:sz], p1[:, 0:sz], ropeC[:, t0:t0 + sz], ALU.mult, [p1.b, ropeC.b], [ta.b])
                                    TT("dve", tb_[:, 0:sz], p2[:, 0:sz], ropeS[:, t0:t0 + sz], ALU.mult, [p2.b, ropeS.b], [tb_.b])
                                    TT("pool", s[:, t0:t0 + sz], ta[:, 0:sz], tb_[:, 0:sz], ALU.add, [ta.b, tb_.b], [s.b])
                                h0 = g0 + 2 * c
                                dma("sp", dst[h0:h0 + 2].rearrange("h d t -> (h d) t"), s[:], r=[s.b], w=[d_z])
                    P.barrier()
                with ExitStack() as ph3:
                    wv = SB(ph3, "wv", [128, KC, 640], BF16)
                    vst = [SB(ph3, "vst%d" % i, [128, 640], BF16) for i in range(2)]
                    dma("pool", wv[:, :, 0:128], w_in[l, :, C_V:C_V + 128].rearrange("(k p) n -> p k n", p=128), w=[wv.b])
                    dma("pool", wv[:, :, 128:640], w_in[l, :, C_HI:C_HI + 512].rearrange("(k p) n -> p k n", p=128), w=[wv.b])
                    for tb in range(NB):
                        ti = tile_of(tb * 128)
                        pa, pb = nextps(), nextps()
                        for kc in range(KC):
                            MM(pa[:, 0:512], hT[:, kc, tb * 128:(tb + 1) * 128], wv[:, kc, 128:640], kc == 0, kc == KC - 1, [wv.b, hT.bs[ti]], [pa.b])
                        for kc in range(KC):
                            MM(pb[:, 0:128], hT[:, kc, tb * 128:(tb + 1) * 128], wv[:, kc, 0:128], kc == 0, kc == KC - 1, [wv.b, hT.bs[ti]], [pb.b])
                        v = vst[tb % 2]
                        CP("act", v[:, 0:128], pb[:, 0:128], [pb.b], [v.b])
                        CP("dve", v[:, 128:640], pa[:, 0:512], [pa.b], [v.b])
                        dma("sp", vtm[tb * 128:(tb + 1) * 128, :], v[:], r=[v.b], w=[d_z])
                simple_group(C_HQ, 4, AF.Silu, hgq)
                for d in range(2):
                    wt = load_w(w_in[l, :, C_FF + d * 512:C_FF + (d + 1) * 512], 512)
                    for c in range(4):
                        s = next_stg()
                        sf = stgf[c % 2]
                        li_ = d * 4 + c

                        def cons(ti, t0, sz, ps, s=s, sf=sf, li_=li_):
                            ta = tmpa[ti % 2]
                            ACT(ta[:, 0:sz], ps[:, 0:sz], AF.Exp, [ps.b], [ta.b], scale=-1.0)
                            TS("dve", ta[:, 0:sz], ta[:, 0:sz], 1.0, None, ALU.add, None, [ta.b], [ta.b])
                            op("dve", lambda E: E.reciprocal(out=ta[:, 0:sz], in_=ta[:, 0:sz]), r=[ta.b], w=[ta.b])
                            TS("dve", ta[:, 0:sz], ta[:, 0:sz], omlb[:, l, li_:li_ + 1], lbv[:, l, li_:li_ + 1], ALU.mult, ALU.add, [ta.b, omlb.b, lbv.b], [ta.b])
                            ACT(sf[:, t0:t0 + sz], ta[:, 0:sz], AF.Ln, [ta.b], [sf.b])
                            TS("dve", s[:, t0:t0 + sz], ta[:, 0:sz], -1.0, 1.0, ALU.mult, ALU.add, [ta.b], [s.b])
                        proj(wt, c * 128, 128, cons)
                        dma("sp", lf[d, c], sf[:], r=[sf.b], w=[d_z])
                        dma("sp", kk[d, c], s[:], r=[s.b], w=[d_z])
                simple_group(C_HG, 4, AF.Sigmoid, hgg)
                simple_group(C_GATE, 24, AF.Sigmoid, gat)
                P.barrier()
            if stop_after == "P2":
                break
            with ExitStack() as ph:
                uT = SB(ph, "uT", [128, 4, NT], BF16)
                yacc = SB(ph, "yacc", [128, NT], F32)
                for c in range(4):
                    dma("sp", uT[:, c, :], zs5[c], r=[d_z], w=[uT.b])
                y2T = uT
                sm_ = {n: SB(ph, "s5" + n, [128, 2, 16], F32) for n in
                       ("lr", "li", "dt", "th", "rho", "sn", "cs", "thr", "ar", "ai", "den", "fr", "fi", "t1", "t2", "tf", "dl", "rho8", "th8")}
                smi = SB(ph, "s5i", [128, 2, 16], I32)
                dma("sp", sm_["lr"][:], s5_lr[:, l, :, :], w=[sm_["lr"].b])
                dma("sp", sm_["li"][:], s5_li[:, l, :, :], w=[sm_["li"].b])
                dma("sp", sm_["dt"][:], s5_ldt[:, l, :, :], w=[sm_["dt"].b])

                def reduce_angle(src, dst, tf, ti_):
                    TS("dve", tf[:], src[:], 1.0 / (2 * PI), None, ALU.mult, None, [src.b], [tf.b])
                    CP("dve", ti_[:], tf[:], [tf.b], [ti_.b])
                    CP("dve", tf[:], ti_[:], [ti_.b], [tf.b])
                    STT(dst[:], tf[:], -2 * PI, src[:], ALU.mult, ALU.add, [tf.b, src.b], [dst.b])
                    TS("dve", dst[:], dst[:], -PI, PI, ALU.max, ALU.min, [dst.b], [dst.b])

                S = sm_
                ACT(S["dt"][:], S["dt"][:], AF.Exp, [S["dt"].b], [S["dt"].b])
                TT("dve", S["th"][:], S["dt"][:], S["li"][:], ALU.mult, [S["dt"].b, S["li"].b], [S["th"].b])
                TT("dve", S["dl"][:], S["dt"][:], S["lr"][:], ALU.mult, [S["dt"].b, S["lr"].b], [S["dl"].b])
                ACT(S["rho"][:], S["dl"][:], AF.Exp, [S["dl"].b], [S["rho"].b])
                reduce_angle(S["th"], S["thr"], S["t1"], smi)
                ACT(S["sn"][:], S["thr"][:], AF.Sin, [S["thr"].b], [S["sn"].b])
                TS("dve", S["t2"][:], S["thr"][:], PI / 2, None, ALU.add, None, [S["thr"].b], [S["t2"].b])
                reduce_angle(S["t2"], S["cs"], S["t1"], smi)
                ACT(S["cs"][:], S["cs"][:], AF.Sin, [S["cs"].b], [S["cs"].b])
                TT("dve", S["ar"][:], S["rho"][:], S["cs"][:], ALU.mult, [S["rho"].b, S["cs"].b], [S["ar"].b])
                TT("dve", S["ai"][:], S["rho"][:], S["sn"][:], ALU.mult, [S["rho"].b, S["sn"].b], [S["ai"].b])
                TT("dve", S["den"][:], S["lr"][:], S["lr"][:], ALU.mult, [S["lr"].b], [S["den"].b])
                TT("dve", S["t1"][:], S["li"][:], S["li"][:], ALU.mult, [S["li"].b], [S["t1"].b])
                TT("dve", S["den"][:], S["den"][:], S["t1"][:], ALU.add, [S["den"].b, S["t1"].b], [S["den"].b])
                op("dve", lambda E: E.reciprocal(out=S["den"][:], in_=S["den"][:]), r=[S["den"].b], w=[S["den"].b])
                TS("dve", S["ar"][:], S["ar"][:], -1.0, None, ALU.add, None, [S["ar"].b], [S["ar"].b])
                TT("dve", S["fr"][:], S["ar"][:], S["lr"][:], ALU.mult, [S["ar"].b, S["lr"].b], [S["fr"].b])
                TT("dve", S["t1"][:], S["ai"][:], S["li"][:], ALU.mult, [S["ai"].b, S["li"].b], [S["t1"].b])
                TT("dve", S["fr"][:], S["fr"][:], S["t1"][:], ALU.add, [S["fr"].b, S["t1"].b], [S["fr"].b])
                TT("dve", S["fr"][:], S["fr"][:], S["den"][:], ALU.mult, [S["fr"].b, S["den"].b], [S["fr"].b])
                TT("dve", S["fi"][:], S["ai"][:], S["lr"][:], ALU.mult, [S["ai"].b, S["lr"].b], [S["fi"].b])
                TT("dve", S["t1"][:], S["ar"][:], S["li"][:], ALU.mult, [S["ar"].b, S["li"].b], [S["t1"].b])
                TT("dve", S["fi"][:], S["fi"][:], S["t1"][:], ALU.subtract, [S["fi"].b, S["t1"].b], [S["fi"].b])
                TT("dve", S["fi"][:], S["fi"][:], S["den"][:], ALU.mult, [S["fi"].b, S["den"].b], [S["fi"].b])

                pwr = SB(ph, "pwr", [128, 9, 2, 16], F32)
                pwi = SB(ph, "pwi", [128, 9, 2, 16], F32)
                npwr = SB(ph, "npwr", [128, 9, 2, 16], F32)
                for tau in range(9):
                    TS("dve", S["t1"][:], S["thr"][:], float(tau), None, ALU.mult, None, [S["thr"].b], [S["t1"].b])
                    reduce_angle(S["t1"], S["t2"], S["tf"], smi)
                    ACT(S["sn"][:], S["t2"][:], AF.Sin, [S["t2"].b], [S["sn"].b])
                    TS("dve", S["t1"][:], S["t2"][:], PI / 2, None, ALU.add, None, [S["t2"].b], [S["t1"].b])
                    reduce_angle(S["t1"], S["cs"], S["tf"], smi)
                    ACT(S["cs"][:], S["cs"][:], AF.Sin, [S["cs"].b], [S["cs"].b])
                    TS("dve", S["t1"][:], S["dl"][:], float(tau), None, ALU.mult, None, [S["dl"].b], [S["t1"].b])
                    ACT(S["t1"][:], S["t1"][:], AF.Exp, [S["t1"].b], [S["t1"].b])
                    TT("dve", pwr[:, tau], S["t1"][:], S["cs"][:], ALU.mult, [S["t1"].b, S["cs"].b], [pwr.b])
                    TT("dve", pwi[:, tau], S["t1"][:], S["sn"][:], ALU.mult, [S["t1"].b, S["sn"].b], [pwi.b])
                TS("dve", npwr[:], pwr[:], -1.0, None, ALU.mult, None, [pwr.b], [npwr.b])
                TS("dve", S["t1"][:], S["dl"][:], 8.0, None, ALU.mult, None, [S["dl"].b], [S["t1"].b])
                ACT(S["rho8"][:], S["t1"][:], AF.Exp, [S["t1"].b], [S["rho8"].b])
                TS("dve", S["t1"][:], S["thr"][:], 8.0, None, ALU.mult, None, [S["thr"].b], [S["t1"].b])
                reduce_angle(S["t1"], S["th8"], S["tf"], smi)

                bp = [SB(ph, "bp%d" % i, [128, 2, 128], F32) for i in range(2)]
                bb = [[SB(ph, "bb%d_%d" % (d, pp), [128, 2, 128], BF16) for pp in range(4)] for d in range(2)]
                dgs = [SB(ph, "dg%d" % i, [128, 3, 4, 128], BF16) for i in range(2)]
                Xb4 = [SB(ph, "Xb4_%d" % i, [128, 4, 2, 128], BF16) for i in range(2)]
                identb4 = SB(ph, "identb4", [128, 4, 128], BF16)
                for i_ in range(4):
                    CP("dve", identb4[:, i_, :], ident_b[:], [ident_b.b], [identb4.b])
                npwi = SB(ph, "npwi", [128, 9, 2, 16], F32)
                TS("dve", npwi[:], pwi[:], -1.0, None, ALU.mult, None, [pwi.b], [npwi.b])
                cf = [SB(ph, "cf%d" % pp, [128, 2, 128], F32) for pp in range(4)]
                lhsC = [SB(ph, "lhsC%d" % pp, [128, 2, 128], BF16) for pp in range(4)]
                xs = [SB(ph, "xs%d" % i, [128, 3, 128], F32) for i in range(2)]
                lhsP = SB(ph, "lhsP", [128, 8, 4, 2, 128], BF16)
                BD = SB(ph, "BD", [128, 8, 128], BF16)
                lhsQ = SB(ph, "lhsQ", [128, 8, 4, 2, 128], BF16)
                diagD = SB(ph, "diagD", [128, 128], F32)
                sdT = SB(ph, "sdT", [128, 4], F32)
                dma("sp", sdT[:], s5_dT[:, l, :], w=[sdT.b])
                iot = SB(ph, "iot64", [128, 64], F32)
                dma("sp", iot[:], c_iota[:, 0, 0:64], w=[iot.b])
                a64 = [SB(ph, "a64_%d" % i, [128, 64], F32) for i in range(3)]
                a64i = SB(ph, "a64i", [128, 64], I32)
                tabC = SB(ph, "tabC", [128, 4, 64], F32)
                tabS = SB(ph, "tabS", [128, 4, 64], F32)
                tabN = SB(ph, "tabN", [128, 4, 64], F32)
                Vt = [SB(ph, "Vt%d" % i, [128, 4, 2, 64], F32) for i in range(2)]
                Wk = [{n: SB(ph, "wk%s%d" % (n, i), [128, 4, 64], F32) for n in ("m1", "m2", "m3", "m4", "gr", "gi")} for i in range(2)]
                Hre = SB(ph, "Hre", [128, 4, 65], F32)
                Him = SB(ph, "Him", [128, 4, 65], F32)
                Hb = [SB(ph, "Hb%d" % i, [128, 2, 4, 64], BF16) for i in range(2)]
                nlong[0] = 4
                vt = SB(ph, "vt", [128, NB, 128], BF16)
                dma("sp", vt[:], vtm[:, 0:128].rearrange("(b p) c -> p b c", p=128), r=[d_z], w=[vt.b])
                mP = SB(ph, "mP", [128, 512], BF16)
                mN = SB(ph, "mN", [128, 512], BF16)
                dma("pool", mP[:], c_maskP[:, :], w=[mP.b])
                dma("pool", mN[:], c_maskN[:, :], w=[mN.b])
                kT2 = SB(ph, "kT2", [64, 2, NT], BF16)
                dma("sp", kT2[:], kT[:, :, :].rearrange("h d t -> d h t"), r=[d_z], w=[kT2.b])
                srow = SB(ph, "srow", [1, 1024], F32)
                dma("sp", srow[:], sinkrow[l, :, :], w=[srow.b])
                sinkts = [SB(ph, "sinkt%d" % i, [128, 512], BF16) for i in range(2)]
                for kvh in range(2):
                    MSET("pool", sinkts[kvh][:], 0.0, [sinkts[kvh].b])
                    ACT(sinkts[kvh][0:1, :], srow[:, kvh * 512:(kvh + 1) * 512], AF.Exp, [srow.b], [sinkts[kvh].b])
                qblk = [SB(ph, "qblk%d" % i, [64, 4, 128], BF16) for i in range(3)]
                oblk = [SB(ph, "oblk%d" % i, [64, 4, 128], BF16) for i in range(3)]
                pts = [SB(ph, "pt%d" % i, [128, 512], BF16) for i in range(3)]
                rds = [SB(ph, "rd%d" % i, [64, 512], F32) for i in range(3)]
                att_steps = [(kvh, qb) for kvh in range(2) for qb in range(NB)]
                att_st = {"i": 0, "ipt": 0, "pend": []}

                def att_issue():
                    idx = att_st["i"]
                    if idx >= len(att_steps):
                        return
                    att_st["i"] += 1
                    kvh, qb = att_steps[idx]
                    qv = qblk[idx % 3]
                    dma("sp", qv[:], qT[4 * kvh:4 * kvh + 4, :, qb * 128:(qb + 1) * 128].rearrange("h d t -> d h t"), r=[d_z], w=[qv.b])
                    if qb < LC // 128:
                        keys = [(kt_, None) for kt_ in range(LC // 128)]
                    else:
                        n = qb - LC // 128
                        keys = [(kt_, None) for kt_ in range(LC // 128)]
                        if n - 1 >= 0:
                            keys.append((qb - 1, mP))
                        keys.append((qb, None))
                        if n + 1 < L // 128:
                            keys.append((qb + 1, mN))
                    pso, psd = nextps(long=True), nextps(long=True)
                    for i, (kt_, msk) in enumerate(keys):
                        pss = nextps()
                        MM(pss[:, :], kT2[:, kvh, kt_ * 128:(kt_ + 1) * 128], qv[:], True, msk is None, [kT2.b, qv.b], [pss.b])
                        if msk is not None:
                            MM(pss[:, :], ident_b[:], msk[:], False, True, [ident_b.b, msk.b], [pss.b])
                        pt = pts[att_st["ipt"] % 3]
                        att_st["ipt"] += 1
                        ACT(pt[:], pss[:, :], AF.Exp, [pss.b], [pt.b], scale=0.125)
                        MM(pso[0:64, :], vt[:, kt_, kvh * 64:(kvh + 1) * 64], pt[:], i == 0, i == len(keys) - 1, [vt.b, pt.b], [pso.b])
                        MM(psd[0:64, :], ones_b[:, 0:64], pt[:], i == 0, False, [ones_b.b, pt.b], [psd.b])
                    MM(psd[0:64, :], ones_b[:, 0:64], sinkts[kvh][:], False, True, [ones_b.b, sinkts[kvh].b], [psd.b])
                    rd = rds[idx % 3]
                    ACT(rd[:], psd[0:64, :], AF.Ln, [psd.b], [rd.b])
                    ACT(rd[:], rd[:], AF.Exp, [rd.b], [rd.b], scale=-1.0)
                    att_st["pend"].append((idx, kvh, qb, pso, rd))

                def att_finalize():
                    if not att_st["pend"]:
                        return
                    idx, kvh, qb, pso, rd = att_st["pend"].pop(0)
                    o_ = oblk[idx % 3]
                    TT("dve", o_[:], pso[0:64, :].rearrange("p (h q) -> p h q", h=4), rd[:].rearrange("p (h q) -> p h q", h=4),
                       ALU.mult, [pso.b, rd.b], [o_.b])
                    dma("sp", yT[4 + 2 * kvh:6 + 2 * kvh, :, qb * 128:(qb + 1) * 128].rearrange("c (two d) t -> d (c two) t", two=2), o_[:],
                        r=[o_.b], w=[d_y])

                it = 0
                for fc in range(4):
                    TS("dve", diagD[:], ident_f[:], sdT[:, fc:fc + 1], None, ALU.mult, None, [ident_f.b, sdT.b], [diagD.b])
                    for pp in range(4):
                        pr = fc * 4 + pp
                        b_ = bp[pp % 2]
                        dma("sp", b_[:, 0, :], s5_Bre[l, pr], w=[b_.b])
                        dma("sp", b_[:, 1, :], s5_Bim[l, pr], w=[b_.b])
                        dma("sp", cf[pp][:, 0, :], s5_Cre[l, pr], w=[cf[pp].b])
                        dma("sp", cf[pp][:, 1, :], s5_Cim[l, pr], w=[cf[pp].b])
                        CP("act", lhsC[pp][:, 0, :], cf[pp][:, 0, :], [cf[pp].b], [lhsC[pp].b])
                        TS("dve", lhsC[pp][:, 1, :], cf[pp][:, 1, :], -1.0, None, ALU.mult, None, [cf[pp].b], [lhsC[pp].b])
                        for d in range(2):
                            fr_ = S["fr"][:, d, pr:pr + 1]
                            fi_ = S["fi"][:, d, pr:pr + 1]
                            x_ = xs[d]
                            o_ = bb[d][pp]
                            TS("dve", x_[:, 0, :], b_[:, 1, :], fi_, None, ALU.mult, None, [b_.b, S["fi"].b], [x_.b])
                            STT(o_[:, 0, :], b_[:, 0, :], fr_, x_[:, 0, :], ALU.mult, ALU.subtract, [b_.b, S["fr"].b, x_.b], [o_.b])
                            TS("dve", x_[:, 1, :], b_[:, 0, :], fi_, None, ALU.mult, None, [b_.b, S["fi"].b], [x_.b])
                            STT(o_[:, 1, :], b_[:, 1, :], fr_, x_[:, 1, :], ALU.mult, ALU.add, [b_.b, S["fr"].b, x_.b], [o_.b])
                    for d in range(2):
                        sl4 = slice(fc * 4, fc * 4 + 4)
                        for tau in range(9):
                            dg = dgs[tau % 2]
                            for vi, pw_ in enumerate((pwr, pwi, npwi)):
                                TT("dve", dg[:, vi], identb4[:], pw_[:, tau, d, sl4].unsqueeze(2).to_broadcast([128, 4, 128]), ALU.mult,
                                   [identb4.b, pw_.b], [dg.b])

                            def four(dst_banks, combos):
                                for pp in range(4):
                                    pq = dst_banks[pp // 2]
                                    base = (pp % 2) * 256
                                    for ri, (l0, r0, l1, r1, bufs) in enumerate(combos(pp)):
                                        o_ap = pq[:, base + ri * 128:base + (ri + 1) * 128]
                                        MM(o_ap, l0, r0, True, False, bufs, [pq.b])
                                        MM(o_ap, l1, r1, False, True, bufs, [pq.b])

                            def vw(pq):
                                return pq[:].rearrange("p (a b c) -> p a b c", a=2, b=2)
                            if tau < 8:
                                pa, pb = nextps(), nextps()
                                four((pa, pb), lambda pp: (
                                    (bb[d][pp][:, 0, :], dg[:, 0, pp, :], bb[d][pp][:, 1, :], dg[:, 2, pp, :], [bb[d][pp].b, dg.b]),
                                    (bb[d][pp][:, 1, :], dg[:, 0, pp, :], bb[d][pp][:, 0, :], dg[:, 1, pp, :], [bb[d][pp].b, dg.b])))
                                CP("act", lhsP[:, tau, 0:2, :, :], vw(pa), [pa.b], [lhsP.b])
                                CP("act", lhsP[:, tau, 2:4, :, :], vw(pb), [pb.b], [lhsP.b])
                                pc, pd = nextps(), nextps()
                                four((pc, pd), lambda pp: (
                                    (dg[:, 0, pp, :], bb[d][pp][:, 0, :], dg[:, 2, pp, :], bb[d][pp][:, 1, :], [bb[d][pp].b, dg.b]),
                                    (dg[:, 0, pp, :], bb[d][pp][:, 1, :], dg[:, 1, pp, :], bb[d][pp][:, 0, :], [bb[d][pp].b, dg.b])))
                                xb = Xb4[tau % 2]
                                CP("act", xb[:, 0:2, :, :], vw(pc), [pc.b], [xb.b])
                                CP("act", xb[:, 2:4, :, :], vw(pd), [pd.b], [xb.b])
                                psd_ = nextps()
                                for pp in range(4):
                                    for ri in range(2):
                                        MM(psd_[:, 0:128], xb[:, pp, ri, :], lhsC[pp][:, ri, :], pp == 0 and ri == 0, pp == 3 and ri == 1,
                                           [xb.b, lhsC[pp].b], [psd_.b])
                                if tau == 0 and d == 0:
                                    TT("dve", BD[:, tau, :], psd_[:, 0:128], diagD[:], ALU.add, [psd_.b, diagD.b], [BD.b])
                                else:
                                    CP("act", BD[:, tau, :], psd_[:, 0:128], [psd_.b], [BD.b])
                            if tau >= 1:
                                t = tau - 1
                                pe_, pf = nextps(), nextps()
                                four((pe_, pf), lambda pp: (
                                    (dg[:, 0, pp, :], lhsC[pp][:, 0, :], dg[:, 1, pp, :], lhsC[pp][:, 1, :], [lhsC[pp].b, dg.b]),
                                    (dg[:, 2, pp, :], lhsC[pp][:, 0, :], dg[:, 0, pp, :], lhsC[pp][:, 1, :], [lhsC[pp].b, dg.b])))
                                CP("act", lhsQ[:, t, 0:2, :, :], vw(pe_), [pe_.b], [lhsQ.b])
                                CP("act", lhsQ[:, t, 2:4, :, :], vw(pf), [pf.b], [lhsQ.b])
                        for pp in range(4):
                            pr = fc * 4 + pp
                            a0, a1, a2 = a64
                            TS("dve", a0[:], iot[:], S["th8"][:, d, pr:pr + 1], None, ALU.mult, None, [iot.b, S["th8"].b], [a0.b])
                            reduce_angle(a0, a1, a2, a64i)
                            ACT(tabS[:, pp, :], a1[:], AF.Sin, [a1.b], [tabS.b])
                            TS("dve", a0[:], a1[:], PI / 2, None, ALU.add, None, [a1.b], [a0.b])
                            reduce_angle(a0, a1, a2, a64i)
                            ACT(tabC[:, pp, :], a1[:], AF.Sin, [a1.b], [tabC.b])
                        TS("dve", tabN[:], tabS[:], -1.0, None, ALU.mult, None, [tabS.b], [tabN.b])
                        tbs = [tabC.b, tabS.b, tabN.b]
                        order = list(range(NTI)) if d == 0 else [0] + list(range(NTI - 1, 0, -1))
                        MSET("dve", Hre[:, :, 0:1], 0.0, [Hre.b])
                        MSET("dve", Him[:, :, 0:1], 0.0, [Him.b])
                        for oi, ti in enumerate(order):
                            t0, sz = TILES[ti]
                            NJ = sz // 8
                            useq = uT[:, fc, t0:t0 + sz] if d == 0 else uT[:, fc, t0:t0 + sz][:, ::-1]
                            us = [useq[:, s_::8] for s_ in range(8)]
                            V = Vt[it % 2]
                            W = Wk[it % 2]
                            hb = Hb[it % 2]
                            it += 1
                            att_finalize()
                            pv = nextps()
                            pvv = pv[:].rearrange("p (a b c) -> p a b c", a=4, b=2)
                            for pp in range(4):
                                for ri in range(2):
                                    for s_ in range(8):
                                        MM(pvv[:, pp, ri, 0:NJ], lhsP[:, 7 - s_, pp, ri, :], us[s_], s_ == 0, s_ == 7, [lhsP.b, uT.b], [pv.b])
                            CP("act", V[:, :, :, 0:NJ], pvv[:, :, :, 0:NJ], [pv.b], [V.b])
                            tC, tS, tN = tabC[:, :, 0:NJ], tabS[:, :, 0:NJ], tabN[:, :, 0:NJ]
                            vre, vim = V[:, :, 0, 0:NJ], V[:, :, 1, 0:NJ]
                            TT("dve", W["m1"][:, :, 0:NJ], vre, tC, ALU.mult, [V.b] + tbs, [W["m1"].b])
                            TT("dve", W["m2"][:, :, 0:NJ], vim, tS, ALU.mult, [V.b] + tbs, [W["m2"].b])
                            TT("dve", W["m1"][:, :, 0:NJ], W["m1"][:, :, 0:NJ], W["m2"][:, :, 0:NJ], ALU.add, [W["m1"].b, W["m2"].b], [W["m1"].b])
                            TT("pool", W["m3"][:, :, 0:NJ], vim, tC, ALU.mult, [V.b] + tbs, [W["m3"].b])
                            TT("pool", W["m4"][:, :, 0:NJ], vre, tN, ALU.mult, [V.b] + tbs, [W["m4"].b])
                            TT("pool", W["m3"][:, :, 0:NJ], W["m3"][:, :, 0:NJ], W["m4"][:, :, 0:NJ], ALU.add, [W["m3"].b, W["m4"].b], [W["m3"].b])
                            for pp in range(4):
                                pr = fc * 4 + pp
                                rho_b = S["rho8"][:, d, pr:pr + 1].to_broadcast([128, NJ])
                                SCAN(W["gr"][:, pp, 0:NJ], rho_b, W["m1"][:, pp, 0:NJ], Hre[:, pp, 0:1], [W["m1"].b, S["rho8"].b, Hre.b], [W["gr"].b])
                                SCAN(W["gi"][:, pp, 0:NJ], rho_b, W["m3"][:, pp, 0:NJ], Him[:, pp, 0:1], [W["m3"].b, S["rho8"].b, Him.b], [W["gi"].b])
                            TT("dve", W["m2"][:, :, 0:NJ], W["gr"][:, :, 0:NJ], tC, ALU.mult, [W["gr"].b] + tbs, [W["m2"].b])
                            TT("dve", W["m4"][:, :, 0:NJ], W["gi"][:, :, 0:NJ], tN, ALU.mult, [W["gi"].b] + tbs, [W["m4"].b])
                            TT("dve", Hre[:, :, 1:NJ + 1], W["m2"][:, :, 0:NJ], W["m4"][:, :, 0:NJ], ALU.add, [W["m2"].b, W["m4"].b], [Hre.b])
                            TT("pool", W["m1"][:, :, 0:NJ], W["gi"][:, :, 0:NJ], tC, ALU.mult, [W["gi"].b] + tbs, [W["m1"].b])
                            TT("pool", W["m3"][:, :, 0:NJ], W["gr"][:, :, 0:NJ], tS, ALU.mult, [W["gr"].b] + tbs, [W["m3"].b])
                            TT("pool", Him[:, :, 1:NJ + 1], W["m1"][:, :, 0:NJ], W["m3"][:, :, 0:NJ], ALU.add, [W["m1"].b, W["m3"].b], [Him.b])
                            CP("pool", hb[:, 0, :, 0:NJ], Hre[:, :, 0:NJ], [Hre.b], [hb.b])
                            CP("pool", hb[:, 1, :, 0:NJ], Him[:, :, 0:NJ], [Him.b], [hb.b])
                            att_issue()
                            py = nextps(long=True)
                            pyv = py[:].rearrange("p (t j) -> p t j", t=8)
                            for t in range(8):
                                nmm = (t + 1) + 8
                                imm = 0
                                for s_ in range(t + 1):
                                    imm += 1
                                    MM(pyv[:, t, 0:NJ], BD[:, t - s_, :], us[s_], imm == 1, imm == nmm, [BD.b, uT.b], [py.b])
                                for pp in range(4):
                                    for ri in range(2):
                                        imm += 1
                                        MM(pyv[:, t, 0:NJ], lhsQ[:, t, pp, ri, :], hb[:, ri, pp, 0:NJ], imm == 1, imm == nmm, [lhsQ.b, hb.b], [py.b])
                            yv = yacc[:, t0:t0 + sz] if d == 0 else yacc[:, t0:t0 + sz][:, ::-1]
                            yv = yv.rearrange("p (j t) -> p t j", t=8)
                            if d == 0:
                                CP("act", yv, pyv[:, :, 0:NJ], [py.b], [yacc.b])
                            else:
                                TT("dve", yv, pyv[:, :, 0:NJ], yv, ALU.add, [py.b, yacc.b], [yacc.b])
                            CP("dve", Hre[:, :, 0:1], Hre[:, :, NJ:NJ + 1], [Hre.b], [Hre.b])
                            CP("dve", Him[:, :, 0:1], Him[:, :, NJ:NJ + 1], [Him.b], [Him.b])
                    ACT(y2T[:, fc, :], yacc[:], AF.Gelu_apprx_tanh, [yacc.b], [y2T.b])
                while att_st["i"] < len(att_steps) or att_st["pend"]:
                    att_finalize()
                    att_issue()
                wg = SB(ph, "wg", [128, 4, 512], BF16)
                bg = SB(ph, "bg", [128, 4], F32)
                gs = [SB(ph, "gs%d" % i, [128, 512], F32) for i in range(2)]
                yst = [SB(ph, "yst%d" % i, [128, 512], BF16) for i in range(2)]
                dma("pool", wg[:], s5_wglu[l].rearrange("(k p) n -> p k n", p=128), w=[wg.b])
                dma("sp", bg[:], s5_bgT[:, l, :], w=[bg.b])
                for co in range(4):
                    for ti, (t0, sz) in enumerate(TILES):
                        ys = yst[ti % 2]
                        ps = nextps()
                        for k in range(4):
                            MM(ps[:, 0:sz], wg[:, k, co * 128:(co + 1) * 128], y2T[:, k, t0:t0 + sz], k == 0, k == 3, [wg.b, y2T.b], [ps.b])
                        g_ = gs[ti % 2]
                        ACT(g_[:, 0:sz], ps[:, 0:sz], AF.Sigmoid, [ps.b, bg.b], [g_.b], bias=bg[:, co:co + 1])
                        TT("dve", ys[:, 0:sz], y2T[:, co, t0:t0 + sz], g_[:, 0:sz], ALU.mult, [y2T.b, g_.b], [ys.b])
                        dma("sp", yT[co, :, t0:t0 + sz], ys[:, 0:sz], r=[ys.b], w=[d_y])
                P.barrier()
                nlong[0] = 2
            if stop_after == "S5":
                break
            if stop_after == "ATT":
                break
            with ExitStack() as ph:
                hgm = SB(ph, "hgm", [32, 2, 128], F32)
                rmask = SB(ph, "rmask", [128, 512], F32)
                hng = SB(ph, "hng", [128, 4], F32)
                dma("sp", hgm[:], c_hgmask[:, :, :], w=[hgm.b])
                dma("sp", rmask[:], c_rmask[:, :], w=[rmask.b])
                dma("sp", hng[:], hg_ngT[:, l, :], w=[hng.b])
                D2 = range(2)
                Sf = [SB(ph, "Sf%d" % d, [128, 4, 128], F32) for d in D2]
                Sb = [SB(ph, "Sb%d" % d, [128, 4, 128], BF16) for d in D2]
                lfts = [SB(ph, "lft%d" % d, [128, 4, 512], F32) for d in D2]
                kkts = [SB(ph, "kkt%d" % d, [128, 4, 512], BF16) for d in D2]
                hqts = [SB(ph, "hqt%d" % d, [128, 4, 512], BF16) for d in D2]
                vchs = [SB(ph, "vch%d" % d, [32, 16, 512], BF16) for d in D2]
                bts = [SB(ph, "hbt%d" % d, [128, 4, 512], F32) for d in D2]
                e1s = [SB(ph, "he1%d" % d, [128, 4, 512], F32) for d in D2]
                e2s = [SB(ph, "he2%d" % d, [128, 4, 512], F32) for d in D2]
                qts = [SB(ph, "hqt_%d" % d, [128, 4, 512], BF16) for d in D2]
                kts = [SB(ph, "hkt_%d" % d, [128, 4, 512], BF16) for d in D2]
                khs = [SB(ph, "hkh_%d" % d, [128, 4, 512], BF16) for d in D2]
                ots = [SB(ph, "hot%d" % d, [128, 4, 512], F32) for d in D2]
                attm = [[SB(ph, "attm%d_%d" % (d, i), [32, 128], BF16) for i in range(2)] for d in D2]
                ktm = [[SB(ph, "ktm%d_%d" % (d, i), [32, 512], BF16) for i in range(2)] for d in D2]
                orders = [list(range(NTI)), [0] + list(range(NTI - 1, 0, -1))]
                for d in D2:
                    MSET("pool", Sf[d][:], 0.0, [Sf[d].b])
                    MSET("pool", Sb[d][:], 0.0, [Sb[d].b])
                ich = [0, 0]

                def hg_setup(d, ti):
                    t0, sz = TILES[ti]
                    nch = sz // 32
                    lft, kkt, hqt, vch, bt, e1, e2, qt, kt, kh = lfts[d], kkts[d], hqts[d], vchs[d], bts[d], e1s[d], e2s[d], qts[d], kts[d], khs[d]
                    dma("sp", lft[:, :, 0:sz], lf[d, :, :, t0:t0 + sz].rearrange("h p t -> p h t"), r=[d_z], w=[lft.b])
                    dma("sp", kkt[:, :, 0:sz], kk[d, :, :, t0:t0 + sz].rearrange("h p t -> p h t"), r=[d_z], w=[kkt.b])
                    dma("sp", hqt[:, :, 0:sz], hgq[:, :, t0:t0 + sz].rearrange("h p t -> p h t"), r=[d_z], w=[hqt.b])
                    dma("sp", vch[:, 0:nch, :], vtm[t0:t0 + sz, 128:640].rearrange("(c p) f -> p c f", p=32), r=[d_z], w=[vch.b])
                    for h in range(4):
                        if d == 0:
                            SCAN(bt[:, h, 0:sz], rmask[:, 0:sz], lft[:, h, 0:sz], 0.0, [rmask.b, lft.b], [bt.b])
                        else:
                            SCAN(bt[:, h, 0:sz][:, ::-1], rmask[:, 0:sz], lft[:, h, 0:sz][:, ::-1], 0.0, [rmask.b, lft.b], [bt.b])
                    jl0 = 31 if d == 0 else 0
                    b4 = bt[:, :, 0:sz].rearrange("p h (c t) -> p h c t", t=32)
                    TT("dve", e2[:, :, 0:sz].rearrange("p h (c t) -> p h c t", t=32), b4[:, :, :, jl0:jl0 + 1].to_broadcast([128, 4, nch, 32]), b4,
                       ALU.subtract, [bt.b], [e2.b])
                    ACT(e1[:, :, 0:sz], bt[:, :, 0:sz], AF.Exp, [bt.b], [e1.b], scale=-1.0)
                    ACT(e2[:, :, 0:sz], e2[:, :, 0:sz], AF.Exp, [e2.b], [e2.b])
                    ACT(bt[:, :, 0:sz], bt[:, :, 0:sz], AF.Exp, [bt.b], [bt.b])
                    TT("dve", qt[:, :, 0:sz], hqt[:, :, 0:sz], bt[:, :, 0:sz], ALU.mult, [hqt.b, bt.b], [qt.b])
                    TT("pool", kt[:, :, 0:sz], kkt[:, :, 0:sz], e1[:, :, 0:sz], ALU.mult, [kkt.b, e1.b], [kt.b])
                    TT("dve", kh[:, :, 0:sz], kkt[:, :, 0:sz], e2[:, :, 0:sz], ALU.mult, [kkt.b, e2.b], [kh.b])

                def hg_chunk(d, ci):
                    vch, bt, qt, kt, kh, ot = vchs[d], bts[d], qts[d], kts[d], khs[d], ots[d]
                    c0 = ci * 32
                    am, km = attm[d][ich[d] % 2], ktm[d][ich[d] % 2]
                    ich[d] += 1
                    psA = nextps()
                    for h in range(4):
                        MM(psA[0:32, h * 32:(h + 1) * 32], kt[:, h, c0:c0 + 32], qt[:, h, c0:c0 + 32], True, True, [kt.b, qt.b], [psA.b])
                    TT("dve", am[:], psA[0:32, 0:128], hgm[:, d, :], ALU.mult, [psA.b, hgm.b], [am.b])
                    psT = nextps()
                    psTb = psT[:].bitcast(BF16)
                    for h in range(4):
                        TR(psTb[0:32, h * 128:(h + 1) * 128], kh[:, h, c0:c0 + 32], ident_b[:], [kh.b, ident_b.b], [psT.b])
                    CP("act", km[:], psTb[0:32, 0:512], [psT.b], [km.b])
                    psO = nextps()
                    for h in range(4):
                        MM(psO[:, h * 32:(h + 1) * 32], vch[:, ci, h * 128:(h + 1) * 128], am[:, h * 32:(h + 1) * 32], True, False, [vch.b, am.b], [psO.b])
                        MM(psO[:, h * 32:(h + 1) * 32], Sb[d][:, h, :], qt[:, h, c0:c0 + 32], False, True, [Sb[d].b, qt.b], [psO.b])
                    CP("act", ot[:, :, c0:c0 + 32], psO[:, 0:128].rearrange("p (h t) -> p h t", h=4), [psO.b], [ot.b])
                    psS = nextps()
                    for h in range(4):
                        MM(psS[:, h * 128:(h + 1) * 128], km[:, h * 128:(h + 1) * 128], vch[:, ci, h * 128:(h + 1) * 128], True, True, [km.b, vch.b], [psS.b])
                    jl = c0 + 31 if d == 0 else c0
                    for h in range(4):
                        STT(Sf[d][:, h, :], Sf[d][:, h, :], bt[:, h, jl:jl + 1], psS[:, h * 128:(h + 1) * 128], ALU.mult, ALU.add, [Sf[d].b, bt.b, psS.b], [Sf[d].b])
                    CP("act", Sb[d][:], Sf[d][:], [Sf[d].b], [Sb[d].b])

                obw = ofw2
                for oi in range(NTI):
                    for d in D2:
                        hg_setup(d, orders[d][oi])
                    nch = TILES[orders[0][oi]][1] // 32
                    for k_ in range(nch):
                        for d in D2:
                            hg_chunk(d, k_ if d == 0 else nch - 1 - k_)
                    for d in D2:
                        t0, sz = TILES[orders[d][oi]]
                        dma("pool", (ofw if d == 0 else obw)[:, :, t0:t0 + sz].rearrange("h p t -> p h t"), ots[d][:, :, 0:sz], r=[ots[d].b], w=[d_of])
                sq = SB(ph, "hsq", [128, 4, 512], BF16)
                rs = SB(ph, "hrs", [128, 4, 512], F32)
                hggts = [SB(ph, "hggt%d" % i, [128, 4, 512], BF16) for i in range(2)]
                ysts = [SB(ph, "hyst%d" % i, [128, 4, 512], BF16) for i in range(2)]
                for ti, (t0, sz) in enumerate(TILES):
                    oa, obt, yh = ots[ti % 2], e1s[ti % 2], e2s[ti % 2]
                    hggt, yst = hggts[ti % 2], ysts[ti % 2]
                    dma("sp", oa[:, :, 0:sz], ofw[:, :, t0:t0 + sz].rearrange("h p t -> p h t"), r=[d_of], w=[oa.b])
                    dma("sp", obt[:, :, 0:sz], obw[:, :, t0:t0 + sz].rearrange("h p t -> p h t"), r=[d_of], w=[obt.b])
                    dma("sp", hggt[:, :, 0:sz], hgg[:, :, t0:t0 + sz].rearrange("h p t -> p h t"), r=[d_z], w=[hggt.b])
                    TT("pool", oa[:, :, 0:sz], oa[:, :, 0:sz], obt[:, :, 0:sz], ALU.add, [oa.b, obt.b], [oa.b])
                    ACT(sq[:, :, 0:sz], oa[:, :, 0:sz], AF.Square, [oa.b], [sq.b])
                    for h in range(4):
                        ps = nextps()
                        MM(ps[:, 0:sz], ones_b[:], sq[:, h, 0:sz], True, True, [ones_b.b, sq.b], [ps.b])
                        ACT(rs[:, h, 0:sz], ps[:, 0:sz], AF.Ln, [ps.b, epsb.b], [rs.b], bias=epsb[:, 0:1], scale=1.0 / 128)
                    ACT(rs[:, :, 0:sz], rs[:, :, 0:sz], AF.Exp, [rs.b], [rs.b], scale=-0.5)
                    for h in range(4):
                        STT(yh[:, h, 0:sz], oa[:, h, 0:sz], hng[:, h:h + 1], rs[:, h, 0:sz], ALU.mult, ALU.mult, [oa.b, hng.b, rs.b], [yh.b])
                    TT("pool", yst[:, :, 0:sz], yh[:, :, 0:sz], hggt[:, :, 0:sz], ALU.mult, [yh.b, hggt.b], [yst.b])
                    dma("sp", yT[8:12, :, t0:t0 + sz].rearrange("h p t -> p h t"), yst[:, :, 0:sz], r=[yst.b], w=[d_y])
                P.barrier()
            if stop_after == "HG":
                break
            with ExitStack() as ph:
                wbr = SB(ph, "wbr", [128, 12, DM], BF16)
                wou = SB(ph, "wou", [128, KC, DM], BF16)
                for n in range(3):
                    dma("pool", wbr[:, n * 4:(n + 1) * 4, :], w_branch[l, n].rearrange("(k p) d -> p k d", p=128), w=[wbr.b])
                dma("pool", wou[:], w_out[l].rearrange("(k p) d -> p k d", p=128), w=[wou.b])
                yts = [SB(ph, "yt%d" % i, [128, 12, 512], BF16) for i in range(2)]
                gts = [SB(ph, "gt%d" % i, [128, 24, 512], BF16) for i in range(2)]
                xt = SB(ph, "mxt", [128, KC, 512], F32)
                mots = [SB(ph, "mot%d" % i, [128, KC, 512], F32) for i in range(2)]
                mt = SB(ph, "mmt", [128, KC, 512], BF16)
                macc = [SB(ph, "macc%d" % i, [128, 512], F32) for i in range(2)]
                mtmp = [SB(ph, "mtmp%d" % i, [128, 512], F32) for i in range(2)]
                sq = SB(ph, "msq", [128, KC, 512], BF16)
                rs = SB(ph, "mrs", [128, 512], F32)
                for ti, (t0, sz) in enumerate(TILES):
                    yt, gt = yts[ti % 2], gts[ti % 2]
                    ot = mots[ti % 2]
                    dma("sp", yt[:, :, 0:sz], yT[:, :, t0:t0 + sz].rearrange("c p t -> p c t"), r=[d_y], w=[yt.b])
                    dma("sp", gt[:, :, 0:sz], gat[:, :, t0:t0 + sz].rearrange("c p t -> p c t"), r=[d_z], w=[gt.b])
                    dma("sp", xt[:, :, 0:sz], xT[:, :, t0:t0 + sz].rearrange("k p t -> p k t"), r=[d_xT], w=[xt.b])
                    for dc in range(KC):
                        ma, mp_ = macc[dc % 2], mtmp[dc % 2]
                        for n in range(3):
                            ps = nextps()
                            for k in range(4):
                                MM(ps[:, 0:sz], wbr[:, n * 4 + k, dc * 128:(dc + 1) * 128], yt[:, n * 4 + k, 0:sz], k == 0, k == 3, [wbr.b, yt.b], [ps.b])
                            g_ = gt[:, n * 8 + dc, 0:sz]
                            if n == 0:
                                TT("dve", ma[:, 0:sz], ps[:, 0:sz], g_, ALU.mult, [ps.b, gt.b], [ma.b])
                            elif n == 1:
                                TT("dve", mp_[:, 0:sz], ps[:, 0:sz], g_, ALU.mult, [ps.b, gt.b], [mp_.b])
                                TT("pool", ma[:, 0:sz], ma[:, 0:sz], mp_[:, 0:sz], ALU.add, [ma.b, mp_.b], [ma.b])
                            else:
                                TT("dve", mp_[:, 0:sz], ps[:, 0:sz], g_, ALU.mult, [ps.b, gt.b], [mp_.b])
                                TT("pool", mt[:, dc, 0:sz], ma[:, 0:sz], mp_[:, 0:sz], ALU.add, [ma.b, mp_.b], [mt.b])
                    for dc in range(KC):
                        ps = nextps()
                        for kc in range(KC):
                            MM(ps[:, 0:sz], wou[:, kc, dc * 128:(dc + 1) * 128], mt[:, kc, 0:sz], kc == 0, kc == KC - 1, [wou.b, mt.b], [ps.b])
                        CP("act", ot[:, dc, 0:sz], ps[:, 0:sz], [ps.b], [ot.b])
                    epilogue(ot, xt, G1, ti, t0, sz, sq, rs, None)
                P.barrier()
            if stop_after == "MIX":
                break
            with ExitStack() as ph:
                hT = SB(ph, "hT2", [128, KC, NT], BF16, nb=NTI)
                with ExitStack() as ph1:
                    norm_phase(ph1, l, A2, 24, hT)
                    P.barrier()
                NU = NT + 3
                Uas = [SB(ph, "Ua%d" % i, [128, NU], F32) for i in range(2)]
                Ugs = [SB(ph, "Ug%d" % i, [128, NU], F32) for i in range(2)]
                Ya = SB(ph, "Ya", [128, NU], F32)
                Yg = SB(ph, "Yg", [128, NU], F32)
                ast = [SB(ph, "ast%d" % i, [128, NU], BF16) for i in range(2)]
                cw = SB(ph, "cw", [128, 44, 3], F32)
                cb = SB(ph, "cb", [128, 44], F32)
                wua = [SB(ph, "wua%d" % i, [128, KC, 128], BF16) for i in range(2)]
                wug = [SB(ph, "wug%d" % i, [128, KC, 128], BF16) for i in range(2)]
                dma("sp", cw[:], convT[:, l, :, :], w=[cw.b])
                dma("sp", cb[:], convbT[:, l, :], w=[cb.b])
                for i_ in range(2):
                    MSET("pool", Uas[i_][:], 0.0, [Uas[i_].b])
                    MSET("pool", Ugs[i_][:], 0.0, [Ugs[i_].b])

                def ucol(t):
                    return t + 1 if t < LC else t + 2
                NY = NT + 1
                for j in range(NFC):
                    wa, wg_ = wua[j % 2], wug[j % 2]
                    Ua, Ug = Uas[j % 2], Ugs[j % 2]
                    dma("pool", wa[:], w_up[l, :, j * 128:(j + 1) * 128].rearrange("(k p) n -> p k n", p=128), w=[wa.b])
                    dma("pool", wg_[:], w_up[l, :, FFN + j * 128:FFN + (j + 1) * 128].rearrange("(k p) n -> p k n", p=128), w=[wg_.b])
                    for (wt, U) in ((wa, Ua), (wg_, Ug)):
                        for ti, (t0, sz) in enumerate(TILES):
                            ps = nextps()
                            for kc in range(KC):
                                MM(ps[:, 0:sz], wt[:, kc, :], hT[:, kc, t0:t0 + sz], kc == 0, kc == KC - 1, [wt.b, hT.bs[ti]], [ps.b])
                            CP("act", U[:, ucol(t0):ucol(t0) + sz], ps[:, 0:sz], [ps.b], [U.b])
                    for (U, Y, cj) in ((Ua, Ya, j), (Ug, Yg, NFC + j)):
                        ACT(Y[:, 0:NY], U[:, 0:NY], AF.Identity, [U.b, cw.b, cb.b], [Y.b], bias=cb[:, cj:cj + 1], scale=cw[:, cj, 0:1])
                        STT(Y[:, 0:NY], U[:, 1:NY + 1], cw[:, cj, 1:2], Y[:, 0:NY], ALU.mult, ALU.add, [U.b, cw.b, Y.b], [Y.b])
                        STT(Y[:, 0:NY], U[:, 2:NY + 2], cw[:, cj, 2:3], Y[:, 0:NY], ALU.mult, ALU.add, [U.b, cw.b, Y.b], [Y.b])
                    a_ = ast[j % 2]
                    ACT(Ya[:, 0:NY], Ya[:, 0:NY], AF.Silu, [Ya.b], [Ya.b])
                    TT("dve", a_[:, 0:NY], Ya[:, 0:NY], Yg[:, 0:NY], ALU.mult, [Ya.b, Yg.b], [a_.b])
                    dma("sp", actT[j, :, 0:LC], a_[:, 0:LC], r=[a_.b], w=[d_act])
                    dma("sp", actT[j, :, LC:NT], a_[:, LC + 1:NT + 1], r=[a_.b], w=[d_act])
                P.barrier()
            with ExitStack() as ph:
                wdn = SB(ph, "wdn", [128, NFC, DM], BF16)
                dma("pool", wdn[:, 0:11, :], w_down[l, 0:11 * 128, :].rearrange("(k p) d -> p k d", p=128), w=[wdn.b])
                dma("pool", wdn[:, 11:22, :], w_down[l, 11 * 128:22 * 128, :].rearrange("(k p) d -> p k d", p=128), w=[wdn.b])
                ats = [SB(ph, "at%d" % i, [128, NFC, 512], BF16) for i in range(2)]
                xt = SB(ph, "fxt", [128, KC, 512], F32)
                fots = [SB(ph, "fot%d" % i, [128, KC, 512], F32) for i in range(2)]
                sq = SB(ph, "fsq", [128, KC, 512], BF16)
                rs = SB(ph, "frs", [128, 512], F32)
                for ti, (t0, sz) in enumerate(TILES):
                    at = ats[ti % 2]
                    ot = fots[ti % 2]
                    dma("sp", at[:, :, 0:sz], actT[:, :, t0:t0 + sz].rearrange("c p t -> p c t"), r=[d_act], w=[at.b])
                    dma("sp", xt[:, :, 0:sz], xT[:, :, t0:t0 + sz].rearrange("k p t -> p k t"), r=[d_xT], w=[xt.b])
                    for dc in range(KC):
                        ps = nextps()
                        for k in range(NFC):
                            MM(ps[:, 0:sz], wdn[:, k, dc * 128:(dc + 1) * 128], at[:, k, 0:sz], k == 0, k == NFC - 1, [wdn.b, at.b], [ps.b])
                        CP("act", ot[:, dc, 0:sz], ps[:, 0:sz], [ps.b], [ot.b])
                    epilogue(ot, xt, G2, ti, t0, sz, sq, rs, None)
                P.barrier()
        if stop_after is None:
            with ExitStack() as ph:
                xtt = [SB(ph, "fxtt%d" % i, [128, KC, 128], F32) for i in range(2)]
                orow = [SB(ph, "orow%d" % i, [128, DM], F32) for i in range(2)]
                for tb in range(LC // 128, NB):
                    a, o_ = xtt[tb % 2], orow[tb % 2]
                    dma("sp", a[:], xT[:, :, tb * 128:(tb + 1) * 128].rearrange("k p t -> p k t"), r=[d_xT], w=[a.b])
                    pa, pb = nextps(), nextps()
                    for kc in range(KC):
                        pp_ = pa if kc < 4 else pb
                        TR(pp_[:, (kc % 4) * 128:(kc % 4 + 1) * 128], a[:, kc, :], ident_f[:], [a.b, ident_f.b], [pp_.b])
                    CP("act", o_[:, 0:512], pa[:, :], [pa.b], [o_.b])
                    CP("dve", o_[:, 512:1024], pb[:, :], [pb.b], [o_.b])
                    dma("sp", out[tb * 128 - LC:(tb + 1) * 128 - LC, :], o_[:], r=[o_.b])
        P.barrier()
    return nc


def prep_shared(inp, cfg):
    L, LC, DEPTH = cfg["L"], cfg["LC"], cfg["DEPTH"]
    NT = L + LC
    f = lambda a: np.ascontiguousarray(np.asarray(a, dtype=np.float32))
    sh = {}
    sh["w_mod"] = f(inp["w_mod"][:DEPTH])
    sh["b_modT"] = f(np.asarray(inp["b_mod"])[:DEPTH].reshape(DEPTH, 48, 128).transpose(2, 0, 1))
    sh["norm_gT"] = f(np.asarray(inp["norm_g"])[:DEPTH].reshape(DEPTH, 4, KC, 128).transpose(3, 0, 1, 2))
    w_in = np.asarray(inp["w_in"])[:DEPTH]
    sh["w_in"] = f(w_in)
    idx = []
    for h in range(8):
        idx += [C_Q + h * 64 + (d + 32) % 64 for d in range(64)]
    for h in range(2):
        idx += [C_K + h * 64 + (d + 32) % 64 for d in range(64)]
    sh["w_rot"] = f(w_in[:, :, np.array(idx)])
    rows = L // 64
    row = np.repeat(np.arange(rows, dtype=np.float32), 64)
    col = np.tile(np.arange(64, dtype=np.float32), rows)
    inv = (10000.0 ** (-np.arange(16, dtype=np.float32) / 16)).astype(np.float32)
    ang = np.concatenate([row[:, None] * inv, col[:, None] * inv], axis=-1)
    cos, sin = np.cos(ang).T, np.sin(ang).T
    C = np.ones((64, NT), np.float32)
    S = np.zeros((64, NT), np.float32)
    C[0:32, LC:] = cos
    C[32:64, LC:] = cos
    S[0:32, LC:] = -sin
    S[32:64, LC:] = sin
    sh["ropeC"], sh["ropeS"] = np.concatenate([C, C], 0), np.concatenate([S, S], 0)
    def st(a):
        a = np.asarray(a)[:DEPTH].reshape(DEPTH, 2, 16, 2, 64)
        return f(a.transpose(3, 4, 0, 1, 2).reshape(128, DEPTH, 2, 16))
    sh["s5_lr"] = st(inp["s5_lam_re"])
    sh["s5_li"] = st(inp["s5_lam_im"])
    ldt = np.asarray(inp["s5_log_dt"])[:DEPTH]
    sh["s5_ldt"] = st(np.repeat(ldt[..., None], 64, axis=-1))
    def padB(b):
        b = np.asarray(b)[:DEPTH]
        o = np.zeros((DEPTH, 16, 128, 128), np.float32)
        for pr in range(16):
            for g2 in range(2):
                s0 = (pr % 4) * 32 + g2 * 16
                o[:, pr, g2 * 64:(g2 + 1) * 64, s0:s0 + 16] = b[:, 2 * pr + g2]
        return o
    def padC(c):
        c = np.asarray(c)[:DEPTH]
        o = np.zeros((DEPTH, 16, 128, 128), np.float32)
        for pr in range(16):
            for g2 in range(2):
                s0 = (pr % 4) * 32 + g2 * 16
                o[:, pr, g2 * 64:(g2 + 1) * 64, s0:s0 + 16] = c[:, 2 * pr + g2].transpose(0, 2, 1)
        return o
    sh["s5_Bre"], sh["s5_Bim"] = padB(inp["s5_b_re"]), padB(inp["s5_b_im"])
    sh["s5_Cre"], sh["s5_Cim"] = padC(inp["s5_c_re"]), padC(inp["s5_c_im"])
    sh["s5_dT"] = f(np.asarray(inp["s5_d"])[:DEPTH].reshape(DEPTH, 4, 128).transpose(2, 0, 1))
    sh["s5_bgT"] = f(np.asarray(inp["s5_b_glu"])[:DEPTH].reshape(DEPTH, 4, 128).transpose(2, 0, 1))
    sh["s5_wglu"] = f(inp["s5_w_glu"][:DEPTH])
    sh["sinkrow"] = f(np.repeat(np.asarray(inp["att_sink"])[:DEPTH], 128, axis=-1).reshape(DEPTH, 1, 1024))
    sh["hg_lbT"] = f(np.asarray(inp["hg_lb_logits"])[:DEPTH].reshape(DEPTH, 2, 4, 128).transpose(3, 0, 1, 2).reshape(128, DEPTH, 8))
    sh["hg_ngT"] = f(np.asarray(inp["hg_norm_g"])[:DEPTH].reshape(DEPTH, 4, 128).transpose(2, 0, 1))
    sh["w_branch"] = f(inp["w_branch"][:DEPTH])
    sh["w_out"] = f(inp["w_out"][:DEPTH])
    sh["w_up"] = f(inp["ffn_w_up"][:DEPTH])
    sh["convT"] = f(np.asarray(inp["ffn_conv_w"])[:DEPTH].reshape(DEPTH, 3, 44, 128).transpose(3, 0, 2, 1))
    sh["convbT"] = f(np.asarray(inp["ffn_conv_b"])[:DEPTH].reshape(DEPTH, 44, 128).transpose(2, 0, 1))
    sh["w_down"] = f(inp["ffn_w_down"][:DEPTH])
    sh["c_ident"] = np.eye(128, dtype=np.float32)
    k = np.arange(128)[:, None]
    q = np.tile(np.arange(128), 4)[None, :]
    sh["c_maskP"] = np.where(k >= q, 0.0, -30000.0).astype(np.float32)
    sh["c_maskN"] = np.where(k <= q, 0.0, -30000.0).astype(np.float32)
    io = np.zeros((128, 2, 512), np.float32)
    io[:, 0, :] = np.arange(1, 513, dtype=np.float32)[None]
    io[:, 1, :] = (512 - np.arange(512, dtype=np.float32))[None]
    sh["c_iota"] = io
    s_ = np.arange(32)[:, None]
    t_ = np.tile(np.arange(32), 4)[None, :]
    hm = np.zeros((32, 2, 128), np.float32)
    hm[:, 0, :] = (s_ <= t_)
    hm[:, 1, :] = (s_ >= t_)
    sh["c_hgmask"] = hm
    rm = np.ones((128, 512), np.float32)
    rm[:, ::32] = 0.0
    sh["c_rmask"] = rm
    return sh


def prep_core(inp, b, cfg, sh):
    L, LC = cfg["L"], cfg["LC"]
    m = dict(sh)
    m["x"] = np.ascontiguousarray(np.asarray(inp["x"], dtype=np.float32)[b, :L])
    m["ctx"] = np.ascontiguousarray(np.asarray(inp["ctx"], dtype=np.float32)[b, :LC])
    cT = np.stack([np.asarray(inp["c"], dtype=np.float32)[b], np.asarray(inp["c_ctx"], dtype=np.float32)], axis=-1)
    m["cT"] = np.ascontiguousarray(cT.reshape(KC, 128, 2).transpose(1, 0, 2))
    return m


_NC_CACHE = {}


def kernel(**inputs):
    cfg = FULL
    key = "full"
    if key not in _NC_CACHE:
        _NC_CACHE[key] = build(cfg)
    nc = _NC_CACHE[key]
    sh = prep_shared(inputs, cfg)
    in_maps = [prep_core(inputs, b, cfg, sh) for b in range(8)]
    res = run_bass_kernel_spmd(nc, in_maps, core_ids=list(range(8)))
    return np.stack([np.asarray(r["out"], dtype=np.float32) for r in res.results], axis=0)
```

```python
import numpy as np
import ml_dtypes
from contextlib import ExitStack
import concourse.bass as bass
import concourse.mybir as mybir
from concourse.bass_utils import run_bass_kernel_spmd

F32 = mybir.dt.float32
BF16 = mybir.dt.bfloat16
I32 = mybir.dt.int32
AF = mybir.ActivationFunctionType
ALU = mybir.AluOpType
PI = float(np.pi)


class Buf:
    __slots__ = ("w", "r")

    def __init__(self):
        self.w = None
        self.r = {}


class Prog:
    def __init__(self, nc, es, ndma=48):
        self.nc = nc
        self.eng = {"pe": nc.tensor, "act": nc.scalar, "dve": nc.vector, "pool": nc.gpsimd, "sp": nc.sync}
        self.sem = {}
        self.cnt = {}
        for k in self.eng:
            self.sem[k] = es.enter_context(nc.semaphore("s_" + k))
            self.cnt[k] = 0
        self.ndma = ndma
        for i in range(ndma):
            self.sem[("d", i)] = es.enter_context(nc.semaphore("s_d%d" % i))
            self.cnt[("d", i)] = 0
        self.known = {e: {} for e in self.eng}
        self.rr = 0
        self.nins = 0

    def _waits(self, e, r, w, extra=()):
        need = {}
        kn = self.known[e]

        def add(tok, same_ok):
            if tok is None:
                return
            k, v = tok
            if k == e and (e == "pe" or not same_ok):
                return
            if kn.get(k, 0) >= v:
                return
            if need.get(k, 0) < v:
                need[k] = v

        for b in r:
            add(b.w, True)
        for b in w:
            add(b.w, True)
            for t in b.r.values():
                add(t, False)
        for t in extra:
            add(t, True)
        E = self.eng[e]
        for k, v in need.items():
            E.wait_ge(self.sem[k], v)
            kn[k] = v
            self.nins += 1

    def op(self, e, fn, r=(), w=()):
        self._waits(e, r, w)
        ins = fn(self.eng[e])
        self.cnt[e] += 1
        ins.then_inc(self.sem[e], 1)
        self.nins += 1
        tok = (e, self.cnt[e])
        for b in r:
            b.r[e] = tok
        for b in w:
            b.w = tok
            b.r = {}
        return tok

    def dma(self, q, out, in_, r=(), w=()):
        i = self.rr
        self.rr = (i + 1) % self.ndma
        key = ("d", i)
        self._waits(q, r, w, extra=[(key, self.cnt[key])])
        ins = self.eng[q].dma_start(out=out, in_=in_)
        self.cnt[key] += 16
        ins.then_inc(self.sem[key], 16)
        self.nins += 1
        tok = (key, self.cnt[key])
        for b in r:
            b.r[key] = tok
        for b in w:
            b.w = tok
            b.r = {}
        return tok

    def barrier(self):
        toks = [(k, v) for k, v in self.cnt.items() if v > 0]
        for e, E in self.eng.items():
            kn = self.known[e]
            for k, v in toks:
                if kn.get(k, 0) < v:
                    E.wait_ge(self.sem[k], v)
                    kn[k] = v
                    self.nins += 1


class Tl:
    def __init__(self, t, nb=1):
        self.t = t
        self.bs = [Buf() for _ in range(nb)]

    @property
    def b(self):
        return self.bs[0]

    def __getitem__(self, k):
        return self.t[k]


DM = 1024
KC = 8
EPS = 1e-6
IN_COLS = 6912
C_S5, C_Q, C_K, C_V, C_HQ, C_FF, C_FB, C_HI, C_HG, C_GATE = 0, 512, 1024, 1152, 1280, 1792, 2304, 2816, 3328, 3840
FFN = 2816
NFC = 22
FULL = dict(L=4096, LC=256, DEPTH=4, taps=())


def build(cfg):
    L, LC, DEPTH = cfg["L"], cfg["LC"], cfg["DEPTH"]
    taps = set(cfg.get("taps", ()))
    stop_after = cfg.get("stop_after", None)
    NT = L + LC
    NB = NT // 128
    TILES = [(0, LC)] + [(LC + 512 * i, 512) for i in range(L // 512)]
    NTI = len(TILES)

    def tile_of(tok):
        for i, (t0, sz) in enumerate(TILES):
            if t0 <= tok < t0 + sz:
                return i
        raise ValueError

    nc = bass.Bass("TRN2", target_bir_lowering=False)

    def din(name, shape, dt=F32):
        return nc.dram_tensor(name, list(shape), dt, kind="ExternalInput").ap()

    def dscr(name, shape, dt):
        kind = "ExternalOutput" if name in taps else "Internal"
        return nc.dram_tensor(name, list(shape), dt, kind=kind).ap()

    x_in = din("x", [L, DM])
    ctx_in = din("ctx", [LC, DM])
    cT_in = din("cT", [128, KC, 2])
    w_mod = din("w_mod", [DEPTH, DM, 6 * DM])
    b_modT = din("b_modT", [128, DEPTH, 48])
    norm_gT = din("norm_gT", [128, DEPTH, 4, KC])
    w_in = din("w_in", [DEPTH, DM, IN_COLS])
    w_rot = din("w_rot", [DEPTH, DM, 640])
    ropeC_in = din("ropeC", [128, NT])
    ropeS_in = din("ropeS", [128, NT])
    s5_lr = din("s5_lr", [128, DEPTH, 2, 16])
    s5_li = din("s5_li", [128, DEPTH, 2, 16])
    s5_ldt = din("s5_ldt", [128, DEPTH, 2, 16])
    s5_Bre = din("s5_Bre", [DEPTH, 16, 128, 128])
    s5_Bim = din("s5_Bim", [DEPTH, 16, 128, 128])
    s5_Cre = din("s5_Cre", [DEPTH, 16, 128, 128])
    s5_Cim = din("s5_Cim", [DEPTH, 16, 128, 128])
    s5_dT = din("s5_dT", [128, DEPTH, 4])
    s5_bgT = din("s5_bgT", [128, DEPTH, 4])
    s5_wglu = din("s5_wglu", [DEPTH, 512, 512])
    sinkrow = din("sinkrow", [DEPTH, 1, 1024])
    hg_lbT = din("hg_lbT", [128, DEPTH, 8])
    hg_ngT = din("hg_ngT", [128, DEPTH, 4])
    w_branch = din("w_branch", [DEPTH, 3, 512, DM])
    w_out = din("w_out", [DEPTH, DM, DM])
    w_up = din("w_up", [DEPTH, DM, 2 * FFN])
    convT = din("convT", [128, DEPTH, 44, 3])
    convbT = din("convbT", [128, DEPTH, 44])
    w_down = din("w_down", [DEPTH, FFN, DM])
    c_ident = din("c_ident", [128, 128])
    c_maskP = din("c_maskP", [128, 512])
    c_maskN = din("c_maskN", [128, 512])
    c_iota = din("c_iota", [128, 2, 512])
    c_hgmask = din("c_hgmask", [32, 2, 128])
    c_rmask = din("c_rmask", [128, 512])
    out = nc.dram_tensor("out", [L, DM], F32, kind="ExternalOutput").ap()

    xT = dscr("xT", [KC, 128, NT], F32)
    zs5 = dscr("zs5", [4, 128, NT], BF16)
    qT = dscr("qT", [8, 64, NT], BF16)
    kT = dscr("kT", [2, 64, NT], BF16)
    vtm = dscr("vtm", [NT, 640], BF16)
    hgq = dscr("hgq", [4, 128, NT], BF16)
    lf = dscr("lf", [2, 4, 128, NT], F32)
    kk = dscr("kk", [2, 4, 128, NT], BF16)
    hgg = dscr("hgg", [4, 128, NT], BF16)
    gat = dscr("gat", [24, 128, NT], BF16)
    yT = dscr("yT", [12, 128, NT], BF16)
    ofw = dscr("ofw", [4, 128, NT], F32)
    ofw2 = dscr("ofw2", [4, 128, NT], F32)
    actT = dscr("actT", [NFC, 128, NT], BF16)
    dbg = dscr("dbg", [128, 4096], F32)
    d_xT, d_z, d_y, d_of, d_act = Buf(), Buf(), Buf(), Buf(), Buf()

    with ExitStack() as es:
        P = Prog(nc, es)
        op, dma = P.op, P.dma

        uid = [0]

        def SB(stack, name, shape, dt, nb=1):
            uid[0] += 1
            return Tl(stack.enter_context(nc.sbuf_tensor("%s_u%d" % (name, uid[0]), list(shape), dt)), nb)

        psum = [Tl(es.enter_context(nc.psum_tensor("ps%d" % i, [128, 512], F32))) for i in range(8)]
        psrr = [0, 0]
        nlong = [2]

        def nextps(long=False):
            if long:
                p = psum[psrr[1] % nlong[0]]
                psrr[1] += 1
            else:
                p = psum[nlong[0] + psrr[0] % (8 - nlong[0])]
                psrr[0] += 1
            return p

        def MM(o, l, r_, st, sp, rb, wb):
            op("pe", lambda E: E.matmul(o, l, r_, start=st, stop=sp), r=rb, w=wb)

        def TR(o, i, idn, rb, wb):
            op("pe", lambda E: E.transpose(o, i, idn), r=rb, w=wb)

        def ACT(o, i, f, rb, wb, bias=None, scale=None):
            kw = {}
            if bias is not None:
                kw["bias"] = bias
            if scale is not None:
                kw["scale"] = scale
            op("act", lambda E: E.activation(out=o, in_=i, func=f, **kw), r=rb, w=wb)

        def CP(e, o, i, rb, wb):
            if e == "act":
                op("act", lambda E: E.copy(out=o, in_=i), r=rb, w=wb)
            else:
                op(e, lambda E: E.tensor_copy(out=o, in_=i), r=rb, w=wb)

        def TT(e, o, a, b_, alu, rb, wb):
            op(e, lambda E: E.tensor_tensor(out=o, in0=a, in1=b_, op=alu), r=rb, w=wb)

        def TS(e, o, a, s1, s2, o0, o1, rb, wb):
            if s2 is None:
                op(e, lambda E: E.tensor_scalar(out=o, in0=a, scalar1=s1, scalar2=None, op0=o0), r=rb, w=wb)
            else:
                op(e, lambda E: E.tensor_scalar(out=o, in0=a, scalar1=s1, scalar2=s2, op0=o0, op1=o1), r=rb, w=wb)

        def STT(o, a, s, b_, o0, o1, rb, wb):
            op("dve", lambda E: E.scalar_tensor_tensor(out=o, in0=a, scalar=s, in1=b_, op0=o0, op1=o1), r=rb, w=wb)

        def SCAN(o, d0, d1, init, rb, wb):
            op("dve", lambda E: E.tensor_tensor_scan(out=o, data0=d0, data1=d1, initial=init, op0=ALU.mult, op1=ALU.add), r=rb, w=wb)

        def MSET(e, o, v, wb):
            op(e, lambda E: E.memset(o, v), w=wb)

        def DBG(c0, ap, n, tl):
            if "dbg" in taps:
                dma("pool", dbg[0:ap.shape[0], c0:c0 + n], ap, r=[tl.b])

        ident_f = SB(es, "ident_f", [128, 128], F32)
        ident_b = SB(es, "ident_b", [128, 128], BF16)
        ones_b = SB(es, "ones_b", [128, 128], BF16)
        neghalf = SB(es, "neghalf", [128, 512], F32)
        modv = SB(es, "modv", [128, DEPTH, 48, 2], F32)
        ngt = SB(es, "ngt", [128, DEPTH, 4, KC], F32)
        lbv = SB(es, "lbv", [128, DEPTH, 8], F32)
        omlb = SB(es, "omlb", [128, DEPTH, 8], F32)
        A1 = SB(es, "A1", [128, KC, 2], F32)
        G1 = SB(es, "G1", [128, KC, 2], F32)
        A2 = SB(es, "A2", [128, KC, 2], F32)
        G2 = SB(es, "G2", [128, KC, 2], F32)
        STAT = [ident_f.b, ident_b.b, ones_b.b, neghalf.b]

        dma("sp", ident_f[:], c_ident[:, :], w=[ident_f.b])
        CP("dve", ident_b[:], ident_f[:], [ident_f.b], [ident_b.b])
        MSET("pool", ones_b[:], 1.0, [ones_b.b])
        MSET("pool", neghalf[:], -0.5, [neghalf.b])
        epsb = SB(es, "epsb", [128, 1], F32)
        MSET("pool", epsb[:], EPS, [epsb.b])
        dma("sp", ngt[:], norm_gT[:, :, :, :], w=[ngt.b])

        with ExitStack() as ph:
            xr = [SB(ph, "xr%d" % i, [128, DM], F32) for i in range(2)]
            xtt = [SB(ph, "xtt%d" % i, [128, KC, 128], F32) for i in range(2)]
            for tb in range(NB):
                src = ctx_in[tb * 128:(tb + 1) * 128, :] if tb < LC // 128 else x_in[tb * 128 - LC:(tb + 1) * 128 - LC, :]
                a, o_ = xr[tb % 2], xtt[tb % 2]
                dma("sp", a[:], src, w=[a.b])
                pa, pb = nextps(), nextps()
                for kc in range(KC):
                    pp_ = pa if kc < 4 else pb
                    TR(pp_[:, (kc % 4) * 128:(kc % 4 + 1) * 128], a[:, kc * 128:(kc + 1) * 128], ident_f[:], [a.b, ident_f.b], [pp_.b])
                CP("act", o_[:, 0:4, :], pa[:].rearrange("p (k t) -> p k t", k=4), [pa.b], [o_.b])
                CP("dve", o_[:, 4:8, :], pb[:].rearrange("p (k t) -> p k t", k=4), [pb.b], [o_.b])
                dma("pool", xT[:, :, tb * 128:(tb + 1) * 128].rearrange("k p t -> p k t"), o_[:], r=[o_.b], w=[d_xT])
            cTt = SB(ph, "cTt", [128, KC, 2], F32)
            scb = SB(ph, "scb", [128, KC, 2], BF16)
            bmt = SB(ph, "bmt", [128, DEPTH, 48], F32)
            wm = [SB(ph, "wm%d" % i, [128, KC, 1024], BF16) for i in range(2)]
            dma("sp", cTt[:], cT_in[:, :, :], w=[cTt.b])
            dma("sp", bmt[:], b_modT[:, :, :], w=[bmt.b])
            ACT(scb[:], cTt[:], AF.Silu, [cTt.b], [scb.b])
            for l in range(DEPTH):
                pm = nextps()
                for grp in range(6):
                    wt = wm[(l * 6 + grp) % 2]
                    dma("pool", wt[:], w_mod[l, :, grp * 1024:(grp + 1) * 1024].rearrange("(k p) n -> p k n", p=128), w=[wt.b])
                    for j in range(8):
                        oc = grp * 8 + j
                        for kc in range(KC):
                            MM(pm[:, oc * 2:oc * 2 + 2], wt[:, kc, j * 128:(j + 1) * 128], scb[:, kc, :], kc == 0, kc == KC - 1, [wt.b, scb.b], [pm.b])
                TT("dve", modv[:, l, :, :], pm[:, 0:96].rearrange("p (c w) -> p c w", w=2),
                   bmt[:, l, :].unsqueeze(2).to_broadcast([128, 48, 2]), ALU.add, [pm.b, bmt.b], [modv.b])
            lg = SB(ph, "lg", [128, DEPTH, 8], F32)
            sm = SB(ph, "sm", [128, 8], F32)
            dma("sp", lg[:], hg_lbT[:, :, :], w=[lg.b])
            ACT(lg[:], lg[:], AF.Exp, [lg.b], [lg.b])
            CP("dve", sm[:], lg[:, 0, :], [lg.b], [sm.b])
            for l in range(1, DEPTH):
                TT("dve", sm[:], sm[:], lg[:, l, :], ALU.add, [sm.b, lg.b], [sm.b])
            op("dve", lambda E: E.reciprocal(out=sm[:], in_=sm[:]), r=[sm.b], w=[sm.b])
            MSET("dve", lbv[:, 0, :], 0.0, [lbv.b])
            for l in range(1, DEPTH):
                TT("dve", lg[:, l, :], lg[:, l, :], sm[:], ALU.mult, [lg.b, sm.b], [lg.b])
                TT("dve", lbv[:, l, :], lbv[:, l - 1, :], lg[:, l, :], ALU.add, [lbv.b, lg.b], [lbv.b])
            TS("dve", omlb[:], lbv[:], -1.0, 1.0, ALU.mult, ALU.add, [lbv.b], [omlb.b])
            P.barrier()

        def mod_scalars(l):
            for (Aq, sc0, gi) in ((A1, 8, 0), (A2, 32, 2)):
                TS("dve", Aq[:], modv[:, l, sc0:sc0 + 8, :], 1.0, None, ALU.add, None, [modv.b], [Aq.b])
                TT("dve", Aq[:], Aq[:], ngt[:, l, gi, :].unsqueeze(2).to_broadcast([128, KC, 2]), ALU.mult, [Aq.b, ngt.b], [Aq.b])
            for (Gq, g0, gi) in ((G1, 16, 1), (G2, 40, 3)):
                TT("dve", Gq[:], modv[:, l, g0:g0 + 8, :], ngt[:, l, gi, :].unsqueeze(2).to_broadcast([128, KC, 2]), ALU.mult, [modv.b, ngt.b], [Gq.b])

        def norm_phase(ph, l, Aq, sh0, hT):
            xts = [SB(ph, "nxt%d" % i, [128, KC, 512], F32) for i in range(2)]
            sqs = [SB(ph, "nsq%d" % i, [128, KC, 512], BF16) for i in range(2)]
            rss = [SB(ph, "nrs%d" % i, [128, 512], F32) for i in range(2)]
            tmp = SB(ph, "ntmp", [128, KC, 512], F32, nb=KC)

            def stage_a(ti):
                t0, sz = TILES[ti]
                xt, sq, rs = xts[ti % 2], sqs[ti % 2], rss[ti % 2]
                dma("sp", xt[:, :, 0:sz], xT[:, :, t0:t0 + sz].rearrange("k p t -> p k t"), r=[d_xT], w=[xt.b])
                ACT(sq[:, :, 0:sz], xt[:, :, 0:sz], AF.Square, [xt.b], [sq.b])
                ps = nextps()
                for kc in range(KC):
                    MM(ps[:, 0:sz], ones_b[:], sq[:, kc, 0:sz], kc == 0, kc == KC - 1, [ones_b.b, sq.b], [ps.b])
                ACT(rs[:, 0:sz], ps[:, 0:sz], AF.Ln, [ps.b, epsb.b], [rs.b], bias=epsb[:, 0:1], scale=1.0 / DM)
                ACT(rs[:, 0:sz], rs[:, 0:sz], AF.Exp, [rs.b], [rs.b], scale=-0.5)

            def stage_b(ti):
                t0, sz = TILES[ti]
                w_ = 1 if ti == 0 else 0
                xt, rs = xts[ti % 2], rss[ti % 2]
                for kc in range(KC):
                    STT(tmp[:, kc, 0:sz], xt[:, kc, 0:sz], Aq[:, kc, w_:w_ + 1], rs[:, 0:sz], ALU.mult, ALU.mult, [xt.b, Aq.b, rs.b], [tmp.bs[kc]])
                    ACT(hT[:, kc, t0:t0 + sz], tmp[:, kc, 0:sz], AF.Identity, [tmp.bs[kc], modv.b], [hT.bs[ti]],
                        bias=modv[:, l, sh0 + kc, w_:w_ + 1])
            stage_a(0)
            for ti in range(NTI):
                stage_b(ti)
                if ti + 1 < NTI:
                    stage_a(ti + 1)

        def epilogue(ot, xt, Gq, ti, t0, sz, sq, rs, tmp):
            w_ = 1 if ti == 0 else 0
            ACT(sq[:, :, 0:sz], ot[:, :, 0:sz], AF.Square, [ot.b], [sq.b])
            ps = nextps()
            for kc in range(KC):
                MM(ps[:, 0:sz], ones_b[:], sq[:, kc, 0:sz], kc == 0, kc == KC - 1, [ones_b.b, sq.b], [ps.b])
            ACT(rs[:, 0:sz], ps[:, 0:sz], AF.Ln, [ps.b, epsb.b], [rs.b], bias=epsb[:, 0:1], scale=1.0 / DM)
            ACT(rs[:, 0:sz], rs[:, 0:sz], AF.Exp, [rs.b], [rs.b], scale=-0.5)
            for kc in range(KC):
                STT(ot[:, kc, 0:sz], ot[:, kc, 0:sz], Gq[:, kc, w_:w_ + 1], rs[:, 0:sz], ALU.mult, ALU.mult, [ot.b, Gq.b, rs.b], [ot.b])
                TT("pool", xt[:, kc, 0:sz], xt[:, kc, 0:sz], ot[:, kc, 0:sz], ALU.add, [xt.b, ot.b], [xt.b])
            dma("pool", xT[:, :, t0:t0 + sz].rearrange("k p t -> p k t"), xt[:, :, 0:sz], r=[xt.b], w=[d_xT])

        for l in range(DEPTH):
            mod_scalars(l)
            with ExitStack() as ph:
                hT = SB(ph, "hT", [128, KC, NT], BF16, nb=NTI)
                with ExitStack() as ph1:
                    norm_phase(ph1, l, A1, 0, hT)
                    P.barrier()
                wts = [SB(ph, "wt%d" % i, [128, KC, 512], BF16) for i in range(3)]
                wrr = [0]
                stg = [SB(ph, "stg%d" % i, [128, NT], BF16) for i in range(3)]
                srr = [0]
                stgf = [SB(ph, "stgf%d" % i, [128, NT], F32) for i in range(2)]
                tmpa = [SB(ph, "tmpa%d" % i, [128, 512], F32) for i in range(2)]
                tmpb = [SB(ph, "tmpb%d" % i, [128, 512], F32) for i in range(2)]

                def load_w(src, ncols):
                    wt = wts[wrr[0] % 3]
                    wrr[0] += 1
                    dma("pool", wt[:, :, 0:ncols], src.rearrange("(k p) n -> p k n", p=128), w=[wt.b])
                    return wt

                def next_stg():
                    s = stg[srr[0] % 3]
                    srr[0] += 1
                    return s

                def proj(wt, off, M, cons):
                    for ti, (t0, sz) in enumerate(TILES):
                        ps = nextps()
                        for kc in range(KC):
                            MM(ps[0:M, 0:sz], wt[:, kc, off:off + M], hT[:, kc, t0:t0 + sz], kc == 0, kc == KC - 1, [wt.b, hT.bs[ti]], [ps.b])
                        cons(ti, t0, sz, ps)

                def simple_group(col0, nchunks, func, dst):
                    for g0 in range(0, nchunks, 4):
                        n = min(4, nchunks - g0)
                        wt = load_w(w_in[l, :, col0 + g0 * 128:col0 + (g0 + n) * 128], n * 128)
                        for c in range(n):
                            s = next_stg()

                            def cons(ti, t0, sz, ps, s=s):
                                if func is None:
                                    CP("act", s[:, t0:t0 + sz], ps[:, 0:sz], [ps.b], [s.b])
                                else:
                                    ACT(s[:, t0:t0 + sz], ps[:, 0:sz], func, [ps.b], [s.b])
                            proj(wt, c * 128, 128, cons)
                            dma("sp", dst[g0 + c], s[:], r=[s.b], w=[d_z])

                simple_group(C_S5, 4, None, zs5)
                with ExitStack() as phq:
                    ropeC = SB(phq, "ropeC", [128, NT], F32)
                    ropeS = SB(phq, "ropeS", [128, NT], F32)
                    dma("sp", ropeC[:], ropeC_in[:, :], w=[ropeC.b])
                    dma("sp", ropeS[:], ropeS_in[:, :], w=[ropeS.b])
                    for (cbase, rbase, nh_, dst) in ((C_Q, 0, 8, qT), (C_K, 512, 2, kT)):
                        for g0 in range(0, nh_, 4):
                            n = min(4, nh_ - g0)
                            wa = load_w(w_in[l, :, cbase + g0 * 64:cbase + (g0 + n) * 64], n * 64)
                            wb = load_w(w_rot[l, :, rbase + g0 * 64:rbase + (g0 + n) * 64], n * 64)
                            for c in range(n // 2):
                                s = next_stg()
                                for ti, (t0, sz) in enumerate(TILES):
                                    p1, p2 = nextps(), nextps()
                                    for kc in range(KC):
                                        MM(p1[:, 0:sz], wa[:, kc, c * 128:(c + 1) * 128], hT[:, kc, t0:t0 + sz], kc == 0, kc == KC - 1, [wa.b, hT.bs[ti]], [p1.b])
                                    for kc in range(KC):
                                        MM(p2[:, 0:sz], wb[:, kc, c * 128:(c + 1) * 128], hT[:, kc, t0:t0 + sz], kc == 0, kc == KC - 1, [wb.b, hT.bs[ti]], [p2.b])
                                    ta, tb_ = tmpa[ti % 2], tmpb[ti % 2]
                                    TT("dve", ta[:, 0:sz], p1[:, 0:sz], ropeC[:, t0:t0 + sz], ALU.mult, [p1.b, ropeC.b], [ta.b])
                                    TT("dve", tb_[:, 0:sz], p2[:, 0:sz], ropeS[:, t0:t0 + sz], ALU.mult, [p2.b, ropeS.b], [tb_.b])
                                    TT("pool", s[:, t0:t0 + sz], ta[:, 0:sz], tb_[:, 0:sz], ALU.add, [ta.b, tb_.b], [s.b])
                                h0 = g0 + 2 * c
                                dma("sp", dst[h0:h0 + 2].rearrange("h d t -> (h d) t"), s[:], r=[s.b], w=[d_z])
                    P.barrier()
                with ExitStack() as ph3:
                    wv = SB(ph3, "wv", [128, KC, 640], BF16)
                    vst = [SB(ph3, "vst%d" % i, [128, 640], BF16) for i in range(2)]
                    dma("pool", wv[:, :, 0:128], w_in[l, :, C_V:C_V + 128].rearrange("(k p) n -> p k n", p=128), w=[wv.b])
                    dma("pool", wv[:, :, 128:640], w_in[l, :, C_HI:C_HI + 512].rearrange("(k p) n -> p k n", p=128), w=[wv.b])
                    for tb in range(NB):
                        ti = tile_of(tb * 128)
                        pa, pb = nextps(), nextps()
                        for kc in range(KC):
                            MM(pa[:, 0:512], hT[:, kc, tb * 128:(tb + 1) * 128], wv[:, kc, 128:640], kc == 0, kc == KC - 1, [wv.b, hT.bs[ti]], [pa.b])
                        for kc in range(KC):
                            MM(pb[:, 0:128], hT[:, kc, tb * 128:(tb + 1) * 128], wv[:, kc, 0:128], kc == 0, kc == KC - 1, [wv.b, hT.bs[ti]], [pb.b])
                        v = vst[tb % 2]
                        CP("act", v[:, 0:128], pb[:, 0:128], [pb.b], [v.b])
                        CP("dve", v[:, 128:640], pa[:, 0:512], [pa.b], [v.b])
                        dma("sp", vtm[tb * 128:(tb + 1) * 128, :], v[:], r=[v.b], w=[d_z])
                simple_group(C_HQ, 4, AF.Silu, hgq)
                for d in range(2):
                    wt = load_w(w_in[l, :, C_FF + d * 512:C_FF + (d + 1) * 512], 512)
                    for c in range(4):
                        s = next_stg()
                        sf = stgf[c % 2]
                        li_ = d * 4 + c

                        def cons(ti, t0, sz, ps, s=s, sf=sf, li_=li_):
                            ta = tmpa[ti % 2]
                            ACT(ta[:, 0:sz], ps[:, 0:sz], AF.Exp, [ps.b], [ta.b], scale=-1.0)
                            TS("dve", ta[:, 0:sz], ta[:, 0:sz], 1.0, None, ALU.add, None, [ta.b], [ta.b])
                            op("dve", lambda E: E.reciprocal(out=ta[:, 0:sz], in_=ta[:, 0:sz]), r=[ta.b], w=[ta.b])
                            TS("dve", ta[:, 0:sz], ta[:, 0:sz], omlb[:, l, li_:li_ + 1], lbv[:, l, li_:li_ + 1], ALU.mult, ALU.add, [ta.b, omlb.b, lbv.b], [ta.b])
                            ACT(sf[:, t0:t0 + sz], ta[:, 0:sz], AF.Ln, [ta.b], [sf.b])
                            TS("dve", s[:, t0:t0 + sz], ta[:, 0:sz], -1.0, 1.0, ALU.mult, ALU.add, [ta.b], [s.b])
                        proj(wt, c * 128, 128, cons)
                        dma("sp", lf[d, c], sf[:], r=[sf.b], w=[d_z])
                        dma("sp", kk[d, c], s[:], r=[s.b], w=[d_z])
                simple_group(C_HG, 4, AF.Sigmoid, hgg)
                simple_group(C_GATE, 24, AF.Sigmoid, gat)
                P.barrier()
            if stop_after == "P2":
                break
            with ExitStack() as ph:
                uT = SB(ph, "uT", [128, 4, NT], BF16)
                yacc = SB(ph, "yacc", [128, NT], F32)
                for c in range(4):
                    dma("sp", uT[:, c, :], zs5[c], r=[d_z], w=[uT.b])
                y2T = uT
                sm_ = {n: SB(ph, "s5" + n, [128, 2, 16], F32) for n in
                       ("lr", "li", "dt", "th", "rho", "sn", "cs", "thr", "ar", "ai", "den", "fr", "fi", "t1", "t2", "tf", "dl", "rho8", "th8")}
                smi = SB(ph, "s5i", [128, 2, 16], I32)
                dma("sp", sm_["lr"][:], s5_lr[:, l, :, :], w=[sm_["lr"].b])
                dma("sp", sm_["li"][:], s5_li[:, l, :, :], w=[sm_["li"].b])
                dma("sp", sm_["dt"][:], s5_ldt[:, l, :, :], w=[sm_["dt"].b])

                def reduce_angle(src, dst, tf, ti_):
                    TS("dve", tf[:], src[:], 1.0 / (2 * PI), None, ALU.mult, None, [src.b], [tf.b])
                    CP("dve", ti_[:], tf[:], [tf.b], [ti_.b])
                    CP("dve", tf[:], ti_[:], [ti_.b], [tf.b])
                    STT(dst[:], tf[:], -2 * PI, src[:], ALU.mult, ALU.add, [tf.b, src.b], [dst.b])
                    TS("dve", dst[:], dst[:], -PI, PI, ALU.max, ALU.min, [dst.b], [dst.b])

                S = sm_
                ACT(S["dt"][:], S["dt"][:], AF.Exp, [S["dt"].b], [S["dt"].b])
                TT("dve", S["th"][:], S["dt"][:], S["li"][:], ALU.mult, [S["dt"].b, S["li"].b], [S["th"].b])
                TT("dve", S["dl"][:], S["dt"][:], S["lr"][:], ALU.mult, [S["dt"].b, S["lr"].b], [S["dl"].b])
                ACT(S["rho"][:], S["dl"][:], AF.Exp, [S["dl"].b], [S["rho"].b])
                reduce_angle(S["th"], S["thr"], S["t1"], smi)
                ACT(S["sn"][:], S["thr"][:], AF.Sin, [S["thr"].b], [S["sn"].b])
                TS("dve", S["t2"][:], S["thr"][:], PI / 2, None, ALU.add, None, [S["thr"].b], [S["t2"].b])
                reduce_angle(S["t2"], S["cs"], S["t1"], smi)
                ACT(S["cs"][:], S["cs"][:], AF.Sin, [S["cs"].b], [S["cs"].b])
                TT("dve", S["ar"][:], S["rho"][:], S["cs"][:], ALU.mult, [S["rho"].b, S["cs"].b], [S["ar"].b])
                TT("dve", S["ai"][:], S["rho"][:], S["sn"][:], ALU.mult, [S["rho"].b, S["sn"].b], [S["ai"].b])
                TT("dve", S["den"][:], S["lr"][:], S["lr"][:], ALU.mult, [S["lr"].b], [S["den"].b])
                TT("dve", S["t1"][:], S["li"][:], S["li"][:], ALU.mult, [S["li"].b], [S["t1"].b])
                TT("dve", S["den"][:], S["den"][:], S["t1"][:], ALU.add, [S["den"].b, S["t1"].b], [S["den"].b])
                op("dve", lambda E: E.reciprocal(out=S["den"][:], in_=S["den"][:]), r=[S["den"].b], w=[S["den"].b])
                TS("dve", S["ar"][:], S["ar"][:], -1.0, None, ALU.add, None, [S["ar"].b], [S["ar"].b])
                TT("dve", S["fr"][:], S["ar"][:], S["lr"][:], ALU.mult, [S["ar"].b, S["lr"].b], [S["fr"].b])
                TT("dve", S["t1"][:], S["ai"][:], S["li"][:], ALU.mult, [S["ai"].b, S["li"].b], [S["t1"].b])
                TT("dve", S["fr"][:], S["fr"][:], S["t1"][:], ALU.add, [S["fr"].b, S["t1"].b], [S["fr"].b])
                TT("dve", S["fr"][:], S["fr"][:], S["den"][:], ALU.mult, [S["fr"].b, S["den"].b], [S["fr"].b])
                TT("dve", S["fi"][:], S["ai"][:], S["lr"][:], ALU.mult, [S["ai"].b, S["lr"].b], [S["fi"].b])
                TT("dve", S["t1"][:], S["ar"][:], S["li"][:], ALU.mult, [S["ar"].b, S["li"].b], [S["t1"].b])
                TT("dve", S["fi"][:], S["fi"][:], S["t1"][:], ALU.subtract, [S["fi"].b, S["t1"].b], [S["fi"].b])
                TT("dve", S["fi"][:], S["fi"][:], S["den"][:], ALU.mult, [S["fi"].b, S["den"].b], [S["fi"].b])

                pwr = SB(ph, "pwr", [128, 9, 2, 16], F32)
                pwi = SB(ph, "pwi", [128, 9, 2, 16], F32)
                npwr = SB(ph, "npwr", [128, 9, 2, 16], F32)
                for tau in range(9):
                    TS("dve", S["t1"][:], S["thr"][:], float(tau), None, ALU.mult, None, [S["thr"].b], [S["t1"].b])
                    reduce_angle(S["t1"], S["t2"], S["tf"], smi)
                    ACT(S["sn"][:], S["t2"][:], AF.Sin, [S["t2"].b], [S["sn"].b])
                    TS("dve", S["t1"][:], S["t2"][:], PI / 2, None, ALU.add, None, [S["t2"].b], [S["t1"].b])
                    reduce_angle(S["t1"], S["cs"], S["tf"], smi)
                    ACT(S["cs"][:], S["cs"][:], AF.Sin, [S["cs"].b], [S["cs"].b])
                    TS("dve", S["t1"][:], S["dl"][:], float(tau), None, ALU.mult, None, [S["dl"].b], [S["t1"].b])
                    ACT(S["t1"][:], S["t1"][:], AF.Exp, [S["t1"].b], [S["t1"].b])
                    TT("dve", pwr[:, tau], S["t1"][:], S["cs"][:], ALU.mult, [S["t1"].b, S["cs"].b], [pwr.b])
                    TT("dve", pwi[:, tau], S["t1"][:], S["sn"][:], ALU.mult, [S["t1"].b, S["sn"].b], [pwi.b])
                TS("dve", npwr[:], pwr[:], -1.0, None, ALU.mult, None, [pwr.b], [npwr.b])
                TS("dve", S["t1"][:], S["dl"][:], 8.0, None, ALU.mult, None, [S["dl"].b], [S["t1"].b])
                ACT(S["rho8"][:], S["t1"][:], AF.Exp, [S["t1"].b], [S["rho8"].b])
                TS("dve", S["t1"][:], S["thr"][:], 8.0, None, ALU.mult, None, [S["thr"].b], [S["t1"].b])
                reduce_angle(S["t1"], S["th8"], S["tf"], smi)

                bp = [SB(ph, "bp%d" % i, [128, 2, 128], F32) for i in range(2)]
                bb = [[SB(ph, "bb%d_%d" % (d, pp), [128, 2, 128], BF16) for pp in range(4)] for d in range(2)]
                dgs = [SB(ph, "dg%d" % i, [128, 3, 4, 128], BF16) for i in range(2)]
                Xb4 = [SB(ph, "Xb4_%d" % i, [128, 4, 2, 128], BF16) for i in range(2)]
                identb4 = SB(ph, "identb4", [128, 4, 128], BF16)
                for i_ in range(4):
                    CP("dve", identb4[:, i_, :], ident_b[:], [ident_b.b], [identb4.b])
                npwi = SB(ph, "npwi", [128, 9, 2, 16], F32)
                TS("dve", npwi[:], pwi[:], -1.0, None, ALU.mult, None, [pwi.b], [npwi.b])
                cf = [SB(ph, "cf%d" % pp, [128, 2, 128], F32) for pp in range(4)]
                lhsC = [SB(ph, "lhsC%d" % pp, [128, 2, 128], BF16) for pp in range(4)]
                xs = [SB(ph, "xs%d" % i, [128, 3, 128], F32) for i in range(2)]
                lhsP = SB(ph, "lhsP", [128, 8, 4, 2, 128], BF16)
                BD = SB(ph, "BD", [128, 8, 128], BF16)
                lhsQ = SB(ph, "lhsQ", [128, 8, 4, 2, 128], BF16)
                diagD = SB(ph, "diagD", [128, 128], F32)
                sdT = SB(ph, "sdT", [128, 4], F32)
                dma("sp", sdT[:], s5_dT[:, l, :], w=[sdT.b])
                iot = SB(ph, "iot64", [128, 64], F32)
                dma("sp", iot[:], c_iota[:, 0, 0:64], w=[iot.b])
                a64 = [SB(ph, "a64_%d" % i, [128, 64], F32) for i in range(3)]
                a64i = SB(ph, "a64i", [128, 64], I32)
                tabC = SB(ph, "tabC", [128, 4, 64], F32)
                tabS = SB(ph, "tabS", [128, 4, 64], F32)
                tabN = SB(ph, "tabN", [128, 4, 64], F32)
                Vt = [SB(ph, "Vt%d" % i, [128, 4, 2, 64], F32) for i in range(2)]
                Wk = [{n: SB(ph, "wk%s%d" % (n, i), [128, 4, 64], F32) for n in ("m1", "m2", "m3", "m4", "gr", "gi")} for i in range(2)]
                Hre = SB(ph, "Hre", [128, 4, 65], F32)
                Him = SB(ph, "Him", [128, 4, 65], F32)
                Hb = [SB(ph, "Hb%d" % i, [128, 2, 4, 64], BF16) for i in range(2)]
                nlong[0] = 4
                vt = SB(ph, "vt", [128, NB, 128], BF16)
                dma("sp", vt[:], vtm[:, 0:128].rearrange("(b p) c -> p b c", p=128), r=[d_z], w=[vt.b])
                mP = SB(ph, "mP", [128, 512], BF16)
                mN = SB(ph, "mN", [128, 512], BF16)
                dma("pool", mP[:], c_maskP[:, :], w=[mP.b])
                dma("pool", mN[:], c_maskN[:, :], w=[mN.b])
                kT2 = SB(ph, "kT2", [64, 2, NT], BF16)
                dma("sp", kT2[:], kT[:, :, :].rearrange("h d t -> d h t"), r=[d_z], w=[kT2.b])
                srow = SB(ph, "srow", [1, 1024], F32)
                dma("sp", srow[:], sinkrow[l, :, :], w=[srow.b])
                sinkts = [SB(ph, "sinkt%d" % i, [128, 512], BF16) for i in range(2)]
                for kvh in range(2):
                    MSET("pool", sinkts[kvh][:], 0.0, [sinkts[kvh].b])
                    ACT(sinkts[kvh][0:1, :], srow[:, kvh * 512:(kvh + 1) * 512], AF.Exp, [srow.b], [sinkts[kvh].b])
                qblk = [SB(ph, "qblk%d" % i, [64, 4, 128], BF16) for i in range(3)]
                oblk = [SB(ph, "oblk%d" % i, [64, 4, 128], BF16) for i in range(3)]
                pts = [SB(ph, "pt%d" % i, [128, 512], BF16) for i in range(3)]
                rds = [SB(ph, "rd%d" % i, [64, 512], F32) for i in range(3)]
                att_steps = [(kvh, qb) for kvh in range(2) for qb in range(NB)]
                att_st = {"i": 0, "ipt": 0, "pend": []}

                def att_issue():
                    idx = att_st["i"]
                    if idx >= len(att_steps):
                        return
                    att_st["i"] += 1
                    kvh, qb = att_steps[idx]
                    qv = qblk[idx % 3]
                    dma("sp", qv[:], qT[4 * kvh:4 * kvh + 4, :, qb * 128:(qb + 1) * 128].rearrange("h d t -> d h t"), r=[d_z], w=[qv.b])
                    if qb < LC // 128:
                        keys = [(kt_, None) for kt_ in range(LC // 128)]
                    else:
                        n = qb - LC // 128
                        keys = [(kt_, None) for kt_ in range(LC // 128)]
                        if n - 1 >= 0:
                            keys.append((qb - 1, mP))
                        keys.append((qb, None))
                        if n + 1 < L // 128:
                            keys.append((qb + 1, mN))
                    pso, psd = nextps(long=True), nextps(long=True)
                    for i, (kt_, msk) in enumerate(keys):
                        pss = nextps()
                        MM(pss[:, :], kT2[:, kvh, kt_ * 128:(kt_ + 1) * 128], qv[:], True, msk is None, [kT2.b, qv.b], [pss.b])
                        if msk is not None:
                            MM(pss[:, :], ident_b[:], msk[:], False, True, [ident_b.b, msk.b], [pss.b])
                        pt = pts[att_st["ipt"] % 3]
                        att_st["ipt"] += 1
                        ACT(pt[:], pss[:, :], AF.Exp, [pss.b], [pt.b], scale=0.125)
                        MM(pso[0:64, :], vt[:, kt_, kvh * 64:(kvh + 1) * 64], pt[:], i == 0, i == len(keys) - 1, [vt.b, pt.b], [pso.b])
                        MM(psd[0:64, :], ones_b[:, 0:64], pt[:], i == 0, False, [ones_b.b, pt.b], [psd.b])
                    MM(psd[0:64, :], ones_b[:, 0:64], sinkts[kvh][:], False, True, [ones_b.b, sinkts[kvh].b], [psd.b])
                    rd = rds[idx % 3]
                    ACT(rd[:], psd[0:64, :], AF.Ln, [psd.b], [rd.b])
                    ACT(rd[:], rd[:], AF.Exp, [rd.b], [rd.b], scale=-1.0)
                    att_st["pend"].append((idx, kvh, qb, pso, rd))

                def att_finalize():
                    if not att_st["pend"]:
                        return
                    idx, kvh, qb, pso, rd = att_st["pend"].pop(0)
                    o_ = oblk[idx % 3]
                    TT("dve", o_[:], pso[0:64, :].rearrange("p (h q) -> p h q", h=4), rd[:].rearrange("p (h q) -> p h q", h=4),
                       ALU.mult, [pso.b, rd.b], [o_.b])
                    dma("sp", yT[4 + 2 * kvh:6 + 2 * kvh, :, qb * 128:(qb + 1) * 128].rearrange("c (two d) t -> d (c two) t", two=2), o_[:],
                        r=[o_.b], w=[d_y])

                it = 0
                for fc in range(4):
                    TS("dve", diagD[:], ident_f[:], sdT[:, fc:fc + 1], None, ALU.mult, None, [ident_f.b, sdT.b], [diagD.b])
                    for pp in range(4):
                        pr = fc * 4 + pp
                        b_ = bp[pp % 2]
                        dma("sp", b_[:, 0, :], s5_Bre[l, pr], w=[b_.b])
                        dma("sp", b_[:, 1, :], s5_Bim[l, pr], w=[b_.b])
                        dma("sp", cf[pp][:, 0, :], s5_Cre[l, pr], w=[cf[pp].b])
                        dma("sp", cf[pp][:, 1, :], s5_Cim[l, pr], w=[cf[pp].b])
                        CP("act", lhsC[pp][:, 0, :], cf[pp][:, 0, :], [cf[pp].b], [lhsC[pp].b])
                        TS("dve", lhsC[pp][:, 1, :], cf[pp][:, 1, :], -1.0, None, ALU.mult, None, [cf[pp].b], [lhsC[pp].b])
                        for d in range(2):
                            fr_ = S["fr"][:, d, pr:pr + 1]
                            fi_ = S["fi"][:, d, pr:pr + 1]
                            x_ = xs[d]
                            o_ = bb[d][pp]
                            TS("dve", x_[:, 0, :], b_[:, 1, :], fi_, None, ALU.mult, None, [b_.b, S["fi"].b], [x_.b])
                            STT(o_[:, 0, :], b_[:, 0, :], fr_, x_[:, 0, :], ALU.mult, ALU.subtract, [b_.b, S["fr"].b, x_.b], [o_.b])
                            TS("dve", x_[:, 1, :], b_[:, 0, :], fi_, None, ALU.mult, None, [b_.b, S["fi"].b], [x_.b])
                            STT(o_[:, 1, :], b_[:, 1, :], fr_, x_[:, 1, :], ALU.mult, ALU.add, [b_.b, S["fr"].b, x_.b], [o_.b])
                    for d in range(2):
                        sl4 = slice(fc * 4, fc * 4 + 4)
                        for tau in range(9):
                            dg = dgs[tau % 2]
                            for vi, pw_ in enumerate((pwr, pwi, npwi)):
                                TT("dve", dg[:, vi], identb4[:], pw_[:, tau, d, sl4].unsqueeze(2).to_broadcast([128, 4, 128]), ALU.mult,
                                   [identb4.b, pw_.b], [dg.b])

                            def four(dst_banks, combos):
                                for pp in range(4):
                                    pq = dst_banks[pp // 2]
                                    base = (pp % 2) * 256
                                    for ri, (l0, r0, l1, r1, bufs) in enumerate(combos(pp)):
                                        o_ap = pq[:, base + ri * 128:base + (ri + 1) * 128]
                                        MM(o_ap, l0, r0, True, False, bufs, [pq.b])
                                        MM(o_ap, l1, r1, False, True, bufs, [pq.b])

                            def vw(pq):
                                return pq[:].rearrange("p (a b c) -> p a b c", a=2, b=2)
                            if tau < 8:
                                pa, pb = nextps(), nextps()
                                four((pa, pb), lambda pp: (
                                    (bb[d][pp][:, 0, :], dg[:, 0, pp, :], bb[d][pp][:, 1, :], dg[:, 2, pp, :], [bb[d][pp].b, dg.b]),
                                    (bb[d][pp][:, 1, :], dg[:, 0, pp, :], bb[d][pp][:, 0, :], dg[:, 1, pp, :], [bb[d][pp].b, dg.b])))
                                CP("act", lhsP[:, tau, 0:2, :, :], vw(pa), [pa.b], [lhsP.b])
                                CP("act", lhsP[:, tau, 2:4, :, :], vw(pb), [pb.b], [lhsP.b])
                                pc, pd = nextps(), nextps()
                                four((pc, pd), lambda pp: (
                                    (dg[:, 0, pp, :], bb[d][pp][:, 0, :], dg[:, 2, pp, :], bb[d][pp][:, 1, :], [bb[d][pp].b, dg.b]),
                                    (dg[:, 0, pp, :], bb[d][pp][:, 1, :], dg[:, 1, pp, :], bb[d][pp][:, 0, :], [bb[d][pp].b, dg.b])))
                                xb = Xb4[tau % 2]
                                CP("act", xb[:, 0:2, :, :], vw(pc), [pc.b], [xb.b])
                                CP("act", xb[:, 2:4, :, :], vw(pd), [pd.b], [xb.b])
                                psd_ = nextps()
                                for pp in range(4):
                                    for ri in range(2):
                                        MM(psd_[:, 0:128], xb[:, pp, ri, :], lhsC[pp][:, ri, :], pp == 0 and ri == 0, pp == 3 and ri == 1,
                                           [xb.b, lhsC[pp].b], [psd_.b])
                                if tau == 0 and d == 0:
                                    TT("dve", BD[:, tau, :], psd_[:, 0:128], diagD[:], ALU.add, [psd_.b, diagD.b], [BD.b])
                                else:
                                    CP("act", BD[:, tau, :], psd_[:, 0:128], [psd_.b], [BD.b])
                            if tau >= 1:
                                t = tau - 1
                                pe_, pf = nextps(), nextps()
                                four((pe_, pf), lambda pp: (
                                    (dg[:, 0, pp, :], lhsC[pp][:, 0, :], dg[:, 1, pp, :], lhsC[pp][:, 1, :], [lhsC[pp].b, dg.b]),
                                    (dg[:, 2, pp, :], lhsC[pp][:, 0, :], dg[:, 0, pp, :], lhsC[pp][:, 1, :], [lhsC[pp].b, dg.b])))
                                CP("act", lhsQ[:, t, 0:2, :, :], vw(pe_), [pe_.b], [lhsQ.b])
                                CP("act", lhsQ[:, t, 2:4, :, :], vw(pf), [pf.b], [lhsQ.b])
                        for pp in range(4):
                            pr = fc * 4 + pp
                            a0, a1, a2 = a64
                            TS("dve", a0[:], iot[:], S["th8"][:, d, pr:pr + 1], None, ALU.mult, None, [iot.b, S["th8"].b], [a0.b])
                            reduce_angle(a0, a1, a2, a64i)
                            ACT(tabS[:, pp, :], a1[:], AF.Sin, [a1.b], [tabS.b])
                            TS("dve", a0[:], a1[:], PI / 2, None, ALU.add, None, [a1.b], [a0.b])
                            reduce_angle(a0, a1, a2, a64i)
                            ACT(tabC[:, pp, :], a1[:], AF.Sin, [a1.b], [tabC.b])
                        TS("dve", tabN[:], tabS[:], -1.0, None, ALU.mult, None, [tabS.b], [tabN.b])
                        tbs = [tabC.b, tabS.b, tabN.b]
                        order = list(range(NTI)) if d == 0 else [0] + list(range(NTI - 1, 0, -1))
                        MSET("dve", Hre[:, :, 0:1], 0.0, [Hre.b])
                        MSET("dve", Him[:, :, 0:1], 0.0, [Him.b])
                        for oi, ti in enumerate(order):
                            t0, sz = TILES[ti]
                            NJ = sz // 8
                            useq = uT[:, fc, t0:t0 + sz] if d == 0 else uT[:, fc, t0:t0 + sz][:, ::-1]
                            us = [useq[:, s_::8] for s_ in range(8)]
                            V = Vt[it % 2]
                            W = Wk[it % 2]
                            hb = Hb[it % 2]
                            it += 1
                            att_finalize()
                            pv = nextps()
                            pvv = pv[:].rearrange("p (a b c) -> p a b c", a=4, b=2)
                            for pp in range(4):
                                for ri in range(2):
                                    for s_ in range(8):
                                        MM(pvv[:, pp, ri, 0:NJ], lhsP[:, 7 - s_, pp, ri, :], us[s_], s_ == 0, s_ == 7, [lhsP.b, uT.b], [pv.b])
                            CP("act", V[:, :, :, 0:NJ], pvv[:, :, :, 0:NJ], [pv.b], [V.b])
                            tC, tS, tN = tabC[:, :, 0:NJ], tabS[:, :, 0:NJ], tabN[:, :, 0:NJ]
                            vre, vim = V[:, :, 0, 0:NJ], V[:, :, 1, 0:NJ]
                            TT("dve", W["m1"][:, :, 0:NJ], vre, tC, ALU.mult, [V.b] + tbs, [W["m1"].b])
                            TT("dve", W["m2"][:, :, 0:NJ], vim, tS, ALU.mult, [V.b] + tbs, [W["m2"].b])
                            TT("dve", W["m1"][:, :, 0:NJ], W["m1"][:, :, 0:NJ], W["m2"][:, :, 0:NJ], ALU.add, [W["m1"].b, W["m2"].b], [W["m1"].b])
                            TT("pool", W["m3"][:, :, 0:NJ], vim, tC, ALU.mult, [V.b] + tbs, [W["m3"].b])
                            TT("pool", W["m4"][:, :, 0:NJ], vre, tN, ALU.mult, [V.b] + tbs, [W["m4"].b])
                            TT("pool", W["m3"][:, :, 0:NJ], W["m3"][:, :, 0:NJ], W["m4"][:, :, 0:NJ], ALU.add, [W["m3"].b, W["m4"].b], [W["m3"].b])
                            for pp in range(4):
                                pr = fc * 4 + pp
                                rho_b = S["rho8"][:, d, pr:pr + 1].to_broadcast([128, NJ])
                                SCAN(W["gr"][:, pp, 0:NJ], rho_b, W["m1"][:, pp, 0:NJ], Hre[:, pp, 0:1], [W["m1"].b, S["rho8"].b, Hre.b], [W["gr"].b])
                                SCAN(W["gi"][:, pp, 0:NJ], rho_b, W["m3"][:, pp, 0:NJ], Him[:, pp, 0:1], [W["m3"].b, S["rho8"].b, Him.b], [W["gi"].b])
                            TT("dve", W["m2"][:, :, 0:NJ], W["gr"][:, :, 0:NJ], tC, ALU.mult, [W["gr"].b] + tbs, [W["m2"].b])
                            TT("dve", W["m4"][:, :, 0:NJ], W["gi"][:, :, 0:NJ], tN, ALU.mult, [W["gi"].b] + tbs, [W["m4"].b])
                            TT("dve", Hre[:, :, 1:NJ + 1], W["m2"][:, :, 0:NJ], W["m4"][:, :, 0:NJ], ALU.add, [W["m2"].b, W["m4"].b], [Hre.b])
                            TT("pool", W["m1"][:, :, 0:NJ], W["gi"][:, :, 0:NJ], tC, ALU.mult, [W["gi"].b] + tbs, [W["m1"].b])
                            TT("pool", W["m3"][:, :, 0:NJ], W["gr"][:, :, 0:NJ], tS, ALU.mult, [W["gr"].b] + tbs, [W["m3"].b])
                            TT("pool", Him[:, :, 1:NJ + 1], W["m1"][:, :, 0:NJ], W["m3"][:, :, 0:NJ], ALU.add, [W["m1"].b, W["m3"].b], [Him.b])
                            CP("pool", hb[:, 0, :, 0:NJ], Hre[:, :, 0:NJ], [Hre.b], [hb.b])
                            CP("pool", hb[:, 1, :, 0:NJ], Him[:, :, 0:NJ], [Him.b], [hb.b])
                            att_issue()
                            py = nextps(long=True)
                            pyv = py[:].rearrange("p (t j) -> p t j", t=8)
                            for t in range(8):
                                nmm = (t + 1) + 8
                                imm = 0
                                for s_ in range(t + 1):
                                    imm += 1
                                    MM(pyv[:, t, 0:NJ], BD[:, t - s_, :], us[s_], imm == 1, imm == nmm, [BD.b, uT.b], [py.b])
                                for pp in range(4):
                                    for ri in range(2):
                                        imm += 1
                                        MM(pyv[:, t, 0:NJ], lhsQ[:, t, pp, ri, :], hb[:, ri, pp, 0:NJ], imm == 1, imm == nmm, [lhsQ.b, hb.b], [py.b])
                            yv = yacc[:, t0:t0 + sz] if d == 0 else yacc[:, t0:t0 + sz][:, ::-1]
                            yv = yv.rearrange("p (j t) -> p t j", t=8)
                            if d == 0:
                                CP("act", yv, pyv[:, :, 0:NJ], [py.b], [yacc.b])
                            else:
                                TT("dve", yv, pyv[:, :, 0:NJ], yv, ALU.add, [py.b, yacc.b], [yacc.b])
                            CP("dve", Hre[:, :, 0:1], Hre[:, :, NJ:NJ + 1], [Hre.b], [Hre.b])
                            CP("dve", Him[:, :, 0:1], Him[:, :, NJ:NJ + 1], [Him.b], [Him.b])
                    ACT(y2T[:, fc, :], yacc[:], AF.Gelu_apprx_tanh, [yacc.b], [y2T.b])
                while att_st["i"] < len(att_steps) or att_st["pend"]:
                    att_finalize()
                    att_issue()
                wg = SB(ph, "wg", [128, 4, 512], BF16)
                bg = SB(ph, "bg", [128, 4], F32)
                gs = [SB(ph, "gs%d" % i, [128, 512], F32) for i in range(2)]
                yst = [SB(ph, "yst%d" % i, [128, 512], BF16) for i in range(2)]
                dma("pool", wg[:], s5_wglu[l].rearrange("(k p) n -> p k n", p=128), w=[wg.b])
                dma("sp", bg[:], s5_bgT[:, l, :], w=[bg.b])
                for co in range(4):
                    for ti, (t0, sz) in enumerate(TILES):
                        ys = yst[ti % 2]
                        ps = nextps()
                        for k in range(4):
                            MM(ps[:, 0:sz], wg[:, k, co * 128:(co + 1) * 128], y2T[:, k, t0:t0 + sz], k == 0, k == 3, [wg.b, y2T.b], [ps.b])
                        g_ = gs[ti % 2]
                        ACT(g_[:, 0:sz], ps[:, 0:sz], AF.Sigmoid, [ps.b, bg.b], [g_.b], bias=bg[:, co:co + 1])
                        TT("dve", ys[:, 0:sz], y2T[:, co, t0:t0 + sz], g_[:, 0:sz], ALU.mult, [y2T.b, g_.b], [ys.b])
                        dma("sp", yT[co, :, t0:t0 + sz], ys[:, 0:sz], r=[ys.b], w=[d_y])
                P.barrier()
                nlong[0] = 2
            if stop_after == "S5":
                break
            if stop_after == "ATT":
                break
            with ExitStack() as ph:
                hgm = SB(ph, "hgm", [32, 2, 128], F32)
                rmask = SB(ph, "rmask", [128, 512], F32)
                hng = SB(ph, "hng", [128, 4], F32)
                dma("sp", hgm[:], c_hgmask[:, :, :], w=[hgm.b])
                dma("sp", rmask[:], c_rmask[:, :], w=[rmask.b])
                dma("sp", hng[:], hg_ngT[:, l, :], w=[hng.b])
                D2 = range(2)
                Sf = [SB(ph, "Sf%d" % d, [128, 4, 128], F32) for d in D2]
                Sb = [SB(ph, "Sb%d" % d, [128, 4, 128], BF16) for d in D2]
                lfts = [SB(ph, "lft%d" % d, [128, 4, 512], F32) for d in D2]
                kkts = [SB(ph, "kkt%d" % d, [128, 4, 512], BF16) for d in D2]
                hqts = [SB(ph, "hqt%d" % d, [128, 4, 512], BF16) for d in D2]
                vchs = [SB(ph, "vch%d" % d, [32, 16, 512], BF16) for d in D2]
                bts = [SB(ph, "hbt%d" % d, [128, 4, 512], F32) for d in D2]
                e1s = [SB(ph, "he1%d" % d, [128, 4, 512], F32) for d in D2]
                e2s = [SB(ph, "he2%d" % d, [128, 4, 512], F32) for d in D2]
                qts = [SB(ph, "hqt_%d" % d, [128, 4, 512], BF16) for d in D2]
                kts = [SB(ph, "hkt_%d" % d, [128, 4, 512], BF16) for d in D2]
                khs = [SB(ph, "hkh_%d" % d, [128, 4, 512], BF16) for d in D2]
                ots = [SB(ph, "hot%d" % d, [128, 4, 512], F32) for d in D2]
                attm = [[SB(ph, "attm%d_%d" % (d, i), [32, 128], BF16) for i in range(2)] for d in D2]
                ktm = [[SB(ph, "ktm%d_%d" % (d, i), [32, 512], BF16) for i in range(2)] for d in D2]
                orders = [list(range(NTI)), [0] + list(range(NTI - 1, 0, -1))]
                for d in D2:
                    MSET("pool", Sf[d][:], 0.0, [Sf[d].b])
                    MSET("pool", Sb[d][:], 0.0, [Sb[d].b])
                ich = [0, 0]

                def hg_setup(d, ti):
                    t0, sz = TILES[ti]
                    nch = sz // 32
                    lft, kkt, hqt, vch, bt, e1, e2, qt, kt, kh = lfts[d], kkts[d], hqts[d], vchs[d], bts[d], e1s[d], e2s[d], qts[d], kts[d], khs[d]
                    dma("sp", lft[:, :, 0:sz], lf[d, :, :, t0:t0 + sz].rearrange("h p t -> p h t"), r=[d_z], w=[lft.b])
                    dma("sp", kkt[:, :, 0:sz], kk[d, :, :, t0:t0 + sz].rearrange("h p t -> p h t"), r=[d_z], w=[kkt.b])
                    dma("sp", hqt[:, :, 0:sz], hgq[:, :, t0:t0 + sz].rearrange("h p t -> p h t"), r=[d_z], w=[hqt.b])
                    dma("sp", vch[:, 0:nch, :], vtm[t0:t0 + sz, 128:640].rearrange("(c p) f -> p c f", p=32), r=[d_z], w=[vch.b])
                    for h in range(4):
                        if d == 0:
                            SCAN(bt[:, h, 0:sz], rmask[:, 0:sz], lft[:, h, 0:sz], 0.0, [rmask.b, lft.b], [bt.b])
                        else:
                            SCAN(bt[:, h, 0:sz][:, ::-1], rmask[:, 0:sz], lft[:, h, 0:sz][:, ::-1], 0.0, [rmask.b, lft.b], [bt.b])
                    jl0 = 31 if d == 0 else 0
                    b4 = bt[:, :, 0:sz].rearrange("p h (c t) -> p h c t", t=32)
                    TT("dve", e2[:, :, 0:sz].rearrange("p h (c t) -> p h c t", t=32), b4[:, :, :, jl0:jl0 + 1].to_broadcast([128, 4, nch, 32]), b4,
                       ALU.subtract, [bt.b], [e2.b])
                    ACT(e1[:, :, 0:sz], bt[:, :, 0:sz], AF.Exp, [bt.b], [e1.b], scale=-1.0)
                    ACT(e2[:, :, 0:sz], e2[:, :, 0:sz], AF.Exp, [e2.b], [e2.b])
                    ACT(bt[:, :, 0:sz], bt[:, :, 0:sz], AF.Exp, [bt.b], [bt.b])
                    TT("dve", qt[:, :, 0:sz], hqt[:, :, 0:sz], bt[:, :, 0:sz], ALU.mult, [hqt.b, bt.b], [qt.b])
                    TT("pool", kt[:, :, 0:sz], kkt[:, :, 0:sz], e1[:, :, 0:sz], ALU.mult, [kkt.b, e1.b], [kt.b])
                    TT("dve", kh[:, :, 0:sz], kkt[:, :, 0:sz], e2[:, :, 0:sz], ALU.mult, [kkt.b, e2.b], [kh.b])

                def hg_chunk(d, ci):
                    vch, bt, qt, kt, kh, ot = vchs[d], bts[d], qts[d], kts[d], khs[d], ots[d]
                    c0 = ci * 32
                    am, km = attm[d][ich[d] % 2], ktm[d][ich[d] % 2]
                    ich[d] += 1
                    psA = nextps()
                    for h in range(4):
                        MM(psA[0:32, h * 32:(h + 1) * 32], kt[:, h, c0:c0 + 32], qt[:, h, c0:c0 + 32], True, True, [kt.b, qt.b], [psA.b])
                    TT("dve", am[:], psA[0:32, 0:128], hgm[:, d, :], ALU.mult, [psA.b, hgm.b], [am.b])
                    psT = nextps()
                    psTb = psT[:].bitcast(BF16)
                    for h in range(4):
                        TR(psTb[0:32, h * 128:(h + 1) * 128], kh[:, h, c0:c0 + 32], ident_b[:], [kh.b, ident_b.b], [psT.b])
                    CP("act", km[:], psTb[0:32, 0:512], [psT.b], [km.b])
                    psO = nextps()
                    for h in range(4):
                        MM(psO[:, h * 32:(h + 1) * 32], vch[:, ci, h * 128:(h + 1) * 128], am[:, h * 32:(h + 1) * 32], True, False, [vch.b, am.b], [psO.b])
                        MM(psO[:, h * 32:(h + 1) * 32], Sb[d][:, h, :], qt[:, h, c0:c0 + 32], False, True, [Sb[d].b, qt.b], [psO.b])
                    CP("act", ot[:, :, c0:c0 + 32], psO[:, 0:128].rearrange("p (h t) -> p h t", h=4), [psO.b], [ot.b])
                    psS = nextps()
                    for h in range(4):
                        MM(psS[:, h * 128:(h + 1) * 128], km[:, h * 128:(h + 1) * 128], vch[:, ci, h * 128:(h + 1) * 128], True, True, [km.b, vch.b], [psS.b])
                    jl = c0 + 31 if d == 0 else c0
                    for h in range(4):
                        STT(Sf[d][:, h, :], Sf[d][:, h, :], bt[:, h, jl:jl + 1], psS[:, h * 128:(h + 1) * 128], ALU.mult, ALU.add, [Sf[d].b, bt.b, psS.b], [Sf[d].b])
                    CP("act", Sb[d][:], Sf[d][:], [Sf[d].b], [Sb[d].b])

                obw = ofw2
                for oi in range(NTI):
                    for d in D2:
                        hg_setup(d, orders[d][oi])
                    nch = TILES[orders[0][oi]][1] // 32
                    for k_ in range(nch):
                        for d in D2:
                            hg_chunk(d, k_ if d == 0 else nch - 1 - k_)
                    for d in D2:
                        t0, sz = TILES[orders[d][oi]]
                        dma("pool", (ofw if d == 0 else obw)[:, :, t0:t0 + sz].rearrange("h p t -> p h t"), ots[d][:, :, 0:sz], r=[ots[d].b], w=[d_of])
                sq = SB(ph, "hsq", [128, 4, 512], BF16)
                rs = SB(ph, "hrs", [128, 4, 512], F32)
                hggts = [SB(ph, "hggt%d" % i, [128, 4, 512], BF16) for i in range(2)]
                ysts = [SB(ph, "hyst%d" % i, [128, 4, 512], BF16) for i in range(2)]
                for ti, (t0, sz) in enumerate(TILES):
                    oa, obt, yh = ots[ti % 2], e1s[ti % 2], e2s[ti % 2]
                    hggt, yst = hggts[ti % 2], ysts[ti % 2]
                    dma("sp", oa[:, :, 0:sz], ofw[:, :, t0:t0 + sz].rearrange("h p t -> p h t"), r=[d_of], w=[oa.b])
                    dma("sp", obt[:, :, 0:sz], obw[:, :, t0:t0 + sz].rearrange("h p t -> p h t"), r=[d_of], w=[obt.b])
                    dma("sp", hggt[:, :, 0:sz], hgg[:, :, t0:t0 + sz].rearrange("h p t -> p h t"), r=[d_z], w=[hggt.b])
                    TT("pool", oa[:, :, 0:sz], oa[:, :, 0:sz], obt[:, :, 0:sz], ALU.add, [oa.b, obt.b], [oa.b])
                    ACT(sq[:, :, 0:sz], oa[:, :, 0:sz], AF.Square, [oa.b], [sq.b])
                    for h in range(4):
                        ps = nextps()
                        MM(ps[:, 0:sz], ones_b[:], sq[:, h, 0:sz], True, True, [ones_b.b, sq.b], [ps.b])
                        ACT(rs[:, h, 0:sz], ps[:, 0:sz], AF.Ln, [ps.b, epsb.b], [rs.b], bias=epsb[:, 0:1], scale=1.0 / 128)
                    ACT(rs[:, :, 0:sz], rs[:, :, 0:sz], AF.Exp, [rs.b], [rs.b], scale=-0.5)
                    for h in range(4):
                        STT(yh[:, h, 0:sz], oa[:, h, 0:sz], hng[:, h:h + 1], rs[:, h, 0:sz], ALU.mult, ALU.mult, [oa.b, hng.b, rs.b], [yh.b])
                    TT("pool", yst[:, :, 0:sz], yh[:, :, 0:sz], hggt[:, :, 0:sz], ALU.mult, [yh.b, hggt.b], [yst.b])
                    dma("pool", yT[8:12, :, t0:t0 + sz].rearrange("h p t -> p h t"), yst[:, :, 0:sz], r=[yst.b], w=[d_y])
                P.barrier()
            if stop_after == "HG":
                break
            with ExitStack() as ph:
                wbr = SB(ph, "wbr", [128, 12, DM], BF16)
                wou = SB(ph, "wou", [128, KC, DM], BF16)
                for n in range(3):
                    dma("pool", wbr[:, n * 4:(n + 1) * 4, :], w_branch[l, n].rearrange("(k p) d -> p k d", p=128), w=[wbr.b])
                dma("pool", wou[:], w_out[l].rearrange("(k p) d -> p k d", p=128), w=[wou.b])
                yts = [SB(ph, "yt%d" % i, [128, 12, 512], BF16) for i in range(2)]
                gts = [SB(ph, "gt%d" % i, [128, 24, 512], BF16) for i in range(2)]
                xt = SB(ph, "mxt", [128, KC, 512], F32)
                mots = [SB(ph, "mot%d" % i, [128, KC, 512], F32) for i in range(2)]
                mt = SB(ph, "mmt", [128, KC, 512], BF16)
                macc = [SB(ph, "macc%d" % i, [128, 512], F32) for i in range(2)]
                mtmp = [SB(ph, "mtmp%d" % i, [128, 512], F32) for i in range(2)]
                sq = SB(ph, "msq", [128, KC, 512], BF16)
                rs = SB(ph, "mrs", [128, 512], F32)
                for ti, (t0, sz) in enumerate(TILES):
                    yt, gt = yts[ti % 2], gts[ti % 2]
                    ot = mots[ti % 2]
                    dma("sp", yt[:, :, 0:sz], yT[:, :, t0:t0 + sz].rearrange("c p t -> p c t"), r=[d_y], w=[yt.b])
                    dma("sp", gt[:, :, 0:sz], gat[:, :, t0:t0 + sz].rearrange("c p t -> p c t"), r=[d_z], w=[gt.b])
                    dma("sp", xt[:, :, 0:sz], xT[:, :, t0:t0 + sz].rearrange("k p t -> p k t"), r=[d_xT], w=[xt.b])
                    for dc in range(KC):
                        ma, mp_ = macc[dc % 2], mtmp[dc % 2]
                        for n in range(3):
                            ps = nextps()
                            for k in range(4):
                                MM(ps[:, 0:sz], wbr[:, n * 4 + k, dc * 128:(dc + 1) * 128], yt[:, n * 4 + k, 0:sz], k == 0, k == 3, [wbr.b, yt.b], [ps.b])
                            g_ = gt[:, n * 8 + dc, 0:sz]
                            if n == 0:
                                TT("dve", ma[:, 0:sz], ps[:, 0:sz], g_, ALU.mult, [ps.b, gt.b], [ma.b])
                            elif n == 1:
                                TT("dve", mp_[:, 0:sz], ps[:, 0:sz], g_, ALU.mult, [ps.b, gt.b], [mp_.b])
                                TT("pool", ma[:, 0:sz], ma[:, 0:sz], mp_[:, 0:sz], ALU.add, [ma.b, mp_.b], [ma.b])
                            else:
                                TT("dve", mp_[:, 0:sz], ps[:, 0:sz], g_, ALU.mult, [ps.b, gt.b], [mp_.b])
                                TT("pool", mt[:, dc, 0:sz], ma[:, 0:sz], mp_[:, 0:sz], ALU.add, [ma.b, mp_.b], [mt.b])
                    for dc in range(KC):
                        ps = nextps()
                        for kc in range(KC):
                            MM(ps[:, 0:sz], wou[:, kc, dc * 128:(dc + 1) * 128], mt[:, kc, 0:sz], kc == 0, kc == KC - 1, [wou.b, mt.b], [ps.b])
                        CP("act", ot[:, dc, 0:sz], ps[:, 0:sz], [ps.b], [ot.b])
                    epilogue(ot, xt, G1, ti, t0, sz, sq, rs, None)
                P.barrier()
            if stop_after == "MIX":
                break
            with ExitStack() as ph:
                hT = SB(ph, "hT2", [128, KC, NT], BF16, nb=NTI)
                with ExitStack() as ph1:
                    norm_phase(ph1, l, A2, 24, hT)
                    P.barrier()
                NU = NT + 3
                Uas = [SB(ph, "Ua%d" % i, [128, NU], F32) for i in range(2)]
                Ugs = [SB(ph, "Ug%d" % i, [128, NU], F32) for i in range(2)]
                Ya = SB(ph, "Ya", [128, NU], F32)
                Yg = SB(ph, "Yg", [128, NU], F32)
                ast = [SB(ph, "ast%d" % i, [128, NU], BF16) for i in range(2)]
                cw = SB(ph, "cw", [128, 44, 3], F32)
                cb = SB(ph, "cb", [128, 44], F32)
                wua = [SB(ph, "wua%d" % i, [128, KC, 128], BF16) for i in range(2)]
                wug = [SB(ph, "wug%d" % i, [128, KC, 128], BF16) for i in range(2)]
                dma("sp", cw[:], convT[:, l, :, :], w=[cw.b])
                dma("sp", cb[:], convbT[:, l, :], w=[cb.b])
                for i_ in range(2):
                    MSET("pool", Uas[i_][:], 0.0, [Uas[i_].b])
                    MSET("pool", Ugs[i_][:], 0.0, [Ugs[i_].b])

                def ucol(t):
                    return t + 1 if t < LC else t + 2
                NY = NT + 1
                for j in range(NFC):
                    wa, wg_ = wua[j % 2], wug[j % 2]
                    Ua, Ug = Uas[j % 2], Ugs[j % 2]
                    dma("pool", wa[:], w_up[l, :, j * 128:(j + 1) * 128].rearrange("(k p) n -> p k n", p=128), w=[wa.b])
                    dma("pool", wg_[:], w_up[l, :, FFN + j * 128:FFN + (j + 1) * 128].rearrange("(k p) n -> p k n", p=128), w=[wg_.b])
                    for (wt, U) in ((wa, Ua), (wg_, Ug)):
                        for ti, (t0, sz) in enumerate(TILES):
                            ps = nextps()
                            for kc in range(KC):
                                MM(ps[:, 0:sz], wt[:, kc, :], hT[:, kc, t0:t0 + sz], kc == 0, kc == KC - 1, [wt.b, hT.bs[ti]], [ps.b])
                            CP("act", U[:, ucol(t0):ucol(t0) + sz], ps[:, 0:sz], [ps.b], [U.b])
                    for (U, Y, cj) in ((Ua, Ya, j), (Ug, Yg, NFC + j)):
                        ACT(Y[:, 0:NY], U[:, 0:NY], AF.Identity, [U.b, cw.b, cb.b], [Y.b], bias=cb[:, cj:cj + 1], scale=cw[:, cj, 0:1])
                        STT(Y[:, 0:NY], U[:, 1:NY + 1], cw[:, cj, 1:2], Y[:, 0:NY], ALU.mult, ALU.add, [U.b, cw.b, Y.b], [Y.b])
                        STT(Y[:, 0:NY], U[:, 2:NY + 2], cw[:, cj, 2:3], Y[:, 0:NY], ALU.mult, ALU.add, [U.b, cw.b, Y.b], [Y.b])
                    a_ = ast[j % 2]
                    ACT(Ya[:, 0:NY], Ya[:, 0:NY], AF.Silu, [Ya.b], [Ya.b])
                    TT("dve", a_[:, 0:NY], Ya[:, 0:NY], Yg[:, 0:NY], ALU.mult, [Ya.b, Yg.b], [a_.b])
                    dma("sp", actT[j, :, 0:LC], a_[:, 0:LC], r=[a_.b], w=[d_act])
                    dma("sp", actT[j, :, LC:NT], a_[:, LC + 1:NT + 1], r=[a_.b], w=[d_act])
                P.barrier()
            with ExitStack() as ph:
                wdn = SB(ph, "wdn", [128, NFC, DM], BF16)
                dma("pool", wdn[:, 0:11, :], w_down[l, 0:11 * 128, :].rearrange("(k p) d -> p k d", p=128), w=[wdn.b])
                dma("pool", wdn[:, 11:22, :], w_down[l, 11 * 128:22 * 128, :].rearrange("(k p) d -> p k d", p=128), w=[wdn.b])
                ats = [SB(ph, "at%d" % i, [128, NFC, 512], BF16) for i in range(2)]
                xt = SB(ph, "fxt", [128, KC, 512], F32)
                fots = [SB(ph, "fot%d" % i, [128, KC, 512], F32) for i in range(2)]
                sq = SB(ph, "fsq", [128, KC, 512], BF16)
                rs = SB(ph, "frs", [128, 512], F32)
                for ti, (t0, sz) in enumerate(TILES):
                    at = ats[ti % 2]
                    ot = fots[ti % 2]
                    dma("sp", at[:, :, 0:sz], actT[:, :, t0:t0 + sz].rearrange("c p t -> p c t"), r=[d_act], w=[at.b])
                    dma("sp", xt[:, :, 0:sz], xT[:, :, t0:t0 + sz].rearrange("k p t -> p k t"), r=[d_xT], w=[xt.b])
                    for dc in range(KC):
                        ps = nextps()
                        for k in range(NFC):
                            MM(ps[:, 0:sz], wdn[:, k, dc * 128:(dc + 1) * 128], at[:, k, 0:sz], k == 0, k == NFC - 1, [wdn.b, at.b], [ps.b])
                        CP("act", ot[:, dc, 0:sz], ps[:, 0:sz], [ps.b], [ot.b])
                    epilogue(ot, xt, G2, ti, t0, sz, sq, rs, None)
                P.barrier()
        if stop_after is None:
            with ExitStack() as ph:
                xtt = [SB(ph, "fxtt%d" % i, [128, KC, 128], F32) for i in range(2)]
                orow = [SB(ph, "orow%d" % i, [128, DM], F32) for i in range(2)]
                for tb in range(LC // 128, NB):
                    a, o_ = xtt[tb % 2], orow[tb % 2]
                    dma("sp", a[:], xT[:, :, tb * 128:(tb + 1) * 128].rearrange("k p t -> p k t"), r=[d_xT], w=[a.b])
                    pa, pb = nextps(), nextps()
                    for kc in range(KC):
                        pp_ = pa if kc < 4 else pb
                        TR(pp_[:, (kc % 4) * 128:(kc % 4 + 1) * 128], a[:, kc, :], ident_f[:], [a.b, ident_f.b], [pp_.b])
                    CP("act", o_[:, 0:512], pa[:, :], [pa.b], [o_.b])
                    CP("dve", o_[:, 512:1024], pb[:, :], [pb.b], [o_.b])
                    dma("pool", out[tb * 128 - LC:(tb + 1) * 128 - LC, :], o_[:], r=[o_.b])
        P.barrier()
    return nc


def prep_shared(inp, cfg):
    L, LC, DEPTH = cfg["L"], cfg["LC"], cfg["DEPTH"]
    NT = L + LC
    f = lambda a: np.ascontiguousarray(np.asarray(a, dtype=np.float32))
    sh = {}
    sh["w_mod"] = f(inp["w_mod"][:DEPTH])
    sh["b_modT"] = f(np.asarray(inp["b_mod"])[:DEPTH].reshape(DEPTH, 48, 128).transpose(2, 0, 1))
    sh["norm_gT"] = f(np.asarray(inp["norm_g"])[:DEPTH].reshape(DEPTH, 4, KC, 128).transpose(3, 0, 1, 2))
    w_in = np.asarray(inp["w_in"])[:DEPTH]
    sh["w_in"] = f(w_in)
    idx = []
    for h in range(8):
        idx += [C_Q + h * 64 + (d + 32) % 64 for d in range(64)]
    for h in range(2):
        idx += [C_K + h * 64 + (d + 32) % 64 for d in range(64)]
    sh["w_rot"] = f(w_in[:, :, np.array(idx)])
    rows = L // 64
    row = np.repeat(np.arange(rows, dtype=np.float32), 64)
    col = np.tile(np.arange(64, dtype=np.float32), rows)
    inv = (10000.0 ** (-np.arange(16, dtype=np.float32) / 16)).astype(np.float32)
    ang = np.concatenate([row[:, None] * inv, col[:, None] * inv], axis=-1)
    cos, sin = np.cos(ang).T, np.sin(ang).T
    C = np.ones((64, NT), np.float32)
    S = np.zeros((64, NT), np.float32)
    C[0:32, LC:] = cos
    C[32:64, LC:] = cos
    S[0:32, LC:] = -sin
    S[32:64, LC:] = sin
    sh["ropeC"], sh["ropeS"] = np.concatenate([C, C], 0), np.concatenate([S, S], 0)
    def st(a):
        a = np.asarray(a)[:DEPTH].reshape(DEPTH, 2, 16, 2, 64)
        return f(a.transpose(3, 4, 0, 1, 2).reshape(128, DEPTH, 2, 16))
    sh["s5_lr"] = st(inp["s5_lam_re"])
    sh["s5_li"] = st(inp["s5_lam_im"])
    ldt = np.asarray(inp["s5_log_dt"])[:DEPTH]
    sh["s5_ldt"] = st(np.repeat(ldt[..., None], 64, axis=-1))
    def padB(b):
        b = np.asarray(b)[:DEPTH]
        o = np.zeros((DEPTH, 16, 128, 128), np.float32)
        for pr in range(16):
            for g2 in range(2):
                s0 = (pr % 4) * 32 + g2 * 16
                o[:, pr, g2 * 64:(g2 + 1) * 64, s0:s0 + 16] = b[:, 2 * pr + g2]
        return o
    def padC(c):
        c = np.asarray(c)[:DEPTH]
        o = np.zeros((DEPTH, 16, 128, 128), np.float32)
        for pr in range(16):
            for g2 in range(2):
                s0 = (pr % 4) * 32 + g2 * 16
                o[:, pr, g2 * 64:(g2 + 1) * 64, s0:s0 + 16] = c[:, 2 * pr + g2].transpose(0, 2, 1)
        return o
    sh["s5_Bre"], sh["s5_Bim"] = padB(inp["s5_b_re"]), padB(inp["s5_b_im"])
    sh["s5_Cre"], sh["s5_Cim"] = padC(inp["s5_c_re"]), padC(inp["s5_c_im"])
    sh["s5_dT"] = f(np.asarray(inp["s5_d"])[:DEPTH].reshape(DEPTH, 4, 128).transpose(2, 0, 1))
    sh["s5_bgT"] = f(np.asarray(inp["s5_b_glu"])[:DEPTH].reshape(DEPTH, 4, 128).transpose(2, 0, 1))
    sh["s5_wglu"] = f(inp["s5_w_glu"][:DEPTH])
    sh["sinkrow"] = f(np.repeat(np.asarray(inp["att_sink"])[:DEPTH], 128, axis=-1).reshape(DEPTH, 1, 1024))
    sh["hg_lbT"] = f(np.asarray(inp["hg_lb_logits"])[:DEPTH].reshape(DEPTH, 2, 4, 128).transpose(3, 0, 1, 2).reshape(128, DEPTH, 8))
    sh["hg_ngT"] = f(np.asarray(inp["hg_norm_g"])[:DEPTH].reshape(DEPTH, 4, 128).transpose(2, 0, 1))
    sh["w_branch"] = f(inp["w_branch"][:DEPTH])
    sh["w_out"] = f(inp["w_out"][:DEPTH])
    sh["w_up"] = f(inp["ffn_w_up"][:DEPTH])
    sh["convT"] = f(np.asarray(inp["ffn_conv_w"])[:DEPTH].reshape(DEPTH, 3, 44, 128).transpose(3, 0, 2, 1))
    sh["convbT"] = f(np.asarray(inp["ffn_conv_b"])[:DEPTH].reshape(DEPTH, 44, 128).transpose(2, 0, 1))
    sh["w_down"] = f(inp["ffn_w_down"][:DEPTH])
    sh["c_ident"] = np.eye(128, dtype=np.float32)
    k = np.arange(128)[:, None]
    q = np.tile(np.arange(128), 4)[None, :]
    sh["c_maskP"] = np.where(k >= q, 0.0, -30000.0).astype(np.float32)
    sh["c_maskN"] = np.where(k <= q, 0.0, -30000.0).astype(np.float32)
    io = np.zeros((128, 2, 512), np.float32)
    io[:, 0, :] = np.arange(1, 513, dtype=np.float32)[None]
    io[:, 1, :] = (512 - np.arange(512, dtype=np.float32))[None]
    sh["c_iota"] = io
    s_ = np.arange(32)[:, None]
    t_ = np.tile(np.arange(32), 4)[None, :]
    hm = np.zeros((32, 2, 128), np.float32)
    hm[:, 0, :] = (s_ <= t_)
    hm[:, 1, :] = (s_ >= t_)
    sh["c_hgmask"] = hm
    rm = np.ones((128, 512), np.float32)
    rm[:, ::32] = 0.0
    sh["c_rmask"] = rm
    return sh


def prep_core(inp, b, cfg, sh):
    L, LC = cfg["L"], cfg["LC"]
    m = dict(sh)
    m["x"] = np.ascontiguousarray(np.asarray(inp["x"], dtype=np.float32)[b, :L])
    m["ctx"] = np.ascontiguousarray(np.asarray(inp["ctx"], dtype=np.float32)[b, :LC])
    cT = np.stack([np.asarray(inp["c"], dtype=np.float32)[b], np.asarray(inp["c_ctx"], dtype=np.float32)], axis=-1)
    m["cT"] = np.ascontiguousarray(cT.reshape(KC, 128, 2).transpose(1, 0, 2))
    return m


_NC_CACHE = {}


def kernel(**inputs):
    cfg = FULL
    key = "full"
    if key not in _NC_CACHE:
        _NC_CACHE[key] = build(cfg)
    nc = _NC_CACHE[key]
    sh = prep_shared(inputs, cfg)
    in_maps = [prep_core(inputs, b, cfg, sh) for b in range(8)]
    res = run_bass_kernel_spmd(nc, in_maps, core_ids=list(range(8)))
    return np.stack([np.asarray(r["out"], dtype=np.float32) for r in res.results], axis=0)
```

```python
import numpy as np
import ml_dtypes
from contextlib import ExitStack
import concourse.bass as bass
import concourse.mybir as mybir
from concourse.bass_utils import run_bass_kernel_spmd

F32 = mybir.dt.float32
BF16 = mybir.dt.bfloat16
I32 = mybir.dt.int32
AF = mybir.ActivationFunctionType
ALU = mybir.AluOpType
PI = float(np.pi)


class Buf:
    __slots__ = ("w", "r")

    def __init__(self):
        self.w = None
        self.r = {}


class Prog:
    def __init__(self, nc, es, ndma=48):
        self.nc = nc
        self.eng = {"pe": nc.tensor, "act": nc.scalar, "dve": nc.vector, "pool": nc.gpsimd, "sp": nc.sync}
        self.sem = {}
        self.cnt = {}
        for k in self.eng:
            self.sem[k] = es.enter_context(nc.semaphore("s_" + k))
            self.cnt[k] = 0
        self.ndma = ndma
        for i in range(ndma):
            self.sem[("d", i)] = es.enter_context(nc.semaphore("s_d%d" % i))
            self.cnt[("d", i)] = 0
        self.known = {e: {} for e in self.eng}
        self.rr = 0
        self.nins = 0

    def _waits(self, e, r, w, extra=()):
        need = {}
        kn = self.known[e]

        def add(tok, same_ok):
            if tok is None:
                return
            k, v = tok
            if k == e and (e == "pe" or not same_ok):
                return
            if kn.get(k, 0) >= v:
                return
            if need.get(k, 0) < v:
                need[k] = v

        for b in r:
            add(b.w, True)
        for b in w:
            add(b.w, True)
            for t in b.r.values():
                add(t, False)
        for t in extra:
            add(t, True)
        E = self.eng[e]
        for k, v in need.items():
            E.wait_ge(self.sem[k], v)
            kn[k] = v
            self.nins += 1

    def op(self, e, fn, r=(), w=()):
        self._waits(e, r, w)
        ins = fn(self.eng[e])
        self.cnt[e] += 1
        ins.then_inc(self.sem[e], 1)
        self.nins += 1
        tok = (e, self.cnt[e])
        for b in r:
            b.r[e] = tok
        for b in w:
            b.w = tok
            b.r = {}
        return tok

    def dma(self, q, out, in_, r=(), w=()):
        i = self.rr
        self.rr = (i + 1) % self.ndma
        key = ("d", i)
        self._waits(q, r, w, extra=[(key, self.cnt[key])])
        ins = self.eng[q].dma_start(out=out, in_=in_)
        self.cnt[key] += 16
        ins.then_inc(self.sem[key], 16)
        self.nins += 1
        tok = (key, self.cnt[key])
        for b in r:
            b.r[key] = tok
        for b in w:
            b.w = tok
            b.r = {}
        return tok

    def barrier(self):
        toks = [(k, v) for k, v in self.cnt.items() if v > 0]
        for e, E in self.eng.items():
            kn = self.known[e]
            for k, v in toks:
                if kn.get(k, 0) < v:
                    E.wait_ge(self.sem[k], v)
                    kn[k] = v
                    self.nins += 1


class Tl:
    def __init__(self, t, nb=1):
        self.t = t
        self.bs = [Buf() for _ in range(nb)]

    @property
    def b(self):
        return self.bs[0]

    def __getitem__(self, k):
        return self.t[k]


DM = 1024
KC = 8
EPS = 1e-6
IN_COLS = 6912
C_S5, C_Q, C_K, C_V, C_HQ, C_FF, C_FB, C_HI, C_HG, C_GATE = 0, 512, 1024, 1152, 1280, 1792, 2304, 2816, 3328, 3840
FFN = 2816
NFC = 22
FULL = dict(L=4096, LC=256, DEPTH=4, taps=())


def build(cfg):
    L, LC, DEPTH = cfg["L"], cfg["LC"], cfg["DEPTH"]
    taps = set(cfg.get("taps", ()))
    stop_after = cfg.get("stop_after", None)
    NT = L + LC
    NB = NT // 128
    TILES = [(0, LC)] + [(LC + 512 * i, 512) for i in range(L // 512)]
    NTI = len(TILES)

    def tile_of(tok):
        for i, (t0, sz) in enumerate(TILES):
            if t0 <= tok < t0 + sz:
                return i
        raise ValueError

    nc = bass.Bass("TRN2", target_bir_lowering=False)

    def din(name, shape, dt=F32):
        return nc.dram_tensor(name, list(shape), dt, kind="ExternalInput").ap()

    def dscr(name, shape, dt):
        kind = "ExternalOutput" if name in taps else "Internal"
        return nc.dram_tensor(name, list(shape), dt, kind=kind).ap()

    x_in = din("x", [L, DM])
    ctx_in = din("ctx", [LC, DM])
    cT_in = din("cT", [128, KC, 2])
    w_mod = din("w_mod", [DEPTH, DM, 6 * DM])
    b_modT = din("b_modT", [128, DEPTH, 48])
    norm_gT = din("norm_gT", [128, DEPTH, 4, KC])
    w_in = din("w_in", [DEPTH, DM, IN_COLS])
    w_rot = din("w_rot", [DEPTH, DM, 640])
    ropeC_in = din("ropeC", [128, NT])
    ropeS_in = din("ropeS", [128, NT])
    s5_lr = din("s5_lr", [128, DEPTH, 2, 16])
    s5_li = din("s5_li", [128, DEPTH, 2, 16])
    s5_ldt = din("s5_ldt", [128, DEPTH, 2, 16])
    s5_Bre = din("s5_Bre", [DEPTH, 16, 128, 128])
    s5_Bim = din("s5_Bim", [DEPTH, 16, 128, 128])
    s5_Cre = din("s5_Cre", [DEPTH, 16, 128, 128])
    s5_Cim = din("s5_Cim", [DEPTH, 16, 128, 128])
    s5_dT = din("s5_dT", [128, DEPTH, 4])
    s5_bgT = din("s5_bgT", [128, DEPTH, 4])
    s5_wglu = din("s5_wglu", [DEPTH, 512, 512])
    sinkrow = din("sinkrow", [DEPTH, 1, 1024])
    hg_lbT = din("hg_lbT", [128, DEPTH, 8])
    hg_ngT = din("hg_ngT", [128, DEPTH, 4])
    w_branch = din("w_branch", [DEPTH, 3, 512, DM])
    w_out = din("w_out", [DEPTH, DM, DM])
    w_up = din("w_up", [DEPTH, DM, 2 * FFN])
    convT = din("convT", [128, DEPTH, 44, 3])
    convbT = din("convbT", [128, DEPTH, 44])
    w_down = din("w_down", [DEPTH, FFN, DM])
    c_ident = din("c_ident", [128, 128])
    c_maskP = din("c_maskP", [128, 512])
    c_maskN = din("c_maskN", [128, 512])
    c_iota = din("c_iota", [128, 2, 512])
    c_hgmask = din("c_hgmask", [32, 2, 128])
    c_rmask = din("c_rmask", [128, 512])
    out = nc.dram_tensor("out", [L, DM], F32, kind="ExternalOutput").ap()

    xT = dscr("xT", [KC, 128, NT], F32)
    zs5 = dscr("zs5", [4, 128, NT], BF16)
    qT = dscr("qT", [8, 64, NT], BF16)
    kT = dscr("kT", [2, 64, NT], BF16)
    vtm = dscr("vtm", [NT, 640], BF16)
    hgq = dscr("hgq", [4, 128, NT], BF16)
    lf = dscr("lf", [2, 4, 128, NT], F32)
    kk = dscr("kk", [2, 4, 128, NT], BF16)
    hgg = dscr("hgg", [4, 128, NT], BF16)
    gat = dscr("gat", [24, 128, NT], BF16)
    yT = dscr("yT", [12, 128, NT], BF16)
    ofw = dscr("ofw", [4, 128, NT], F32)
    ofw2 = dscr("ofw2", [4, 128, NT], F32)
    actT = dscr("actT", [NFC, 128, NT], BF16)
    dbg = dscr("dbg", [128, 4096], F32)
    d_xT, d_z, d_y, d_of, d_act = Buf(), Buf(), Buf(), Buf(), Buf()

    with ExitStack() as es:
        P = Prog(nc, es)
        op, dma = P.op, P.dma

        uid = [0]

        def SB(stack, name, shape, dt, nb=1):
            uid[0] += 1
            return Tl(stack.enter_context(nc.sbuf_tensor("%s_u%d" % (name, uid[0]), list(shape), dt)), nb)

        psum = [Tl(es.enter_context(nc.psum_tensor("ps%d" % i, [128, 512], F32))) for i in range(8)]
        psrr = [0, 0]
        nlong = [2]

        def nextps(long=False):
            if long:
                p = psum[psrr[1] % nlong[0]]
                psrr[1] += 1
            else:
                p = psum[nlong[0] + psrr[0] % (8 - nlong[0])]
                psrr[0] += 1
            return p

        def MM(o, l, r_, st, sp, rb, wb):
            op("pe", lambda E: E.matmul(o, l, r_, start=st, stop=sp), r=rb, w=wb)

        def TR(o, i, idn, rb, wb):
            op("pe", lambda E: E.transpose(o, i, idn), r=rb, w=wb)

        def ACT(o, i, f, rb, wb, bias=None, scale=None):
            kw = {}
            if bias is not None:
                kw["bias"] = bias
            if scale is not None:
                kw["scale"] = scale
            op("act", lambda E: E.activation(out=o, in_=i, func=f, **kw), r=rb, w=wb)

        def CP(e, o, i, rb, wb):
            if e == "act":
                op("act", lambda E: E.copy(out=o, in_=i), r=rb, w=wb)
            else:
                op(e, lambda E: E.tensor_copy(out=o, in_=i), r=rb, w=wb)

        def TT(e, o, a, b_, alu, rb, wb):
            op(e, lambda E: E.tensor_tensor(out=o, in0=a, in1=b_, op=alu), r=rb, w=wb)

        def TS(e, o, a, s1, s2, o0, o1, rb, wb):
            if s2 is None:
                op(e, lambda E: E.tensor_scalar(out=o, in0=a, scalar1=s1, scalar2=None, op0=o0), r=rb, w=wb)
            else:
                op(e, lambda E: E.tensor_scalar(out=o, in0=a, scalar1=s1, scalar2=s2, op0=o0, op1=o1), r=rb, w=wb)

        def STT(o, a, s, b_, o0, o1, rb, wb):
            op("dve", lambda E: E.scalar_tensor_tensor(out=o, in0=a, scalar=s, in1=b_, op0=o0, op1=o1), r=rb, w=wb)

        def SCAN(o, d0, d1, init, rb, wb):
            op("dve", lambda E: E.tensor_tensor_scan(out=o, data0=d0, data1=d1, initial=init, op0=ALU.mult, op1=ALU.add), r=rb, w=wb)

        def MSET(e, o, v, wb):
            op(e, lambda E: E.memset(o, v), w=wb)

        def DBG(c0, ap, n, tl):
            if "dbg" in taps:
                dma("pool", dbg[0:ap.shape[0], c0:c0 + n], ap, r=[tl.b])

        ident_f = SB(es, "ident_f", [128, 128], F32)
        ident_b = SB(es, "ident_b", [128, 128], BF16)
        ones_b = SB(es, "ones_b", [128, 128], BF16)
        neghalf = SB(es, "neghalf", [128, 512], F32)
        modv = SB(es, "modv", [128, DEPTH, 48, 2], F32)
        ngt = SB(es, "ngt", [128, DEPTH, 4, KC], F32)
        lbv = SB(es, "lbv", [128, DEPTH, 8], F32)
        omlb = SB(es, "omlb", [128, DEPTH, 8], F32)
        A1 = SB(es, "A1", [128, KC, 2], F32)
        G1 = SB(es, "G1", [128, KC, 2], F32)
        A2 = SB(es, "A2", [128, KC, 2], F32)
        G2 = SB(es, "G2", [128, KC, 2], F32)
        STAT = [ident_f.b, ident_b.b, ones_b.b, neghalf.b]

        dma("sp", ident_f[:], c_ident[:, :], w=[ident_f.b])
        CP("dve", ident_b[:], ident_f[:], [ident_f.b], [ident_b.b])
        MSET("pool", ones_b[:], 1.0, [ones_b.b])
        MSET("pool", neghalf[:], -0.5, [neghalf.b])
        epsb = SB(es, "epsb", [128, 1], F32)
        MSET("pool", epsb[:], EPS, [epsb.b])
        dma("sp", ngt[:], norm_gT[:, :, :, :], w=[ngt.b])

        with ExitStack() as ph:
            xr = [SB(ph, "xr%d" % i, [128, DM], F32) for i in range(2)]
            xtt = [SB(ph, "xtt%d" % i, [128, KC, 128], F32) for i in range(2)]
            for tb in range(NB):
                src = ctx_in[tb * 128:(tb + 1) * 128, :] if tb < LC // 128 else x_in[tb * 128 - LC:(tb + 1) * 128 - LC, :]
                a, o_ = xr[tb % 2], xtt[tb % 2]
                dma("sp", a[:], src, w=[a.b])
                pa, pb = nextps(), nextps()
                for kc in range(KC):
                    pp_ = pa if kc < 4 else pb
                    TR(pp_[:, (kc % 4) * 128:(kc % 4 + 1) * 128], a[:, kc * 128:(kc + 1) * 128], ident_f[:], [a.b, ident_f.b], [pp_.b])
                CP("act", o_[:, 0:4, :], pa[:].rearrange("p (k t) -> p k t", k=4), [pa.b], [o_.b])
                CP("dve", o_[:, 4:8, :], pb[:].rearrange("p (k t) -> p k t", k=4), [pb.b], [o_.b])
                dma("pool", xT[:, :, tb * 128:(tb + 1) * 128].rearrange("k p t -> p k t"), o_[:], r=[o_.b], w=[d_xT])
            cTt = SB(ph, "cTt", [128, KC, 2], F32)
            scb = SB(ph, "scb", [128, KC, 2], BF16)
            bmt = SB(ph, "bmt", [128, DEPTH, 48], F32)
            wm = [SB(ph, "wm%d" % i, [128, KC, 1024], BF16) for i in range(2)]
            dma("sp", cTt[:], cT_in[:, :, :], w=[cTt.b])
            dma("sp", bmt[:], b_modT[:, :, :], w=[bmt.b])
            ACT(scb[:], cTt[:], AF.Silu, [cTt.b], [scb.b])
            for l in range(DEPTH):
                pm = nextps()
                for grp in range(6):
                    wt = wm[(l * 6 + grp) % 2]
                    dma("pool", wt[:], w_mod[l, :, grp * 1024:(grp + 1) * 1024].rearrange("(k p) n -> p k n", p=128), w=[wt.b])
                    for j in range(8):
                        oc = grp * 8 + j
                        for kc in range(KC):
                            MM(pm[:, oc * 2:oc * 2 + 2], wt[:, kc, j * 128:(j + 1) * 128], scb[:, kc, :], kc == 0, kc == KC - 1, [wt.b, scb.b], [pm.b])
                TT("dve", modv[:, l, :, :], pm[:, 0:96].rearrange("p (c w) -> p c w", w=2),
                   bmt[:, l, :].unsqueeze(2).to_broadcast([128, 48, 2]), ALU.add, [pm.b, bmt.b], [modv.b])
            lg = SB(ph, "lg", [128, DEPTH, 8], F32)
            sm = SB(ph, "sm", [128, 8], F32)
            dma("sp", lg[:], hg_lbT[:, :, :], w=[lg.b])
            ACT(lg[:], lg[:], AF.Exp, [lg.b], [lg.b])
            CP("dve", sm[:], lg[:, 0, :], [lg.b], [sm.b])
            for l in range(1, DEPTH):
                TT("dve", sm[:], sm[:], lg[:, l, :], ALU.add, [sm.b, lg.b], [sm.b])
            op("dve", lambda E: E.reciprocal(out=sm[:], in_=sm[:]), r=[sm.b], w=[sm.b])
            MSET("dve", lbv[:, 0, :], 0.0, [lbv.b])
            for l in range(1, DEPTH):
                TT("dve", lg[:, l, :], lg[:, l, :], sm[:], ALU.mult, [lg.b, sm.b], [lg.b])
                TT("dve", lbv[:, l, :], lbv[:, l - 1, :], lg[:, l, :], ALU.add, [lbv.b, lg.b], [lbv.b])
            TS("dve", omlb[:], lbv[:], -1.0, 1.0, ALU.mult, ALU.add, [lbv.b], [omlb.b])
            P.barrier()

        def mod_scalars(l):
            for (Aq, sc0, gi) in ((A1, 8, 0), (A2, 32, 2)):
                TS("dve", Aq[:], modv[:, l, sc0:sc0 + 8, :], 1.0, None, ALU.add, None, [modv.b], [Aq.b])
                TT("dve", Aq[:], Aq[:], ngt[:, l, gi, :].unsqueeze(2).to_broadcast([128, KC, 2]), ALU.mult, [Aq.b, ngt.b], [Aq.b])
            for (Gq, g0, gi) in ((G1, 16, 1), (G2, 40, 3)):
                TT("dve", Gq[:], modv[:, l, g0:g0 + 8, :], ngt[:, l, gi, :].unsqueeze(2).to_broadcast([128, KC, 2]), ALU.mult, [modv.b, ngt.b], [Gq.b])

        def norm_phase(ph, l, Aq, sh0, hT):
            xts = [SB(ph, "nxt%d" % i, [128, KC, 512], F32) for i in range(2)]
            sqs = [SB(ph, "nsq%d" % i, [128, KC, 512], BF16) for i in range(2)]
            rss = [SB(ph, "nrs%d" % i, [128, 512], F32) for i in range(2)]
            tmp = SB(ph, "ntmp", [128, KC, 512], F32, nb=KC)

            def stage_a(ti):
                t0, sz = TILES[ti]
                xt, sq, rs = xts[ti % 2], sqs[ti % 2], rss[ti % 2]
                dma("sp", xt[:, :, 0:sz], xT[:, :, t0:t0 + sz].rearrange("k p t -> p k t"), r=[d_xT], w=[xt.b])
                ACT(sq[:, :, 0:sz], xt[:, :, 0:sz], AF.Square, [xt.b], [sq.b])
                ps = nextps()
                for kc in range(KC):
                    MM(ps[:, 0:sz], ones_b[:], sq[:, kc, 0:sz], kc == 0, kc == KC - 1, [ones_b.b, sq.b], [ps.b])
                ACT(rs[:, 0:sz], ps[:, 0:sz], AF.Ln, [ps.b, epsb.b], [rs.b], bias=epsb[:, 0:1], scale=1.0 / DM)
                ACT(rs[:, 0:sz], rs[:, 0:sz], AF.Exp, [rs.b], [rs.b], scale=-0.5)

            def stage_b(ti):
                t0, sz = TILES[ti]
                w_ = 1 if ti == 0 else 0
                xt, rs = xts[ti % 2], rss[ti % 2]
                for kc in range(KC):
                    STT(tmp[:, kc, 0:sz], xt[:, kc, 0:sz], Aq[:, kc, w_:w_ + 1], rs[:, 0:sz], ALU.mult, ALU.mult, [xt.b, Aq.b, rs.b], [tmp.bs[kc]])
                    ACT(hT[:, kc, t0:t0 + sz], tmp[:, kc, 0:sz], AF.Identity, [tmp.bs[kc], modv.b], [hT.bs[ti]],
                        bias=modv[:, l, sh0 + kc, w_:w_ + 1])
            stage_a(0)
            for ti in range(NTI):
                stage_b(ti)
                if ti + 1 < NTI:
                    stage_a(ti + 1)

        def epilogue(ot, xt, Gq, ti, t0, sz, sq, rs, tmp):
            w_ = 1 if ti == 0 else 0
            ACT(sq[:, :, 0:sz], ot[:, :, 0:sz], AF.Square, [ot.b], [sq.b])
            ps = nextps()
            for kc in range(KC):
                MM(ps[:, 0:sz], ones_b[:], sq[:, kc, 0:sz], kc == 0, kc == KC - 1, [ones_b.b, sq.b], [ps.b])
            ACT(rs[:, 0:sz], ps[:, 0:sz], AF.Ln, [ps.b, epsb.b], [rs.b], bias=epsb[:, 0:1], scale=1.0 / DM)
            ACT(rs[:, 0:sz], rs[:, 0:sz], AF.Exp, [rs.b], [rs.b], scale=-0.5)
            for kc in range(KC):
                STT(ot[:, kc, 0:sz], ot[:, kc, 0:sz], Gq[:, kc, w_:w_ + 1], rs[:, 0:sz], ALU.mult, ALU.mult, [ot.b, Gq.b, rs.b], [ot.b])
                TT("pool", xt[:, kc, 0:sz], xt[:, kc, 0:sz], ot[:, kc, 0:sz], ALU.add, [xt.b, ot.b], [xt.b])
            dma("pool", xT[:, :, t0:t0 + sz].rearrange("k p t -> p k t"), xt[:, :, 0:sz], r=[xt.b], w=[d_xT])

        for l in range(DEPTH):
            mod_scalars(l)
            with ExitStack() as ph:
                hT = SB(ph, "hT", [128, KC, NT], BF16, nb=NTI)
                with ExitStack() as ph1:
                    norm_phase(ph1, l, A1, 0, hT)
                    P.barrier()
                wts = [SB(ph, "wt%d" % i, [128, KC, 512], BF16) for i in range(3)]
                wrr = [0]
                stg = [SB(ph, "stg%d" % i, [128, NT], BF16) for i in range(3)]
                srr = [0]
                stgf = [SB(ph, "stgf%d" % i, [128, NT], F32) for i in range(2)]
                tmpa = [SB(ph, "tmpa%d" % i, [128, 512], F32) for i in range(2)]
                tmpb = [SB(ph, "tmpb%d" % i, [128, 512], F32) for i in range(2)]

                def load_w(src, ncols):
                    wt = wts[wrr[0] % 3]
                    wrr[0] += 1
                    dma("pool", wt[:, :, 0:ncols], src.rearrange("(k p) n -> p k n", p=128), w=[wt.b])
                    return wt

                def next_stg():
                    s = stg[srr[0] % 3]
                    srr[0] += 1
                    return s

                def proj(wt, off, M, cons):
                    for ti, (t0, sz) in enumerate(TILES):
                        ps = nextps()
                        for kc in range(KC):
                            MM(ps[0:M, 0:sz], wt[:, kc, off:off + M], hT[:, kc, t0:t0 + sz], kc == 0, kc == KC - 1, [wt.b, hT.bs[ti]], [ps.b])
                        cons(ti, t0, sz, ps)

                def simple_group(col0, nchunks, func, dst):
                    for g0 in range(0, nchunks, 4):
                        n = min(4, nchunks - g0)
                        wt = load_w(w_in[l, :, col0 + g0 * 128:col0 + (g0 + n) * 128], n * 128)
                        for c in range(n):
                            s = next_stg()

                            def cons(ti, t0, sz, ps, s=s):
                                if func is None:
                                    CP("act", s[:, t0:t0 + sz], ps[:, 0:sz], [ps.b], [s.b])
                                else:
                                    ACT(s[:, t0:t0 + sz], ps[:, 0:sz], func, [ps.b], [s.b])
                            proj(wt, c * 128, 128, cons)
                            dma("sp", dst[g0 + c], s[:], r=[s.b], w=[d_z])

                simple_group(C_S5, 4, None, zs5)
                with ExitStack() as phq:
                    ropeC = SB(phq, "ropeC", [128, NT], F32)
                    ropeS = SB(phq, "ropeS", [128, NT], F32)
                    dma("sp", ropeC[:], ropeC_in[:, :], w=[ropeC.b])
                    dma("sp", ropeS[:], ropeS_in[:, :], w=[ropeS.b])
                    for (cbase, rbase, nh_, dst) in ((C_Q, 0, 8, qT), (C_K, 512, 2, kT)):
                        for g0 in range(0, nh_, 4):
                            n = min(4, nh_ - g0)
                            wa = load_w(w_in[l, :, cbase + g0 * 64:cbase + (g0 + n) * 64], n * 64)
                            wb = load_w(w_rot[l, :, rbase + g0 * 64:rbase + (g0 + n) * 64], n * 64)
                            for c in range(n // 2):
                                s = next_stg()
                                for ti, (t0, sz) in enumerate(TILES):
                                    p1, p2 = nextps(), nextps()
                                    for kc in range(KC):
                                        MM(p1[:, 0:sz], wa[:, kc, c * 128:(c + 1) * 128], hT[:, kc, t0:t0 + sz], kc == 0, kc == KC - 1, [wa.b, hT.bs[ti]], [p1.b])
                                    for kc in range(KC):
                                        MM(p2[:, 0:sz], wb[:, kc, c * 128:(c + 1) * 128], hT[:, kc, t0:t0 + sz], kc == 0, kc == KC - 1, [wb.b, hT.bs[ti]], [p2.b])
                                    ta, tb_ = tmpa[ti % 2], tmpb[ti % 2]
                                    TT("dve", ta[:, 0:sz], p1[:, 0:sz], ropeC[:, t0:t0 + sz], ALU.mult, [p1.b, ropeC.b], [ta.b])
                                    TT("dve", tb_[:, 0:sz], p2[:, 0:sz], ropeS[:, t0:t0 + sz], ALU.mult, [p2.b, ropeS.b], [tb_.b])
                                    TT("pool", s[:, t0:t0 + sz], ta[:, 0:sz], tb_[:, 0:sz], ALU.add, [ta.b, tb_.b], [s.b])
                                h0 = g0 + 2 * c
                                dma("sp", dst[h0:h0 + 2].rearrange("h d t -> (h d) t"), s[:], r=[s.b], w=[d_z])
                    P.barrier()
                with ExitStack() as ph3:
                    wv = SB(ph3, "wv", [128, KC, 640], BF16)
                    vst = [SB(ph3, "vst%d" % i, [128, 640], BF16) for i in range(2)]
                    dma("pool", wv[:, :, 0:128], w_in[l, :, C_V:C_V + 128].rearrange("(k p) n -> p k n", p=128), w=[wv.b])
                    dma("pool", wv[:, :, 128:640], w_in[l, :, C_HI:C_HI + 512].rearrange("(k p) n -> p k n", p=128), w=[wv.b])
                    for tb in range(NB):
                        ti = tile_of(tb * 128)
                        pa, pb = nextps(), nextps()
                        for kc in range(KC):
                            MM(pa[:, 0:512], hT[:, kc, tb * 128:(tb + 1) * 128], wv[:, kc, 128:640], kc == 0, kc == KC - 1, [wv.b, hT.bs[ti]], [pa.b])
                        for kc in range(KC):
                            MM(pb[:, 0:128], hT[:, kc, tb * 128:(tb + 1) * 128], wv[:, kc, 0:128], kc == 0, kc == KC - 1, [wv.b, hT.bs[ti]], [pb.b])
                        v = vst[tb % 2]
                        CP("act", v[:, 0:128], pb[:, 0:128], [pb.b], [v.b])
                        CP("dve", v[:, 128:640], pa[:, 0:512], [pa.b], [v.b])
                        dma("sp", vtm[tb * 128:(tb + 1) * 128, :], v[:], r=[v.b], w=[d_z])
                simple_group(C_HQ, 4, AF.Silu, hgq)
                for d in range(2):
                    wt = load_w(w_in[l, :, C_FF + d * 512:C_FF + (d + 1) * 512], 512)
                    for c in range(4):
                        s = next_stg()
                        sf = stgf[c % 2]
                        li_ = d * 4 + c

                        def cons(ti, t0, sz, ps, sf=sf):
                            ACT(sf[:, t0:t0 + sz], ps[:, 0:sz], AF.Sigmoid, [ps.b], [sf.b])
                        proj(wt, c * 128, 128, cons)
                        TS("dve", sf[:], sf[:], omlb[:, l, li_:li_ + 1], lbv[:, l, li_:li_ + 1], ALU.mult, ALU.add, [sf.b, omlb.b, lbv.b], [sf.b])
                        TS("dve", s[:], sf[:], -1.0, 1.0, ALU.mult, ALU.add, [sf.b], [s.b])
                        ACT(sf[:], sf[:], AF.Ln, [sf.b], [sf.b])
                        dma("sp", lf[d, c], sf[:], r=[sf.b], w=[d_z])
                        dma("sp", kk[d, c], s[:], r=[s.b], w=[d_z])
                simple_group(C_HG, 4, AF.Sigmoid, hgg)
                simple_group(C_GATE, 24, AF.Sigmoid, gat)
                P.barrier()
            if stop_after == "P2":
                break
            with ExitStack() as ph:
                uT = SB(ph, "uT", [128, 4, NT], BF16)
                yacc = SB(ph, "yacc", [128, NT], F32)
                for c in range(4):
                    dma("sp", uT[:, c, :], zs5[c], r=[d_z], w=[uT.b])
                y2T = uT
                sm_ = {n: SB(ph, "s5" + n, [128, 2, 16], F32) for n in
                       ("lr", "li", "dt", "th", "rho", "sn", "cs", "thr", "ar", "ai", "den", "fr", "fi", "t1", "t2", "tf", "dl", "rho8", "th8")}
                smi = SB(ph, "s5i", [128, 2, 16], I32)
                dma("sp", sm_["lr"][:], s5_lr[:, l, :, :], w=[sm_["lr"].b])
                dma("sp", sm_["li"][:], s5_li[:, l, :, :], w=[sm_["li"].b])
                dma("sp", sm_["dt"][:], s5_ldt[:, l, :, :], w=[sm_["dt"].b])

                def reduce_angle(src, dst, tf, ti_):
                    TS("dve", tf[:], src[:], 1.0 / (2 * PI), None, ALU.mult, None, [src.b], [tf.b])
                    CP("dve", ti_[:], tf[:], [tf.b], [ti_.b])
                    CP("dve", tf[:], ti_[:], [ti_.b], [tf.b])
                    STT(dst[:], tf[:], -2 * PI, src[:], ALU.mult, ALU.add, [tf.b, src.b], [dst.b])
                    TS("dve", dst[:], dst[:], -PI, PI, ALU.max, ALU.min, [dst.b], [dst.b])

                S = sm_
                ACT(S["dt"][:], S["dt"][:], AF.Exp, [S["dt"].b], [S["dt"].b])
                TT("dve", S["th"][:], S["dt"][:], S["li"][:], ALU.mult, [S["dt"].b, S["li"].b], [S["th"].b])
                TT("dve", S["dl"][:], S["dt"][:], S["lr"][:], ALU.mult, [S["dt"].b, S["lr"].b], [S["dl"].b])
                ACT(S["rho"][:], S["dl"][:], AF.Exp, [S["dl"].b], [S["rho"].b])
                reduce_angle(S["th"], S["thr"], S["t1"], smi)
                ACT(S["sn"][:], S["thr"][:], AF.Sin, [S["thr"].b], [S["sn"].b])
                TS("dve", S["t2"][:], S["thr"][:], PI / 2, None, ALU.add, None, [S["thr"].b], [S["t2"].b])
                reduce_angle(S["t2"], S["cs"], S["t1"], smi)
                ACT(S["cs"][:], S["cs"][:], AF.Sin, [S["cs"].b], [S["cs"].b])
                TT("dve", S["ar"][:], S["rho"][:], S["cs"][:], ALU.mult, [S["rho"].b, S["cs"].b], [S["ar"].b])
                TT("dve", S["ai"][:], S["rho"][:], S["sn"][:], ALU.mult, [S["rho"].b, S["sn"].b], [S["ai"].b])
                TT("dve", S["den"][:], S["lr"][:], S["lr"][:], ALU.mult, [S["lr"].b], [S["den"].b])
                TT("dve", S["t1"][:], S["li"][:], S["li"][:], ALU.mult, [S["li"].b], [S["t1"].b])
                TT("dve", S["den"][:], S["den"][:], S["t1"][:], ALU.add, [S["den"].b, S["t1"].b], [S["den"].b])
                op("dve", lambda E: E.reciprocal(out=S["den"][:], in_=S["den"][:]), r=[S["den"].b], w=[S["den"].b])
                TS("dve", S["ar"][:], S["ar"][:], -1.0, None, ALU.add, None, [S["ar"].b], [S["ar"].b])
                TT("dve", S["fr"][:], S["ar"][:], S["lr"][:], ALU.mult, [S["ar"].b, S["lr"].b], [S["fr"].b])
                TT("dve", S["t1"][:], S["ai"][:], S["li"][:], ALU.mult, [S["ai"].b, S["li"].b], [S["t1"].b])
                TT("dve", S["fr"][:], S["fr"][:], S["t1"][:], ALU.add, [S["fr"].b, S["t1"].b], [S["fr"].b])
                TT("dve", S["fr"][:], S["fr"][:], S["den"][:], ALU.mult, [S["fr"].b, S["den"].b], [S["fr"].b])
                TT("dve", S["fi"][:], S["ai"][:], S["lr"][:], ALU.mult, [S["ai"].b, S["lr"].b], [S["fi"].b])
                TT("dve", S["t1"][:], S["ar"][:], S["li"][:], ALU.mult, [S["ar"].b, S["li"].b], [S["t1"].b])
                TT("dve", S["fi"][:], S["fi"][:], S["t1"][:], ALU.subtract, [S["fi"].b, S["t1"].b], [S["fi"].b])
                TT("dve", S["fi"][:], S["fi"][:], S["den"][:], ALU.mult, [S["fi"].b, S["den"].b], [S["fi"].b])

                pwr = SB(ph, "pwr", [128, 9, 2, 16], F32)
                pwi = SB(ph, "pwi", [128, 9, 2, 16], F32)
                npwr = SB(ph, "npwr", [128, 9, 2, 16], F32)
                for tau in range(9):
                    TS("dve", S["t1"][:], S["thr"][:], float(tau), None, ALU.mult, None, [S["thr"].b], [S["t1"].b])
                    reduce_angle(S["t1"], S["t2"], S["tf"], smi)
                    ACT(S["sn"][:], S["t2"][:], AF.Sin, [S["t2"].b], [S["sn"].b])
                    TS("dve", S["t1"][:], S["t2"][:], PI / 2, None, ALU.add, None, [S["t2"].b], [S["t1"].b])
                    reduce_angle(S["t1"], S["cs"], S["tf"], smi)
                    ACT(S["cs"][:], S["cs"][:], AF.Sin, [S["cs"].b], [S["cs"].b])
                    TS("dve", S["t1"][:], S["dl"][:], float(tau), None, ALU.mult, None, [S["dl"].b], [S["t1"].b])
                    ACT(S["t1"][:], S["t1"][:], AF.Exp, [S["t1"].b], [S["t1"].b])
                    TT("dve", pwr[:, tau], S["t1"][:], S["cs"][:], ALU.mult, [S["t1"].b, S["cs"].b], [pwr.b])
                    TT("dve", pwi[:, tau], S["t1"][:], S["sn"][:], ALU.mult, [S["t1"].b, S["sn"].b], [pwi.b])
                TS("dve", npwr[:], pwr[:], -1.0, None, ALU.mult, None, [pwr.b], [npwr.b])
                TS("dve", S["t1"][:], S["dl"][:], 8.0, None, ALU.mult, None, [S["dl"].b], [S["t1"].b])
                ACT(S["rho8"][:], S["t1"][:], AF.Exp, [S["t1"].b], [S["rho8"].b])
                TS("dve", S["t1"][:], S["thr"][:], 8.0, None, ALU.mult, None, [S["thr"].b], [S["t1"].b])
                reduce_angle(S["t1"], S["th8"], S["tf"], smi)

                bp = [SB(ph, "bp%d" % i, [128, 2, 128], F32) for i in range(2)]
                bb = [[SB(ph, "bb%d_%d" % (d, pp), [128, 2, 128], BF16) for pp in range(4)] for d in range(2)]
                dgs = [SB(ph, "dg%d" % i, [128, 3, 4, 128], BF16) for i in range(2)]
                Xb4 = [SB(ph, "Xb4_%d" % i, [128, 4, 2, 128], BF16) for i in range(2)]
                identb4 = SB(ph, "identb4", [128, 4, 128], BF16)
                for i_ in range(4):
                    CP("dve", identb4[:, i_, :], ident_b[:], [ident_b.b], [identb4.b])
                npwi = SB(ph, "npwi", [128, 9, 2, 16], F32)
                TS("dve", npwi[:], pwi[:], -1.0, None, ALU.mult, None, [pwi.b], [npwi.b])
                cf = [SB(ph, "cf%d" % pp, [128, 2, 128], F32) for pp in range(4)]
                lhsC = [SB(ph, "lhsC%d" % pp, [128, 2, 128], BF16) for pp in range(4)]
                xs = [SB(ph, "xs%d" % i, [128, 3, 128], F32) for i in range(2)]
                lhsP = SB(ph, "lhsP", [128, 8, 4, 2, 128], BF16)
                BD = SB(ph, "BD", [128, 8, 128], BF16)
                lhsQ = SB(ph, "lhsQ", [128, 8, 4, 2, 128], BF16)
                diagD = SB(ph, "diagD", [128, 128], F32)
                sdT = SB(ph, "sdT", [128, 4], F32)
                dma("sp", sdT[:], s5_dT[:, l, :], w=[sdT.b])
                iot = SB(ph, "iot64", [128, 64], F32)
                dma("sp", iot[:], c_iota[:, 0, 0:64], w=[iot.b])
                a64 = [SB(ph, "a64_%d" % i, [128, 64], F32) for i in range(3)]
                a64i = SB(ph, "a64i", [128, 64], I32)
                tabC = SB(ph, "tabC", [128, 4, 64], F32)
                tabS = SB(ph, "tabS", [128, 4, 64], F32)
                tabN = SB(ph, "tabN", [128, 4, 64], F32)
                Vt = [SB(ph, "Vt%d" % i, [128, 4, 2, 64], F32) for i in range(2)]
                Wk = [{n: SB(ph, "wk%s%d" % (n, i), [128, 4, 64], F32) for n in ("m1", "m2", "m3", "m4", "gr", "gi")} for i in range(2)]
                Hre = SB(ph, "Hre", [128, 4, 65], F32)
                Him = SB(ph, "Him", [128, 4, 65], F32)
                Hb = [SB(ph, "Hb%d" % i, [128, 2, 4, 64], BF16) for i in range(2)]
                nlong[0] = 4
                vt = SB(ph, "vt", [128, NB, 128], BF16)
                dma("sp", vt[:], vtm[:, 0:128].rearrange("(b p) c -> p b c", p=128), r=[d_z], w=[vt.b])
                mP = SB(ph, "mP", [128, 512], BF16)
                mN = SB(ph, "mN", [128, 512], BF16)
                dma("pool", mP[:], c_maskP[:, :], w=[mP.b])
                dma("pool", mN[:], c_maskN[:, :], w=[mN.b])
                kT2 = SB(ph, "kT2", [64, 2, NT], BF16)
                dma("sp", kT2[:], kT[:, :, :].rearrange("h d t -> d h t"), r=[d_z], w=[kT2.b])
                srow = SB(ph, "srow", [1, 1024], F32)
                dma("sp", srow[:], sinkrow[l, :, :], w=[srow.b])
                sinkts = [SB(ph, "sinkt%d" % i, [128, 512], BF16) for i in range(2)]
                for kvh in range(2):
                    MSET("pool", sinkts[kvh][:], 0.0, [sinkts[kvh].b])
                    ACT(sinkts[kvh][0:1, :], srow[:, kvh * 512:(kvh + 1) * 512], AF.Exp, [srow.b], [sinkts[kvh].b])
                qblk = [SB(ph, "qblk%d" % i, [64, 4, 128], BF16) for i in range(3)]
                oblk = [SB(ph, "oblk%d" % i, [64, 4, 128], BF16) for i in range(3)]
                pts = [SB(ph, "pt%d" % i, [128, 512], BF16) for i in range(3)]
                rds = [SB(ph, "rd%d" % i, [64, 512], F32) for i in range(3)]
                att_steps = [(kvh, qb) for kvh in range(2) for qb in range(NB)]
                att_st = {"i": 0, "ipt": 0, "pend": []}

                def att_issue():
                    idx = att_st["i"]
                    if idx >= len(att_steps):
                        return
                    att_st["i"] += 1
                    kvh, qb = att_steps[idx]
                    qv = qblk[idx % 3]
                    dma("sp", qv[:], qT[4 * kvh:4 * kvh + 4, :, qb * 128:(qb + 1) * 128].rearrange("h d t -> d h t"), r=[d_z], w=[qv.b])
                    if qb < LC // 128:
                        keys = [(kt_, None) for kt_ in range(LC // 128)]
                    else:
                        n = qb - LC // 128
                        keys = [(kt_, None) for kt_ in range(LC // 128)]
                        if n - 1 >= 0:
                            keys.append((qb - 1, mP))
                        keys.append((qb, None))
                        if n + 1 < L // 128:
                            keys.append((qb + 1, mN))
                    pso, psd = nextps(long=True), nextps(long=True)
                    for i, (kt_, msk) in enumerate(keys):
                        pss = nextps()
                        MM(pss[:, :], kT2[:, kvh, kt_ * 128:(kt_ + 1) * 128], qv[:], True, msk is None, [kT2.b, qv.b], [pss.b])
                        if msk is not None:
                            MM(pss[:, :], ident_b[:], msk[:], False, True, [ident_b.b, msk.b], [pss.b])
                        pt = pts[att_st["ipt"] % 3]
                        att_st["ipt"] += 1
                        ACT(pt[:], pss[:, :], AF.Exp, [pss.b], [pt.b], scale=0.125)
                        MM(pso[0:64, :], vt[:, kt_, kvh * 64:(kvh + 1) * 64], pt[:], i == 0, i == len(keys) - 1, [vt.b, pt.b], [pso.b])
                        MM(psd[0:64, :], ones_b[:, 0:64], pt[:], i == 0, False, [ones_b.b, pt.b], [psd.b])
                    MM(psd[0:64, :], ones_b[:, 0:64], sinkts[kvh][:], False, True, [ones_b.b, sinkts[kvh].b], [psd.b])
                    rd = rds[idx % 3]
                    ACT(rd[:], psd[0:64, :], AF.Ln, [psd.b], [rd.b])
                    ACT(rd[:], rd[:], AF.Exp, [rd.b], [rd.b], scale=-1.0)
                    att_st["pend"].append((idx, kvh, qb, pso, rd))

                def att_finalize():
                    if not att_st["pend"]:
                        return
                    idx, kvh, qb, pso, rd = att_st["pend"].pop(0)
                    o_ = oblk[idx % 3]
                    TT("dve", o_[:], pso[0:64, :].rearrange("p (h q) -> p h q", h=4), rd[:].rearrange("p (h q) -> p h q", h=4),
                       ALU.mult, [pso.b, rd.b], [o_.b])
                    dma("sp", yT[4 + 2 * kvh:6 + 2 * kvh, :, qb * 128:(qb + 1) * 128].rearrange("c (two d) t -> d (c two) t", two=2), o_[:],
                        r=[o_.b], w=[d_y])

                it = 0
                for fc in range(4):
                    TS("dve", diagD[:], ident_f[:], sdT[:, fc:fc + 1], None, ALU.mult, None, [ident_f.b, sdT.b], [diagD.b])
                    for pp in range(4):
                        pr = fc * 4 + pp
                        b_ = bp[pp % 2]
                        dma("sp", b_[:, 0, :], s5_Bre[l, pr], w=[b_.b])
                        dma("sp", b_[:, 1, :], s5_Bim[l, pr], w=[b_.b])
                        dma("sp", cf[pp][:, 0, :], s5_Cre[l, pr], w=[cf[pp].b])
                        dma("sp", cf[pp][:, 1, :], s5_Cim[l, pr], w=[cf[pp].b])
                        CP("act", lhsC[pp][:, 0, :], cf[pp][:, 0, :], [cf[pp].b], [lhsC[pp].b])
                        TS("dve", lhsC[pp][:, 1, :], cf[pp][:, 1, :], -1.0, None, ALU.mult, None, [cf[pp].b], [lhsC[pp].b])
                        for d in range(2):
                            fr_ = S["fr"][:, d, pr:pr + 1]
                            fi_ = S["fi"][:, d, pr:pr + 1]
                            x_ = xs[d]
                            o_ = bb[d][pp]
                            TS("dve", x_[:, 0, :], b_[:, 1, :], fi_, None, ALU.mult, None, [b_.b, S["fi"].b], [x_.b])
                            STT(o_[:, 0, :], b_[:, 0, :], fr_, x_[:, 0, :], ALU.mult, ALU.subtract, [b_.b, S["fr"].b, x_.b], [o_.b])
                            TS("dve", x_[:, 1, :], b_[:, 0, :], fi_, None, ALU.mult, None, [b_.b, S["fi"].b], [x_.b])
                            STT(o_[:, 1, :], b_[:, 1, :], fr_, x_[:, 1, :], ALU.mult, ALU.add, [b_.b, S["fr"].b, x_.b], [o_.b])
                    for d in range(2):
                        sl4 = slice(fc * 4, fc * 4 + 4)
                        for tau in range(9):
                            dg = dgs[tau % 2]
                            for vi, pw_ in enumerate((pwr, pwi, npwi)):
                                TT("dve", dg[:, vi], identb4[:], pw_[:, tau, d, sl4].unsqueeze(2).to_broadcast([128, 4, 128]), ALU.mult,
                                   [identb4.b, pw_.b], [dg.b])

                            def four(dst_banks, combos):
                                for pp in range(4):
                                    pq = dst_banks[pp // 2]
                                    base = (pp % 2) * 256
                                    for ri, (l0, r0, l1, r1, bufs) in enumerate(combos(pp)):
                                        o_ap = pq[:, base + ri * 128:base + (ri + 1) * 128]
                                        MM(o_ap, l0, r0, True, False, bufs, [pq.b])
                                        MM(o_ap, l1, r1, False, True, bufs, [pq.b])

                            def vw(pq):
                                return pq[:].rearrange("p (a b c) -> p a b c", a=2, b=2)
                            if tau < 8:
                                pa, pb = nextps(), nextps()
                                four((pa, pb), lambda pp: (
                                    (bb[d][pp][:, 0, :], dg[:, 0, pp, :], bb[d][pp][:, 1, :], dg[:, 2, pp, :], [bb[d][pp].b, dg.b]),
                                    (bb[d][pp][:, 1, :], dg[:, 0, pp, :], bb[d][pp][:, 0, :], dg[:, 1, pp, :], [bb[d][pp].b, dg.b])))
                                CP("act", lhsP[:, tau, 0:2, :, :], vw(pa), [pa.b], [lhsP.b])
                                CP("act", lhsP[:, tau, 2:4, :, :], vw(pb), [pb.b], [lhsP.b])
                                pc, pd = nextps(), nextps()
                                four((pc, pd), lambda pp: (
                                    (dg[:, 0, pp, :], bb[d][pp][:, 0, :], dg[:, 2, pp, :], bb[d][pp][:, 1, :], [bb[d][pp].b, dg.b]),
                                    (dg[:, 0, pp, :], bb[d][pp][:, 1, :], dg[:, 1, pp, :], bb[d][pp][:, 0, :], [bb[d][pp].b, dg.b])))
                                xb = Xb4[tau % 2]
                                CP("act", xb[:, 0:2, :, :], vw(pc), [pc.b], [xb.b])
                                CP("act", xb[:, 2:4, :, :], vw(pd), [pd.b], [xb.b])
                                psd_ = nextps()
                                for pp in range(4):
                                    for ri in range(2):
                                        MM(psd_[:, 0:128], xb[:, pp, ri, :], lhsC[pp][:, ri, :], pp == 0 and ri == 0, pp == 3 and ri == 1,
                                           [xb.b, lhsC[pp].b], [psd_.b])
                                if tau == 0 and d == 0:
                                    TT("dve", BD[:, tau, :], psd_[:, 0:128], diagD[:], ALU.add, [psd_.b, diagD.b], [BD.b])
                                else:
                                    CP("act", BD[:, tau, :], psd_[:, 0:128], [psd_.b], [BD.b])
                            if tau >= 1:
                                t = tau - 1
                                pe_, pf = nextps(), nextps()
                                four((pe_, pf), lambda pp: (
                                    (dg[:, 0, pp, :], lhsC[pp][:, 0, :], dg[:, 1, pp, :], lhsC[pp][:, 1, :], [lhsC[pp].b, dg.b]),
                                    (dg[:, 2, pp, :], lhsC[pp][:, 0, :], dg[:, 0, pp, :], lhsC[pp][:, 1, :], [lhsC[pp].b, dg.b])))
                                CP("act", lhsQ[:, t, 0:2, :, :], vw(pe_), [pe_.b], [lhsQ.b])
                                CP("act", lhsQ[:, t, 2:4, :, :], vw(pf), [pf.b], [lhsQ.b])
                        for pp in range(4):
                            pr = fc * 4 + pp
                            a0, a1, a2 = a64
                            TS("dve", a0[:], iot[:], S["th8"][:, d, pr:pr + 1], None, ALU.mult, None, [iot.b, S["th8"].b], [a0.b])
                            reduce_angle(a0, a1, a2, a64i)
                            ACT(tabS[:, pp, :], a1[:], AF.Sin, [a1.b], [tabS.b])
                            TS("dve", a0[:], a1[:], PI / 2, None, ALU.add, None, [a1.b], [a0.b])
                            reduce_angle(a0, a1, a2, a64i)
                            ACT(tabC[:, pp, :], a1[:], AF.Sin, [a1.b], [tabC.b])
                        TS("dve", tabN[:], tabS[:], -1.0, None, ALU.mult, None, [tabS.b], [tabN.b])
                        tbs = [tabC.b, tabS.b, tabN.b]
                        order = list(range(NTI)) if d == 0 else [0] + list(range(NTI - 1, 0, -1))
                        MSET("dve", Hre[:, :, 0:1], 0.0, [Hre.b])
                        MSET("dve", Him[:, :, 0:1], 0.0, [Him.b])
                        for oi, ti in enumerate(order):
                            t0, sz = TILES[ti]
                            NJ = sz // 8
                            useq = uT[:, fc, t0:t0 + sz] if d == 0 else uT[:, fc, t0:t0 + sz][:, ::-1]
                            us = [useq[:, s_::8] for s_ in range(8)]
                            V = Vt[it % 2]
                            W = Wk[it % 2]
                            hb = Hb[it % 2]
                            it += 1
                            att_finalize()
                            pv = nextps()
                            pvv = pv[:].rearrange("p (a b c) -> p a b c", a=4, b=2)
                            for pp in range(4):
                                for ri in range(2):
                                    for s_ in range(8):
                                        MM(pvv[:, pp, ri, 0:NJ], lhsP[:, 7 - s_, pp, ri, :], us[s_], s_ == 0, s_ == 7, [lhsP.b, uT.b], [pv.b])
                            CP("act", V[:, :, :, 0:NJ], pvv[:, :, :, 0:NJ], [pv.b], [V.b])
                            tC, tS, tN = tabC[:, :, 0:NJ], tabS[:, :, 0:NJ], tabN[:, :, 0:NJ]
                            vre, vim = V[:, :, 0, 0:NJ], V[:, :, 1, 0:NJ]
                            TT("dve", W["m1"][:, :, 0:NJ], vre, tC, ALU.mult, [V.b] + tbs, [W["m1"].b])
                            TT("dve", W["m2"][:, :, 0:NJ], vim, tS, ALU.mult, [V.b] + tbs, [W["m2"].b])
                            TT("dve", W["m1"][:, :, 0:NJ], W["m1"][:, :, 0:NJ], W["m2"][:, :, 0:NJ], ALU.add, [W["m1"].b, W["m2"].b], [W["m1"].b])
                            TT("pool", W["m3"][:, :, 0:NJ], vim, tC, ALU.mult, [V.b] + tbs, [W["m3"].b])
                            TT("pool", W["m4"][:, :, 0:NJ], vre, tN, ALU.mult, [V.b] + tbs, [W["m4"].b])
                            TT("pool", W["m3"][:, :, 0:NJ], W["m3"][:, :, 0:NJ], W["m4"][:, :, 0:NJ], ALU.add, [W["m3"].b, W["m4"].b], [W["m3"].b])
                            for pp in range(4):
                                pr = fc * 4 + pp
                                rho_b = S["rho8"][:, d, pr:pr + 1].to_broadcast([128, NJ])
                                SCAN(W["gr"][:, pp, 0:NJ], rho_b, W["m1"][:, pp, 0:NJ], Hre[:, pp, 0:1], [W["m1"].b, S["rho8"].b, Hre.b], [W["gr"].b])
                                SCAN(W["gi"][:, pp, 0:NJ], rho_b, W["m3"][:, pp, 0:NJ], Him[:, pp, 0:1], [W["m3"].b, S["rho8"].b, Him.b], [W["gi"].b])
                            TT("dve", W["m2"][:, :, 0:NJ], W["gr"][:, :, 0:NJ], tC, ALU.mult, [W["gr"].b] + tbs, [W["m2"].b])
                            TT("dve", W["m4"][:, :, 0:NJ], W["gi"][:, :, 0:NJ], tN, ALU.mult, [W["gi"].b] + tbs, [W["m4"].b])
                            TT("dve", Hre[:, :, 1:NJ + 1], W["m2"][:, :, 0:NJ], W["m4"][:, :, 0:NJ], ALU.add, [W["m2"].b, W["m4"].b], [Hre.b])
                            TT("pool", W["m1"][:, :, 0:NJ], W["gi"][:, :, 0:NJ], tC, ALU.mult, [W["gi"].b] + tbs, [W["m1"].b])
                            TT("pool", W["m3"][:, :, 0:NJ], W["gr"][:, :, 0:NJ], tS, ALU.mult, [W["gr"].b] + tbs, [W["m3"].b])
                            TT("pool", Him[:, :, 1:NJ + 1], W["m1"][:, :, 0:NJ], W["m3"][:, :, 0:NJ], ALU.add, [W["m1"].b, W["m3"].b], [Him.b])
                            CP("pool", hb[:, 0, :, 0:NJ], Hre[:, :, 0:NJ], [Hre.b], [hb.b])
                            CP("pool", hb[:, 1, :, 0:NJ], Him[:, :, 0:NJ], [Him.b], [hb.b])
                            att_issue()
                            py = nextps(long=True)
                            pyv = py[:].rearrange("p (t j) -> p t j", t=8)
                            for t in range(8):
                                nmm = (t + 1) + 8
                                imm = 0
                                for s_ in range(t + 1):
                                    imm += 1
                                    MM(pyv[:, t, 0:NJ], BD[:, t - s_, :], us[s_], imm == 1, imm == nmm, [BD.b, uT.b], [py.b])
                                for pp in range(4):
                                    for ri in range(2):
                                        imm += 1
                                        MM(pyv[:, t, 0:NJ], lhsQ[:, t, pp, ri, :], hb[:, ri, pp, 0:NJ], imm == 1, imm == nmm, [lhsQ.b, hb.b], [py.b])
                            yv = yacc[:, t0:t0 + sz] if d == 0 else yacc[:, t0:t0 + sz][:, ::-1]
                            yv = yv.rearrange("p (j t) -> p t j", t=8)
                            if d == 0:
                                CP("act", yv, pyv[:, :, 0:NJ], [py.b], [yacc.b])
                            else:
                                TT("dve", yv, pyv[:, :, 0:NJ], yv, ALU.add, [py.b, yacc.b], [yacc.b])
                            CP("dve", Hre[:, :, 0:1], Hre[:, :, NJ:NJ + 1], [Hre.b], [Hre.b])
                            CP("dve", Him[:, :, 0:1], Him[:, :, NJ:NJ + 1], [Him.b], [Him.b])
                    ACT(y2T[:, fc, :], yacc[:], AF.Gelu_apprx_tanh, [yacc.b], [y2T.b])
                while att_st["i"] < len(att_steps) or att_st["pend"]:
                    att_finalize()
                    att_issue()
                wg = SB(ph, "wg", [128, 4, 512], BF16)
                bg = SB(ph, "bg", [128, 4], F32)
                gs = [SB(ph, "gs%d" % i, [128, 512], F32) for i in range(2)]
                yst = [SB(ph, "yst%d" % i, [128, 512], BF16) for i in range(2)]
                dma("pool", wg[:], s5_wglu[l].rearrange("(k p) n -> p k n", p=128), w=[wg.b])
                dma("sp", bg[:], s5_bgT[:, l, :], w=[bg.b])
                for co in range(4):
                    for ti, (t0, sz) in enumerate(TILES):
                        ys = yst[ti % 2]
                        ps = nextps()
                        for k in range(4):
                            MM(ps[:, 0:sz], wg[:, k, co * 128:(co + 1) * 128], y2T[:, k, t0:t0 + sz], k == 0, k == 3, [wg.b, y2T.b], [ps.b])
                        g_ = gs[ti % 2]
                        ACT(g_[:, 0:sz], ps[:, 0:sz], AF.Sigmoid, [ps.b, bg.b], [g_.b], bias=bg[:, co:co + 1])
                        TT("dve", ys[:, 0:sz], y2T[:, co, t0:t0 + sz], g_[:, 0:sz], ALU.mult, [y2T.b, g_.b], [ys.b])
                        dma("sp", yT[co, :, t0:t0 + sz], ys[:, 0:sz], r=[ys.b], w=[d_y])
                P.barrier()
                nlong[0] = 2
            if stop_after == "S5":
                break
            if stop_after == "ATT":
                break
            with ExitStack() as ph:
                hgm = SB(ph, "hgm", [32, 2, 128], F32)
                rmask = SB(ph, "rmask", [128, 512], F32)
                hng = SB(ph, "hng", [128, 4], F32)
                dma("sp", hgm[:], c_hgmask[:, :, :], w=[hgm.b])
                dma("sp", rmask[:], c_rmask[:, :], w=[rmask.b])
                dma("sp", hng[:], hg_ngT[:, l, :], w=[hng.b])
                D2 = range(2)
                Sf = [SB(ph, "Sf%d" % d, [128, 4, 128], F32) for d in D2]
                Sb = [SB(ph, "Sb%d" % d, [128, 4, 128], BF16) for d in D2]
                lfts = [SB(ph, "lft%d" % d, [128, 4, 512], F32) for d in D2]
                kkts = [SB(ph, "kkt%d" % d, [128, 4, 512], BF16) for d in D2]
                hqts = [SB(ph, "hqt%d" % d, [128, 4, 512], BF16) for d in D2]
                vchs = [SB(ph, "vch%d" % d, [32, 16, 512], BF16) for d in D2]
                bts = [SB(ph, "hbt%d" % d, [128, 4, 512], F32) for d in D2]
                e1s = [SB(ph, "he1%d" % d, [128, 4, 512], F32) for d in D2]
                e2s = [SB(ph, "he2%d" % d, [128, 4, 512], F32) for d in D2]
                qts = [SB(ph, "hqt_%d" % d, [128, 4, 512], BF16) for d in D2]
                kts = [SB(ph, "hkt_%d" % d, [128, 4, 512], BF16) for d in D2]
                khs = [SB(ph, "hkh_%d" % d, [128, 4, 512], BF16) for d in D2]
                ots = [SB(ph, "hot%d" % d, [128, 4, 512], F32) for d in D2]
                attm = [[SB(ph, "attm%d_%d" % (d, i), [32, 128], BF16) for i in range(2)] for d in D2]
                ktm = [[SB(ph, "ktm%d_%d" % (d, i), [32, 512], BF16) for i in range(2)] for d in D2]
                orders = [list(range(NTI)), [0] + list(range(NTI - 1, 0, -1))]
                for d in D2:
                    MSET("pool", Sf[d][:], 0.0, [Sf[d].b])
                    MSET("pool", Sb[d][:], 0.0, [Sb[d].b])
                ich = [0, 0]

                def hg_setup(d, ti):
                    t0, sz = TILES[ti]
                    nch = sz // 32
                    lft, kkt, hqt, vch, bt, e1, e2, qt, kt, kh = lfts[d], kkts[d], hqts[d], vchs[d], bts[d], e1s[d], e2s[d], qts[d], kts[d], khs[d]
                    dma("sp", lft[:, :, 0:sz], lf[d, :, :, t0:t0 + sz].rearrange("h p t -> p h t"), r=[d_z], w=[lft.b])
                    dma("sp", kkt[:, :, 0:sz], kk[d, :, :, t0:t0 + sz].rearrange("h p t -> p h t"), r=[d_z], w=[kkt.b])
                    dma("sp", hqt[:, :, 0:sz], hgq[:, :, t0:t0 + sz].rearrange("h p t -> p h t"), r=[d_z], w=[hqt.b])
                    dma("sp", vch[:, 0:nch, :], vtm[t0:t0 + sz, 128:640].rearrange("(c p) f -> p c f", p=32), r=[d_z], w=[vch.b])
                    for h in range(4):
                        if d == 0:
                            SCAN(bt[:, h, 0:sz], rmask[:, 0:sz], lft[:, h, 0:sz], 0.0, [rmask.b, lft.b], [bt.b])
                        else:
                            SCAN(bt[:, h, 0:sz][:, ::-1], rmask[:, 0:sz], lft[:, h, 0:sz][:, ::-1], 0.0, [rmask.b, lft.b], [bt.b])
                    jl0 = 31 if d == 0 else 0
                    b4 = bt[:, :, 0:sz].rearrange("p h (c t) -> p h c t", t=32)
                    TT("dve", e2[:, :, 0:sz].rearrange("p h (c t) -> p h c t", t=32), b4[:, :, :, jl0:jl0 + 1].to_broadcast([128, 4, nch, 32]), b4,
                       ALU.subtract, [bt.b], [e2.b])
                    ACT(e1[:, :, 0:sz], bt[:, :, 0:sz], AF.Exp, [bt.b], [e1.b], scale=-1.0)
                    ACT(e2[:, :, 0:sz], e2[:, :, 0:sz], AF.Exp, [e2.b], [e2.b])
                    ACT(bt[:, :, 0:sz], bt[:, :, 0:sz], AF.Exp, [bt.b], [bt.b])
                    TT("dve", qt[:, :, 0:sz], hqt[:, :, 0:sz], bt[:, :, 0:sz], ALU.mult, [hqt.b, bt.b], [qt.b])
                    TT("pool", kt[:, :, 0:sz], kkt[:, :, 0:sz], e1[:, :, 0:sz], ALU.mult, [kkt.b, e1.b], [kt.b])
                    TT("dve", kh[:, :, 0:sz], kkt[:, :, 0:sz], e2[:, :, 0:sz], ALU.mult, [kkt.b, e2.b], [kh.b])

                def hg_chunk(d, ci):
                    vch, bt, qt, kt, kh, ot = vchs[d], bts[d], qts[d], kts[d], khs[d], ots[d]
                    c0 = ci * 32
                    am, km = attm[d][ich[d] % 2], ktm[d][ich[d] % 2]
                    ich[d] += 1
                    psA = nextps()
                    for h in range(4):
                        MM(psA[0:32, h * 32:(h + 1) * 32], kt[:, h, c0:c0 + 32], qt[:, h, c0:c0 + 32], True, True, [kt.b, qt.b], [psA.b])
                    TT("dve", am[:], psA[0:32, 0:128], hgm[:, d, :], ALU.mult, [psA.b, hgm.b], [am.b])
                    psT = nextps()
                    psTb = psT[:].bitcast(BF16)
                    for h in range(4):
                        TR(psTb[0:32, h * 128:(h + 1) * 128], kh[:, h, c0:c0 + 32], ident_b[:], [kh.b, ident_b.b], [psT.b])
                    CP("act", km[:], psTb[0:32, 0:512], [psT.b], [km.b])
                    psO = nextps()
                    for h in range(4):
                        MM(psO[:, h * 32:(h + 1) * 32], vch[:, ci, h * 128:(h + 1) * 128], am[:, h * 32:(h + 1) * 32], True, False, [vch.b, am.b], [psO.b])
                        MM(psO[:, h * 32:(h + 1) * 32], Sb[d][:, h, :], qt[:, h, c0:c0 + 32], False, True, [Sb[d].b, qt.b], [psO.b])
                    CP("act", ot[:, :, c0:c0 + 32], psO[:, 0:128].rearrange("p (h t) -> p h t", h=4), [psO.b], [ot.b])
                    psS = nextps()
                    for h in range(4):
                        MM(psS[:, h * 128:(h + 1) * 128], km[:, h * 128:(h + 1) * 128], vch[:, ci, h * 128:(h + 1) * 128], True, True, [km.b, vch.b], [psS.b])
                    jl = c0 + 31 if d == 0 else c0
                    for h in range(4):
                        STT(Sf[d][:, h, :], Sf[d][:, h, :], bt[:, h, jl:jl + 1], psS[:, h * 128:(h + 1) * 128], ALU.mult, ALU.add, [Sf[d].b, bt.b, psS.b], [Sf[d].b])
                    CP("act", Sb[d][:], Sf[d][:], [Sf[d].b], [Sb[d].b])

                obw = ofw2
                for oi in range(NTI):
                    for d in D2:
                        hg_setup(d, orders[d][oi])
                    nch = TILES[orders[0][oi]][1] // 32
                    for k_ in range(nch):
                        for d in D2:
                            hg_chunk(d, k_ if d == 0 else nch - 1 - k_)
                    for d in D2:
                        t0, sz = TILES[orders[d][oi]]
                        dma("pool", (ofw if d == 0 else obw)[:, :, t0:t0 + sz].rearrange("h p t -> p h t"), ots[d][:, :, 0:sz], r=[ots[d].b], w=[d_of])
                sq = SB(ph, "hsq", [128, 4, 512], BF16)
                rs = SB(ph, "hrs", [128, 4, 512], F32)
                hggts = [SB(ph, "hggt%d" % i, [128, 4, 512], BF16) for i in range(2)]
                ysts = [SB(ph, "hyst%d" % i, [128, 4, 512], BF16) for i in range(2)]
                for ti, (t0, sz) in enumerate(TILES):
                    oa, obt, yh = ots[ti % 2], e1s[ti % 2], e2s[ti % 2]
                    hggt, yst = hggts[ti % 2], ysts[ti % 2]
                    dma("sp", oa[:, :, 0:sz], ofw[:, :, t0:t0 + sz].rearrange("h p t -> p h t"), r=[d_of], w=[oa.b])
                    dma("sp", obt[:, :, 0:sz], obw[:, :, t0:t0 + sz].rearrange("h p t -> p h t"), r=[d_of], w=[obt.b])
                    dma("sp", hggt[:, :, 0:sz], hgg[:, :, t0:t0 + sz].rearrange("h p t -> p h t"), r=[d_z], w=[hggt.b])
                    TT("pool", oa[:, :, 0:sz], oa[:, :, 0:sz], obt[:, :, 0:sz], ALU.add, [oa.b, obt.b], [oa.b])
                    ACT(sq[:, :, 0:sz], oa[:, :, 0:sz], AF.Square, [oa.b], [sq.b])
                    for h in range(4):
                        ps = nextps()
                        MM(ps[:, 0:sz], ones_b[:], sq[:, h, 0:sz], True, True, [ones_b.b, sq.b], [ps.b])
                        ACT(rs[:, h, 0:sz], ps[:, 0:sz], AF.Ln, [ps.b, epsb.b], [rs.b], bias=epsb[:, 0:1], scale=1.0 / 128)
                    ACT(rs[:, :, 0:sz], rs[:, :, 0:sz], AF.Exp, [rs.b], [rs.b], scale=-0.5)
                    for h in range(4):
                        STT(yh[:, h, 0:sz], oa[:, h, 0:sz], hng[:, h:h + 1], rs[:, h, 0:sz], ALU.mult, ALU.mult, [oa.b, hng.b, rs.b], [yh.b])
                    TT("pool", yst[:, :, 0:sz], yh[:, :, 0:sz], hggt[:, :, 0:sz], ALU.mult, [yh.b, hggt.b], [yst.b])
                    dma("pool", yT[8:12, :, t0:t0 + sz].rearrange("h p t -> p h t"), yst[:, :, 0:sz], r=[yst.b], w=[d_y])
                P.barrier()
            if stop_after == "HG":
                break
            with ExitStack() as ph:
                wbr = SB(ph, "wbr", [128, 12, DM], BF16)
                wou = SB(ph, "wou", [128, KC, DM], BF16)
                for n in range(3):
                    dma("pool", wbr[:, n * 4:(n + 1) * 4, :], w_branch[l, n].rearrange("(k p) d -> p k d", p=128), w=[wbr.b])
                dma("pool", wou[:], w_out[l].rearrange("(k p) d -> p k d", p=128), w=[wou.b])
                yts = [SB(ph, "yt%d" % i, [128, 12, 512], BF16) for i in range(2)]
                gts = [SB(ph, "gt%d" % i, [128, 24, 512], BF16) for i in range(2)]
                xt = SB(ph, "mxt", [128, KC, 512], F32)
                mots = [SB(ph, "mot%d" % i, [128, KC, 512], F32) for i in range(2)]
                mt = SB(ph, "mmt", [128, KC, 512], BF16)
                macc = [SB(ph, "macc%d" % i, [128, 512], F32) for i in range(2)]
                mtmp = [SB(ph, "mtmp%d" % i, [128, 512], F32) for i in range(2)]
                sq = SB(ph, "msq", [128, KC, 512], BF16)
                rs = SB(ph, "mrs", [128, 512], F32)
                for ti, (t0, sz) in enumerate(TILES):
                    yt, gt = yts[ti % 2], gts[ti % 2]
                    ot = mots[ti % 2]
                    dma("sp", yt[:, :, 0:sz], yT[:, :, t0:t0 + sz].rearrange("c p t -> p c t"), r=[d_y], w=[yt.b])
                    dma("sp", gt[:, :, 0:sz], gat[:, :, t0:t0 + sz].rearrange("c p t -> p c t"), r=[d_z], w=[gt.b])
                    dma("sp", xt[:, :, 0:sz], xT[:, :, t0:t0 + sz].rearrange("k p t -> p k t"), r=[d_xT], w=[xt.b])
                    for dc in range(KC):
                        ma, mp_ = macc[dc % 2], mtmp[dc % 2]
                        for n in range(3):
                            ps = nextps()
                            for k in range(4):
                                MM(ps[:, 0:sz], wbr[:, n * 4 + k, dc * 128:(dc + 1) * 128], yt[:, n * 4 + k, 0:sz], k == 0, k == 3, [wbr.b, yt.b], [ps.b])
                            g_ = gt[:, n * 8 + dc, 0:sz]
                            if n == 0:
                                TT("dve", ma[:, 0:sz], ps[:, 0:sz], g_, ALU.mult, [ps.b, gt.b], [ma.b])
                            elif n == 1:
                                TT("dve", mp_[:, 0:sz], ps[:, 0:sz], g_, ALU.mult, [ps.b, gt.b], [mp_.b])
                                TT("pool", ma[:, 0:sz], ma[:, 0:sz], mp_[:, 0:sz], ALU.add, [ma.b, mp_.b], [ma.b])
                            else:
                                TT("dve", mp_[:, 0:sz], ps[:, 0:sz], g_, ALU.mult, [ps.b, gt.b], [mp_.b])
                                TT("pool", mt[:, dc, 0:sz], ma[:, 0:sz], mp_[:, 0:sz], ALU.add, [ma.b, mp_.b], [mt.b])
                    for dc in range(KC):
                        ps = nextps()
                        for kc in range(KC):
                            MM(ps[:, 0:sz], wou[:, kc, dc * 128:(dc + 1) * 128], mt[:, kc, 0:sz], kc == 0, kc == KC - 1, [wou.b, mt.b], [ps.b])
                        CP("act", ot[:, dc, 0:sz], ps[:, 0:sz], [ps.b], [ot.b])
                    epilogue(ot, xt, G1, ti, t0, sz, sq, rs, None)
                P.barrier()
            if stop_after == "MIX":
                break
            with ExitStack() as ph:
                hT = SB(ph, "hT2", [128, KC, NT], BF16, nb=NTI)
                with ExitStack() as ph1:
                    norm_phase(ph1, l, A2, 24, hT)
                    P.barrier()
                NU = NT + 3
                Uas = [SB(ph, "Ua%d" % i, [128, NU], F32) for i in range(2)]
                Ugs = [SB(ph, "Ug%d" % i, [128, NU], F32) for i in range(2)]
                Ya = SB(ph, "Ya", [128, NU], F32)
                Yg = SB(ph, "Yg", [128, NU], F32)
                ast = [SB(ph, "ast%d" % i, [128, NU], BF16) for i in range(2)]
                cw = SB(ph, "cw", [128, 44, 3], F32)
                cb = SB(ph, "cb", [128, 44], F32)
                wua = [SB(ph, "wua%d" % i, [128, KC, 128], BF16) for i in range(2)]
                wug = [SB(ph, "wug%d" % i, [128, KC, 128], BF16) for i in range(2)]
                dma("sp", cw[:], convT[:, l, :, :], w=[cw.b])
                dma("sp", cb[:], convbT[:, l, :], w=[cb.b])
                for i_ in range(2):
                    MSET("pool", Uas[i_][:], 0.0, [Uas[i_].b])
                    MSET("pool", Ugs[i_][:], 0.0, [Ugs[i_].b])

                def ucol(t):
                    return t + 1 if t < LC else t + 2
                NY = NT + 1
                for j in range(NFC):
                    wa, wg_ = wua[j % 2], wug[j % 2]
                    Ua, Ug = Uas[j % 2], Ugs[j % 2]
                    dma("pool", wa[:], w_up[l, :, j * 128:(j + 1) * 128].rearrange("(k p) n -> p k n", p=128), w=[wa.b])
                    dma("pool", wg_[:], w_up[l, :, FFN + j * 128:FFN + (j + 1) * 128].rearrange("(k p) n -> p k n", p=128), w=[wg_.b])
                    for (wt, U) in ((wa, Ua), (wg_, Ug)):
                        for ti, (t0, sz) in enumerate(TILES):
                            ps = nextps()
                            for kc in range(KC):
                                MM(ps[:, 0:sz], wt[:, kc, :], hT[:, kc, t0:t0 + sz], kc == 0, kc == KC - 1, [wt.b, hT.bs[ti]], [ps.b])
                            CP("act", U[:, ucol(t0):ucol(t0) + sz], ps[:, 0:sz], [ps.b], [U.b])
                    for (U, Y, cj) in ((Ua, Ya, j), (Ug, Yg, NFC + j)):
                        ACT(Y[:, 0:NY], U[:, 0:NY], AF.Identity, [U.b, cw.b, cb.b], [Y.b], bias=cb[:, cj:cj + 1], scale=cw[:, cj, 0:1])
                        STT(Y[:, 0:NY], U[:, 1:NY + 1], cw[:, cj, 1:2], Y[:, 0:NY], ALU.mult, ALU.add, [U.b, cw.b, Y.b], [Y.b])
                        STT(Y[:, 0:NY], U[:, 2:NY + 2], cw[:, cj, 2:3], Y[:, 0:NY], ALU.mult, ALU.add, [U.b, cw.b, Y.b], [Y.b])
                    a_ = ast[j % 2]
                    ACT(Ya[:, 0:NY], Ya[:, 0:NY], AF.Silu, [Ya.b], [Ya.b])
                    TT("dve", a_[:, 0:NY], Ya[:, 0:NY], Yg[:, 0:NY], ALU.mult, [Ya.b, Yg.b], [a_.b])
                    dma("sp", actT[j, :, 0:LC], a_[:, 0:LC], r=[a_.b], w=[d_act])
                    dma("sp", actT[j, :, LC:NT], a_[:, LC + 1:NT + 1], r=[a_.b], w=[d_act])
                P.barrier()
            with ExitStack() as ph:
                wdn = SB(ph, "wdn", [128, NFC, DM], BF16)
                dma("pool", wdn[:, 0:11, :], w_down[l, 0:11 * 128, :].rearrange("(k p) d -> p k d", p=128), w=[wdn.b])
                dma("pool", wdn[:, 11:22, :], w_down[l, 11 * 128:22 * 128, :].rearrange("(k p) d -> p k d", p=128), w=[wdn.b])
                ats = [SB(ph, "at%d" % i, [128, NFC, 512], BF16) for i in range(2)]
                xt = SB(ph, "fxt", [128, KC, 512], F32)
                fots = [SB(ph, "fot%d" % i, [128, KC, 512], F32) for i in range(2)]
                sq = SB(ph, "fsq", [128, KC, 512], BF16)
                rs = SB(ph, "frs", [128, 512], F32)
                for ti, (t0, sz) in enumerate(TILES):
                    at = ats[ti % 2]
                    ot = fots[ti % 2]
                    dma("sp", at[:, :, 0:sz], actT[:, :, t0:t0 + sz].rearrange("c p t -> p c t"), r=[d_act], w=[at.b])
                    dma("sp", xt[:, :, 0:sz], xT[:, :, t0:t0 + sz].rearrange("k p t -> p k t"), r=[d_xT], w=[xt.b])
                    for dc in range(KC):
                        ps = nextps()
                        for k in range(NFC):
                            MM(ps[:, 0:sz], wdn[:, k, dc * 128:(dc + 1) * 128], at[:, k, 0:sz], k == 0, k == NFC - 1, [wdn.b, at.b], [ps.b])
                        CP("act", ot[:, dc, 0:sz], ps[:, 0:sz], [ps.b], [ot.b])
                    epilogue(ot, xt, G2, ti, t0, sz, sq, rs, None)
                P.barrier()
        if stop_after is None:
            with ExitStack() as ph:
                xtt = [SB(ph, "fxtt%d" % i, [128, KC, 128], F32) for i in range(2)]
                orow = [SB(ph, "orow%d" % i, [128, DM], F32) for i in range(2)]
                for tb in range(LC // 128, NB):
                    a, o_ = xtt[tb % 2], orow[tb % 2]
                    dma("sp", a[:], xT[:, :, tb * 128:(tb + 1) * 128].rearrange("k p t -> p k t"), r=[d_xT], w=[a.b])
                    pa, pb = nextps(), nextps()
                    for kc in range(KC):
                        pp_ = pa if kc < 4 else pb
                        TR(pp_[:, (kc % 4) * 128:(kc % 4 + 1) * 128], a[:, kc, :], ident_f[:], [a.b, ident_f.b], [pp_.b])
                    CP("act", o_[:, 0:512], pa[:, :], [pa.b], [o_.b])
                    CP("dve", o_[:, 512:1024], pb[:, :], [pb.b], [o_.b])
                    dma("pool", out[tb * 128 - LC:(tb + 1) * 128 - LC, :], o_[:], r=[o_.b])
        P.barrier()
    return nc


def prep_shared(inp, cfg):
    L, LC, DEPTH = cfg["L"], cfg["LC"], cfg["DEPTH"]
    NT = L + LC
    f = lambda a: np.ascontiguousarray(np.asarray(a, dtype=np.float32))
    sh = {}
    sh["w_mod"] = f(inp["w_mod"][:DEPTH])
    sh["b_modT"] = f(np.asarray(inp["b_mod"])[:DEPTH].reshape(DEPTH, 48, 128).transpose(2, 0, 1))
    sh["norm_gT"] = f(np.asarray(inp["norm_g"])[:DEPTH].reshape(DEPTH, 4, KC, 128).transpose(3, 0, 1, 2))
    w_in = np.asarray(inp["w_in"])[:DEPTH]
    sh["w_in"] = f(w_in)
    idx = []
    for h in range(8):
        idx += [C_Q + h * 64 + (d + 32) % 64 for d in range(64)]
    for h in range(2):
        idx += [C_K + h * 64 + (d + 32) % 64 for d in range(64)]
    sh["w_rot"] = f(w_in[:, :, np.array(idx)])
    rows = L // 64
    row = np.repeat(np.arange(rows, dtype=np.float32), 64)
    col = np.tile(np.arange(64, dtype=np.float32), rows)
    inv = (10000.0 ** (-np.arange(16, dtype=np.float32) / 16)).astype(np.float32)
    ang = np.concatenate([row[:, None] * inv, col[:, None] * inv], axis=-1)
    cos, sin = np.cos(ang).T, np.sin(ang).T
    C = np.ones((64, NT), np.float32)
    S = np.zeros((64, NT), np.float32)
    C[0:32, LC:] = cos
    C[32:64, LC:] = cos
    S[0:32, LC:] = -sin
    S[32:64, LC:] = sin
    sh["ropeC"], sh["ropeS"] = np.concatenate([C, C], 0), np.concatenate([S, S], 0)
    def st(a):
        a = np.asarray(a)[:DEPTH].reshape(DEPTH, 2, 16, 2, 64)
        return f(a.transpose(3, 4, 0, 1, 2).reshape(128, DEPTH, 2, 16))
    sh["s5_lr"] = st(inp["s5_lam_re"])
    sh["s5_li"] = st(inp["s5_lam_im"])
    ldt = np.asarray(inp["s5_log_dt"])[:DEPTH]
    sh["s5_ldt"] = st(np.repeat(ldt[..., None], 64, axis=-1))
    def padB(b):
        b = np.asarray(b)[:DEPTH]
        o = np.zeros((DEPTH, 16, 128, 128), np.float32)
        for pr in range(16):
            for g2 in range(2):
                s0 = (pr % 4) * 32 + g2 * 16
                o[:, pr, g2 * 64:(g2 + 1) * 64, s0:s0 + 16] = b[:, 2 * pr + g2]
        return o
    def padC(c):
        c = np.asarray(c)[:DEPTH]
        o = np.zeros((DEPTH, 16, 128, 128), np.float32)
        for pr in range(16):
            for g2 in range(2):
                s0 = (pr % 4) * 32 + g2 * 16
                o[:, pr, g2 * 64:(g2 + 1) * 64, s0:s0 + 16] = c[:, 2 * pr + g2].transpose(0, 2, 1)
        return o
    sh["s5_Bre"], sh["s5_Bim"] = padB(inp["s5_b_re"]), padB(inp["s5_b_im"])
    sh["s5_Cre"], sh["s5_Cim"] = padC(inp["s5_c_re"]), padC(inp["s5_c_im"])
    sh["s5_dT"] = f(np.asarray(inp["s5_d"])[:DEPTH].reshape(DEPTH, 4, 128).transpose(2, 0, 1))
    sh["s5_bgT"] = f(np.asarray(inp["s5_b_glu"])[:DEPTH].reshape(DEPTH, 4, 128).transpose(2, 0, 1))
    sh["s5_wglu"] = f(inp["s5_w_glu"][:DEPTH])
    sh["sinkrow"] = f(np.repeat(np.asarray(inp["att_sink"])[:DEPTH], 128, axis=-1).reshape(DEPTH, 1, 1024))
    sh["hg_lbT"] = f(np.asarray(inp["hg_lb_logits"])[:DEPTH].reshape(DEPTH, 2, 4, 128).transpose(3, 0, 1, 2).reshape(128, DEPTH, 8))
    sh["hg_ngT"] = f(np.asarray(inp["hg_norm_g"])[:DEPTH].reshape(DEPTH, 4, 128).transpose(2, 0, 1))
    sh["w_branch"] = f(inp["w_branch"][:DEPTH])
    sh["w_out"] = f(inp["w_out"][:DEPTH])
    sh["w_up"] = f(inp["ffn_w_up"][:DEPTH])
    sh["convT"] = f(np.asarray(inp["ffn_conv_w"])[:DEPTH].reshape(DEPTH, 3, 44, 128).transpose(3, 0, 2, 1))
    sh["convbT"] = f(np.asarray(inp["ffn_conv_b"])[:DEPTH].reshape(DEPTH, 44, 128).transpose(2, 0, 1))
    sh["w_down"] = f(inp["ffn_w_down"][:DEPTH])
    sh["c_ident"] = np.eye(128, dtype=np.float32)
    k = np.arange(128)[:, None]
    q = np.tile(np.arange(128), 4)[None, :]
    sh["c_maskP"] = np.where(k >= q, 0.0, -30000.0).astype(np.float32)
    sh["c_maskN"] = np.where(k <= q, 0.0, -30000.0).astype(np.float32)
    io = np.zeros((128, 2, 512), np.float32)
    io[:, 0, :] = np.arange(1, 513, dtype=np.float32)[None]
    io[:, 1, :] = (512 - np.arange(512, dtype=np.float32))[None]
    sh["c_iota"] = io
    s_ = np.arange(32)[:, None]
    t_ = np.tile(np.arange(32), 4)[None, :]
    hm = np.zeros((32, 2, 128), np.float32)
    hm[:, 0, :] = (s_ <= t_)
    hm[:, 1, :] = (s_ >= t_)
    sh["c_hgmask"] = hm
    rm = np.ones((128, 512), np.float32)
    rm[:, ::32] = 0.0
    sh["c_rmask"] = rm
    return sh


def prep_core(inp, b, cfg, sh):
    L, LC = cfg["L"], cfg["LC"]
    m = dict(sh)
    m["x"] = np.ascontiguousarray(np.asarray(inp["x"], dtype=np.float32)[b, :L])
    m["ctx"] = np.ascontiguousarray(np.asarray(inp["ctx"], dtype=np.float32)[b, :LC])
    cT = np.stack([np.asarray(inp["c"], dtype=np.float32)[b], np.asarray(inp["c_ctx"], dtype=np.float32)], axis=-1)
    m["cT"] = np.ascontiguousarray(cT.reshape(KC, 128, 2).transpose(1, 0, 2))
    return m


_NC_CACHE = {}


def kernel(**inputs):
    cfg = FULL
    key = "full"
    if key not in _NC_CACHE:
        _NC_CACHE[key] = build(cfg)
    nc = _NC_CACHE[key]
    sh = prep_shared(inputs, cfg)
    in_maps = [prep_core(inputs, b, cfg, sh) for b in range(8)]
    res = run_bass_kernel_spmd(nc, in_maps, core_ids=list(range(8)))
    return np.stack([np.asarray(r["out"], dtype=np.float32) for r in res.results], axis=0)
```
